# Optimizing a Trainium2 kernel written in Bass

```python
import jax
import jax.numpy as jnp
from jax import lax
import numpy as np

D_MODEL = 1024
BATCH = 8
SEQ = 4096
DEPTH = 2

GRID_W = 64
CTX_LEN = 256

F32 = jnp.float32

SG_WIDTH = D_MODEL // 4
SG_GROUPS = 4
SG_GDIM = SG_WIDTH // SG_GROUPS
SG_CHUNK = 128
MLA_HEADS = 8
MLA_NOPE = 64
MLA_ROPE = 32
MLA_VDIM = 64
MLA_WIDTH = MLA_HEADS * MLA_VDIM
Q_LORA = 384
KV_LORA = 256
ROPE_AXIS = MLA_ROPE // 2
ROPE_BASE = 10000.0
ATTN_BLOCK = 128
MLA_SCALE = (MLA_NOPE + MLA_ROPE) ** -0.5
ML_HEADS = 4
ML_DIM = 64
ML_WIDTH = ML_HEADS * ML_DIM
ML_CHUNK = 64
ML_CONV = 5
N_SG = 2 * SG_WIDTH
N_MLA = Q_LORA + KV_LORA + MLA_ROPE
N_ML = 4 * ML_WIDTH + 4 * ML_HEADS
N_IN = N_SG + N_MLA + N_ML
MIX_WIDTH = SG_WIDTH + MLA_WIDTH + ML_WIDTH
N_EXPERTS = 16
EXPERT_FF = 1024
EC_CAPACITY = 2
ALPHA = (2 * DEPTH) ** 0.25
BETA = (8 * DEPTH) ** -0.25
LN_EPS = 1e-6

kernel_name = 'hybrid_diffusion_ec_moe_trunk'


def plain_norm(x):
    x32 = x.astype(F32)
    mu = jnp.mean(x32, -1, keepdims=True)
    var = jnp.mean(jnp.square(x32 - mu), -1, keepdims=True)
    return ((x32 - mu) * lax.rsqrt(var + LN_EPS)).astype(x.dtype)


def layer_norm(x, g, b):
    return plain_norm(x) * g + b


def rms_norm(x, g):
    x32 = x.astype(F32)
    y = x32 * lax.rsqrt(jnp.mean(jnp.square(x32), -1, keepdims=True) + LN_EPS)
    return y.astype(x.dtype) * g


def modulate(x, shift, scale):
    return plain_norm(x) * (1 + scale) + shift


def axial_rope_tables(rows):
    row = jnp.repeat(jnp.arange(rows, dtype=F32), GRID_W)
    col = jnp.tile(jnp.arange(GRID_W, dtype=F32), rows)
    inv = ROPE_BASE ** (-jnp.arange(ROPE_AXIS // 2, dtype=F32) * 2.0 / ROPE_AXIS)
    ang = jnp.concatenate([row[:, None] * inv, col[:, None] * inv], axis=-1)
    return jnp.cos(ang), jnp.sin(ang)


def apply_axial_rope(x, cos, sin):
    half = ROPE_AXIS // 2
    parts = []
    for a in range(2):
        z = x[..., a * ROPE_AXIS:(a + 1) * ROPE_AXIS]
        cs = cos[:, a * half:(a + 1) * half].astype(x.dtype)
        sn = sin[:, a * half:(a + 1) * half].astype(x.dtype)
        z1, z2 = z[..., :half], z[..., half:]
        parts += [z1 * cs - z2 * sn, z2 * cs + z1 * sn]
    return jnp.concatenate(parts, axis=-1)


def chunk_spatial_gating(z, ln_g, ln_b, w_s, b_s):
    B, T, _ = z.shape
    z = jax.nn.gelu(z)
    u, v = z[..., :SG_WIDTH], z[..., SG_WIDTH:]
    v = layer_norm(v, ln_g, ln_b).reshape(B, T // SG_CHUNK, SG_CHUNK, SG_GROUPS, SG_GDIM)
    mixed = jnp.einsum('gts,bnsgc->bntgc', w_s, v) + b_s.T[:, :, None]
    return u * mixed.reshape(B, T, SG_WIDTH)


def mla_q(z, q_norm_g, w_uq, rope):
    B, T, _ = z.shape
    c_q = rms_norm(z[..., :Q_LORA], q_norm_g)
    q = (c_q @ w_uq).reshape(B, T, MLA_HEADS, MLA_NOPE + MLA_ROPE).transpose(0, 2, 1, 3)
    q_nope, q_rope = q[..., :MLA_NOPE], q[..., MLA_NOPE:]
    if rope is not None:
        q_rope = apply_axial_rope(q_rope, rope[0], rope[1])
    return q_nope, q_rope


def mla_kv(z, kv_norm_g, w_ukv, rope):
    B, T, _ = z.shape
    c_kv = rms_norm(z[..., Q_LORA:Q_LORA + KV_LORA], kv_norm_g)
    k_rope = z[..., Q_LORA + KV_LORA:]
    kv = (c_kv @ w_ukv).reshape(B, T, MLA_HEADS, MLA_NOPE + MLA_VDIM).transpose(0, 2, 1, 3)
    k_nope, v = kv[..., :MLA_NOPE], kv[..., MLA_NOPE:]
    if rope is not None:
        k_rope = apply_axial_rope(k_rope, rope[0], rope[1])
    return k_nope, k_rope, v


def attend(q_nope, q_rope, k_nope, k_rope, v):
    s = jnp.einsum('bhqd,bhkd->bhqk', q_nope, k_nope) + jnp.einsum('bhqd,bkd->bhqk', q_rope, k_rope)
    p = jax.nn.softmax(s.astype(F32) * MLA_SCALE, axis=-1).astype(v.dtype)
    return jnp.einsum('bhqk,bhkd->bhqd', p, v)


def blocked_attention(q_nope, q_rope, k_nope, k_rope, v):
    B, H, T, _ = q_nope.shape
    nb = T // ATTN_BLOCK
    qn = q_nope.reshape(B, H, nb, ATTN_BLOCK, MLA_NOPE).transpose(2, 0, 1, 3, 4)
    qr = q_rope.reshape(B, H, nb, ATTN_BLOCK, MLA_ROPE).transpose(2, 0, 1, 3, 4)
    out = lax.map(lambda qs: attend(qs[0], qs[1], k_nope, k_rope, v), (qn, qr))
    return out.transpose(1, 2, 0, 3, 4).reshape(B, H, T, MLA_VDIM)


def heads_to_width(h):
    B, H, T, d = h.shape
    return h.transpose(0, 2, 1, 3).reshape(B, T, H * d)


def depthwise_conv(x, w, b):
    pad = (ML_CONV - 1) // 2
    y = lax.conv_general_dilated(x, w[:, None, :], window_strides=(1,), padding=[(pad, pad)],
                                 dimension_numbers=('NWC', 'WIO', 'NWC'),
                                 feature_group_count=x.shape[-1])
    return y + b


def mlstm_inputs(z, conv_w, conv_b, f_bias):
    B, T, _ = z.shape
    W = ML_WIDTH
    qk = jax.nn.silu(depthwise_conv(z[..., :2 * W], conv_w, conv_b))
    heads = lambda a: a.reshape(B, T, ML_HEADS, ML_DIM).transpose(0, 2, 1, 3).astype(F32)
    q = heads(qk[..., :W])
    k = heads(qk[..., W:]) * (ML_DIM ** -0.5)
    v = heads(z[..., 2 * W:3 * W])
    o = z[..., 3 * W:4 * W]
    gates = z[..., 4 * W:].astype(F32).reshape(B, T, 2, 2, ML_HEADS)
    log_i = gates[:, :, :, 0].transpose(2, 0, 3, 1)
    log_f = jax.nn.log_sigmoid(gates[:, :, :, 1] + f_bias.astype(F32)).transpose(2, 0, 3, 1)
    return q, k, v, o, log_i, log_f


def mlstm_chunkwise(q, k, v, log_i, log_f, state, with_output):
    B, H, T, d = k.shape
    L = ML_CHUNK
    nc = T // L
    k = k.reshape(B, H, nc, L, d)
    v = v.reshape(B, H, nc, L, d)
    li = log_i.reshape(B, H, nc, L)
    b = jnp.cumsum(log_f.reshape(B, H, nc, L), axis=-1)
    b_end = b[..., -1]
    g = b_end[..., None] - b + li
    m_loc = jnp.max(g, axis=-1)
    w = jnp.exp(g - m_loc[..., None])
    C_loc = jnp.einsum('bhnl,bhnli,bhnlj->bhnij', w, v, k)
    n_loc = jnp.einsum('bhnl,bhnlj->bhnj', w, k)

    def step(carry, inp):
        C, n, m = carry
        be, ml, Cl, nl = inp
        m_new = jnp.maximum(be + m, ml)
        a = jnp.exp(be + m - m_new)
        s = jnp.exp(ml - m_new)
        new = (a[..., None, None] * C + s[..., None, None] * Cl, a[..., None] * n + s[..., None] * nl, m_new)
        return new, (C, n, m)

    to_front = lambda a: jnp.moveaxis(a, 2, 0)
    final, starts = lax.scan(step, state, (to_front(b_end), to_front(m_loc), to_front(C_loc), to_front(n_loc)))
    if not with_output:
        return None, final
    C0, n0, m0 = [jnp.moveaxis(a, 0, 2) for a in starts]
    q = q.reshape(B, H, nc, L, d)
    lower = jnp.tril(jnp.ones((L, L), dtype=bool))
    dmat = jnp.where(lower, b[..., :, None] - b[..., None, :] + li[..., None, :], -jnp.inf)
    a_t = b + m0[..., None]
    m_t = jnp.maximum(a_t, jnp.max(dmat, axis=-1))
    p = jnp.exp(dmat - m_t[..., None]) * jnp.einsum('bhntd,bhnsd->bhnts', q, k)
    inter = jnp.exp(a_t - m_t)
    num = inter[..., None] * jnp.einsum('bhnij,bhntj->bhnti', C0, q) + jnp.einsum('bhnts,bhnsd->bhntd', p, v)
    den = inter * jnp.einsum('bhnj,bhntj->bhnt', n0, q) + jnp.sum(p, axis=-1)
    h = num / jnp.maximum(jnp.abs(den), jnp.exp(-m_t))[..., None]
    return h.reshape(B, H, T, d), final


def mlstm_out(h, o, norm_g):
    B, H, T, d = h.shape
    h = h.transpose(0, 2, 1, 3)
    mu = jnp.mean(h, -1, keepdims=True)
    var = jnp.mean(jnp.square(h - mu), -1, keepdims=True)
    hn = ((h - mu) * lax.rsqrt(var + LN_EPS)).reshape(B, T, H * d).astype(o.dtype) * norm_g
    return jax.nn.sigmoid(o) * hn


def _rev(a, flip):
    return jnp.flip(a, axis=2) if flip else a


def mlstm_bidirectional(zl, zc, conv_w, conv_b, f_bias, norm_g, ctx_output):
    ql, kl, vl, ol, lil, lfl = mlstm_inputs(zl, conv_w, conv_b, f_bias)
    qc, kc, vc, oc, lic, lfc = mlstm_inputs(zc, conv_w, conv_b, f_bias)
    B = kl.shape[0]
    zero = (jnp.zeros((B, ML_HEADS, ML_DIM, ML_DIM), F32), jnp.zeros((B, ML_HEADS, ML_DIM), F32),
            jnp.zeros((B, ML_HEADS), F32))
    hl_dirs, hc_dirs = [], []
    for d in range(2):
        fl = d == 1
        hc_d, st = mlstm_chunkwise(_rev(qc, fl), _rev(kc, fl), _rev(vc, fl), _rev(lic[d], fl),
                                   _rev(lfc[d], fl), zero, ctx_output)
        hl_d, _ = mlstm_chunkwise(_rev(ql, fl), _rev(kl, fl), _rev(vl, fl), _rev(lil[d], fl),
                                  _rev(lfl[d], fl), st, True)
        hl_dirs.append(_rev(hl_d, fl))
        if ctx_output:
            hc_dirs.append(_rev(hc_d, fl))
    yl = mlstm_out(hl_dirs[0] + hl_dirs[1], ol, norm_g)
    yc = mlstm_out(hc_dirs[0] + hc_dirs[1], oc, norm_g) if ctx_output else None
    return yl, yc


def expert_choice_moe(h, w_router, w_gate, w_up, w_down):
    B, T, D = h.shape
    cap = EC_CAPACITY * T // N_EXPERTS
    aff = jax.nn.softmax((h @ w_router).astype(F32), axis=-1)
    gate, idx = lax.top_k(aff.transpose(0, 2, 1), cap)
    xe = jax.vmap(lambda hb, ib: hb[ib])(h, idx)
    hid = jax.nn.silu(jnp.einsum('becd,edf->becf', xe, w_gate)) * jnp.einsum('becd,edf->becf', xe, w_up)
    ye = jnp.einsum('becf,efd->becd', hid, w_down) * gate[..., None].astype(h.dtype)
    flat = (idx + jnp.arange(B, dtype=idx.dtype)[:, None, None] * T).reshape(-1)
    out = jnp.zeros((B * T, D), h.dtype).at[flat].add(ye.reshape(-1, D))
    return out.reshape(B, T, D)


def token_mixers(zl, zc, p, rope, ctx_output):
    al, bl, ml = zl[..., :N_SG], zl[..., N_SG:N_SG + N_MLA], zl[..., N_SG + N_MLA:]
    ac, bc, mc = zc[..., :N_SG], zc[..., N_SG:N_SG + N_MLA], zc[..., N_SG + N_MLA:]
    ya_l = chunk_spatial_gating(al, p['sg_ln_g'], p['sg_ln_b'], p['sg_w'], p['sg_b'])
    kn_l, kr_l, v_l = mla_kv(bl, p['kv_norm_g'], p['w_ukv'], rope)
    kn_c, kr_c, v_c = mla_kv(bc, p['kv_norm_g'], p['w_ukv'], None)
    qn_l, qr_l = mla_q(bl, p['q_norm_g'], p['w_uq'], rope)
    yb_l = heads_to_width(blocked_attention(qn_l, qr_l, jnp.concatenate([kn_l, kn_c], axis=2),
                                            jnp.concatenate([kr_l, kr_c], axis=1),
                                            jnp.concatenate([v_l, v_c], axis=2)))
    yc_l, yc_c = mlstm_bidirectional(ml, mc, p['ml_conv_w'], p['ml_conv_b'], p['ml_f_bias'],
                                     p['ml_norm_g'], ctx_output)
    y_l = jnp.concatenate([ya_l, yb_l, yc_l], axis=-1) @ p['w_out']
    if not ctx_output:
        return y_l, None
    ya_c = chunk_spatial_gating(ac, p['sg_ln_g'], p['sg_ln_b'], p['sg_w'], p['sg_b'])
    qn_c, qr_c = mla_q(bc, p['q_norm_g'], p['w_uq'], None)
    yb_c = heads_to_width(attend(qn_c, qr_c, kn_c, kr_c, v_c))
    y_c = jnp.concatenate([ya_c, yb_c, yc_c], axis=-1) @ p['w_out']
    return y_l, y_c


def trunk_layer(xl, xc, c, c_ctx, rope, p, update_ctx):
    ada_l = jax.nn.silu(c) @ p['w_ada'] + p['b_ada']
    ada_c = jax.nn.silu(c_ctx) @ p['w_ada'] + p['b_ada']
    sh1_l, sc1_l, g1_l, sh2_l, sc2_l, g2_l = [a[:, None, :] for a in jnp.split(ada_l, 6, axis=-1)]
    sh1_c, sc1_c, g1_c, sh2_c, sc2_c, g2_c = jnp.split(ada_c, 6, axis=-1)
    zl = modulate(xl, sh1_l, sc1_l) @ p['w_in'] + p['b_in']
    zc = modulate(xc, sh1_c, sc1_c) @ p['w_in'] + p['b_in']
    y_l, y_c = token_mixers(zl, zc, p, rope, update_ctx)
    xl = layer_norm(ALPHA * xl + g1_l * y_l, p['ln1_g'], p['ln1_b'])
    f_l = expert_choice_moe(modulate(xl, sh2_l, sc2_l), p['w_router'], p['w_gate'], p['w_up'], p['w_down'])
    xl = layer_norm(ALPHA * xl + g2_l * f_l, p['ln2_g'], p['ln2_b'])
    if update_ctx:
        xc = layer_norm(ALPHA * xc + g1_c * y_c, p['ln1_g'], p['ln1_b'])
        f_c = expert_choice_moe(modulate(xc, sh2_c, sc2_c), p['w_router'], p['w_gate'], p['w_up'], p['w_down'])
        xc = layer_norm(ALPHA * xc + g2_c * f_c, p['ln2_g'], p['ln2_b'])
    return xl, xc


def setup_inputs(seed: int = 0) -> dict:
    key = jax.random.key(seed)
    ks = iter(jax.random.split(key, 40))
    nrm = lambda shape, s: jax.random.normal(next(ks), shape, F32) * s
    L, D = DEPTH, D_MODEL
    return {
        'x': nrm((BATCH, SEQ, D), 1.0),
        'c': nrm((BATCH, D), 1.0),
        'ctx': nrm((BATCH, CTX_LEN, D), 1.0),
        'c_ctx': nrm((D,), 1.0),
        'w_ada': nrm((L, D, 6 * D), 0.5 * D ** -0.5),
        'b_ada': nrm((L, 6 * D), 0.02),
        'w_in': nrm((L, D, N_IN), D ** -0.5),
        'b_in': nrm((L, N_IN), 0.02),
        'sg_ln_g': 1.0 + nrm((L, SG_WIDTH), 0.02),
        'sg_ln_b': nrm((L, SG_WIDTH), 0.02),
        'sg_w': nrm((L, SG_GROUPS, SG_CHUNK, SG_CHUNK), SG_CHUNK ** -0.5),
        'sg_b': 1.0 + nrm((L, SG_GROUPS, SG_CHUNK), 0.02),
        'q_norm_g': 1.0 + nrm((L, Q_LORA), 0.02),
        'kv_norm_g': 1.0 + nrm((L, KV_LORA), 0.02),
        'w_uq': nrm((L, Q_LORA, MLA_HEADS * (MLA_NOPE + MLA_ROPE)), Q_LORA ** -0.5),
        'w_ukv': nrm((L, KV_LORA, MLA_HEADS * (MLA_NOPE + MLA_VDIM)), KV_LORA ** -0.5),
        'ml_conv_w': nrm((L, ML_CONV, 2 * ML_WIDTH), ML_CONV ** -0.5),
        'ml_conv_b': nrm((L, 2 * ML_WIDTH), 0.02),
        'ml_f_bias': jax.random.uniform(next(ks), (L, 2, ML_HEADS), F32, 3.0, 6.0),
        'ml_norm_g': 1.0 + nrm((L, ML_WIDTH), 0.02),
        'w_out': nrm((L, MIX_WIDTH, D), BETA * MIX_WIDTH ** -0.5),
        'ln1_g': 1.0 + nrm((L, D), 0.02),
        'ln1_b': nrm((L, D), 0.02),
        'w_router': nrm((L, D, N_EXPERTS), D ** -0.5),
        'w_gate': nrm((L, N_EXPERTS, D, EXPERT_FF), D ** -0.5),
        'w_up': nrm((L, N_EXPERTS, D, EXPERT_FF), D ** -0.5),
        'w_down': nrm((L, N_EXPERTS, EXPERT_FF, D), BETA * EXPERT_FF ** -0.5),
        'ln2_g': 1.0 + nrm((L, D), 0.02),
        'ln2_b': nrm((L, D), 0.02),
    }


def reference(x, c, ctx, c_ctx, w_ada, b_ada, w_in, b_in, sg_ln_g, sg_ln_b, sg_w, sg_b,
              q_norm_g, kv_norm_g, w_uq, w_ukv, ml_conv_w, ml_conv_b, ml_f_bias, ml_norm_g,
              w_out, ln1_g, ln1_b, w_router, w_gate, w_up, w_down, ln2_g, ln2_b):
    rows = x.shape[1] // GRID_W
    rope = axial_rope_tables(rows)
    xl, xc = x, ctx
    for l in range(DEPTH):
        p = {
            'w_ada': w_ada[l], 'b_ada': b_ada[l], 'w_in': w_in[l], 'b_in': b_in[l],
            'sg_ln_g': sg_ln_g[l], 'sg_ln_b': sg_ln_b[l], 'sg_w': sg_w[l], 'sg_b': sg_b[l],
            'q_norm_g': q_norm_g[l], 'kv_norm_g': kv_norm_g[l], 'w_uq': w_uq[l], 'w_ukv': w_ukv[l],
            'ml_conv_w': ml_conv_w[l], 'ml_conv_b': ml_conv_b[l], 'ml_f_bias': ml_f_bias[l],
            'ml_norm_g': ml_norm_g[l], 'w_out': w_out[l], 'ln1_g': ln1_g[l], 'ln1_b': ln1_b[l],
            'w_router': w_router[l], 'w_gate': w_gate[l], 'w_up': w_up[l], 'w_down': w_down[l],
            'ln2_g': ln2_g[l], 'ln2_b': ln2_b[l],
        }
        xl, xc = trunk_layer(xl, xc, c, c_ctx, rope, p, l < DEPTH - 1)
    return xl
```

```python
import numpy as np
from contextlib import ExitStack
import concourse.bass as bass
import concourse.mybir as mybir
from concourse.bass_utils import run_bass_kernel_spmd

F32 = mybir.dt.float32; BF16 = mybir.dt.bfloat16; U32 = mybir.dt.uint32
AF = mybir.ActivationFunctionType; ALU = mybir.AluOpType; AX = mybir.AxisListType

D = 1024; T = 4352; NT = 34; TC = 256; TL = 4096
NWIN = 2256
ALPHA = 4 ** 0.25
EPS = 1e-6
MLA_SCALE = 96 ** -0.5
NSTEP = 68


class Sched:
    NSLOT = 6

    def __init__(self, nc, es):
        self.nc = nc
        self.E = {'pe': nc.tensor, 'dve': nc.vector, 'act': nc.scalar, 'pool': nc.gpsimd, 'sp': nc.sync}
        self.sem = {}; self.cnt = {}
        for k in self.E:
            self.sem[k] = es.enter_context(nc.semaphore('s_' + k)); self.cnt[k] = 0
        self.slots = {}
        for q in ('sp', 'pool'):
            self.slots[q] = []
            for i in range(self.NSLOT):
                k = f'd{q}{i}'
                self.sem[k] = es.enter_context(nc.semaphore('s_' + k)); self.cnt[k] = 0
                self.slots[q].append(k)
        self.rr = {'sp': 0, 'pool': 0}
        self.waited = {k: {} for k in self.E}
        self.lastw = {}; self.readers = {}

    def _wait(self, eng, ev):
        k, v = ev
        if v <= 0: return
        if k == eng and eng == 'pe': return
        if self.waited[eng].get(k, 0) >= v: return
        self.E[eng].wait_ge(self.sem[k], v); self.waited[eng][k] = v

    def _deps(self, eng, r, w):
        deps = {}
        def add(k, v):
            if deps.get(k, 0) < v: deps[k] = v
        for b in r:
            ev = self.lastw.get(b)
            if ev: add(*ev)
        for b in w:
            ev = self.lastw.get(b)
            if ev: add(*ev)
            for k, v in self.readers.get(b, {}).items(): add(k, v)
        for k, v in deps.items(): self._wait(eng, (k, v))

    def _commit(self, ev, r, w):
        for b in r:
            d = self.readers.setdefault(b, {})
            if d.get(ev[0], 0) < ev[1]: d[ev[0]] = ev[1]
        for b in w:
            self.lastw[b] = ev; self.readers[b] = {}

    def op(self, eng, fn, r=(), w=()):
        self._deps(eng, r, w)
        inst = fn(self.E[eng])
        self.cnt[eng] += 1
        inst.then_inc(self.sem[eng], 1)
        self._commit((eng, self.cnt[eng]), r, w)

    def dma(self, q, fn, r=(), w=()):
        self._deps(q, r, w)
        i = self.rr[q]; self.rr[q] = (i + 1) % self.NSLOT
        k = self.slots[q][i]
        self._wait(q, (k, self.cnt[k]))
        inst = fn(self.E[q])
        self.cnt[k] += 16
        inst.then_inc(self.sem[k], 16)
        self._commit((k, self.cnt[k]), r, w)

    def barrier(self):
        for e in self.E:
            for k in self.sem:
                if k == e: continue
                self._wait(e, (k, self.cnt[k]))
        self.lastw = {}; self.readers = {}


def build(debug=False, nlayers=2, stop=None):
    nc = bass.Bass("TRN2", target_bir_lowering=False)
    dbg_outs = []

    def din(name, shape, dt=F32):
        return nc.dram_tensor(name, list(shape), dt, kind="ExternalInput").ap()

    def dscr(name, shape, dt=F32):
        if debug:
            dbg_outs.append(name)
            return nc.dram_tensor(name, list(shape), dt, kind="ExternalOutput").ap()
        return nc.dram_tensor(name, list(shape), dt, kind="Internal").ap()

    L = 2
    xin = din("xin", [T, D]); cvec = din("cvec", [2, D])
    w_ada = din("w_ada", [L, D, 6 * D]); b_ada = din("b_ada", [L, 6 * D])
    w_in = din("w_in", [L, D, NWIN]); b_in = din("b_in", [L, NWIN])
    sg_ln_g = din("sg_ln_g", [L, 256]); sg_ln_b = din("sg_ln_b", [L, 256])
    sg_w = din("sg_w", [L, 4, 128, 128]); sg_b = din("sg_b", [L, 4, 128])
    q_norm_g = din("q_norm_g", [L, 384]); kv_norm_g = din("kv_norm_g", [L, 256])
    w_uq = din("w_uq", [L, 384, 1024]); w_ukv = din("w_ukv", [L, 256, 1024])
    ml_conv_w = din("ml_conv_w", [L, 5, 512]); ml_conv_b = din("ml_conv_b", [L, 512])
    ml_f_bias = din("ml_f_bias", [L, 8]); ml_norm_g = din("ml_norm_g", [L, 256])
    w_out = din("w_out", [L, D, D]); ln1_g = din("ln1_g", [L, D]); ln1_b = din("ln1_b", [L, D])
    w_router = din("w_router", [L, D, 16])
    w_gate = din("w_gate", [L, 16, D, D]); w_up = din("w_up", [L, 16, D, D]); w_down = din("w_down", [L, 16, D, D])
    ln2_g = din("ln2_g", [L, D]); ln2_b = din("ln2_b", [L, D])
    c_ident = din("c_ident", [128, 128])
    c_rope = din("c_rope", [2, 32, T])
    c_mask = din("c_mask", [2, 128, 64])
    c_misc = din("c_misc", [8, 16])
    c_eye8 = din("c_eye8", [8, 8, NSTEP])
    c_prev = din("c_prev", [NSTEP, NSTEP])
    c_rst = din("c_rst", [8, T])

    y = nc.dram_tensor("y", [TL, D], F32, kind="ExternalOutput").ap()

    xs1 = dscr("xs1", [T, D]); x1_d = dscr("x1_d", [T, D]); hm_d = dscr("hm_d", [T, D]); moe_d = dscr("moe_d", [T, D])
    catT_d = dscr("catT_d", [D, T], BF16)
    qT_d = dscr("qT_d", [8, 96, T], BF16); kT_d = dscr("kT_d", [8, 96, T], BF16)
    vaug_d = dscr("vaug_d", [T, 520], BF16)
    mlqk_d = dscr("mlqk_d", [8, 64, T]); gates_d = dscr("gates_d", [16, T])
    mlv_d = dscr("mlv_d", [T, 256], BF16); sigo_d = dscr("sigo_d", [T, 256])
    ada_d = dscr("ada_d", [96, 128])
    aff_d = dscr("aff_d", [16, T])

    with ExitStack() as top:
        S = Sched(nc, top)
        sbn = [0]

        def sb(es, shape, dt=F32, name=None):
            sbn[0] += 1
            return es.enter_context(nc.sbuf_tensor(f"{name or 't'}_{sbn[0]}", list(shape), dt))

        ps = [top.enter_context(nc.psum_tensor(f"ps{i}", [128, 512], F32)) for i in range(8)]
        PK = [f"ps{i}" for i in range(8)]
        ident = sb(top, [128, 128], F32, "ident")
        ones_bf = sb(top, [128, 512], BF16, "ones_bf")
        ones_f = sb(top, [128, 512], F32, "ones_f")
        zeros_f = sb(top, [128, 1024], F32, "zeros_f")
        epsc = sb(top, [128, 1], F32, "epsc")
        ada = sb(top, [128, 96], F32, "ada")
        adp = sb(top, [128, 96], F32, "adp")

        S.dma('sp', lambda e: e.dma_start(out=ident[:], in_=c_ident), w=['ident'])
        S.op('dve', lambda e: e.memset(ones_bf[:], 1.0), w=['ones_bf'])
        S.op('dve', lambda e: e.memset(ones_f[:], 1.0), w=['ones_f'])
        S.op('dve', lambda e: e.memset(zeros_f[:], 0.0), w=['zeros_f'])
        S.op('dve', lambda e: e.memset(epsc[:], EPS), w=['epsc'])

        def MM(out, lhsT, rhs, st, sp_, r, w):
            S.op('pe', lambda e: e.matmul(out, lhsT, rhs, start=st, stop=sp_), r, w)

        def TR(out, in_, r, w, n=128):
            S.op('pe', lambda e: e.transpose(out, in_, ident[:n, :n]), list(r) + ['ident'], w)

        def ln_stats(es_tag, xap, width, mv, rstd, nmr, stats, rk):
            nch = width // 256 if width >= 256 else 1
            cw = width // nch
            for c in range(nch):
                S.op('dve', lambda e, c=c: e.bn_stats(stats[:, c, :], xap[:, c * cw:(c + 1) * cw]), r=rk, w=[es_tag + 'st'])
            S.op('dve', lambda e: e.bn_aggr(mv[:], stats[:, 0:nch, :]), r=[es_tag + 'st'], w=[es_tag + 'mv'])
            S.op('act', lambda e: e.activation(out=rstd[:], in_=mv[:, 1:2], func=AF.Sqrt, bias=epsc[:, 0:1], scale=1.0),
                 r=[es_tag + 'mv', 'epsc'], w=[es_tag + 'rs'])
            S.op('dve', lambda e: e.reciprocal(rstd[:], rstd[:]), r=[es_tag + 'rs'], w=[es_tag + 'rs'])
            S.op('dve', lambda e: e.scalar_tensor_tensor(out=nmr[:], in0=mv[:, 0:1], scalar=-1.0, in1=rstd[:], op0=ALU.mult, op1=ALU.mult),
                 r=[es_tag + 'mv', es_tag + 'rs'], w=[es_tag + 'nm'])

        def phase0(l):
            with ExitStack() as es:
                cfm = sb(es, [128, 2, 8]); sfm = sb(es, [128, 2, 8])
                brow = sb(es, [1, 6 * D])
                wa = [sb(es, [128, 8, 768]) for _ in range(2)]
                adaT = sb(es, [96, 128])
                with nc.allow_non_contiguous_dma(reason="tiny"):
                    for m_ in range(2):
                        S.dma('sp', lambda e, m_=m_: e.dma_start(out=cfm[:, m_, :], in_=cvec[m_].rearrange("(c p) -> p c", p=128)), w=['cfm'])
                S.dma('sp', lambda e: e.dma_start(out=brow[:], in_=b_ada[l:l + 1, :]), w=['brow'])
                S.op('act', lambda e: e.activation(out=sfm[:], in_=cfm[:], func=AF.Silu), r=['cfm'], w=['sfm'])
                for blk in range(8):
                    wt = wa[blk % 2]; wk = f'wa{blk % 2}'
                    S.dma('sp', lambda e: e.dma_start(out=wt[:], in_=w_ada[l, :, blk * 768:(blk + 1) * 768].rearrange("(c p) n -> p c n", p=128)), w=[wk])
                    for nn in range(6):
                        n = blk * 6 + nn
                        for c in range(8):
                            MM(ps[0][:, n * 2:(n + 1) * 2], wt[:, c, nn * 128:(nn + 1) * 128], sfm[:, :, c], c == 0, False, [wk, 'sfm'], [PK[0]])
                        MM(ps[0][:, n * 2:(n + 1) * 2], brow[0:1, n * 128:(n + 1) * 128], ones_f[0:1, 0:2], False, True, ['brow', 'ones_f'], [PK[0]])
                S.op('dve', lambda e: e.tensor_copy(ada[:], ps[0][:, 0:96]), r=[PK[0]], w=['ada'])
                S.op('dve', lambda e: e.tensor_scalar(out=adp[:], in0=ada[:], scalar1=1.0, scalar2=None, op0=ALU.add), r=['ada'], w=['adp'])
                TR(ps[1][:96, 0:128], ada[:, :], ['ada'], [PK[1]])
                S.op('dve', lambda e: e.tensor_copy(adaT[:], ps[1][:96, 0:128]), r=[PK[1]], w=['adaT'])
                S.dma('sp', lambda e: e.dma_start(out=ada_d, in_=adaT[:]), r=['adaT'], w=['ada_d'])
                for ti in range(NT):
                    S.dma('sp', lambda e, ti=ti: e.dma_start(out=moe_d[ti * 128:(ti + 1) * 128, :], in_=zeros_f[:]), r=['zeros_f'], w=[('moe', ti)])
                S.barrier()

        def ada_col(j, cc, m):
            n = (j * 8 + cc) * 2 + m
            return n

        def bcast_load(es, q, dst, j, m, key):
            src = ada_d.rearrange("(n m) p -> m n p", m=2)[m, j * 8:(j + 1) * 8, :].partition_broadcast(128)
            S.dma(q, lambda e: e.dma_start(out=dst[:].rearrange("p (c f) -> p c f", c=8), in_=src), r=['ada_d'], w=[key])

        def vec_bcast(q, dst, vec_ap, key):
            S.dma(q, lambda e: e.dma_start(out=dst[:], in_=vec_ap.partition_broadcast(128)), w=[key])

        def blocks():
            out = [(0, 256, [0, 1])]
            for b in range(8):
                out.append((256 + 512 * b, 512, [2 + 4 * b + i for i in range(4)]))
            return out

        def phase1(l, xsrc):
            with ExitStack() as es:
                win = sb(es, [128, 8, NWIN], BF16); brow = sb(es, [1, NWIN], BF16)
                wuq = sb(es, [128, 3, 1024], BF16); wukv = sb(es, [128, 2, 1024], BF16)
                wsT = sb(es, [128, 4, 128], BF16)
                w32 = sb(es, [128, 3, 1024]); wsf = sb(es, [128, 512]); gq = sb(es, [128, 3]); gkv = sb(es, [128, 2])
                sgbT = sb(es, [128, 4]); lng = sb(es, [128, 256]); lnb = sb(es, [128, 256])
                rope = [sb(es, [96, 2, 512]) for _ in range(2)]
                xt = [sb(es, [128, D]) for _ in range(2)]
                xn = [sb(es, [128, D]) for _ in range(2)]
                xmT = [sb(es, [128, 8, 512], BF16) for _ in range(2)]
                st6 = sb(es, [128, 4, 6]); mv = sb(es, [128, 2]); rstd = sb(es, [128, 1]); nmr = sb(es, [128, 1])
                st6b = sb(es, [128, 4, 6]); mvb = sb(es, [128, 2]); rstdb = sb(es, [128, 1]); nmrb = sb(es, [128, 1])
                gl = [sb(es, [128, 512]) for _ in range(2)]
                vn = sb(es, [128, 256]); vnb = sb(es, [128, 256], BF16)
                ya = [sb(es, [128, 256]) for _ in range(2)]
                yaT = [sb(es, [128, 2, 128], BF16) for _ in range(2)]
                mlv = [sb(es, [128, 256], BF16) for _ in range(2)]
                sgo = [sb(es, [128, 256]) for _ in range(2)]
                cqT = sb(es, [128, 3, 512], BF16); ckvT = sb(es, [128, 2, 512], BF16)
                sq = [sb(es, [128, 512]) for _ in range(2)]
                rq = sb(es, [96, 512]); rkv = sb(es, [64, 512]); rkvc = sb(es, [128, 4])
                qs = [sb(es, [96, 512]) for _ in range(2)]; qb = [sb(es, [96, 512]) for _ in range(2)]
                r1 = sb(es, [96, 512]); r2 = sb(es, [96, 512])
                qTt = [sb(es, [96, 512], BF16) for _ in range(2)]
                kTt = sb(es, [96, 8, 512], BF16)
                krt = sb(es, [96, 512], BF16)
                va = [sb(es, [128, 8, 65], BF16) for _ in range(2)]
                fm32 = [sb(es, [64, 512]) for _ in range(2)]

                for c in range(8):
                    S.dma('pool', lambda e, c=c: e.dma_start(out=win[:, c, :], in_=w_in[l, c * 128:(c + 1) * 128, :]), w=['win'])
                S.dma('pool', lambda e: e.dma_start(out=brow[:], in_=b_in[l:l + 1, :]), w=['brow'])
                with nc.allow_non_contiguous_dma(reason="tiny"):
                    S.dma('sp', lambda e: e.dma_start(out=gq[:], in_=q_norm_g[l].rearrange("(c p) -> p c", p=128)), w=['gq'])
                    S.dma('sp', lambda e: e.dma_start(out=gkv[:], in_=kv_norm_g[l].rearrange("(c p) -> p c", p=128)), w=['gkv'])
                    S.dma('sp', lambda e: e.dma_start(out=sgbT[:], in_=sg_b[l].rearrange("g t -> t g")), w=['sgbT'])
                S.dma('sp', lambda e: e.dma_start(out=w32[:], in_=w_uq[l].rearrange("(c p) n -> p c n", p=128)), w=['w32'])
                for c in range(3):
                    S.op('dve', lambda e, c=c: e.tensor_scalar(out=wuq[:, c, :], in0=w32[:, c, :], scalar1=gq[:, c:c + 1], scalar2=None, op0=ALU.mult),
                         r=['w32', 'gq'], w=['wuq'])
                S.dma('sp', lambda e: e.dma_start(out=w32[:, 0:2, :], in_=w_ukv[l].rearrange("(c p) n -> p c n", p=128)), r=[], w=['w32'])
                for c in range(2):
                    S.op('dve', lambda e, c=c: e.tensor_scalar(out=wukv[:, c, :], in0=w32[:, c, :], scalar1=gkv[:, c:c + 1], scalar2=None, op0=ALU.mult),
                         r=['w32', 'gkv'], w=['wukv'])
                S.dma('sp', lambda e: e.dma_start(out=wsf[:].rearrange("p (g s) -> p g s", g=4), in_=sg_w[l].rearrange("g t s -> t g s")), w=['wsf'])
                for g in range(4):
                    TR(ps[0][:, g * 128:(g + 1) * 128], wsf[:, g * 128:(g + 1) * 128], ['wsf'], [PK[0]])
                S.op('dve', lambda e: e.tensor_copy(wsT[:].rearrange("p g t -> p (g t)"), ps[0][:, 0:512]), r=[PK[0]], w=['wsT'])
                vec_bcast('sp', lng, sg_ln_g[l], 'lng'); vec_bcast('sp', lnb, sg_ln_b[l], 'lnb')

                bi = 0
                for (t0, N, tiles) in blocks():
                    bp = bi % 2; bi += 1
                    xm = xmT[bp]; xmk = f'xmT{bp}'
                    m_ada = 1 if t0 == 0 else 0
                    S.dma('sp', lambda e, bp=bp: e.dma_start(out=rope[bp][64:96, :, 0:N], in_=c_rope[:, :, t0:t0 + N].rearrange("a p t -> p a t")), w=[f'rope{bp}'])
                    for li, ti in enumerate(tiles):
                        p2 = ti % 2
                        xk = f'xt{p2}'; xnk = f'xn{p2}'
                        S.dma('sp', lambda e, ti=ti, p2=p2: e.dma_start(out=xt[p2][:], in_=xsrc[ti * 128:(ti + 1) * 128, :]), w=[xk])
                        ln_stats('l1', xt[p2], D, mv, rstd, nmr, st6, [xk])
                        S.op('act', lambda e, p2=p2: e.activation(out=xn[p2][:], in_=xt[p2][:], func=AF.Identity, bias=nmr[:, 0:1], scale=rstd[:, 0:1]),
                             r=[xk, 'l1rs', 'l1nm'], w=[xnk])
                        for half in range(2):
                            pb = ps[half]
                            for c4 in range(4):
                                cc = half * 4 + c4
                                TR(pb[:, c4 * 128:(c4 + 1) * 128], xn[p2][:, cc * 128:(cc + 1) * 128], [xnk], [PK[half]])
                            for c4 in range(4):
                                cc = half * 4 + c4
                                eng = 'act' if c4 % 2 == 0 else 'dve'
                                if eng == 'act':
                                    S.op('act', lambda e, cc=cc, c4=c4, pb=pb, li=li: e.activation(out=xm[:, cc, li * 128:(li + 1) * 128], in_=pb[:, c4 * 128:(c4 + 1) * 128], func=AF.Identity,
                                                                                         bias=ada[:, ada_col(0, cc, m_ada):ada_col(0, cc, m_ada) + 1], scale=adp[:, ada_col(1, cc, m_ada):ada_col(1, cc, m_ada) + 1]),
                                         r=[PK[half], 'ada', 'adp'], w=[xmk])
                                else:
                                    S.op('dve', lambda e, cc=cc, c4=c4, pb=pb, li=li: e.tensor_scalar(out=xm[:, cc, li * 128:(li + 1) * 128], in0=pb[:, c4 * 128:(c4 + 1) * 128],
                                                                                            scalar1=adp[:, ada_col(1, cc, m_ada):ada_col(1, cc, m_ada) + 1], scalar2=ada[:, ada_col(0, cc, m_ada):ada_col(0, cc, m_ada) + 1],
                                                                                            op0=ALU.mult, op1=ALU.add),
                                         r=[PK[half], 'ada', 'adp'], w=[xmk])
                        for c in range(8):
                            MM(ps[2][:, :], xm[:, c, li * 128:(li + 1) * 128], win[:, c, 0:512], c == 0, False, [xmk, 'win'], [PK[2]])
                        MM(ps[2][:, :], ones_bf[0:1, 0:128], brow[0:1, 0:512], False, True, ['ones_bf', 'brow'], [PK[2]])
                        g_ = gl[p2]; gk = f'gl{p2}'
                        S.op('act', lambda e, g_=g_: e.activation(out=g_[:], in_=ps[2][:, :], func=AF.Gelu_apprx_tanh), r=[PK[2]], w=[gk])
                        ln_stats('l2', g_[:, 256:512], 256, mvb, rstdb, nmrb, st6b, [gk])
                        S.op('act', lambda e, g_=g_: e.activation(out=vn[:], in_=g_[:, 256:512], func=AF.Identity, bias=nmrb[:, 0:1], scale=rstdb[:, 0:1]),
                             r=[gk, 'l2rs', 'l2nm'], w=['vn'])
                        S.op('dve', lambda e: e.tensor_tensor(out=vn[:], in0=vn[:], in1=lng[:], op=ALU.mult), r=['vn', 'lng'], w=['vn'])
                        S.op('dve', lambda e: e.tensor_tensor(out=vnb[:], in0=vn[:], in1=lnb[:], op=ALU.add), r=['vn', 'lnb'], w=['vnb'])
                        for g in range(4):
                            MM(ps[3][:, g * 64:(g + 1) * 64], wsT[:, g, :], vnb[:, g * 64:(g + 1) * 64], True, True, ['wsT', 'vnb'], [PK[3]])
                        yat = ya[p2]; yak = f'ya{p2}'
                        for g in range(4):
                            S.op('dve', lambda e, g=g, yat=yat, g_=g_: e.scalar_tensor_tensor(out=yat[:, g * 64:(g + 1) * 64], in0=ps[3][:, g * 64:(g + 1) * 64], scalar=sgbT[:, g:g + 1],
                                                                                       in1=g_[:, g * 64:(g + 1) * 64], op0=ALU.add, op1=ALU.mult),
                                 r=[PK[3], 'sgbT', gk], w=[yak])
                        for c in range(2):
                            TR(ps[3][:, 256 + c * 128:256 + (c + 1) * 128], yat[:, c * 128:(c + 1) * 128], [yak], [PK[3]])
                        yT = yaT[p2]; yTk = f'yaT{p2}'
                        S.op('act', lambda e, yT=yT: e.copy(out=yT[:].rearrange("p c t -> p (c t)"), in_=ps[3][:, 256:512]), r=[PK[3]], w=[yTk])
                        S.dma('sp', lambda e, yT=yT, ti=ti: e.dma_start(out=catT_d[0:256, ti * 128:(ti + 1) * 128].rearrange("(c p) t -> p c t", p=128), in_=yT[:]), r=[yTk], w=[('catA', ti)])
                        for c in range(8):
                            MM(ps[4][:, :], xm[:, c, li * 128:(li + 1) * 128], win[:, c, 512:1024], c == 0, False, [xmk, 'win'], [PK[4]])
                        MM(ps[4][:, :], ones_bf[0:1, 0:128], brow[0:1, 512:1024], False, True, ['ones_bf', 'brow'], [PK[4]])
                        S.op('dve', lambda e, p2=p2: e.tensor_copy(mlv[p2][:], ps[4][:, 0:256]), r=[PK[4]], w=[f'mlv{p2}'])
                        S.op('act', lambda e, p2=p2: e.activation(out=sgo[p2][:], in_=ps[4][:, 256:512], func=AF.Sigmoid), r=[PK[4]], w=[f'sgo{p2}'])
                        S.dma('sp', lambda e, p2=p2, ti=ti: e.dma_start(out=mlv_d[ti * 128:(ti + 1) * 128, :], in_=mlv[p2][:]), r=[f'mlv{p2}'], w=[('mlv_d', ti)])
                        S.dma('sp', lambda e, p2=p2, ti=ti: e.dma_start(out=sigo_d[ti * 128:(ti + 1) * 128, :], in_=sgo[p2][:]), r=[f'sgo{p2}'], w=[('sigo_d', ti)])

                    FM0 = 1024
                    def fm_mm(pst, pk, col0, M, pbase=0):
                        for c in range(8):
                            MM(pst[pbase:pbase + M, 0:N], win[:, c, col0:col0 + M], xm[:, c, 0:N], c == 0, False, ['win', xmk], [pk])
                        MM(pst[pbase:pbase + M, 0:N], brow[0:1, col0:col0 + M], ones_bf[0:1, 0:N], False, True, ['brow', 'ones_bf'], [pk])
                    for c in range(3):
                        fm_mm(ps[5], PK[5], FM0 + c * 128, 128)
                        S.op('act', lambda e, c=c: e.copy(out=cqT[:, c, 0:N], in_=ps[5][:, 0:N]), r=[PK[5]], w=['cqT'])
                        S.op('act', lambda e, c=c: e.activation(out=sq[c % 2][:, 0:N], in_=ps[5][:, 0:N], func=AF.Square), r=[PK[5]], w=[f'sq{c % 2}'])
                        MM(ps[6][0:96, 0:N], ones_f[:, 0:96], sq[c % 2][:, 0:N], c == 0, c == 2, ['ones_f', f'sq{c % 2}'], [PK[6]])
                    S.op('act', lambda e: e.activation(out=rq[:, 0:N], in_=ps[6][0:96, 0:N], func=AF.Sqrt, bias=epsc[0:96, 0:1], scale=1.0 / 384), r=[PK[6], 'epsc'], w=['rq'])
                    S.op('dve', lambda e: e.reciprocal(rq[:, 0:N], rq[:, 0:N]), r=['rq'], w=['rq'])
                    rp = rope[bp]; rpk = f'rope{bp}'
                    for h in range(8):
                        hp = h % 2
                        for c in range(3):
                            MM(ps[5][0:96, 0:N], wuq[:, c, h * 96:(h + 1) * 96], cqT[:, c, 0:N], c == 0, c == 2, ['wuq', 'cqT'], [PK[5]])
                        for c in range(3):
                            MM(ps[7][64:96, 0:N], wuq[:, c, 768 + h * 32:768 + (h + 1) * 32], cqT[:, c, 0:N], c == 0, c == 2, ['wuq', 'cqT'], [PK[7]])
                        S.op('dve', lambda e, hp=hp: e.tensor_tensor(out=qs[hp][:, 0:N], in0=ps[5][0:96, 0:N], in1=rq[:, 0:N], op=ALU.mult), r=[PK[5], 'rq'], w=[f'qs{hp}'])
                        S.op('dve', lambda e, hp=hp: e.tensor_tensor(out=qb[hp][64:96, 0:N], in0=ps[7][64:96, 0:N], in1=rq[64:96, 0:N], op=ALU.mult), r=[PK[7], 'rq'], w=[f'qb{hp}'])
                        S.op('act', lambda e, hp=hp: e.copy(out=qTt[hp][0:64, 0:N], in_=qs[hp][0:64, 0:N]), r=[f'qs{hp}'], w=[f'qTt{hp}'])
                        S.op('pool', lambda e, hp=hp: e.tensor_tensor(out=r1[64:96, 0:N], in0=qs[hp][64:96, 0:N], in1=rp[64:96, 0, 0:N], op=ALU.mult), r=[f'qs{hp}', rpk], w=['r1'])
                        S.op('pool', lambda e, hp=hp: e.tensor_tensor(out=r2[64:96, 0:N], in0=qb[hp][64:96, 0:N], in1=rp[64:96, 1, 0:N], op=ALU.mult), r=[f'qb{hp}', rpk], w=['r2'])
                        S.op('pool', lambda e, hp=hp: e.tensor_tensor(out=qTt[hp][64:96, 0:N], in0=r1[64:96, 0:N], in1=r2[64:96, 0:N], op=ALU.add), r=['r1', 'r2'], w=[f'qTt{hp}'])
                        S.dma('sp', lambda e, hp=hp, h=h: e.dma_start(out=qT_d[h, :, t0:t0 + N], in_=qTt[hp][:, 0:N]), r=[f'qTt{hp}'], w=[('qT_d', h, t0)])
                    for c in range(2):
                        fm_mm(ps[5], PK[5], FM0 + 384 + c * 128, 128)
                        S.op('act', lambda e, c=c: e.copy(out=ckvT[:, c, 0:N], in_=ps[5][:, 0:N]), r=[PK[5]], w=['ckvT'])
                        S.op('act', lambda e, c=c: e.activation(out=sq[c][:, 0:N], in_=ps[5][:, 0:N], func=AF.Square), r=[PK[5]], w=[f'sq{c}'])
                    for c in range(2):
                        MM(ps[6][0:64, 0:N], ones_f[:, 0:64], sq[c][:, 0:N], c == 0, c == 1, ['ones_f', f'sq{c}'], [PK[6]])
                    S.op('act', lambda e: e.activation(out=rkv[:, 0:N], in_=ps[6][0:64, 0:N], func=AF.Sqrt, bias=epsc[0:64, 0:1], scale=1.0 / 256), r=[PK[6], 'epsc'], w=['rkv'])
                    S.op('dve', lambda e: e.reciprocal(rkv[:, 0:N], rkv[:, 0:N]), r=['rkv'], w=['rkv'])
                    for li in range(len(tiles)):
                        for c in range(2):
                            MM(ps[7][:, li:li + 1], sq[c][:, li * 128:(li + 1) * 128], ones_f[:, 0:1], c == 0, c == 1, [f'sq{c}', 'ones_f'], [PK[7]])
                    nl = len(tiles)
                    S.op('act', lambda e: e.activation(out=rkvc[:, 0:nl], in_=ps[7][:, 0:nl], func=AF.Sqrt, bias=epsc[:, 0:1], scale=1.0 / 256), r=[PK[7], 'epsc'], w=['rkvc'])
                    S.op('dve', lambda e: e.reciprocal(rkvc[:, 0:nl], rkvc[:, 0:nl]), r=['rkvc'], w=['rkvc'])
                    fm_mm(ps[5], PK[5], FM0 + 640, 32, pbase=64)
                    fm_mm(ps[7], PK[7], FM0 + 672, 32, pbase=64)
                    S.op('dve', lambda e: e.tensor_tensor(out=r1[64:96, 0:N], in0=ps[5][64:96, 0:N], in1=rp[64:96, 0, 0:N], op=ALU.mult), r=[PK[5], rpk], w=['r1'])
                    S.op('dve', lambda e: e.tensor_tensor(out=r2[64:96, 0:N], in0=ps[7][64:96, 0:N], in1=rp[64:96, 1, 0:N], op=ALU.mult), r=[PK[7], rpk], w=['r2'])
                    S.op('dve', lambda e: e.tensor_tensor(out=krt[64:96, 0:N], in0=r1[64:96, 0:N], in1=r2[64:96, 0:N], op=ALU.add), r=['r1', 'r2'], w=['krt'])
                    for h in range(8):
                        for c in range(2):
                            MM(ps[5][0:64, 0:N], wukv[:, c, h * 64:(h + 1) * 64], ckvT[:, c, 0:N], c == 0, c == 1, ['wukv', 'ckvT'], [PK[5]])
                        S.op('dve', lambda e, h=h: e.tensor_tensor(out=kTt[0:64, h, 0:N], in0=ps[5][0:64, 0:N], in1=rkv[:, 0:N], op=ALU.mult), r=[PK[5], 'rkv'], w=['kTt'])
                        S.op('pool', lambda e, h=h: e.tensor_copy(kTt[64:96, h, 0:N], krt[64:96, 0:N]), r=['krt'], w=['kTt'])
                    S.dma('sp', lambda e: e.dma_start(out=kT_d[:, :, t0:t0 + N].rearrange("h p t -> p h t"), in_=kTt[:, :, 0:N]), r=['kTt'], w=[('kT_d', t0)])
                    for li, ti in enumerate(tiles):
                        vp = li % 2
                        for c in range(2):
                            MM(ps[6][:, :], ckvT[:, c, li * 128:(li + 1) * 128], wukv[:, c, 512:1024], c == 0, c == 1, ['ckvT', 'wukv'], [PK[6]])
                        S.op('dve', lambda e, vp=vp: e.memset(va[vp][:, :, 64:65], 1.0), w=[f'va{vp}'])
                        S.op('dve', lambda e, vp=vp, li=li: e.tensor_scalar(out=va[vp][:, :, 0:64], in0=ps[6][:, :].rearrange("p (h d) -> p h d", h=8), scalar1=rkvc[:, li:li + 1], scalar2=None, op0=ALU.mult),
                             r=[PK[6], 'rkvc'], w=[f'va{vp}'])
                        S.dma('sp', lambda e, vp=vp, ti=ti: e.dma_start(out=vaug_d[ti * 128:(ti + 1) * 128, :], in_=va[vp][:].rearrange("p h d -> p (h d)")), r=[f'va{vp}'], w=[('vaug_d', ti)])
                    for g in range(8):
                        fp_ = g % 2
                        fm_mm(ps[5], PK[5], FM0 + 704 + g * 64, 64)
                        S.op('act', lambda e, fp_=fp_: e.copy(out=fm32[fp_][:, 0:N], in_=ps[5][0:64, 0:N]), r=[PK[5]], w=[f'fm32{fp_}'])
                        S.dma('sp', lambda e, fp_=fp_, g=g: e.dma_start(out=mlqk_d[g, :, t0:t0 + N], in_=fm32[fp_][:, 0:N]), r=[f'fm32{fp_}'], w=[('mlqk_d', g, t0)])
                    fm_mm(ps[5], PK[5], FM0 + 1216, 16)
                    S.op('act', lambda e: e.copy(out=fm32[0][0:16, 0:N], in_=ps[5][0:16, 0:N]), r=[PK[5]], w=['fm320'])
                    S.dma('sp', lambda e: e.dma_start(out=gates_d[:, t0:t0 + N], in_=fm32[0][0:16, 0:N]), r=['fm320'], w=[('gates_d', t0)])
                S.barrier()

        def phase2(l, with_ctx_q):
            with ExitStack() as es:
                vaug = sb(es, [128, NT, 520], BF16)
                kT = [sb(es, [96, T], BF16) for _ in range(2)]
                qT = [sb(es, [96, T], BF16) for _ in range(2)]
                pT = [sb(es, [128, 512], BF16) for _ in range(3)]
                oT = sb(es, [65, 512]); sel = sb(es, [65, 64]); rec = sb(es, [64, 512])
                on = [sb(es, [64, 512], BF16) for _ in range(2)]
                S.dma('sp', lambda e: e.dma_start(out=vaug[:], in_=vaug_d.rearrange("(t p) c -> p t c", p=128)), w=['vaug'])
                S.op('dve', lambda e: e.memset(sel[:], 0.0), w=['sel'])
                S.op('dve', lambda e: e.memset(sel[64:65, :], 1.0), w=['sel'])
                cnt = 0; oc = 0
                for h in range(8):
                    hp = h % 2
                    S.dma('sp', lambda e, h=h, hp=hp: e.dma_start(out=kT[hp][:], in_=kT_d[h]), w=[f'kT{hp}'])
                    S.dma('sp', lambda e, h=h, hp=hp: e.dma_start(out=qT[hp][:], in_=qT_d[h]), w=[f'qT{hp}'])
                    for (t0, N, tiles) in blocks():
                        if t0 == 0 and not with_ctx_q: continue
                        ktiles = [0, 1] if t0 == 0 else list(range(NT))
                        for i, kt in enumerate(ktiles):
                            sbk = 2 + cnt % 3; pk = cnt % 3; cnt += 1
                            MM(ps[sbk][:, 0:N], kT[hp][:, kt * 128:(kt + 1) * 128], qT[hp][:, t0:t0 + N], True, True, [f'kT{hp}', f'qT{hp}'], [PK[sbk]])
                            S.op('act', lambda e, sbk=sbk, pk=pk: e.activation(out=pT[pk][:, 0:N], in_=ps[sbk][:, 0:N], func=AF.Exp, scale=MLA_SCALE), r=[PK[sbk]], w=[f'pT{pk}'])
                            MM(ps[0][0:65, 0:N], vaug[:, kt, h * 65:(h + 1) * 65], pT[pk][:, 0:N], i == 0, i == len(ktiles) - 1, ['vaug', f'pT{pk}'], [PK[0]])
                        S.op('dve', lambda e: e.tensor_copy(oT[:, 0:N], ps[0][0:65, 0:N]), r=[PK[0]], w=['oT'])
                        MM(ps[1][0:64, 0:N], sel[:, :], oT[:, 0:N], True, True, ['sel', 'oT'], [PK[1]])
                        S.op('dve', lambda e: e.reciprocal(rec[:, 0:N], ps[1][0:64, 0:N]), r=[PK[1]], w=['rec'])
                        op_ = oc % 2; oc += 1
                        S.op('dve', lambda e, op_=op_: e.tensor_tensor(out=on[op_][:, 0:N], in0=oT[0:64, 0:N], in1=rec[:, 0:N], op=ALU.mult), r=['oT', 'rec'], w=[f'on{op_}'])
                        S.dma('sp', lambda e, op_=op_, h=h: e.dma_start(out=catT_d[256 + h * 64:256 + (h + 1) * 64, t0:t0 + N], in_=on[op_][:, 0:N]), r=[f'on{op_}'], w=[('catB', h, t0)])
                S.barrier()

        def chunk_of(d, p):
            if d == 0: return p
            return 3 - p if p < 4 else 71 - p

        def phase3(l):
            with ExitStack() as es:
                qT = sb(es, [64, 4, T], BF16); kT = sb(es, [64, 4, T], BF16)
                ktok = sb(es, [128, NT, 256], BF16)
                vaug = sb(es, [128, NT, 4, 65], BF16)
                cw = sb(es, [64, 5, 8]); cb = sb(es, [64, 8])
                cols = sb(es, [128, NT, 24]); bc = sb(es, [64, 3, 8, NSTEP])
                msk = sb(es, [128, 2, 64]); misc = sb(es, [8, 16]); eye8 = sb(es, [8, 8, NSTEP]); prev = sb(es, [NSTEP, NSTEP])
                fbn = sb(es, [8, 1])
                S.dma('sp', lambda e: e.dma_start(out=msk[:], in_=c_mask.rearrange("d p t -> p d t")), w=['msk'])
                S.dma('sp', lambda e: e.dma_start(out=misc[:], in_=c_misc), w=['misc'])
                S.dma('sp', lambda e: e.dma_start(out=eye8[:], in_=c_eye8), w=['eye8'])
                S.dma('sp', lambda e: e.dma_start(out=prev[:], in_=c_prev), w=['prev'])
                with nc.allow_non_contiguous_dma(reason="tiny"):
                    for j_ in range(5):
                        S.dma('sp', lambda e, j_=j_: e.dma_start(out=cw[:, j_, :], in_=ml_conv_w[l, j_].rearrange("(g p) -> p g", p=64)), w=['cw'])
                    S.dma('sp', lambda e: e.dma_start(out=cb[:], in_=ml_conv_b[l].rearrange("(g p) -> p g", p=64)), w=['cb'])
                    S.dma('sp', lambda e: e.dma_start(out=fbn[:], in_=ml_f_bias[l].rearrange("(a b) -> a b", b=1)), w=['fbn'])
                S.op('dve', lambda e: e.tensor_scalar(out=fbn[:], in0=fbn[:], scalar1=-1.0, scalar2=None, op0=ALU.mult), r=['fbn'], w=['fbn'])
                for ti in range(NT):
                    S.dma('sp', lambda e, ti=ti: e.dma_start(out=vaug[:, ti, :, 0:64], in_=mlv_d[ti * 128:(ti + 1) * 128, :].rearrange("p (h d) -> p h d", h=4)), w=['vaug'])
                S.op('dve', lambda e: e.memset(vaug[:, :, :, 64:65], 1.0), w=['vaug'])
                with ExitStack() as es2:
                    pre = [sb(es2, [64, 4360])] * 2
                    acc = [sb(es2, [64, 4356])] * 2
                    for g in range(8):
                        gp = 0; pr = pre[gp]; ac = acc[gp]; prk = f'pre{gp}'; ack = f'acc{gp}'
                        S.op('pool', lambda e, pr=pr: e.memset(pr[:, 0:2], 0.0), w=[prk])
                        S.op('pool', lambda e, pr=pr: e.memset(pr[:, 258:262], 0.0), w=[prk])
                        S.op('pool', lambda e, pr=pr: e.memset(pr[:, 4358:4360], 0.0), w=[prk])
                        S.dma('sp', lambda e, pr=pr, g=g: e.dma_start(out=pr[:, 2:258], in_=mlqk_d[g, :, 0:256]), w=[prk])
                        S.dma('sp', lambda e, pr=pr, g=g: e.dma_start(out=pr[:, 262:4358], in_=mlqk_d[g, :, 256:T]), w=[prk])
                        S.op('dve', lambda e, pr=pr, ac=ac, g=g: e.tensor_scalar(out=ac[:], in0=pr[:, 0:4356], scalar1=cw[:, 0, g:g + 1], scalar2=cb[:, g:g + 1], op0=ALU.mult, op1=ALU.add),
                             r=[prk, 'cw', 'cb'], w=[ack])
                        for j in range(1, 5):
                            S.op('dve', lambda e, pr=pr, ac=ac, g=g, j=j: e.scalar_tensor_tensor(out=ac[:], in0=pr[:, j:j + 4356], scalar=cw[:, j, g:g + 1], in1=ac[:], op0=ALU.mult, op1=ALU.add),
                                 r=[prk, 'cw', ack], w=[ack])
                        S.op('act', lambda e, ac=ac: e.activation(out=ac[:], in_=ac[:], func=AF.Silu), r=[ack], w=[ack])
                        dst = qT if g < 4 else kT; dk = 'qT3' if g < 4 else 'kT3'; h = g % 4
                        sc = 1.0 if g < 4 else 0.125
                        S.op('act', lambda e, ac=ac, dst=dst, h=h, sc=sc: e.mul(out=dst[:, h, 0:256], in_=ac[:, 0:256], mul=sc), r=[ack], w=[dk])
                        S.op('act', lambda e, ac=ac, dst=dst, h=h, sc=sc: e.mul(out=dst[:, h, 256:T], in_=ac[:, 260:4356], mul=sc), r=[ack], w=[dk])
                        if g >= 4:
                            S.op('act', lambda e, ac=ac: e.mul(out=ac[:], in_=ac[:], mul=0.125), r=[ack], w=[ack])
                            for ti in range(NT):
                                c0 = ti * 128 if ti < 2 else ti * 128 + 4
                                pb = 1 + ti % 2
                                TR(ps[pb][:, 0:64], ac[:, c0:c0 + 128], [ack], [PK[pb]], n=64)
                                if ti % 2:
                                    S.op('dve', lambda e, ti=ti, pb=pb, h=h: e.tensor_copy(ktok[:, ti, h * 64:(h + 1) * 64], ps[pb][:, 0:64]), r=[PK[pb]], w=['ktok'])
                                else:
                                    S.op('act', lambda e, ti=ti, pb=pb, h=h: e.copy(out=ktok[:, ti, h * 64:(h + 1) * 64], in_=ps[pb][:, 0:64]), r=[PK[pb]], w=['ktok'])
                    S.barrier()
                with ExitStack() as es2:
                    SEG = 2176; CH = 34
                    Bc = sb(es2, [8, T]); U = sb(es2, [8, T])
                    G = [sb(es2, [8, SEG]) for _ in range(4)]
                    tot = sb(es2, [8, NSTEP]); umax = sb(es2, [8, NSTEP])
                    sm = {k: sb(es2, [8, NSTEP], name='sm_' + k) for k in ['be_s', 'ml_s', 'um_s', 'm', 'mprev', 'a', 's', 'mm_s', 'inter', 'mm_n', 't1', 't2']}
                    xT = sb(es2, [NSTEP, 8]); exp3 = sb(es2, [8, 8, NSTEP])
                    for sg_ in range(2):
                        o = sg_ * SEG; co = sg_ * CH
                        LI = G[0]; LF = G[1]; rst = G[2]; TMP = G[3]
                        S.dma('sp', lambda e: e.dma_start(out=rst[:], in_=c_rst[:, o:o + SEG]), w=['G2'])
                        for d in range(2):
                            S.dma('sp', lambda e, d=d: e.dma_start(out=LI[d * 4:(d + 1) * 4, :], in_=gates_d[d * 8:d * 8 + 4, o:o + SEG]), w=['G0'])
                            S.dma('sp', lambda e, d=d: e.dma_start(out=LF[d * 4:(d + 1) * 4, :], in_=gates_d[d * 8 + 4:d * 8 + 8, o:o + SEG]), w=['G1'])
                        S.op('act', lambda e: e.activation(out=LF[:], in_=LF[:], func=AF.Exp, bias=fbn[:, 0:1], scale=-1.0), r=['G1', 'fbn'], w=['G1'])
                        S.op('act', lambda e: e.activation(out=LF[:], in_=LF[:], func=AF.Ln, bias=ones_f[0:8, 0:1], scale=1.0), r=['G1', 'ones_f'], w=['G1'])
                        S.op('dve', lambda e: e.tensor_scalar(out=LF[:], in0=LF[:], scalar1=-1.0, scalar2=None, op0=ALU.mult), r=['G1'], w=['G1'])
                        Bs = Bc[:, o:o + SEG]; Us = U[:, o:o + SEG]
                        S.op('dve', lambda e: e.tensor_tensor_scan(out=Bs, data0=rst[:], data1=LF[:], initial=0.0, op0=ALU.mult, op1=ALU.add), r=['G2', 'G1'], w=['Bc'])
                        S.op('dve', lambda e: e.tensor_reduce(out=tot[:, co:co + CH], in_=LF[:].rearrange("p (c t) -> p c t", t=64), axis=AX.X, op=ALU.add), r=['G1'], w=['tot'])
                        S.op('dve', lambda e: e.tensor_tensor(out=TMP[:].rearrange("p (c t) -> p c t", t=64), in0=LF[:].rearrange("p (c t) -> p c t", t=64),
                                                              in1=tot[:, co:co + CH].unsqueeze(2).to_broadcast([8, CH, 64]), op=ALU.add), r=['G1', 'tot'], w=['G3'])
                        S.op('dve', lambda e: e.tensor_scalar(out=TMP[:], in0=TMP[:], scalar1=misc[:, 0:1], scalar2=None, op0=ALU.mult), r=['G3', 'misc'], w=['G3'])
                        S.op('dve', lambda e: e.scalar_tensor_tensor(out=Bs, in0=Bs, scalar=misc[:, 1:2], in1=TMP[:], op0=ALU.mult, op1=ALU.add), r=['Bc', 'misc', 'G3'], w=['Bc'])
                        S.op('dve', lambda e: e.tensor_tensor(out=Us, in0=LI[:], in1=Bs, op=ALU.subtract), r=['G0', 'Bc'], w=['U'])
                        S.op('dve', lambda e: e.tensor_reduce(out=umax[:, co:co + CH], in_=Us.rearrange("p (c t) -> p c t", t=64), axis=AX.X, op=ALU.max), r=['U'], w=['umax'])

                    def to_scan_order(dst, src, sk, dk):
                        TR(ps[1][0:NSTEP, 0:8], src[:, :], [sk], [PK[1]], n=8)
                        S.op('dve', lambda e: e.tensor_copy(xT[:], ps[1][0:NSTEP, 0:8]), r=[PK[1]], w=['xT'])
                        MM(ps[2][0:8, 0:NSTEP], xT[:, :], prev[:, :], True, True, ['xT', 'prev'], [PK[2]])
                        S.op('dve', lambda e: e.tensor_scalar(out=sm['t1'][:], in0=ps[2][0:8, 0:NSTEP], scalar1=misc[:, 0:1], scalar2=None, op0=ALU.mult), r=[PK[2], 'misc'], w=['t1'])
                        S.op('dve', lambda e: e.scalar_tensor_tensor(out=dst[:], in0=src[:], scalar=misc[:, 2:3], in1=sm['t1'][:], op0=ALU.mult, op1=ALU.add), r=[sk, 'misc', 't1'], w=[dk])

                    to_scan_order(sm['be_s'], tot, 'tot', 'be_s')
                    to_scan_order(sm['um_s'], umax, 'umax', 'um_s')
                    S.op('dve', lambda e: e.tensor_tensor(out=sm['ml_s'][:], in0=sm['be_s'][:], in1=sm['um_s'][:], op=ALU.add), r=['be_s', 'um_s'], w=['ml_s'])
                    S.op('dve', lambda e: e.tensor_tensor_scan(out=sm['m'][:], data0=sm['be_s'][:], data1=sm['ml_s'][:], initial=0.0, op0=ALU.add, op1=ALU.max), r=['be_s', 'ml_s'], w=['m'])
                    S.op('dve', lambda e: e.memset(sm['mprev'][:, 0:1], 0.0), w=['mprev'])
                    S.op('dve', lambda e: e.tensor_copy(sm['mprev'][:, 1:NSTEP], sm['m'][:, 0:NSTEP - 1]), r=['m'], w=['mprev'])
                    S.op('dve', lambda e: e.tensor_tensor(out=sm['t2'][:], in0=sm['be_s'][:], in1=sm['mprev'][:], op=ALU.add), r=['be_s', 'mprev'], w=['t2'])
                    S.op('dve', lambda e: e.tensor_tensor(out=sm['t2'][:], in0=sm['t2'][:], in1=sm['m'][:], op=ALU.subtract), r=['t2', 'm'], w=['t2'])
                    S.op('act', lambda e: e.activation(out=sm['a'][:], in_=sm['t2'][:], func=AF.Exp), r=['t2'], w=['a'])
                    S.op('dve', lambda e: e.tensor_tensor(out=sm['t2'][:], in0=sm['ml_s'][:], in1=sm['m'][:], op=ALU.subtract), r=['ml_s', 'm', 'a'], w=['t2'])
                    S.op('act', lambda e: e.activation(out=sm['s'][:], in_=sm['t2'][:], func=AF.Exp), r=['t2'], w=['s'])
                    S.op('dve', lambda e: e.tensor_tensor(out=sm['mm_s'][:], in0=sm['mprev'][:], in1=sm['um_s'][:], op=ALU.max), r=['mprev', 'um_s'], w=['mm_s'])
                    S.op('dve', lambda e: e.tensor_tensor(out=sm['t2'][:], in0=sm['mprev'][:], in1=sm['mm_s'][:], op=ALU.subtract), r=['mprev', 'mm_s', 's'], w=['t2'])
                    S.op('act', lambda e: e.activation(out=sm['inter'][:], in_=sm['t2'][:], func=AF.Exp), r=['t2'], w=['inter'])
                    to_scan_order(sm['mm_n'], sm['mm_s'], 'mm_s', 'mm_n')
                    for wi, nm in enumerate(['a', 's', 'inter']):
                        S.op('dve', lambda e, nm=nm: e.tensor_tensor(out=exp3[:], in0=eye8[:], in1=sm[nm][:].unsqueeze(1).to_broadcast([8, 8, NSTEP]), op=ALU.mult), r=['eye8', nm], w=['exp3'])
                        for hf in range(2):
                            MM(ps[3][0:64, 0:4 * NSTEP], ones_f[0:8, 0:64], exp3[:, hf * 4:(hf + 1) * 4, :].rearrange("p a b -> p (a b)"), True, True, ['ones_f', 'exp3'], [PK[3]])
                            S.op('dve', lambda e, wi=wi, hf=hf: e.tensor_copy(bc[:, wi, hf * 4:(hf + 1) * 4, :].rearrange("p a b -> p (a b)"), ps[3][0:64, 0:4 * NSTEP]), r=[PK[3]], w=['bc'])
                    for sg_ in range(2):
                        o = sg_ * SEG; co = sg_ * CH
                        TMP = G[3]; RW = G[0:3]
                        u3 = U[:, o:o + SEG].rearrange("p (c t) -> p c t", t=64); b3 = Bc[:, o:o + SEG].rearrange("p (c t) -> p c t", t=64)
                        t3 = TMP[:].rearrange("p (c t) -> p c t", t=64)
                        S.op('dve', lambda e: e.tensor_tensor(out=t3, in0=u3, in1=umax[:, co:co + CH].unsqueeze(2).to_broadcast([8, CH, 64]), op=ALU.subtract), r=['U', 'umax'], w=['G3'])
                        S.op('act', lambda e: e.activation(out=RW[0][:], in_=TMP[:], func=AF.Exp), r=['G3'], w=['G0'])
                        S.op('dve', lambda e: e.tensor_tensor(out=t3, in0=u3, in1=sm['mm_n'][:, co:co + CH].unsqueeze(2).to_broadcast([8, CH, 64]), op=ALU.subtract), r=['U', 'mm_n'], w=['G3'])
                        S.op('act', lambda e: e.activation(out=RW[1][:], in_=TMP[:], func=AF.Exp), r=['G3'], w=['G1'])
                        S.op('dve', lambda e: e.tensor_tensor(out=t3, in0=b3, in1=sm['mm_n'][:, co:co + CH].unsqueeze(2).to_broadcast([8, CH, 64]), op=ALU.add), r=['Bc', 'mm_n'], w=['G3'])
                        S.op('act', lambda e: e.activation(out=RW[2][:], in_=TMP[:], func=AF.Exp, scale=-1.0), r=['G3'], w=['G2'])
                        for tl in range(17):
                            ti = sg_ * 17 + tl
                            pb = 1 + ti % 2
                            for k3 in range(3):
                                TR(ps[pb][:, k3 * 8:(k3 + 1) * 8], RW[k3][:, tl * 128:(tl + 1) * 128], [f'G{k3}'], [PK[pb]], n=8)
                            S.op('dve', lambda e, ti=ti, pb=pb: e.tensor_copy(cols[:, ti, :], ps[pb][:, 0:24]), r=[PK[pb]], w=['cols'])
                    S.barrier()
                hsum = sb(es, [128, NT, 256])
                S.op('pool', lambda e: e.memset(hsum[:], 0.0), w=['hsum'])
                Cst = sb(es, [64, 8, 65]); C0b = [sb(es, [64, 4, 65], BF16) for _ in range(2)]
                tmpC = sb(es, [64, 4, 65])
                wv = [sb(es, [128, 4, 65], BF16) for _ in range(2)]
                tS = [sb(es, [128, 4, 64]) for _ in range(2)]
                pTm = [sb(es, [128, 4, 64], BF16) for _ in range(2)]
                tI = sb(es, [128, 260]); tH = sb(es, [128, 260])
                dn = sb(es, [128, 4]); hd = [sb(es, [128, 4, 64]) for _ in range(2)]
                S.op('dve', lambda e: e.memset(Cst[:], 0.0), w=['Cst'])
                it = 0
                for p in range(NSTEP):
                    for d in range(2):
                        c = chunk_of(d, p); ti = c // 2; hb = c % 2; P0 = hb * 64; P1 = P0 + 64
                        t0 = c * 64
                        ip = it % 2; it += 1
                        S.op('dve', lambda e, ip=ip, ti=ti, d=d, P0=P0, P1=P1: e.tensor_tensor(out=wv[ip][P0:P1], in0=vaug[P0:P1, ti], in1=cols[P0:P1, ti, d * 4:(d + 1) * 4].unsqueeze(2).to_broadcast([64, 4, 65]), op=ALU.mult),
                             r=['vaug', 'cols'], w=[f'wv{ip}'])
                        for h in range(4):
                            MM(ps[4][0:64, h * 65:(h + 1) * 65], ktok[P0:P1, ti, h * 64:(h + 1) * 64], wv[ip][P0:P1, h, :], True, True, ['ktok', f'wv{ip}'], [PK[4]])
                        S.op('dve', lambda e, ip=ip, d=d, p=p: e.tensor_tensor(out=C0b[ip][:], in0=Cst[:, d * 4:(d + 1) * 4, :], in1=bc[:, 2, d * 4:(d + 1) * 4, p].unsqueeze(2).to_broadcast([64, 4, 65]), op=ALU.mult),
                             r=['Cst', 'bc'], w=[f'C0b{ip}'])
                        S.op('dve', lambda e, d=d, p=p: e.tensor_tensor(out=tmpC[:], in0=ps[4][0:64, 0:260].rearrange("p (h c) -> p h c", h=4), in1=bc[:, 1, d * 4:(d + 1) * 4, p].unsqueeze(2).to_broadcast([64, 4, 65]), op=ALU.mult),
                             r=[PK[4], 'bc'], w=['tmpC'])
                        S.op('dve', lambda e, d=d, p=p: e.tensor_tensor(out=Cst[:, d * 4:(d + 1) * 4, :], in0=Cst[:, d * 4:(d + 1) * 4, :], in1=bc[:, 0, d * 4:(d + 1) * 4, p].unsqueeze(2).to_broadcast([64, 4, 65]), op=ALU.mult),
                             r=['Cst', 'bc'], w=['Cst'])
                        S.op('dve', lambda e, d=d: e.tensor_tensor(out=Cst[:, d * 4:(d + 1) * 4, :], in0=Cst[:, d * 4:(d + 1) * 4, :], in1=tmpC[:], op=ALU.add), r=['Cst', 'tmpC'], w=['Cst'])
                        for h in range(4):
                            MM(ps[5][P0:P1, h * 64:(h + 1) * 64], kT[:, h, t0:t0 + 64], qT[:, h, t0:t0 + 64], True, True, ['kT3', 'qT3'], [PK[5]])
                        S.op('dve', lambda e, ip=ip, ti=ti, d=d, P0=P0, P1=P1: e.tensor_tensor(out=tS[ip][P0:P1], in0=ps[5][P0:P1, 0:256].rearrange("p (h t) -> p h t", h=4),
                                                                                       in1=cols[P0:P1, ti, 8 + d * 4:8 + (d + 1) * 4].unsqueeze(2).to_broadcast([64, 4, 64]), op=ALU.mult),
                             r=[PK[5], 'cols'], w=[f'tS{ip}'])
                        S.op('pool', lambda e, ip=ip, d=d, P0=P0, P1=P1: e.tensor_tensor(out=pTm[ip][P0:P1], in0=tS[ip][P0:P1], in1=msk[P0:P1, d, :].unsqueeze(1).to_broadcast([64, 4, 64]), op=ALU.mult),
                             r=[f'tS{ip}', 'msk'], w=[f'pTm{ip}'])
                        for h in range(4):
                            MM(ps[6][P0:P1, h * 65:(h + 1) * 65], pTm[ip][P0:P1, h, :], vaug[P0:P1, ti, h, :], True, True, [f'pTm{ip}', 'vaug'], [PK[6]])
                        for h in range(4):
                            MM(ps[7][P0:P1, h * 65:(h + 1) * 65], qT[:, h, t0:t0 + 64], C0b[ip][:, h, :], True, True, ['qT3', f'C0b{ip}'], [PK[7]])
                        S.op('act', lambda e, P0=P0, P1=P1: e.copy(out=tI[P0:P1, :], in_=ps[7][P0:P1, 0:260]), r=[PK[7]], w=['tI'])
                        S.op('dve', lambda e, P0=P0, P1=P1: e.tensor_tensor(out=tH[P0:P1, :], in0=ps[6][P0:P1, 0:260], in1=tI[P0:P1, :], op=ALU.add), r=[PK[6], 'tI'], w=['tH'])
                        ph = tH[P0:P1, :].rearrange("p (h c) -> p h c", h=4)
                        S.op('act', lambda e, ph=ph, P0=P0, P1=P1: e.activation(out=dn[P0:P1, :], in_=ph[:, :, 64], func=AF.Abs), r=['tH'], w=['dn'])
                        S.op('dve', lambda e, ti=ti, d=d, P0=P0, P1=P1: e.tensor_tensor(out=dn[P0:P1, :], in0=dn[P0:P1, :], in1=cols[P0:P1, ti, 16 + d * 4:16 + (d + 1) * 4], op=ALU.max), r=['dn', 'cols'], w=['dn'])
                        S.op('dve', lambda e, P0=P0, P1=P1: e.reciprocal(dn[P0:P1, :], dn[P0:P1, :]), r=['dn'], w=['dn'])
                        S.op('dve', lambda e, ip=ip, ph=ph, P0=P0, P1=P1: e.tensor_tensor(out=hd[ip][P0:P1], in0=ph[:, :, 0:64], in1=dn[P0:P1, :].unsqueeze(2).to_broadcast([64, 4, 64]), op=ALU.mult),
                             r=['tH', 'dn'], w=[f'hd{ip}'])
                        S.op('pool', lambda e, ip=ip, ti=ti, P0=P0, P1=P1: e.tensor_tensor(out=hsum[P0:P1, ti, :], in0=hsum[P0:P1, ti, :], in1=hd[ip][P0:P1].rearrange("p h d -> p (h d)"), op=ALU.add),
                             r=['hsum', f'hd{ip}'], w=['hsum'])
                with ExitStack() as es2:
                    ngb = sb(es2, [128, 256]); so = [sb(es2, [128, 256]) for _ in range(2)]
                    mu = sb(es2, [128, 4]); var = sb(es2, [128, 4]); cen = sb(es2, [128, 4, 64]); sqq = sb(es2, [128, 4, 64])
                    yc = [sb(es2, [128, 256]) for _ in range(2)]; ycT = [sb(es2, [128, 2, 128], BF16) for _ in range(2)]
                    vec_bcast('sp', ngb, ml_norm_g[l], 'ngb')
                    for ti in range(NT):
                        p2 = ti % 2
                        S.dma('sp', lambda e, ti=ti, p2=p2: e.dma_start(out=so[p2][:], in_=sigo_d[ti * 128:(ti + 1) * 128, :]), w=[f'so{p2}'])
                        h3 = hsum[:, ti, :].rearrange("p (h d) -> p h d", h=4)
                        S.op('dve', lambda e, h3=h3: e.tensor_reduce(out=mu[:], in_=h3, axis=AX.X, op=ALU.add), r=['hsum'], w=['mu'])
                        S.op('dve', lambda e: e.tensor_scalar(out=mu[:], in0=mu[:], scalar1=1.0 / 64, scalar2=None, op0=ALU.mult), r=['mu'], w=['mu'])
                        S.op('dve', lambda e, h3=h3: e.tensor_tensor(out=cen[:], in0=h3, in1=mu[:].unsqueeze(2).to_broadcast([128, 4, 64]), op=ALU.subtract), r=['hsum', 'mu'], w=['cen'])
                        S.op('dve', lambda e: e.tensor_tensor(out=sqq[:], in0=cen[:], in1=cen[:], op=ALU.mult), r=['cen'], w=['sqq'])
                        S.op('dve', lambda e: e.tensor_reduce(out=var[:], in_=sqq[:], axis=AX.X, op=ALU.add), r=['sqq'], w=['var'])
                        S.op('act', lambda e: e.activation(out=var[:], in_=var[:], func=AF.Sqrt, bias=epsc[:, 0:1], scale=1.0 / 64), r=['var', 'epsc'], w=['var'])
                        S.op('dve', lambda e: e.reciprocal(var[:], var[:]), r=['var'], w=['var'])
                        S.op('dve', lambda e: e.tensor_tensor(out=cen[:], in0=cen[:], in1=var[:].unsqueeze(2).to_broadcast([128, 4, 64]), op=ALU.mult), r=['cen', 'var'], w=['cen'])
                        S.op('dve', lambda e: e.tensor_tensor(out=cen[:].rearrange("p h d -> p (h d)"), in0=cen[:].rearrange("p h d -> p (h d)"), in1=ngb[:], op=ALU.mult), r=['cen', 'ngb'], w=['cen'])
                        S.op('dve', lambda e, p2=p2: e.tensor_tensor(out=yc[p2][:], in0=cen[:].rearrange("p h d -> p (h d)"), in1=so[p2][:], op=ALU.mult), r=['cen', f'so{p2}'], w=[f'yc{p2}'])
                        for c in range(2):
                            TR(ps[1 + p2][:, c * 128:(c + 1) * 128], yc[p2][:, c * 128:(c + 1) * 128], [f'yc{p2}'], [PK[1 + p2]])
                        S.op('act', lambda e, p2=p2: e.copy(out=ycT[p2][:].rearrange("p c t -> p (c t)"), in_=ps[1 + p2][:, 0:256]), r=[PK[1 + p2]], w=[f'ycT{p2}'])
                        S.dma('sp', lambda e, p2=p2, ti=ti: e.dma_start(out=catT_d[768:1024, ti * 128:(ti + 1) * 128].rearrange("(c p) t -> p c t", p=128), in_=ycT[p2][:]), r=[f'ycT{p2}'], w=[('catC', ti)])
                S.barrier()

        def phase4(l, xsrc, tiles):
            with ExitStack() as es:
                wout = sb(es, [128, 8, D], BF16); wr = sb(es, [128, 8, 16])
                g1b = [sb(es, [128, D]) for _ in range(2)]; scb = [sb(es, [128, D]) for _ in range(2)]; shb = [sb(es, [128, D]) for _ in range(2)]
                lg = sb(es, [128, D]); lb = sb(es, [128, D])
                cat = [sb(es, [128, 8, 128], BF16) for _ in range(2)]
                xt = [sb(es, [128, D]) for _ in range(2)]
                rr = sb(es, [128, D]); x1 = [sb(es, [128, D]) for _ in range(2)]; hm = [sb(es, [128, D]) for _ in range(2)]
                hmT = sb(es, [128, 8, 128])
                st6 = sb(es, [128, 4, 6]); mv = sb(es, [128, 2]); rstd = sb(es, [128, 1]); nmr = sb(es, [128, 1])
                lgt = sb(es, [128, 16]); mx = sb(es, [128, 1]); ssum = sb(es, [128, 1]); affT = sb(es, [16, T])
                for c in range(8):
                    S.dma('pool', lambda e, c=c: e.dma_start(out=wout[:, c, :], in_=w_out[l, c * 128:(c + 1) * 128, :]), w=['wout'])
                S.dma('sp', lambda e: e.dma_start(out=wr[:], in_=w_router[l].rearrange("(c p) n -> p c n", p=128)), w=['wr'])
                for m in range(2):
                    bcast_load(es, 'sp', g1b[m], 2, m, f'g1b{m}'); bcast_load(es, 'sp', scb[m], 4, m, f'scb{m}'); bcast_load(es, 'sp', shb[m], 3, m, f'shb{m}')
                    S.op('dve', lambda e, m=m: e.tensor_scalar(out=scb[m][:], in0=scb[m][:], scalar1=1.0, scalar2=None, op0=ALU.add), r=[f'scb{m}'], w=[f'scb{m}'])
                vec_bcast('sp', lg, ln1_g[l], 'lg'); vec_bcast('sp', lb, ln1_b[l], 'lb')
                for ti in tiles:
                    p2 = ti % 2; m = 1 if ti < 2 else 0
                    with nc.allow_non_contiguous_dma(reason="catT tile"):
                        S.dma('sp', lambda e, ti=ti, p2=p2: e.dma_start(out=cat[p2][:], in_=catT_d[:, ti * 128:(ti + 1) * 128].rearrange("(c p) t -> p c t", p=128)), w=[f'cat{p2}'])
                    S.dma('sp', lambda e, ti=ti, p2=p2: e.dma_start(out=xt[p2][:], in_=xsrc[ti * 128:(ti + 1) * 128, :]), w=[f'xt{p2}'])
                    for half in range(2):
                        for c in range(8):
                            MM(ps[half][:, :], cat[p2][:, c, :], wout[:, c, half * 512:(half + 1) * 512], c == 0, c == 7, [f'cat{p2}', 'wout'], [PK[half]])
                        S.op('dve', lambda e, half=half, m=m: e.tensor_tensor(out=rr[:, half * 512:(half + 1) * 512], in0=ps[half][:, :], in1=g1b[m][:, half * 512:(half + 1) * 512], op=ALU.mult),
                             r=[PK[half], f'g1b{m}'], w=['rr'])
                    S.op('dve', lambda e, p2=p2: e.scalar_tensor_tensor(out=rr[:], in0=xt[p2][:], scalar=ALPHA, in1=rr[:], op0=ALU.mult, op1=ALU.add), r=[f'xt{p2}', 'rr'], w=['rr'])
                    ln_stats('l4', rr, D, mv, rstd, nmr, st6, ['rr'])
                    S.op('act', lambda e: e.activation(out=rr[:], in_=rr[:], func=AF.Identity, bias=nmr[:, 0:1], scale=rstd[:, 0:1]), r=['rr', 'l4rs', 'l4nm'], w=['rr'])
                    S.op('dve', lambda e: e.tensor_tensor(out=rr[:], in0=rr[:], in1=lg[:], op=ALU.mult), r=['rr', 'lg'], w=['rr'])
                    S.op('dve', lambda e, p2=p2: e.tensor_tensor(out=x1[p2][:], in0=rr[:], in1=lb[:], op=ALU.add), r=['rr', 'lb'], w=[f'x1{p2}'])
                    S.dma('sp', lambda e, ti=ti, p2=p2: e.dma_start(out=x1_d[ti * 128:(ti + 1) * 128, :], in_=x1[p2][:]), r=[f'x1{p2}'], w=[('x1_d', ti)])
                    ln_stats('l5', x1[p2], D, mv, rstd, nmr, st6, [f'x1{p2}'])
                    S.op('act', lambda e, p2=p2: e.activation(out=rr[:], in_=x1[p2][:], func=AF.Identity, bias=nmr[:, 0:1], scale=rstd[:, 0:1]), r=[f'x1{p2}', 'l5rs', 'l5nm'], w=['rr'])
                    S.op('dve', lambda e, m=m: e.tensor_tensor(out=rr[:], in0=rr[:], in1=scb[m][:], op=ALU.mult), r=['rr', f'scb{m}'], w=['rr'])
                    S.op('dve', lambda e, p2=p2, m=m: e.tensor_tensor(out=hm[p2][:], in0=rr[:], in1=shb[m][:], op=ALU.add), r=['rr', f'shb{m}'], w=[f'hm{p2}'])
                    S.dma('sp', lambda e, ti=ti, p2=p2: e.dma_start(out=hm_d[ti * 128:(ti + 1) * 128, :], in_=hm[p2][:]), r=[f'hm{p2}'], w=[('hm_d', ti)])
                    for half in range(2):
                        for c4 in range(4):
                            TR(ps[2 + half][:, c4 * 128:(c4 + 1) * 128], hm[p2][:, (half * 4 + c4) * 128:(half * 4 + c4 + 1) * 128], [f'hm{p2}'], [PK[2 + half]])
                        if half == 0:
                            S.op('act', lambda e: e.copy(out=hmT[:, 0:4, :].rearrange("p c t -> p (c t)"), in_=ps[2][:, :]), r=[PK[2]], w=['hmT'])
                        else:
                            S.op('dve', lambda e: e.tensor_copy(hmT[:, 4:8, :].rearrange("p c t -> p (c t)"), ps[3][:, :]), r=[PK[3]], w=['hmT'])
                    for c in range(8):
                        MM(ps[4][:, 0:16], hmT[:, c, :], wr[:, c, :], c == 0, c == 7, ['hmT', 'wr'], [PK[4]])
                    S.op('dve', lambda e: e.tensor_reduce(out=mx[:], in_=ps[4][:, 0:16], axis=AX.X, op=ALU.max), r=[PK[4]], w=['mx'])
                    S.op('dve', lambda e: e.tensor_scalar(out=mx[:], in0=mx[:], scalar1=-1.0, scalar2=None, op0=ALU.mult), r=['mx'], w=['mx'])
                    S.op('act', lambda e: e.activation(out=lgt[:], in_=ps[4][:, 0:16], func=AF.Exp, bias=mx[:, 0:1], scale=1.0, accum_out=ssum[:]), r=[PK[4], 'mx'], w=['lgt', 'ssum'])
                    S.op('dve', lambda e: e.reciprocal(ssum[:], ssum[:]), r=['ssum'], w=['ssum'])
                    S.op('dve', lambda e: e.tensor_scalar(out=lgt[:], in0=lgt[:], scalar1=ssum[:, 0:1], scalar2=None, op0=ALU.mult), r=['lgt', 'ssum'], w=['lgt'])
                    TR(ps[5][0:16, 0:128], lgt[:, :], ['lgt'], [PK[5]])
                    S.op('dve', lambda e, ti=ti: e.tensor_copy(affT[:, ti * 128:(ti + 1) * 128], ps[5][0:16, 0:128]), r=[PK[5]], w=['affT'])
                t_lo = tiles[0] * 128
                S.dma('sp', lambda e: e.dma_start(out=aff_d[:, t_lo:T], in_=affT[:, t_lo:T]), r=['affT'], w=['aff_d'])
                S.barrier()

        def phase56(l, with_ctx):
            with ExitStack() as es:
                idxT = sb(es, [128, 5, 16], U32); gateT = sb(es, [128, 5, 16])
                es2 = ExitStack()
                aw = sb(es2, [16, TL]); vals = sb(es2, [16, 512]); idx = sb(es2, [16, 512], U32); idxf = sb(es2, [16, 512])
                awc = sb(es2, [16, 256]); valsc = sb(es2, [16, 32]); idxc = sb(es2, [16, 32], U32); idxcf = sb(es2, [16, 32])
                S.dma('sp', lambda e: e.dma_start(out=aw[:], in_=aff_d[:, 256:T]), w=['aw'])
                for r_ in range(64):
                    S.op('dve', lambda e, r_=r_: e.max(out=vals[:, r_ * 8:(r_ + 1) * 8], in_=aw[:]), r=['aw'], w=['vals'])
                    S.op('dve', lambda e, r_=r_: e.max_index(out=idx[:, r_ * 8:(r_ + 1) * 8], in_max=vals[:, r_ * 8:(r_ + 1) * 8], in_values=aw[:]), r=['aw', 'vals'], w=['idx'])
                    S.op('dve', lambda e, r_=r_: e.match_replace(out=aw[:], in_to_replace=vals[:, r_ * 8:(r_ + 1) * 8], in_values=aw[:], imm_value=-1.0), r=['aw', 'vals'], w=['aw'])
                S.op('dve', lambda e: e.tensor_copy(idxf[:], idx[:]), r=['idx'], w=['idxf'])
                S.op('dve', lambda e: e.tensor_scalar(out=idxf[:], in0=idxf[:], scalar1=256.0, scalar2=None, op0=ALU.add), r=['idxf'], w=['idxf'])
                for st in range(4):
                    TR(ps[0][:, 0:16], idxf[:, st * 128:(st + 1) * 128], ['idxf'], [PK[0]], n=16)
                    S.op('dve', lambda e, st=st: e.tensor_copy(idxT[:, st, :], ps[0][:, 0:16]), r=[PK[0]], w=['idxT'])
                    TR(ps[1][:, 0:16], vals[:, st * 128:(st + 1) * 128], ['vals'], [PK[1]], n=16)
                    S.op('dve', lambda e, st=st: e.tensor_copy(gateT[:, st, :], ps[1][:, 0:16]), r=[PK[1]], w=['gateT'])
                if with_ctx:
                    S.dma('sp', lambda e: e.dma_start(out=awc[:], in_=aff_d[:, 0:256]), w=['awc'])
                    for r_ in range(4):
                        S.op('dve', lambda e, r_=r_: e.max(out=valsc[:, r_ * 8:(r_ + 1) * 8], in_=awc[:]), r=['awc'], w=['valsc'])
                        S.op('dve', lambda e, r_=r_: e.max_index(out=idxc[:, r_ * 8:(r_ + 1) * 8], in_max=valsc[:, r_ * 8:(r_ + 1) * 8], in_values=awc[:]), r=['awc', 'valsc'], w=['idxc'])
                        S.op('dve', lambda e, r_=r_: e.match_replace(out=awc[:], in_to_replace=valsc[:, r_ * 8:(r_ + 1) * 8], in_values=awc[:], imm_value=-1.0), r=['awc', 'valsc'], w=['awc'])
                    S.op('dve', lambda e: e.tensor_copy(idxcf[:], idxc[:]), r=['idxc'], w=['idxcf'])
                    TR(ps[0][0:32, 0:16], idxcf[:, :], ['idxcf'], [PK[0]], n=16)
                    S.op('dve', lambda e: e.tensor_copy(idxT[0:32, 4, :], ps[0][0:32, 0:16]), r=[PK[0]], w=['idxT'])
                    TR(ps[1][0:32, 0:16], valsc[:, :], ['valsc'], [PK[1]], n=16)
                    S.op('dve', lambda e: e.tensor_copy(gateT[0:32, 4, :], ps[1][0:32, 0:16]), r=[PK[1]], w=['gateT'])
                S.barrier(); es2.close()
                NS = 544 if with_ctx else 512
                nst = 5 if with_ctx else 4
                wg = [sb(es, [128, 8, D], BF16) for _ in range(2)]; wu = [sb(es, [128, 8, D], BF16) for _ in range(2)]; wd = [sb(es, [128, 8, D], BF16) for _ in range(2)]
                xe = [sb(es, [128, 5, D]) for _ in range(2)]
                xeT = sb(es, [128, 8, 544], BF16); hidT = sb(es, [128, 8, 544], BF16)
                sg = [sb(es, [128, 544]) for _ in range(2)]
                ye = [sb(es, [128, D]) for _ in range(2)]

                def load_expert(e_):
                    ep = e_ % 2
                    for c in range(8):
                        S.dma('pool', lambda e, c=c: e.dma_start(out=wg[ep][:, c, :], in_=w_gate[l, e_, c * 128:(c + 1) * 128, :]), w=[f'wg{ep}'])
                        S.dma('pool', lambda e, c=c: e.dma_start(out=wu[ep][:, c, :], in_=w_up[l, e_, c * 128:(c + 1) * 128, :]), w=[f'wu{ep}'])
                        S.dma('pool', lambda e, c=c: e.dma_start(out=wd[ep][:, c, :], in_=w_down[l, e_, c * 128:(c + 1) * 128, :]), w=[f'wd{ep}'])
                    for st in range(nst):
                        n = 128 if st < 4 else 32
                        S.dma('pool', lambda e, st=st, n=n: e.indirect_dma_start(out=xe[ep][0:n, st, :], out_offset=None, in_=hm_d[:, :],
                                                                              in_offset=bass.IndirectOffsetOnAxis(ap=idxT[0:n, st, e_:e_ + 1], axis=0)),
                              r=['idxT'], w=[f'xe{ep}'])

                load_expert(0)
                yc_ = 0
                for e_ in range(16):
                    ep = e_ % 2
                    if e_ + 1 < 16: load_expert(e_ + 1)
                    for st in range(nst):
                        n = 128 if st < 4 else 32
                        for half in range(2):
                            for c4 in range(4):
                                cc = half * 4 + c4
                                TR(ps[half][:, c4 * 128:c4 * 128 + n], xe[ep][0:n, st, cc * 128:(cc + 1) * 128], [f'xe{ep}'], [PK[half]], n=n)
                            src = ps[half][:, :].rearrange("p (c t) -> p c t", c=4)[:, :, 0:n]
                            if half == 0:
                                S.op('act', lambda e, st=st, n=n, src=src: e.copy(out=xeT[:, 0:4, st * 128:st * 128 + n], in_=src), r=[PK[0]], w=['xeT'])
                            else:
                                S.op('dve', lambda e, st=st, n=n, src=src: e.tensor_copy(xeT[:, 4:8, st * 128:st * 128 + n], src), r=[PK[1]], w=['xeT'])
                    for fc in range(8):
                        for (n0, n1) in ([(0, 512), (512, 544)] if with_ctx else [(0, 512)]):
                            pg = ps[2] if n0 == 0 else ps[4]; pu = ps[3] if n0 == 0 else ps[5]
                            pgk = PK[2] if n0 == 0 else PK[4]; puk = PK[3] if n0 == 0 else PK[5]
                            nn = n1 - n0
                            for c in range(8):
                                MM(pg[:, 0:nn], wg[ep][:, c, fc * 128:(fc + 1) * 128], xeT[:, c, n0:n1], c == 0, c == 7, [f'wg{ep}', 'xeT'], [pgk])
                            for c in range(8):
                                MM(pu[:, 0:nn], wu[ep][:, c, fc * 128:(fc + 1) * 128], xeT[:, c, n0:n1], c == 0, c == 7, [f'wu{ep}', 'xeT'], [puk])
                            sp2 = fc % 2
                            S.op('act', lambda e, pg=pg, nn=nn, sp2=sp2: e.activation(out=sg[sp2][:, 0:nn], in_=pg[:, 0:nn], func=AF.Silu), r=[pgk], w=[f'sg{sp2}'])
                            S.op('dve', lambda e, pu=pu, nn=nn, n0=n0, n1=n1, fc=fc, sp2=sp2: e.tensor_tensor(out=hidT[:, fc, n0:n1], in0=pu[:, 0:nn], in1=sg[sp2][:, 0:nn], op=ALU.mult), r=[puk, f'sg{sp2}'], w=['hidT'])
                    for st in range(nst):
                        n = 128 if st < 4 else 32
                        yp = yc_ % 2; yc_ += 1
                        for half in range(2):
                            for fc in range(8):
                                MM(ps[6 + half][0:n, :], hidT[:, fc, st * 128:st * 128 + n], wd[ep][:, fc, half * 512:(half + 1) * 512], fc == 0, fc == 7, ['hidT', f'wd{ep}'], [PK[6 + half]])
                            eng = 'act' if half == 0 else 'dve'
                            if half == 0:
                                S.op('act', lambda e, n=n, st=st, yp=yp: e.activation(out=ye[yp][0:n, 0:512], in_=ps[6][0:n, :], func=AF.Identity, scale=gateT[0:n, st, e_:e_ + 1]), r=[PK[6], 'gateT'], w=[f'ye{yp}'])
                            else:
                                S.op('dve', lambda e, n=n, st=st, yp=yp: e.tensor_scalar(out=ye[yp][0:n, 512:1024], in0=ps[7][0:n, :], scalar1=gateT[0:n, st, e_:e_ + 1], scalar2=None, op0=ALU.mult), r=[PK[7], 'gateT'], w=[f'ye{yp}'])
                        S.dma('pool', lambda e, st=st, n=n, yp=yp: e.indirect_dma_start(out=moe_d[:, :], out_offset=bass.IndirectOffsetOnAxis(ap=idxT[0:n, st, e_:e_ + 1], axis=0),
                                                                                  in_=ye[yp][0:n, :], in_offset=None, compute_op=ALU.add),
                              r=[f'ye{yp}', 'idxT'], w=['moe_d'])
                S.barrier()

        def phase7(l, dst, tiles, final):
            with ExitStack() as es:
                g2b = [sb(es, [128, D]) for _ in range(2)]
                lg = sb(es, [128, D]); lb = sb(es, [128, D])
                xt = [sb(es, [128, D]) for _ in range(2)]; ft = [sb(es, [128, D]) for _ in range(2)]
                rr = sb(es, [128, D]); xo = [sb(es, [128, D]) for _ in range(2)]
                st6 = sb(es, [128, 4, 6]); mv = sb(es, [128, 2]); rstd = sb(es, [128, 1]); nmr = sb(es, [128, 1])
                for m in range(2):
                    bcast_load(es, 'sp', g2b[m], 5, m, f'g2b{m}')
                vec_bcast('sp', lg, ln2_g[l], 'lg'); vec_bcast('sp', lb, ln2_b[l], 'lb')
                for ti in tiles:
                    p2 = ti % 2; m = 1 if ti < 2 else 0
                    S.dma('sp', lambda e, ti=ti, p2=p2: e.dma_start(out=xt[p2][:], in_=x1_d[ti * 128:(ti + 1) * 128, :]), w=[f'xt{p2}'])
                    S.dma('sp', lambda e, ti=ti, p2=p2: e.dma_start(out=ft[p2][:], in_=moe_d[ti * 128:(ti + 1) * 128, :]), w=[f'ft{p2}'])
                    S.op('dve', lambda e, p2=p2, m=m: e.tensor_tensor(out=rr[:], in0=ft[p2][:], in1=g2b[m][:], op=ALU.mult), r=[f'ft{p2}', f'g2b{m}'], w=['rr'])
                    S.op('dve', lambda e, p2=p2: e.scalar_tensor_tensor(out=rr[:], in0=xt[p2][:], scalar=ALPHA, in1=rr[:], op0=ALU.mult, op1=ALU.add), r=[f'xt{p2}', 'rr'], w=['rr'])
                    ln_stats('l7', rr, D, mv, rstd, nmr, st6, ['rr'])
                    S.op('act', lambda e: e.activation(out=rr[:], in_=rr[:], func=AF.Identity, bias=nmr[:, 0:1], scale=rstd[:, 0:1]), r=['rr', 'l7rs', 'l7nm'], w=['rr'])
                    S.op('dve', lambda e: e.tensor_tensor(out=rr[:], in0=rr[:], in1=lg[:], op=ALU.mult), r=['rr', 'lg'], w=['rr'])
                    S.op('dve', lambda e, p2=p2: e.tensor_tensor(out=xo[p2][:], in0=rr[:], in1=lb[:], op=ALU.add), r=['rr', 'lb'], w=[f'xo{p2}'])
                    if final:
                        S.dma('sp', lambda e, ti=ti, p2=p2: e.dma_start(out=dst[(ti - 2) * 128:(ti - 1) * 128, :], in_=xo[p2][:]), r=[f'xo{p2}'], w=[('out', ti)])
                    else:
                        S.dma('sp', lambda e, ti=ti, p2=p2: e.dma_start(out=dst[ti * 128:(ti + 1) * 128, :], in_=xo[p2][:]), r=[f'xo{p2}'], w=[('out', ti)])
                S.barrier()

        for l in range(nlayers):
            last = (l == 1)
            xsrc = xin if l == 0 else xs1
            tiles = list(range(2, NT)) if last else list(range(NT))
            phase0(l)
            if stop == 'p0': break
            phase1(l, xsrc)
            if stop == 'p1': break
            phase2(l, with_ctx_q=not last)
            if stop == 'p2': break
            phase3(l)
            if stop == 'p3': break
            phase4(l, xsrc, tiles)
            if stop == 'p4': break
            phase56(l, with_ctx=not last)
            if stop == 'p6': break
            phase7(l, y if last else xs1, tiles, final=last)
        S.barrier()
    return nc, dbg_outs


def _consts():
    c = {}
    c['c_ident'] = np.eye(128, dtype=np.float32)
    t = np.arange(TL)
    row = (t // 64).astype(np.float32); col = (t % 64).astype(np.float32)
    inv = (10000.0 ** (-np.arange(8, dtype=np.float32) * 2.0 / 16)).astype(np.float32)
    ang = np.concatenate([row[:, None] * inv, col[:, None] * inv], axis=-1).astype(np.float32)
    cos = np.cos(ang).astype(np.float32); sin = np.sin(ang).astype(np.float32)
    cosT = np.ones((32, T), np.float32); sinT = np.zeros((32, T), np.float32)
    for a in range(2):
        for i in range(8):
            cosT[a * 16 + i, TC:] = cos[:, a * 8 + i]; cosT[a * 16 + 8 + i, TC:] = cos[:, a * 8 + i]
            sinT[a * 16 + i, TC:] = -sin[:, a * 8 + i]; sinT[a * 16 + 8 + i, TC:] = sin[:, a * 8 + i]
    c['c_rope'] = np.stack([cosT, sinT]).astype(np.float32)
    s = np.arange(64)[:, None]; tt = np.arange(64)[None, :]
    m0 = (s <= tt).astype(np.float32); m1 = (s >= tt).astype(np.float32)
    c['c_mask'] = np.stack([np.concatenate([m0, m0], 0), np.concatenate([m1, m1], 0)]).astype(np.float32)
    misc = np.zeros((8, 16), np.float32)
    misc[4:, 0] = 1.0; misc[:4, 1] = 1.0; misc[4:, 1] = -1.0; misc[:4, 2] = 1.0
    c['c_misc'] = misc
    e8 = np.zeros((8, 8, NSTEP), np.float32)
    for k in range(8): e8[k, k, :] = 1.0
    c['c_eye8'] = e8
    pr = np.zeros((NSTEP, NSTEP), np.float32)
    for p in range(NSTEP):
        cidx = 3 - p if p < 4 else 71 - p
        pr[cidx, p] = 1.0
    c['c_prev'] = pr
    rst = np.ones((8, T), np.float32); rst[:, ::64] = 0.0
    c['c_rst'] = rst
    return c


def _prep_weights(inp):
    w = {}
    sw = np.arange(32)
    for a in range(2):
        for i in range(8):
            sw[a * 16 + i] = a * 16 + 8 + i; sw[a * 16 + 8 + i] = a * 16 + i
    cols = np.concatenate([np.arange(0, 512), np.arange(1696, 2208),
                           np.arange(512, 1152), np.arange(1152, 1184), 1152 + sw,
                           np.arange(1184, 1696), np.arange(2208, 2224)])
    assert cols.size == NWIN
    w['w_in'] = np.ascontiguousarray(inp['w_in'][:, :, cols]); w['b_in'] = np.ascontiguousarray(inp['b_in'][:, cols])
    uq = inp['w_uq']
    swc = np.concatenate([h * 96 + 64 + sw for h in range(8)])
    w['w_uq'] = np.ascontiguousarray(np.concatenate([uq, uq[:, :, swc]], axis=-1))
    ukv = inp['w_ukv']
    kc = np.concatenate([np.arange(h * 128, h * 128 + 64) for h in range(8)])
    vc = np.concatenate([np.arange(h * 128 + 64, h * 128 + 128) for h in range(8)])
    w['w_ukv'] = np.ascontiguousarray(np.concatenate([ukv[:, :, kc], ukv[:, :, vc]], axis=-1))
    w['ml_f_bias'] = np.ascontiguousarray(inp['ml_f_bias'].reshape(2, 8))
    for k in ['w_ada', 'b_ada', 'sg_ln_g', 'sg_ln_b', 'sg_w', 'sg_b', 'q_norm_g', 'kv_norm_g', 'ml_conv_w', 'ml_conv_b', 'ml_norm_g',
              'w_out', 'ln1_g', 'ln1_b', 'w_router', 'w_gate', 'w_up', 'w_down', 'ln2_g', 'ln2_b']:
        w[k] = np.ascontiguousarray(inp[k])
    return w


_CACHE = {}


def kernel(**inputs):
    inp = {k: np.asarray(v, dtype=np.float32) for k, v in inputs.items()}
    if 'nc' not in _CACHE:
        _CACHE['nc'] = build()[0]
    nc = _CACHE['nc']
    consts = _consts(); w = _prep_weights(inp)
    in_maps = []
    for b in range(8):
        m = dict(consts); m.update(w)
        m['xin'] = np.ascontiguousarray(np.concatenate([inp['ctx'][b], inp['x'][b]], axis=0))
        m['cvec'] = np.ascontiguousarray(np.stack([inp['c'][b], inp['c_ctx']]))
        in_maps.append(m)
    res = run_bass_kernel_spmd(nc, in_maps, core_ids=list(range(8)))
    return np.stack([np.asarray(r['y'], dtype=np.float32) for r in res.results], axis=0)
```

```python
import numpy as np
from contextlib import ExitStack
import concourse.bass as bass
import concourse.mybir as mybir
from concourse.bass_utils import run_bass_kernel_spmd

F32 = mybir.dt.float32; BF16 = mybir.dt.bfloat16; U32 = mybir.dt.uint32
AF = mybir.ActivationFunctionType; ALU = mybir.AluOpType; AX = mybir.AxisListType

D = 1024; T = 4352; NT = 34; TC = 256; TL = 4096
NWIN = 2256
ALPHA = 4 ** 0.25
EPS = 1e-6
MLA_SCALE = 96 ** -0.5
NSTEP = 68


class Sched:
    NSLOT = 6

    def __init__(self, nc, es):
        self.nc = nc
        self.E = {'pe': nc.tensor, 'dve': nc.vector, 'act': nc.scalar, 'pool': nc.gpsimd, 'sp': nc.sync}
        self.sem = {}; self.cnt = {}
        for k in self.E:
            self.sem[k] = es.enter_context(nc.semaphore('s_' + k)); self.cnt[k] = 0
        self.slots = {}
        for q in ('sp', 'pool'):
            self.slots[q] = []
            for i in range(self.NSLOT):
                k = f'd{q}{i}'
                self.sem[k] = es.enter_context(nc.semaphore('s_' + k)); self.cnt[k] = 0
                self.slots[q].append(k)
        self.rr = {'sp': 0, 'pool': 0}
        self.waited = {k: {} for k in self.E}
        self.lastw = {}; self.readers = {}

    def _wait(self, eng, ev):
        k, v = ev
        if v <= 0: return
        if k == eng and eng == 'pe': return
        if self.waited[eng].get(k, 0) >= v: return
        self.E[eng].wait_ge(self.sem[k], v); self.waited[eng][k] = v

    def _deps(self, eng, r, w):
        deps = {}
        def add(k, v):
            if deps.get(k, 0) < v: deps[k] = v
        for b in r:
            ev = self.lastw.get(b)
            if ev: add(*ev)
        for b in w:
            ev = self.lastw.get(b)
            if ev: add(*ev)
            for k, v in self.readers.get(b, {}).items(): add(k, v)
        for k, v in deps.items(): self._wait(eng, (k, v))

    def _commit(self, ev, r, w):
        for b in r:
            d = self.readers.setdefault(b, {})
            if d.get(ev[0], 0) < ev[1]: d[ev[0]] = ev[1]
        for b in w:
            self.lastw[b] = ev; self.readers[b] = {}

    def op(self, eng, fn, r=(), w=()):
        self._deps(eng, r, w)
        inst = fn(self.E[eng])
        self.cnt[eng] += 1
        inst.then_inc(self.sem[eng], 1)
        self._commit((eng, self.cnt[eng]), r, w)

    def dma(self, q, fn, r=(), w=()):
        self._deps(q, r, w)
        i = self.rr[q]; self.rr[q] = (i + 1) % self.NSLOT
        k = self.slots[q][i]
        self._wait(q, (k, self.cnt[k]))
        inst = fn(self.E[q])
        self.cnt[k] += 16
        inst.then_inc(self.sem[k], 16)
        self._commit((k, self.cnt[k]), r, w)

    def barrier(self):
        for e in self.E:
            for k in self.sem:
                if k == e: continue
                self._wait(e, (k, self.cnt[k]))
        self.lastw = {}; self.readers = {}


def build(debug=False, nlayers=2, stop=None):
    nc = bass.Bass("TRN2", target_bir_lowering=False)
    dbg_outs = []

    def din(name, shape, dt=F32):
        return nc.dram_tensor(name, list(shape), dt, kind="ExternalInput").ap()

    def dscr(name, shape, dt=F32):
        if debug:
            dbg_outs.append(name)
            return nc.dram_tensor(name, list(shape), dt, kind="ExternalOutput").ap()
        return nc.dram_tensor(name, list(shape), dt, kind="Internal").ap()

    L = 2
    xin = din("xin", [T, D]); cvec = din("cvec", [2, D])
    w_ada = din("w_ada", [L, D, 6 * D]); b_ada = din("b_ada", [L, 6 * D])
    w_in = din("w_in", [L, D, NWIN]); b_in = din("b_in", [L, NWIN])
    sg_ln_g = din("sg_ln_g", [L, 256]); sg_ln_b = din("sg_ln_b", [L, 256])
    sg_w = din("sg_w", [L, 4, 128, 128]); sg_b = din("sg_b", [L, 4, 128])
    q_norm_g = din("q_norm_g", [L, 384]); kv_norm_g = din("kv_norm_g", [L, 256])
    w_uq = din("w_uq", [L, 384, 1024]); w_ukv = din("w_ukv", [L, 256, 1024])
    ml_conv_w = din("ml_conv_w", [L, 5, 512]); ml_conv_b = din("ml_conv_b", [L, 512])
    ml_f_bias = din("ml_f_bias", [L, 8]); ml_norm_g = din("ml_norm_g", [L, 256])
    w_out = din("w_out", [L, D, D]); ln1_g = din("ln1_g", [L, D]); ln1_b = din("ln1_b", [L, D])
    w_router = din("w_router", [L, D, 16])
    w_gate = din("w_gate", [L, 16, D, D]); w_up = din("w_up", [L, 16, D, D]); w_down = din("w_down", [L, 16, D, D])
    ln2_g = din("ln2_g", [L, D]); ln2_b = din("ln2_b", [L, D])
    c_ident = din("c_ident", [128, 128])
    c_rope = din("c_rope", [2, 32, T])
    c_mask = din("c_mask", [2, 128, 64])
    c_misc = din("c_misc", [8, 16])
    c_eye8 = din("c_eye8", [8, 8, NSTEP])
    c_prev = din("c_prev", [NSTEP, NSTEP])
    c_rst = din("c_rst", [8, T])

    y = nc.dram_tensor("y", [TL, D], F32, kind="ExternalOutput").ap()

    xs1 = dscr("xs1", [T, D]); x1_d = dscr("x1_d", [T, D]); hm_d = dscr("hm_d", [T, D]); moe_d = dscr("moe_d", [T, D])
    catT_d = dscr("catT_d", [D, T], BF16)
    qT_d = dscr("qT_d", [8, 96, T], BF16); kT_d = dscr("kT_d", [8, 96, T], BF16)
    vaug_d = dscr("vaug_d", [T, 520], BF16)
    mlqk_d = dscr("mlqk_d", [8, 64, T]); gates_d = dscr("gates_d", [16, T])
    mlv_d = dscr("mlv_d", [T, 256], BF16); sigo_d = dscr("sigo_d", [T, 256])
    ada_d = dscr("ada_d", [96, 128])
    aff_d = dscr("aff_d", [16, T])

    with ExitStack() as top:
        S = Sched(nc, top)
        sbn = [0]

        def sb(es, shape, dt=F32, name=None):
            sbn[0] += 1
            return es.enter_context(nc.sbuf_tensor(f"{name or 't'}_{sbn[0]}", list(shape), dt))

        ps = [top.enter_context(nc.psum_tensor(f"ps{i}", [128, 512], F32)) for i in range(8)]
        PK = [f"ps{i}" for i in range(8)]
        ident = sb(top, [128, 128], F32, "ident")
        ones_bf = sb(top, [128, 512], BF16, "ones_bf")
        ones_f = sb(top, [128, 512], F32, "ones_f")
        zeros_f = sb(top, [128, 1024], F32, "zeros_f")
        epsc = sb(top, [128, 1], F32, "epsc")
        ada = sb(top, [128, 96], F32, "ada")
        adp = sb(top, [128, 96], F32, "adp")

        S.dma('sp', lambda e: e.dma_start(out=ident[:], in_=c_ident), w=['ident'])
        S.op('dve', lambda e: e.memset(ones_bf[:], 1.0), w=['ones_bf'])
        S.op('dve', lambda e: e.memset(ones_f[:], 1.0), w=['ones_f'])
        S.op('dve', lambda e: e.memset(zeros_f[:], 0.0), w=['zeros_f'])
        S.op('dve', lambda e: e.memset(epsc[:], EPS), w=['epsc'])

        def MM(out, lhsT, rhs, st, sp_, r, w):
            S.op('pe', lambda e: e.matmul(out, lhsT, rhs, start=st, stop=sp_), r, w)

        def TR(out, in_, r, w, n=128):
            S.op('pe', lambda e: e.transpose(out, in_, ident[:n, :n]), list(r) + ['ident'], w)

        def ln_stats(es_tag, xap, width, mv, rstd, nmr, stats, rk):
            nch = width // 256 if width >= 256 else 1
            cw = width // nch
            for c in range(nch):
                S.op('dve', lambda e, c=c: e.bn_stats(stats[:, c, :], xap[:, c * cw:(c + 1) * cw]), r=rk, w=[es_tag + 'st'])
            S.op('dve', lambda e: e.bn_aggr(mv[:], stats[:, 0:nch, :]), r=[es_tag + 'st'], w=[es_tag + 'mv'])
            S.op('act', lambda e: e.activation(out=rstd[:], in_=mv[:, 1:2], func=AF.Sqrt, bias=epsc[:, 0:1], scale=1.0),
                 r=[es_tag + 'mv', 'epsc'], w=[es_tag + 'rs'])
            S.op('dve', lambda e: e.reciprocal(rstd[:], rstd[:]), r=[es_tag + 'rs'], w=[es_tag + 'rs'])
            S.op('dve', lambda e: e.scalar_tensor_tensor(out=nmr[:], in0=mv[:, 0:1], scalar=-1.0, in1=rstd[:], op0=ALU.mult, op1=ALU.mult),
                 r=[es_tag + 'mv', es_tag + 'rs'], w=[es_tag + 'nm'])

        def phase0(l):
            with ExitStack() as es:
                cfm = sb(es, [128, 2, 8]); sfm = sb(es, [128, 2, 8])
                brow = sb(es, [1, 6 * D])
                wa = [sb(es, [128, 8, 768]) for _ in range(2)]
                adaT = sb(es, [96, 128])
                with nc.allow_non_contiguous_dma(reason="tiny"):
                    for m_ in range(2):
                        S.dma('sp', lambda e, m_=m_: e.dma_start(out=cfm[:, m_, :], in_=cvec[m_].rearrange("(c p) -> p c", p=128)), w=['cfm'])
                S.dma('sp', lambda e: e.dma_start(out=brow[:], in_=b_ada[l:l + 1, :]), w=['brow'])
                S.op('act', lambda e: e.activation(out=sfm[:], in_=cfm[:], func=AF.Silu), r=['cfm'], w=['sfm'])
                for blk in range(8):
                    wt = wa[blk % 2]; wk = f'wa{blk % 2}'
                    S.dma('sp', lambda e: e.dma_start(out=wt[:], in_=w_ada[l, :, blk * 768:(blk + 1) * 768].rearrange("(c p) n -> p c n", p=128)), w=[wk])
                    for nn in range(6):
                        n = blk * 6 + nn
                        for c in range(8):
                            MM(ps[0][:, n * 2:(n + 1) * 2], wt[:, c, nn * 128:(nn + 1) * 128], sfm[:, :, c], c == 0, False, [wk, 'sfm'], [PK[0]])
                        MM(ps[0][:, n * 2:(n + 1) * 2], brow[0:1, n * 128:(n + 1) * 128], ones_f[0:1, 0:2], False, True, ['brow', 'ones_f'], [PK[0]])
                S.op('dve', lambda e: e.tensor_copy(ada[:], ps[0][:, 0:96]), r=[PK[0]], w=['ada'])
                S.op('dve', lambda e: e.tensor_scalar(out=adp[:], in0=ada[:], scalar1=1.0, scalar2=None, op0=ALU.add), r=['ada'], w=['adp'])
                TR(ps[1][:96, 0:128], ada[:, :], ['ada'], [PK[1]])
                S.op('dve', lambda e: e.tensor_copy(adaT[:], ps[1][:96, 0:128]), r=[PK[1]], w=['adaT'])
                S.dma('sp', lambda e: e.dma_start(out=ada_d, in_=adaT[:]), r=['adaT'], w=['ada_d'])
                for ti in range(NT):
                    S.dma('sp', lambda e, ti=ti: e.dma_start(out=moe_d[ti * 128:(ti + 1) * 128, :], in_=zeros_f[:]), r=['zeros_f'], w=[('moe', ti)])
                S.barrier()

        def ada_col(j, cc, m):
            n = (j * 8 + cc) * 2 + m
            return n

        def bcast_load(es, q, dst, j, m, key):
            src = ada_d.rearrange("(n m) p -> m n p", m=2)[m, j * 8:(j + 1) * 8, :].partition_broadcast(128)
            S.dma(q, lambda e: e.dma_start(out=dst[:].rearrange("p (c f) -> p c f", c=8), in_=src), r=['ada_d'], w=[key])

        def vec_bcast(q, dst, vec_ap, key):
            S.dma(q, lambda e: e.dma_start(out=dst[:], in_=vec_ap.partition_broadcast(128)), w=[key])

        def blocks():
            out = [(0, 256, [0, 1])]
            for b in range(8):
                out.append((256 + 512 * b, 512, [2 + 4 * b + i for i in range(4)]))
            return out

        def phase1(l, xsrc):
            with ExitStack() as es:
                win = sb(es, [128, 8, NWIN], BF16); brow = sb(es, [1, NWIN], BF16)
                wuq = sb(es, [128, 3, 1024], BF16); wukv = sb(es, [128, 2, 1024], BF16)
                wsT = sb(es, [128, 4, 128], BF16)
                w32 = sb(es, [128, 3, 1024]); wsf = sb(es, [128, 512]); gq = sb(es, [128, 3]); gkv = sb(es, [128, 2])
                sgbT = sb(es, [128, 4]); lng = sb(es, [128, 256]); lnb = sb(es, [128, 256])
                rope = [sb(es, [96, 2, 512]) for _ in range(2)]
                xt = [sb(es, [128, D]) for _ in range(2)]
                xn = [sb(es, [128, D]) for _ in range(2)]
                xmT = [sb(es, [128, 8, 512], BF16) for _ in range(2)]
                st6 = sb(es, [128, 4, 6]); mv = sb(es, [128, 2]); rstd = sb(es, [128, 1]); nmr = sb(es, [128, 1])
                st6b = sb(es, [128, 4, 6]); mvb = sb(es, [128, 2]); rstdb = sb(es, [128, 1]); nmrb = sb(es, [128, 1])
                gl = [sb(es, [128, 512]) for _ in range(2)]
                vn = sb(es, [128, 256]); vnb = sb(es, [128, 256], BF16)
                ya = [sb(es, [128, 256]) for _ in range(2)]
                yaT = [sb(es, [128, 2, 128], BF16) for _ in range(2)]
                mlv = [sb(es, [128, 256], BF16) for _ in range(2)]
                sgo = [sb(es, [128, 256]) for _ in range(2)]
                cqT = sb(es, [128, 3, 512], BF16); ckvT = sb(es, [128, 2, 512], BF16)
                sq = [sb(es, [128, 512]) for _ in range(2)]
                rq = sb(es, [96, 512]); rkv = sb(es, [64, 512]); rkvc = sb(es, [128, 4])
                qs = [sb(es, [96, 512]) for _ in range(2)]; qb = [sb(es, [96, 512]) for _ in range(2)]
                r1 = sb(es, [96, 512]); r2 = sb(es, [96, 512])
                qTt = [sb(es, [96, 512], BF16) for _ in range(2)]
                kTt = sb(es, [96, 8, 512], BF16)
                krt = sb(es, [96, 512], BF16)
                va = [sb(es, [128, 8, 65], BF16) for _ in range(2)]
                fm32 = [sb(es, [64, 512]) for _ in range(2)]

                for c in range(8):
                    S.dma('pool', lambda e, c=c: e.dma_start(out=win[:, c, :], in_=w_in[l, c * 128:(c + 1) * 128, :]), w=['win'])
                S.dma('pool', lambda e: e.dma_start(out=brow[:], in_=b_in[l:l + 1, :]), w=['brow'])
                with nc.allow_non_contiguous_dma(reason="tiny"):
                    S.dma('sp', lambda e: e.dma_start(out=gq[:], in_=q_norm_g[l].rearrange("(c p) -> p c", p=128)), w=['gq'])
                    S.dma('sp', lambda e: e.dma_start(out=gkv[:], in_=kv_norm_g[l].rearrange("(c p) -> p c", p=128)), w=['gkv'])
                    S.dma('sp', lambda e: e.dma_start(out=sgbT[:], in_=sg_b[l].rearrange("g t -> t g")), w=['sgbT'])
                S.dma('sp', lambda e: e.dma_start(out=w32[:], in_=w_uq[l].rearrange("(c p) n -> p c n", p=128)), w=['w32'])
                for c in range(3):
                    S.op('dve', lambda e, c=c: e.tensor_scalar(out=wuq[:, c, :], in0=w32[:, c, :], scalar1=gq[:, c:c + 1], scalar2=None, op0=ALU.mult),
                         r=['w32', 'gq'], w=['wuq'])
                S.dma('sp', lambda e: e.dma_start(out=w32[:, 0:2, :], in_=w_ukv[l].rearrange("(c p) n -> p c n", p=128)), r=[], w=['w32'])
                for c in range(2):
                    S.op('dve', lambda e, c=c: e.tensor_scalar(out=wukv[:, c, :], in0=w32[:, c, :], scalar1=gkv[:, c:c + 1], scalar2=None, op0=ALU.mult),
                         r=['w32', 'gkv'], w=['wukv'])
                S.dma('sp', lambda e: e.dma_start(out=wsf[:].rearrange("p (g s) -> p g s", g=4), in_=sg_w[l].rearrange("g t s -> t g s")), w=['wsf'])
                for g in range(4):
                    TR(ps[0][:, g * 128:(g + 1) * 128], wsf[:, g * 128:(g + 1) * 128], ['wsf'], [PK[0]])
                S.op('dve', lambda e: e.tensor_copy(wsT[:].rearrange("p g t -> p (g t)"), ps[0][:, 0:512]), r=[PK[0]], w=['wsT'])
                vec_bcast('sp', lng, sg_ln_g[l], 'lng'); vec_bcast('sp', lnb, sg_ln_b[l], 'lnb')

                bi = 0
                for (t0, N, tiles) in blocks():
                    bp = bi % 2; bi += 1
                    xm = xmT[bp]; xmk = f'xmT{bp}'
                    m_ada = 1 if t0 == 0 else 0
                    S.dma('sp', lambda e, bp=bp: e.dma_start(out=rope[bp][64:96, :, 0:N], in_=c_rope[:, :, t0:t0 + N].rearrange("a p t -> p a t")), w=[f'rope{bp}'])
                    for li, ti in enumerate(tiles):
                        p2 = ti % 2
                        xk = f'xt{p2}'; xnk = f'xn{p2}'
                        S.dma('sp', lambda e, ti=ti, p2=p2: e.dma_start(out=xt[p2][:], in_=xsrc[ti * 128:(ti + 1) * 128, :]), w=[xk])
                        ln_stats('l1', xt[p2], D, mv, rstd, nmr, st6, [xk])
                        S.op('act', lambda e, p2=p2: e.activation(out=xn[p2][:], in_=xt[p2][:], func=AF.Identity, bias=nmr[:, 0:1], scale=rstd[:, 0:1]),
                             r=[xk, 'l1rs', 'l1nm'], w=[xnk])
                        for half in range(2):
                            pb = ps[half]
                            for c4 in range(4):
                                cc = half * 4 + c4
                                TR(pb[:, c4 * 128:(c4 + 1) * 128], xn[p2][:, cc * 128:(cc + 1) * 128], [xnk], [PK[half]])
                            for c4 in range(4):
                                cc = half * 4 + c4
                                eng = 'act' if c4 % 2 == 0 else 'dve'
                                if eng == 'act':
                                    S.op('act', lambda e, cc=cc, c4=c4, pb=pb, li=li: e.activation(out=xm[:, cc, li * 128:(li + 1) * 128], in_=pb[:, c4 * 128:(c4 + 1) * 128], func=AF.Identity,
                                                                                         bias=ada[:, ada_col(0, cc, m_ada):ada_col(0, cc, m_ada) + 1], scale=adp[:, ada_col(1, cc, m_ada):ada_col(1, cc, m_ada) + 1]),
                                         r=[PK[half], 'ada', 'adp'], w=[xmk])
                                else:
                                    S.op('dve', lambda e, cc=cc, c4=c4, pb=pb, li=li: e.tensor_scalar(out=xm[:, cc, li * 128:(li + 1) * 128], in0=pb[:, c4 * 128:(c4 + 1) * 128],
                                                                                            scalar1=adp[:, ada_col(1, cc, m_ada):ada_col(1, cc, m_ada) + 1], scalar2=ada[:, ada_col(0, cc, m_ada):ada_col(0, cc, m_ada) + 1],
                                                                                            op0=ALU.mult, op1=ALU.add),
                                         r=[PK[half], 'ada', 'adp'], w=[xmk])
                        for c in range(8):
                            MM(ps[2][:, :], xm[:, c, li * 128:(li + 1) * 128], win[:, c, 0:512], c == 0, False, [xmk, 'win'], [PK[2]])
                        MM(ps[2][:, :], ones_bf[0:1, 0:128], brow[0:1, 0:512], False, True, ['ones_bf', 'brow'], [PK[2]])
                        g_ = gl[p2]; gk = f'gl{p2}'
                        S.op('act', lambda e, g_=g_: e.activation(out=g_[:], in_=ps[2][:, :], func=AF.Gelu_apprx_tanh), r=[PK[2]], w=[gk])
                        ln_stats('l2', g_[:, 256:512], 256, mvb, rstdb, nmrb, st6b, [gk])
                        S.op('act', lambda e, g_=g_: e.activation(out=vn[:], in_=g_[:, 256:512], func=AF.Identity, bias=nmrb[:, 0:1], scale=rstdb[:, 0:1]),
                             r=[gk, 'l2rs', 'l2nm'], w=['vn'])
                        S.op('dve', lambda e: e.tensor_tensor(out=vn[:], in0=vn[:], in1=lng[:], op=ALU.mult), r=['vn', 'lng'], w=['vn'])
                        S.op('dve', lambda e: e.tensor_tensor(out=vnb[:], in0=vn[:], in1=lnb[:], op=ALU.add), r=['vn', 'lnb'], w=['vnb'])
                        for g in range(4):
                            MM(ps[3][:, g * 64:(g + 1) * 64], wsT[:, g, :], vnb[:, g * 64:(g + 1) * 64], True, True, ['wsT', 'vnb'], [PK[3]])
                        yat = ya[p2]; yak = f'ya{p2}'
                        for g in range(4):
                            S.op('dve', lambda e, g=g, yat=yat, g_=g_: e.scalar_tensor_tensor(out=yat[:, g * 64:(g + 1) * 64], in0=ps[3][:, g * 64:(g + 1) * 64], scalar=sgbT[:, g:g + 1],
                                                                                       in1=g_[:, g * 64:(g + 1) * 64], op0=ALU.add, op1=ALU.mult),
                                 r=[PK[3], 'sgbT', gk], w=[yak])
                        for c in range(2):
                            TR(ps[3][:, 256 + c * 128:256 + (c + 1) * 128], yat[:, c * 128:(c + 1) * 128], [yak], [PK[3]])
                        yT = yaT[p2]; yTk = f'yaT{p2}'
                        S.op('act', lambda e, yT=yT: e.copy(out=yT[:].rearrange("p c t -> p (c t)"), in_=ps[3][:, 256:512]), r=[PK[3]], w=[yTk])
                        S.dma('sp', lambda e, yT=yT, ti=ti: e.dma_start(out=catT_d[0:256, ti * 128:(ti + 1) * 128].rearrange("(c p) t -> p c t", p=128), in_=yT[:]), r=[yTk], w=[('catA', ti)])
                        for c in range(8):
                            MM(ps[4][:, :], xm[:, c, li * 128:(li + 1) * 128], win[:, c, 512:1024], c == 0, False, [xmk, 'win'], [PK[4]])
                        MM(ps[4][:, :], ones_bf[0:1, 0:128], brow[0:1, 512:1024], False, True, ['ones_bf', 'brow'], [PK[4]])
                        S.op('dve', lambda e, p2=p2: e.tensor_copy(mlv[p2][:], ps[4][:, 0:256]), r=[PK[4]], w=[f'mlv{p2}'])
                        S.op('act', lambda e, p2=p2: e.activation(out=sgo[p2][:], in_=ps[4][:, 256:512], func=AF.Sigmoid), r=[PK[4]], w=[f'sgo{p2}'])
                        S.dma('sp', lambda e, p2=p2, ti=ti: e.dma_start(out=mlv_d[ti * 128:(ti + 1) * 128, :], in_=mlv[p2][:]), r=[f'mlv{p2}'], w=[('mlv_d', ti)])
                        S.dma('sp', lambda e, p2=p2, ti=ti: e.dma_start(out=sigo_d[ti * 128:(ti + 1) * 128, :], in_=sgo[p2][:]), r=[f'sgo{p2}'], w=[('sigo_d', ti)])

                    FM0 = 1024
                    def fm_mm(pst, pk, col0, M, pbase=0):
                        for c in range(8):
                            MM(pst[pbase:pbase + M, 0:N], win[:, c, col0:col0 + M], xm[:, c, 0:N], c == 0, False, ['win', xmk], [pk])
                        MM(pst[pbase:pbase + M, 0:N], brow[0:1, col0:col0 + M], ones_bf[0:1, 0:N], False, True, ['brow', 'ones_bf'], [pk])
                    for c in range(3):
                        fm_mm(ps[5], PK[5], FM0 + c * 128, 128)
                        S.op('act', lambda e, c=c: e.copy(out=cqT[:, c, 0:N], in_=ps[5][:, 0:N]), r=[PK[5]], w=['cqT'])
                        S.op('act', lambda e, c=c: e.activation(out=sq[c % 2][:, 0:N], in_=ps[5][:, 0:N], func=AF.Square), r=[PK[5]], w=[f'sq{c % 2}'])
                        MM(ps[6][0:96, 0:N], ones_f[:, 0:96], sq[c % 2][:, 0:N], c == 0, c == 2, ['ones_f', f'sq{c % 2}'], [PK[6]])
                    S.op('act', lambda e: e.activation(out=rq[:, 0:N], in_=ps[6][0:96, 0:N], func=AF.Sqrt, bias=epsc[0:96, 0:1], scale=1.0 / 384), r=[PK[6], 'epsc'], w=['rq'])
                    S.op('dve', lambda e: e.reciprocal(rq[:, 0:N], rq[:, 0:N]), r=['rq'], w=['rq'])
                    rp = rope[bp]; rpk = f'rope{bp}'
                    for h in range(8):
                        hp = h % 2
                        for c in range(3):
                            MM(ps[5][0:96, 0:N], wuq[:, c, h * 96:(h + 1) * 96], cqT[:, c, 0:N], c == 0, c == 2, ['wuq', 'cqT'], [PK[5]])
                        for c in range(3):
                            MM(ps[7][64:96, 0:N], wuq[:, c, 768 + h * 32:768 + (h + 1) * 32], cqT[:, c, 0:N], c == 0, c == 2, ['wuq', 'cqT'], [PK[7]])
                        S.op('dve', lambda e, hp=hp: e.tensor_tensor(out=qs[hp][:, 0:N], in0=ps[5][0:96, 0:N], in1=rq[:, 0:N], op=ALU.mult), r=[PK[5], 'rq'], w=[f'qs{hp}'])
                        S.op('dve', lambda e, hp=hp: e.tensor_tensor(out=qb[hp][64:96, 0:N], in0=ps[7][64:96, 0:N], in1=rq[64:96, 0:N], op=ALU.mult), r=[PK[7], 'rq'], w=[f'qb{hp}'])
                        S.op('act', lambda e, hp=hp: e.copy(out=qTt[hp][0:64, 0:N], in_=qs[hp][0:64, 0:N]), r=[f'qs{hp}'], w=[f'qTt{hp}'])
                        S.op('pool', lambda e, hp=hp: e.tensor_tensor(out=r1[64:96, 0:N], in0=qs[hp][64:96, 0:N], in1=rp[64:96, 0, 0:N], op=ALU.mult), r=[f'qs{hp}', rpk], w=['r1'])
                        S.op('pool', lambda e, hp=hp: e.tensor_tensor(out=r2[64:96, 0:N], in0=qb[hp][64:96, 0:N], in1=rp[64:96, 1, 0:N], op=ALU.mult), r=[f'qb{hp}', rpk], w=['r2'])
                        S.op('pool', lambda e, hp=hp: e.tensor_tensor(out=qTt[hp][64:96, 0:N], in0=r1[64:96, 0:N], in1=r2[64:96, 0:N], op=ALU.add), r=['r1', 'r2'], w=[f'qTt{hp}'])
                        S.dma('sp', lambda e, hp=hp, h=h: e.dma_start(out=qT_d[h, :, t0:t0 + N], in_=qTt[hp][:, 0:N]), r=[f'qTt{hp}'], w=[('qT_d', h, t0)])
                    for c in range(2):
                        fm_mm(ps[5], PK[5], FM0 + 384 + c * 128, 128)
                        S.op('act', lambda e, c=c: e.copy(out=ckvT[:, c, 0:N], in_=ps[5][:, 0:N]), r=[PK[5]], w=['ckvT'])
                        S.op('act', lambda e, c=c: e.activation(out=sq[c][:, 0:N], in_=ps[5][:, 0:N], func=AF.Square), r=[PK[5]], w=[f'sq{c}'])
                    for c in range(2):
                        MM(ps[6][0:64, 0:N], ones_f[:, 0:64], sq[c][:, 0:N], c == 0, c == 1, ['ones_f', f'sq{c}'], [PK[6]])
                    S.op('act', lambda e: e.activation(out=rkv[:, 0:N], in_=ps[6][0:64, 0:N], func=AF.Sqrt, bias=epsc[0:64, 0:1], scale=1.0 / 256), r=[PK[6], 'epsc'], w=['rkv'])
                    S.op('dve', lambda e: e.reciprocal(rkv[:, 0:N], rkv[:, 0:N]), r=['rkv'], w=['rkv'])
                    for li in range(len(tiles)):
                        for c in range(2):
                            MM(ps[7][:, li:li + 1], sq[c][:, li * 128:(li + 1) * 128], ones_f[:, 0:1], c == 0, c == 1, [f'sq{c}', 'ones_f'], [PK[7]])
                    nl = len(tiles)
                    S.op('act', lambda e: e.activation(out=rkvc[:, 0:nl], in_=ps[7][:, 0:nl], func=AF.Sqrt, bias=epsc[:, 0:1], scale=1.0 / 256), r=[PK[7], 'epsc'], w=['rkvc'])
                    S.op('dve', lambda e: e.reciprocal(rkvc[:, 0:nl], rkvc[:, 0:nl]), r=['rkvc'], w=['rkvc'])
                    fm_mm(ps[5], PK[5], FM0 + 640, 32, pbase=64)
                    fm_mm(ps[7], PK[7], FM0 + 672, 32, pbase=64)
                    S.op('dve', lambda e: e.tensor_tensor(out=r1[64:96, 0:N], in0=ps[5][64:96, 0:N], in1=rp[64:96, 0, 0:N], op=ALU.mult), r=[PK[5], rpk], w=['r1'])
                    S.op('dve', lambda e: e.tensor_tensor(out=r2[64:96, 0:N], in0=ps[7][64:96, 0:N], in1=rp[64:96, 1, 0:N], op=ALU.mult), r=[PK[7], rpk], w=['r2'])
                    S.op('dve', lambda e: e.tensor_tensor(out=krt[64:96, 0:N], in0=r1[64:96, 0:N], in1=r2[64:96, 0:N], op=ALU.add), r=['r1', 'r2'], w=['krt'])
                    for h in range(8):
                        for c in range(2):
                            MM(ps[5][0:64, 0:N], wukv[:, c, h * 64:(h + 1) * 64], ckvT[:, c, 0:N], c == 0, c == 1, ['wukv', 'ckvT'], [PK[5]])
                        S.op('dve', lambda e, h=h: e.tensor_tensor(out=kTt[0:64, h, 0:N], in0=ps[5][0:64, 0:N], in1=rkv[:, 0:N], op=ALU.mult), r=[PK[5], 'rkv'], w=['kTt'])
                        S.op('pool', lambda e, h=h: e.tensor_copy(kTt[64:96, h, 0:N], krt[64:96, 0:N]), r=['krt'], w=['kTt'])
                    S.dma('sp', lambda e: e.dma_start(out=kT_d[:, :, t0:t0 + N].rearrange("h p t -> p h t"), in_=kTt[:, :, 0:N]), r=['kTt'], w=[('kT_d', t0)])
                    for li, ti in enumerate(tiles):
                        vp = li % 2
                        for c in range(2):
                            MM(ps[6][:, :], ckvT[:, c, li * 128:(li + 1) * 128], wukv[:, c, 512:1024], c == 0, c == 1, ['ckvT', 'wukv'], [PK[6]])
                        S.op('dve', lambda e, vp=vp: e.memset(va[vp][:, :, 64:65], 1.0), w=[f'va{vp}'])
                        S.op('dve', lambda e, vp=vp, li=li: e.tensor_scalar(out=va[vp][:, :, 0:64], in0=ps[6][:, :].rearrange("p (h d) -> p h d", h=8), scalar1=rkvc[:, li:li + 1], scalar2=None, op0=ALU.mult),
                             r=[PK[6], 'rkvc'], w=[f'va{vp}'])
                        S.dma('sp', lambda e, vp=vp, ti=ti: e.dma_start(out=vaug_d[ti * 128:(ti + 1) * 128, :], in_=va[vp][:].rearrange("p h d -> p (h d)")), r=[f'va{vp}'], w=[('vaug_d', ti)])
                    for g in range(8):
                        fp_ = g % 2
                        fm_mm(ps[5], PK[5], FM0 + 704 + g * 64, 64)
                        S.op('act', lambda e, fp_=fp_: e.copy(out=fm32[fp_][:, 0:N], in_=ps[5][0:64, 0:N]), r=[PK[5]], w=[f'fm32{fp_}'])
                        S.dma('sp', lambda e, fp_=fp_, g=g: e.dma_start(out=mlqk_d[g, :, t0:t0 + N], in_=fm32[fp_][:, 0:N]), r=[f'fm32{fp_}'], w=[('mlqk_d', g, t0)])
                    fm_mm(ps[5], PK[5], FM0 + 1216, 16)
                    S.op('act', lambda e: e.copy(out=fm32[0][0:16, 0:N], in_=ps[5][0:16, 0:N]), r=[PK[5]], w=['fm320'])
                    S.dma('sp', lambda e: e.dma_start(out=gates_d[:, t0:t0 + N], in_=fm32[0][0:16, 0:N]), r=['fm320'], w=[('gates_d', t0)])
                S.barrier()

        def phase2(l, with_ctx_q):
            with ExitStack() as es:
                vaug = sb(es, [128, NT, 520], BF16)
                kT = [sb(es, [96, T], BF16) for _ in range(2)]
                qT = [sb(es, [96, T], BF16) for _ in range(2)]
                pT = [sb(es, [128, 512], BF16) for _ in range(3)]
                oT = sb(es, [65, 512]); sel = sb(es, [65, 64]); rec = sb(es, [64, 512])
                on = [sb(es, [64, 512], BF16) for _ in range(2)]
                S.dma('sp', lambda e: e.dma_start(out=vaug[:], in_=vaug_d.rearrange("(t p) c -> p t c", p=128)), w=['vaug'])
                S.op('dve', lambda e: e.memset(sel[:], 0.0), w=['sel'])
                S.op('dve', lambda e: e.memset(sel[64:65, :], 1.0), w=['sel'])
                def load_head(h):
                    hp = h % 2
                    S.dma('sp', lambda e: e.dma_start(out=kT[hp][:], in_=kT_d[h]), w=[f'kT{hp}'])
                    S.dma('sp', lambda e: e.dma_start(out=qT[hp][:], in_=qT_d[h]), w=[f'qT{hp}'])
                items = []
                for h in range(8):
                    for bi_, (t0, N, tiles) in enumerate(blocks()):
                        if t0 == 0 and not with_ctx_q: continue
                        ktiles = [0, 1] if t0 == 0 else list(range(NT))
                        for i, kt in enumerate(ktiles):
                            items.append((h, t0, N, i, kt, len(ktiles)))
                LAG = 2
                blk_ctr = [0]

                def do_pv(n):
                    h, t0, N, i, kt, nk = items[n]
                    hp = h % 2; pk = n % 3
                    ob = blk_ctr[0] % 2
                    MM(ps[ob][0:65, 0:N], vaug[:, kt, h * 65:(h + 1) * 65], pT[pk][:, 0:N], i == 0, i == nk - 1, ['vaug', f'pT{pk}'], [PK[ob]])
                    if i == nk - 1:
                        blk_ctr[0] += 1
                        S.op('dve', lambda e: e.tensor_copy(oT[:, 0:N], ps[ob][0:65, 0:N]), r=[PK[ob]], w=['oT'])
                        MM(ps[5][0:64, 0:N], sel[:, :], oT[:, 0:N], True, True, ['sel', 'oT'], [PK[5]])
                        S.op('dve', lambda e: e.reciprocal(rec[:, 0:N], ps[5][0:64, 0:N]), r=[PK[5]], w=['rec'])
                        op_ = blk_ctr[0] % 2
                        S.op('dve', lambda e: e.tensor_tensor(out=on[op_][:, 0:N], in0=oT[0:64, 0:N], in1=rec[:, 0:N], op=ALU.mult), r=['oT', 'rec'], w=[f'on{op_}'])
                        S.dma('sp', lambda e: e.dma_start(out=catT_d[256 + h * 64:256 + (h + 1) * 64, t0:t0 + N], in_=on[op_][:, 0:N]), r=[f'on{op_}'], w=[('catB', h, t0)])

                load_head(0)
                cur_h = -1
                for n, (h, t0, N, i, kt, nk) in enumerate(items):
                    if h != cur_h:
                        cur_h = h
                        if h + 1 < 8: load_head(h + 1)
                    hp = h % 2; sbk = 2 + n % 3; pk = n % 3
                    MM(ps[sbk][:, 0:N], kT[hp][:, kt * 128:(kt + 1) * 128], qT[hp][:, t0:t0 + N], True, True, [f'kT{hp}', f'qT{hp}'], [PK[sbk]])
                    S.op('act', lambda e, sbk=sbk, pk=pk, N=N: e.activation(out=pT[pk][:, 0:N], in_=ps[sbk][:, 0:N], func=AF.Exp, scale=MLA_SCALE), r=[PK[sbk]], w=[f'pT{pk}'])
                    if n >= LAG: do_pv(n - LAG)
                for n in range(max(0, len(items) - LAG), len(items)):
                    do_pv(n)
                S.barrier()

        def chunk_of(d, p):
            if d == 0: return p
            return 3 - p if p < 4 else 71 - p

        def phase3(l):
            with ExitStack() as es:
                qT = sb(es, [64, 4, T], BF16); kT = sb(es, [64, 4, T], BF16)
                ktok = sb(es, [128, NT, 256], BF16)
                vaug = sb(es, [128, NT, 4, 65], BF16)
                cw = sb(es, [64, 5, 8]); cb = sb(es, [64, 8])
                cols = sb(es, [128, NT, 24]); bc = sb(es, [64, 3, 8, NSTEP])
                msk = sb(es, [128, 2, 64]); misc = sb(es, [8, 16]); eye8 = sb(es, [8, 8, NSTEP]); prev = sb(es, [NSTEP, NSTEP])
                fbn = sb(es, [8, 1])
                S.dma('sp', lambda e: e.dma_start(out=msk[:], in_=c_mask.rearrange("d p t -> p d t")), w=['msk'])
                S.dma('sp', lambda e: e.dma_start(out=misc[:], in_=c_misc), w=['misc'])
                S.dma('sp', lambda e: e.dma_start(out=eye8[:], in_=c_eye8), w=['eye8'])
                S.dma('sp', lambda e: e.dma_start(out=prev[:], in_=c_prev), w=['prev'])
                with nc.allow_non_contiguous_dma(reason="tiny"):
                    for j_ in range(5):
                        S.dma('sp', lambda e, j_=j_: e.dma_start(out=cw[:, j_, :], in_=ml_conv_w[l, j_].rearrange("(g p) -> p g", p=64)), w=['cw'])
                    S.dma('sp', lambda e: e.dma_start(out=cb[:], in_=ml_conv_b[l].rearrange("(g p) -> p g", p=64)), w=['cb'])
                    S.dma('sp', lambda e: e.dma_start(out=fbn[:], in_=ml_f_bias[l].rearrange("(a b) -> a b", b=1)), w=['fbn'])
                S.op('dve', lambda e: e.tensor_scalar(out=fbn[:], in0=fbn[:], scalar1=-1.0, scalar2=None, op0=ALU.mult), r=['fbn'], w=['fbn'])
                for ti in range(NT):
                    S.dma('sp', lambda e, ti=ti: e.dma_start(out=vaug[:, ti, :, 0:64], in_=mlv_d[ti * 128:(ti + 1) * 128, :].rearrange("p (h d) -> p h d", h=4)), w=['vaug'])
                S.op('dve', lambda e: e.memset(vaug[:, :, :, 64:65], 1.0), w=['vaug'])
                with ExitStack() as es2:
                    pre = [sb(es2, [64, 4360])] * 2
                    acc = [sb(es2, [64, 4356])] * 2
                    for g in range(8):
                        gp = 0; pr = pre[gp]; ac = acc[gp]; prk = f'pre{gp}'; ack = f'acc{gp}'
                        S.op('pool', lambda e, pr=pr: e.memset(pr[:, 0:2], 0.0), w=[prk])
                        S.op('pool', lambda e, pr=pr: e.memset(pr[:, 258:262], 0.0), w=[prk])
                        S.op('pool', lambda e, pr=pr: e.memset(pr[:, 4358:4360], 0.0), w=[prk])
                        S.dma('sp', lambda e, pr=pr, g=g: e.dma_start(out=pr[:, 2:258], in_=mlqk_d[g, :, 0:256]), w=[prk])
                        S.dma('sp', lambda e, pr=pr, g=g: e.dma_start(out=pr[:, 262:4358], in_=mlqk_d[g, :, 256:T]), w=[prk])
                        S.op('dve', lambda e, pr=pr, ac=ac, g=g: e.tensor_scalar(out=ac[:], in0=pr[:, 0:4356], scalar1=cw[:, 0, g:g + 1], scalar2=cb[:, g:g + 1], op0=ALU.mult, op1=ALU.add),
                             r=[prk, 'cw', 'cb'], w=[ack])
                        for j in range(1, 5):
                            S.op('dve', lambda e, pr=pr, ac=ac, g=g, j=j: e.scalar_tensor_tensor(out=ac[:], in0=pr[:, j:j + 4356], scalar=cw[:, j, g:g + 1], in1=ac[:], op0=ALU.mult, op1=ALU.add),
                                 r=[prk, 'cw', ack], w=[ack])
                        S.op('act', lambda e, ac=ac: e.activation(out=ac[:], in_=ac[:], func=AF.Silu), r=[ack], w=[ack])
                        dst = qT if g < 4 else kT; dk = 'qT3' if g < 4 else 'kT3'; h = g % 4
                        sc = 1.0 if g < 4 else 0.125
                        S.op('act', lambda e, ac=ac, dst=dst, h=h, sc=sc: e.mul(out=dst[:, h, 0:256], in_=ac[:, 0:256], mul=sc), r=[ack], w=[dk])
                        S.op('act', lambda e, ac=ac, dst=dst, h=h, sc=sc: e.mul(out=dst[:, h, 256:T], in_=ac[:, 260:4356], mul=sc), r=[ack], w=[dk])
                        if g >= 4:
                            S.op('act', lambda e, ac=ac: e.mul(out=ac[:], in_=ac[:], mul=0.125), r=[ack], w=[ack])
                            for ti in range(NT):
                                c0 = ti * 128 if ti < 2 else ti * 128 + 4
                                pb = 1 + ti % 2
                                TR(ps[pb][:, 0:64], ac[:, c0:c0 + 128], [ack], [PK[pb]], n=64)
                                if ti % 2:
                                    S.op('dve', lambda e, ti=ti, pb=pb, h=h: e.tensor_copy(ktok[:, ti, h * 64:(h + 1) * 64], ps[pb][:, 0:64]), r=[PK[pb]], w=['ktok'])
                                else:
                                    S.op('act', lambda e, ti=ti, pb=pb, h=h: e.copy(out=ktok[:, ti, h * 64:(h + 1) * 64], in_=ps[pb][:, 0:64]), r=[PK[pb]], w=['ktok'])
                    S.barrier()
                with ExitStack() as es2:
                    SEG = 2176; CH = 34
                    Bc = sb(es2, [8, T]); U = sb(es2, [8, T])
                    G = [sb(es2, [8, SEG]) for _ in range(4)]
                    tot = sb(es2, [8, NSTEP]); umax = sb(es2, [8, NSTEP])
                    sm = {k: sb(es2, [8, NSTEP], name='sm_' + k) for k in ['be_s', 'ml_s', 'um_s', 'm', 'mprev', 'a', 's', 'mm_s', 'inter', 'mm_n', 't1', 't2']}
                    xT = sb(es2, [NSTEP, 8]); exp3 = sb(es2, [8, 8, NSTEP])
                    for sg_ in range(2):
                        o = sg_ * SEG; co = sg_ * CH
                        LI = G[0]; LF = G[1]; rst = G[2]; TMP = G[3]
                        S.dma('sp', lambda e: e.dma_start(out=rst[:], in_=c_rst[:, o:o + SEG]), w=['G2'])
                        for d in range(2):
                            S.dma('sp', lambda e, d=d: e.dma_start(out=LI[d * 4:(d + 1) * 4, :], in_=gates_d[d * 8:d * 8 + 4, o:o + SEG]), w=['G0'])
                            S.dma('sp', lambda e, d=d: e.dma_start(out=LF[d * 4:(d + 1) * 4, :], in_=gates_d[d * 8 + 4:d * 8 + 8, o:o + SEG]), w=['G1'])
                        S.op('act', lambda e: e.activation(out=LF[:], in_=LF[:], func=AF.Exp, bias=fbn[:, 0:1], scale=-1.0), r=['G1', 'fbn'], w=['G1'])
                        S.op('act', lambda e: e.activation(out=LF[:], in_=LF[:], func=AF.Ln, bias=ones_f[0:8, 0:1], scale=1.0), r=['G1', 'ones_f'], w=['G1'])
                        S.op('dve', lambda e: e.tensor_scalar(out=LF[:], in0=LF[:], scalar1=-1.0, scalar2=None, op0=ALU.mult), r=['G1'], w=['G1'])
                        Bs = Bc[:, o:o + SEG]; Us = U[:, o:o + SEG]
                        S.op('dve', lambda e: e.tensor_tensor_scan(out=Bs, data0=rst[:], data1=LF[:], initial=0.0, op0=ALU.mult, op1=ALU.add), r=['G2', 'G1'], w=['Bc'])
                        S.op('dve', lambda e: e.tensor_reduce(out=tot[:, co:co + CH], in_=LF[:].rearrange("p (c t) -> p c t", t=64), axis=AX.X, op=ALU.add), r=['G1'], w=['tot'])
                        S.op('dve', lambda e: e.tensor_tensor(out=TMP[:].rearrange("p (c t) -> p c t", t=64), in0=LF[:].rearrange("p (c t) -> p c t", t=64),
                                                              in1=tot[:, co:co + CH].unsqueeze(2).to_broadcast([8, CH, 64]), op=ALU.add), r=['G1', 'tot'], w=['G3'])
                        S.op('dve', lambda e: e.tensor_scalar(out=TMP[:], in0=TMP[:], scalar1=misc[:, 0:1], scalar2=None, op0=ALU.mult), r=['G3', 'misc'], w=['G3'])
                        S.op('dve', lambda e: e.scalar_tensor_tensor(out=Bs, in0=Bs, scalar=misc[:, 1:2], in1=TMP[:], op0=ALU.mult, op1=ALU.add), r=['Bc', 'misc', 'G3'], w=['Bc'])
                        S.op('dve', lambda e: e.tensor_tensor(out=Us, in0=LI[:], in1=Bs, op=ALU.subtract), r=['G0', 'Bc'], w=['U'])
                        S.op('dve', lambda e: e.tensor_reduce(out=umax[:, co:co + CH], in_=Us.rearrange("p (c t) -> p c t", t=64), axis=AX.X, op=ALU.max), r=['U'], w=['umax'])

                    def to_scan_order(dst, src, sk, dk):
                        TR(ps[1][0:NSTEP, 0:8], src[:, :], [sk], [PK[1]], n=8)
                        S.op('dve', lambda e: e.tensor_copy(xT[:], ps[1][0:NSTEP, 0:8]), r=[PK[1]], w=['xT'])
                        MM(ps[2][0:8, 0:NSTEP], xT[:, :], prev[:, :], True, True, ['xT', 'prev'], [PK[2]])
                        S.op('dve', lambda e: e.tensor_scalar(out=sm['t1'][:], in0=ps[2][0:8, 0:NSTEP], scalar1=misc[:, 0:1], scalar2=None, op0=ALU.mult), r=[PK[2], 'misc'], w=['t1'])
                        S.op('dve', lambda e: e.scalar_tensor_tensor(out=dst[:], in0=src[:], scalar=misc[:, 2:3], in1=sm['t1'][:], op0=ALU.mult, op1=ALU.add), r=[sk, 'misc', 't1'], w=[dk])

                    to_scan_order(sm['be_s'], tot, 'tot', 'be_s')
                    to_scan_order(sm['um_s'], umax, 'umax', 'um_s')
                    S.op('dve', lambda e: e.tensor_tensor(out=sm['ml_s'][:], in0=sm['be_s'][:], in1=sm['um_s'][:], op=ALU.add), r=['be_s', 'um_s'], w=['ml_s'])
                    S.op('dve', lambda e: e.tensor_tensor_scan(out=sm['m'][:], data0=sm['be_s'][:], data1=sm['ml_s'][:], initial=0.0, op0=ALU.add, op1=ALU.max), r=['be_s', 'ml_s'], w=['m'])
                    S.op('dve', lambda e: e.memset(sm['mprev'][:, 0:1], 0.0), w=['mprev'])
                    S.op('dve', lambda e: e.tensor_copy(sm['mprev'][:, 1:NSTEP], sm['m'][:, 0:NSTEP - 1]), r=['m'], w=['mprev'])
                    S.op('dve', lambda e: e.tensor_tensor(out=sm['t2'][:], in0=sm['be_s'][:], in1=sm['mprev'][:], op=ALU.add), r=['be_s', 'mprev'], w=['t2'])
                    S.op('dve', lambda e: e.tensor_tensor(out=sm['t2'][:], in0=sm['t2'][:], in1=sm['m'][:], op=ALU.subtract), r=['t2', 'm'], w=['t2'])
                    S.op('act', lambda e: e.activation(out=sm['a'][:], in_=sm['t2'][:], func=AF.Exp), r=['t2'], w=['a'])
                    S.op('dve', lambda e: e.tensor_tensor(out=sm['t2'][:], in0=sm['ml_s'][:], in1=sm['m'][:], op=ALU.subtract), r=['ml_s', 'm', 'a'], w=['t2'])
                    S.op('act', lambda e: e.activation(out=sm['s'][:], in_=sm['t2'][:], func=AF.Exp), r=['t2'], w=['s'])
                    S.op('dve', lambda e: e.tensor_tensor(out=sm['mm_s'][:], in0=sm['mprev'][:], in1=sm['um_s'][:], op=ALU.max), r=['mprev', 'um_s'], w=['mm_s'])
                    S.op('dve', lambda e: e.tensor_tensor(out=sm['t2'][:], in0=sm['mprev'][:], in1=sm['mm_s'][:], op=ALU.subtract), r=['mprev', 'mm_s', 's'], w=['t2'])
                    S.op('act', lambda e: e.activation(out=sm['inter'][:], in_=sm['t2'][:], func=AF.Exp), r=['t2'], w=['inter'])
                    to_scan_order(sm['mm_n'], sm['mm_s'], 'mm_s', 'mm_n')
                    for wi, nm in enumerate(['a', 's', 'inter']):
                        S.op('dve', lambda e, nm=nm: e.tensor_tensor(out=exp3[:], in0=eye8[:], in1=sm[nm][:].unsqueeze(1).to_broadcast([8, 8, NSTEP]), op=ALU.mult), r=['eye8', nm], w=['exp3'])
                        for hf in range(2):
                            MM(ps[3][0:64, 0:4 * NSTEP], ones_f[0:8, 0:64], exp3[:, hf * 4:(hf + 1) * 4, :].rearrange("p a b -> p (a b)"), True, True, ['ones_f', 'exp3'], [PK[3]])
                            S.op('dve', lambda e, wi=wi, hf=hf: e.tensor_copy(bc[:, wi, hf * 4:(hf + 1) * 4, :].rearrange("p a b -> p (a b)"), ps[3][0:64, 0:4 * NSTEP]), r=[PK[3]], w=['bc'])
                    for sg_ in range(2):
                        o = sg_ * SEG; co = sg_ * CH
                        TMP = G[3]; RW = G[0:3]
                        u3 = U[:, o:o + SEG].rearrange("p (c t) -> p c t", t=64); b3 = Bc[:, o:o + SEG].rearrange("p (c t) -> p c t", t=64)
                        t3 = TMP[:].rearrange("p (c t) -> p c t", t=64)
                        S.op('dve', lambda e: e.tensor_tensor(out=t3, in0=u3, in1=umax[:, co:co + CH].unsqueeze(2).to_broadcast([8, CH, 64]), op=ALU.subtract), r=['U', 'umax'], w=['G3'])
                        S.op('act', lambda e: e.activation(out=RW[0][:], in_=TMP[:], func=AF.Exp), r=['G3'], w=['G0'])
                        S.op('dve', lambda e: e.tensor_tensor(out=t3, in0=u3, in1=sm['mm_n'][:, co:co + CH].unsqueeze(2).to_broadcast([8, CH, 64]), op=ALU.subtract), r=['U', 'mm_n'], w=['G3'])
                        S.op('act', lambda e: e.activation(out=RW[1][:], in_=TMP[:], func=AF.Exp), r=['G3'], w=['G1'])
                        S.op('dve', lambda e: e.tensor_tensor(out=t3, in0=b3, in1=sm['mm_n'][:, co:co + CH].unsqueeze(2).to_broadcast([8, CH, 64]), op=ALU.add), r=['Bc', 'mm_n'], w=['G3'])
                        S.op('act', lambda e: e.activation(out=RW[2][:], in_=TMP[:], func=AF.Exp, scale=-1.0), r=['G3'], w=['G2'])
                        for tl in range(17):
                            ti = sg_ * 17 + tl
                            pb = 1 + ti % 2
                            for k3 in range(3):
                                TR(ps[pb][:, k3 * 8:(k3 + 1) * 8], RW[k3][:, tl * 128:(tl + 1) * 128], [f'G{k3}'], [PK[pb]], n=8)
                            S.op('dve', lambda e, ti=ti, pb=pb: e.tensor_copy(cols[:, ti, :], ps[pb][:, 0:24]), r=[PK[pb]], w=['cols'])
                    S.barrier()
                hsum = sb(es, [128, NT, 256])
                S.op('pool', lambda e: e.memset(hsum[:], 0.0), w=[('hsum', ti_) for ti_ in range(NT)])
                Cst = sb(es, [64, 8, 65]); C0b = [sb(es, [64, 4, 65], BF16) for _ in range(2)]
                tmpC_ = [sb(es, [64, 4, 65]) for _ in range(2)]
                wv = [sb(es, [128, 4, 65], BF16) for _ in range(2)]
                tS = [sb(es, [128, 4, 64]) for _ in range(2)]
                pTm = [sb(es, [128, 4, 64], BF16) for _ in range(2)]
                tI_ = [sb(es, [128, 260]) for _ in range(2)]; tH_ = [sb(es, [128, 260]) for _ in range(2)]
                dn_ = [sb(es, [128, 4]) for _ in range(2)]; hd = [sb(es, [128, 4, 64]) for _ in range(2)]
                S.op('dve', lambda e: e.memset(Cst[:], 0.0), w=['Cst0', 'Cst1'])
                it = 0
                for p in range(NSTEP):
                    for d in range(2):
                        c = chunk_of(d, p); ti = c // 2; hb = c % 2; P0 = hb * 64; P1 = P0 + 64
                        t0 = c * 64
                        ip = it % 2; it += 1
                        tmpC = tmpC_[d]; tI = tI_[d]; tH = tH_[d]; dn = dn_[d]
                        pC = ps[d * 4]; pS = ps[d * 4 + 1]; pA = ps[d * 4 + 2]; pB = ps[d * 4 + 3]
                        kC = PK[d * 4]; kS = PK[d * 4 + 1]; kA = PK[d * 4 + 2]; kB = PK[d * 4 + 3]
                        Ck = f'Cst{d}'; tCk = f'tmpC{d}'; tIk = f'tI{d}'; tHk = f'tH{d}'; dnk = f'dn{d}'
                        S.op('dve', lambda e, ip=ip, ti=ti, d=d, P0=P0, P1=P1: e.tensor_tensor(out=wv[ip][P0:P1], in0=vaug[P0:P1, ti], in1=cols[P0:P1, ti, d * 4:(d + 1) * 4].unsqueeze(2).to_broadcast([64, 4, 65]), op=ALU.mult),
                             r=['vaug', 'cols'], w=[f'wv{ip}'])
                        for h in range(4):
                            MM(pC[0:64, h * 65:(h + 1) * 65], ktok[P0:P1, ti, h * 64:(h + 1) * 64], wv[ip][P0:P1, h, :], True, True, ['ktok', f'wv{ip}'], [kC])
                        S.op('dve', lambda e, ip=ip, d=d, p=p: e.tensor_tensor(out=C0b[ip][:], in0=Cst[:, d * 4:(d + 1) * 4, :], in1=bc[:, 2, d * 4:(d + 1) * 4, p].unsqueeze(2).to_broadcast([64, 4, 65]), op=ALU.mult),
                             r=[Ck, 'bc'], w=[f'C0b{ip}'])
                        S.op('dve', lambda e, d=d, p=p: e.tensor_tensor(out=tmpC[:], in0=pC[0:64, 0:260].rearrange("p (h c) -> p h c", h=4), in1=bc[:, 1, d * 4:(d + 1) * 4, p].unsqueeze(2).to_broadcast([64, 4, 65]), op=ALU.mult),
                             r=[kC, 'bc'], w=[tCk])
                        S.op('dve', lambda e, d=d, p=p: e.tensor_tensor(out=Cst[:, d * 4:(d + 1) * 4, :], in0=Cst[:, d * 4:(d + 1) * 4, :], in1=bc[:, 0, d * 4:(d + 1) * 4, p].unsqueeze(2).to_broadcast([64, 4, 65]), op=ALU.mult),
                             r=[Ck, 'bc'], w=[Ck])
                        S.op('dve', lambda e, d=d: e.tensor_tensor(out=Cst[:, d * 4:(d + 1) * 4, :], in0=Cst[:, d * 4:(d + 1) * 4, :], in1=tmpC[:], op=ALU.add), r=[Ck, tCk], w=[Ck])
                        for h in range(4):
                            MM(pS[P0:P1, h * 64:(h + 1) * 64], kT[:, h, t0:t0 + 64], qT[:, h, t0:t0 + 64], True, True, ['kT3', 'qT3'], [kS])
                        S.op('dve', lambda e, ip=ip, ti=ti, d=d, P0=P0, P1=P1: e.tensor_tensor(out=tS[ip][P0:P1], in0=pS[P0:P1, 0:256].rearrange("p (h t) -> p h t", h=4),
                                                                                       in1=cols[P0:P1, ti, 8 + d * 4:8 + (d + 1) * 4].unsqueeze(2).to_broadcast([64, 4, 64]), op=ALU.mult),
                             r=[kS, 'cols'], w=[f'tS{ip}'])
                        S.op('pool', lambda e, ip=ip, d=d, P0=P0, P1=P1: e.tensor_tensor(out=pTm[ip][P0:P1], in0=tS[ip][P0:P1], in1=msk[P0:P1, d, :].unsqueeze(1).to_broadcast([64, 4, 64]), op=ALU.mult),
                             r=[f'tS{ip}', 'msk'], w=[f'pTm{ip}'])
                        for h in range(4):
                            MM(pA[P0:P1, h * 65:(h + 1) * 65], pTm[ip][P0:P1, h, :], vaug[P0:P1, ti, h, :], True, True, [f'pTm{ip}', 'vaug'], [kA])
                        for h in range(4):
                            MM(pB[P0:P1, h * 65:(h + 1) * 65], qT[:, h, t0:t0 + 64], C0b[ip][:, h, :], True, True, ['qT3', f'C0b{ip}'], [kB])
                        S.op('act', lambda e, P0=P0, P1=P1: e.copy(out=tI[P0:P1, :], in_=pB[P0:P1, 0:260]), r=[kB], w=[tIk])
                        S.op('dve', lambda e, P0=P0, P1=P1: e.tensor_tensor(out=tH[P0:P1, :], in0=pA[P0:P1, 0:260], in1=tI[P0:P1, :], op=ALU.add), r=[kA, tIk], w=[tHk])
                        ph = tH[P0:P1, :].rearrange("p (h c) -> p h c", h=4)
                        S.op('act', lambda e, ph=ph, P0=P0, P1=P1: e.activation(out=dn[P0:P1, :], in_=ph[:, :, 64], func=AF.Abs), r=[tHk], w=[dnk])
                        S.op('dve', lambda e, ti=ti, d=d, P0=P0, P1=P1: e.tensor_tensor(out=dn[P0:P1, :], in0=dn[P0:P1, :], in1=cols[P0:P1, ti, 16 + d * 4:16 + (d + 1) * 4], op=ALU.max), r=[dnk, 'cols'], w=[dnk])
                        S.op('dve', lambda e, P0=P0, P1=P1: e.reciprocal(dn[P0:P1, :], dn[P0:P1, :]), r=[dnk], w=[dnk])
                        S.op('dve', lambda e, ip=ip, ph=ph, P0=P0, P1=P1: e.tensor_tensor(out=hd[ip][P0:P1], in0=ph[:, :, 0:64], in1=dn[P0:P1, :].unsqueeze(2).to_broadcast([64, 4, 64]), op=ALU.mult),
                             r=[tHk, dnk], w=[f'hd{ip}'])
                        S.op('pool', lambda e, ip=ip, ti=ti, P0=P0, P1=P1: e.tensor_tensor(out=hsum[P0:P1, ti, :], in0=hsum[P0:P1, ti, :], in1=hd[ip][P0:P1].rearrange("p h d -> p (h d)"), op=ALU.add),
                             r=[('hsum', ti), f'hd{ip}'], w=[('hsum', ti)])
                with ExitStack() as es2:
                    ngb = sb(es2, [128, 256]); so = [sb(es2, [128, 256]) for _ in range(2)]
                    mu = sb(es2, [128, 4]); var = sb(es2, [128, 4]); cen = sb(es2, [128, 4, 64]); sqq = sb(es2, [128, 4, 64])
                    yc = [sb(es2, [128, 256]) for _ in range(2)]; ycT = [sb(es2, [128, 2, 128], BF16) for _ in range(2)]
                    vec_bcast('sp', ngb, ml_norm_g[l], 'ngb')
                    for ti in range(NT):
                        p2 = ti % 2
                        S.dma('sp', lambda e, ti=ti, p2=p2: e.dma_start(out=so[p2][:], in_=sigo_d[ti * 128:(ti + 1) * 128, :]), w=[f'so{p2}'])
                        h3 = hsum[:, ti, :].rearrange("p (h d) -> p h d", h=4)
                        S.op('dve', lambda e, h3=h3: e.tensor_reduce(out=mu[:], in_=h3, axis=AX.X, op=ALU.add), r=[('hsum', ti)], w=['mu'])
                        S.op('dve', lambda e: e.tensor_scalar(out=mu[:], in0=mu[:], scalar1=1.0 / 64, scalar2=None, op0=ALU.mult), r=['mu'], w=['mu'])
                        S.op('dve', lambda e, h3=h3: e.tensor_tensor(out=cen[:], in0=h3, in1=mu[:].unsqueeze(2).to_broadcast([128, 4, 64]), op=ALU.subtract), r=[('hsum', ti), 'mu'], w=['cen'])
                        S.op('dve', lambda e: e.tensor_tensor(out=sqq[:], in0=cen[:], in1=cen[:], op=ALU.mult), r=['cen'], w=['sqq'])
                        S.op('dve', lambda e: e.tensor_reduce(out=var[:], in_=sqq[:], axis=AX.X, op=ALU.add), r=['sqq'], w=['var'])
                        S.op('act', lambda e: e.activation(out=var[:], in_=var[:], func=AF.Sqrt, bias=epsc[:, 0:1], scale=1.0 / 64), r=['var', 'epsc'], w=['var'])
                        S.op('dve', lambda e: e.reciprocal(var[:], var[:]), r=['var'], w=['var'])
                        S.op('dve', lambda e: e.tensor_tensor(out=cen[:], in0=cen[:], in1=var[:].unsqueeze(2).to_broadcast([128, 4, 64]), op=ALU.mult), r=['cen', 'var'], w=['cen'])
                        S.op('dve', lambda e: e.tensor_tensor(out=cen[:].rearrange("p h d -> p (h d)"), in0=cen[:].rearrange("p h d -> p (h d)"), in1=ngb[:], op=ALU.mult), r=['cen', 'ngb'], w=['cen'])
                        S.op('dve', lambda e, p2=p2: e.tensor_tensor(out=yc[p2][:], in0=cen[:].rearrange("p h d -> p (h d)"), in1=so[p2][:], op=ALU.mult), r=['cen', f'so{p2}'], w=[f'yc{p2}'])
                        for c in range(2):
                            TR(ps[1 + p2][:, c * 128:(c + 1) * 128], yc[p2][:, c * 128:(c + 1) * 128], [f'yc{p2}'], [PK[1 + p2]])
                        S.op('act', lambda e, p2=p2: e.copy(out=ycT[p2][:].rearrange("p c t -> p (c t)"), in_=ps[1 + p2][:, 0:256]), r=[PK[1 + p2]], w=[f'ycT{p2}'])
                        S.dma('sp', lambda e, p2=p2, ti=ti: e.dma_start(out=catT_d[768:1024, ti * 128:(ti + 1) * 128].rearrange("(c p) t -> p c t", p=128), in_=ycT[p2][:]), r=[f'ycT{p2}'], w=[('catC', ti)])
                S.barrier()

        def phase4(l, xsrc, tiles):
            with ExitStack() as es:
                wout = sb(es, [128, 8, D], BF16); wr = sb(es, [128, 8, 16])
                g1b = [sb(es, [128, D]) for _ in range(2)]; scb = [sb(es, [128, D]) for _ in range(2)]; shb = [sb(es, [128, D]) for _ in range(2)]
                lg = sb(es, [128, D]); lb = sb(es, [128, D])
                cat = [sb(es, [128, 8, 128], BF16) for _ in range(2)]
                xt = [sb(es, [128, D]) for _ in range(2)]
                rr = sb(es, [128, D]); x1 = [sb(es, [128, D]) for _ in range(2)]; hm = [sb(es, [128, D]) for _ in range(2)]
                hmT = sb(es, [128, 8, 128])
                st6 = sb(es, [128, 4, 6]); mv = sb(es, [128, 2]); rstd = sb(es, [128, 1]); nmr = sb(es, [128, 1])
                lgt = sb(es, [128, 16]); mx = sb(es, [128, 1]); ssum = sb(es, [128, 1]); affT = sb(es, [16, T])
                for c in range(8):
                    S.dma('pool', lambda e, c=c: e.dma_start(out=wout[:, c, :], in_=w_out[l, c * 128:(c + 1) * 128, :]), w=['wout'])
                S.dma('sp', lambda e: e.dma_start(out=wr[:], in_=w_router[l].rearrange("(c p) n -> p c n", p=128)), w=['wr'])
                for m in range(2):
                    bcast_load(es, 'sp', g1b[m], 2, m, f'g1b{m}'); bcast_load(es, 'sp', scb[m], 4, m, f'scb{m}'); bcast_load(es, 'sp', shb[m], 3, m, f'shb{m}')
                    S.op('dve', lambda e, m=m: e.tensor_scalar(out=scb[m][:], in0=scb[m][:], scalar1=1.0, scalar2=None, op0=ALU.add), r=[f'scb{m}'], w=[f'scb{m}'])
                vec_bcast('sp', lg, ln1_g[l], 'lg'); vec_bcast('sp', lb, ln1_b[l], 'lb')
                for ti in tiles:
                    p2 = ti % 2; m = 1 if ti < 2 else 0
                    with nc.allow_non_contiguous_dma(reason="catT tile"):
                        S.dma('sp', lambda e, ti=ti, p2=p2: e.dma_start(out=cat[p2][:], in_=catT_d[:, ti * 128:(ti + 1) * 128].rearrange("(c p) t -> p c t", p=128)), w=[f'cat{p2}'])
                    S.dma('sp', lambda e, ti=ti, p2=p2: e.dma_start(out=xt[p2][:], in_=xsrc[ti * 128:(ti + 1) * 128, :]), w=[f'xt{p2}'])
                    for half in range(2):
                        for c in range(8):
                            MM(ps[half][:, :], cat[p2][:, c, :], wout[:, c, half * 512:(half + 1) * 512], c == 0, c == 7, [f'cat{p2}', 'wout'], [PK[half]])
                        S.op('dve', lambda e, half=half, m=m: e.tensor_tensor(out=rr[:, half * 512:(half + 1) * 512], in0=ps[half][:, :], in1=g1b[m][:, half * 512:(half + 1) * 512], op=ALU.mult),
                             r=[PK[half], f'g1b{m}'], w=['rr'])
                    S.op('dve', lambda e, p2=p2: e.scalar_tensor_tensor(out=rr[:], in0=xt[p2][:], scalar=ALPHA, in1=rr[:], op0=ALU.mult, op1=ALU.add), r=[f'xt{p2}', 'rr'], w=['rr'])
                    ln_stats('l4', rr, D, mv, rstd, nmr, st6, ['rr'])
                    S.op('act', lambda e: e.activation(out=rr[:], in_=rr[:], func=AF.Identity, bias=nmr[:, 0:1], scale=rstd[:, 0:1]), r=['rr', 'l4rs', 'l4nm'], w=['rr'])
                    S.op('dve', lambda e: e.tensor_tensor(out=rr[:], in0=rr[:], in1=lg[:], op=ALU.mult), r=['rr', 'lg'], w=['rr'])
                    S.op('dve', lambda e, p2=p2: e.tensor_tensor(out=x1[p2][:], in0=rr[:], in1=lb[:], op=ALU.add), r=['rr', 'lb'], w=[f'x1{p2}'])
                    S.dma('sp', lambda e, ti=ti, p2=p2: e.dma_start(out=x1_d[ti * 128:(ti + 1) * 128, :], in_=x1[p2][:]), r=[f'x1{p2}'], w=[('x1_d', ti)])
                    ln_stats('l5', x1[p2], D, mv, rstd, nmr, st6, [f'x1{p2}'])
                    S.op('act', lambda e, p2=p2: e.activation(out=rr[:], in_=x1[p2][:], func=AF.Identity, bias=nmr[:, 0:1], scale=rstd[:, 0:1]), r=[f'x1{p2}', 'l5rs', 'l5nm'], w=['rr'])
                    S.op('dve', lambda e, m=m: e.tensor_tensor(out=rr[:], in0=rr[:], in1=scb[m][:], op=ALU.mult), r=['rr', f'scb{m}'], w=['rr'])
                    S.op('dve', lambda e, p2=p2, m=m: e.tensor_tensor(out=hm[p2][:], in0=rr[:], in1=shb[m][:], op=ALU.add), r=['rr', f'shb{m}'], w=[f'hm{p2}'])
                    S.dma('sp', lambda e, ti=ti, p2=p2: e.dma_start(out=hm_d[ti * 128:(ti + 1) * 128, :], in_=hm[p2][:]), r=[f'hm{p2}'], w=[('hm_d', ti)])
                    for half in range(2):
                        for c4 in range(4):
                            TR(ps[2 + half][:, c4 * 128:(c4 + 1) * 128], hm[p2][:, (half * 4 + c4) * 128:(half * 4 + c4 + 1) * 128], [f'hm{p2}'], [PK[2 + half]])
                        if half == 0:
                            S.op('act', lambda e: e.copy(out=hmT[:, 0:4, :].rearrange("p c t -> p (c t)"), in_=ps[2][:, :]), r=[PK[2]], w=['hmT'])
                        else:
                            S.op('dve', lambda e: e.tensor_copy(hmT[:, 4:8, :].rearrange("p c t -> p (c t)"), ps[3][:, :]), r=[PK[3]], w=['hmT'])
                    for c in range(8):
                        MM(ps[4][:, 0:16], hmT[:, c, :], wr[:, c, :], c == 0, c == 7, ['hmT', 'wr'], [PK[4]])
                    S.op('dve', lambda e: e.tensor_reduce(out=mx[:], in_=ps[4][:, 0:16], axis=AX.X, op=ALU.max), r=[PK[4]], w=['mx'])
                    S.op('dve', lambda e: e.tensor_scalar(out=mx[:], in0=mx[:], scalar1=-1.0, scalar2=None, op0=ALU.mult), r=['mx'], w=['mx'])
                    S.op('act', lambda e: e.activation(out=lgt[:], in_=ps[4][:, 0:16], func=AF.Exp, bias=mx[:, 0:1], scale=1.0, accum_out=ssum[:]), r=[PK[4], 'mx'], w=['lgt', 'ssum'])
                    S.op('dve', lambda e: e.reciprocal(ssum[:], ssum[:]), r=['ssum'], w=['ssum'])
                    S.op('dve', lambda e: e.tensor_scalar(out=lgt[:], in0=lgt[:], scalar1=ssum[:, 0:1], scalar2=None, op0=ALU.mult), r=['lgt', 'ssum'], w=['lgt'])
                    TR(ps[5][0:16, 0:128], lgt[:, :], ['lgt'], [PK[5]])
                    S.op('dve', lambda e, ti=ti: e.tensor_copy(affT[:, ti * 128:(ti + 1) * 128], ps[5][0:16, 0:128]), r=[PK[5]], w=['affT'])
                t_lo = tiles[0] * 128
                S.dma('sp', lambda e: e.dma_start(out=aff_d[:, t_lo:T], in_=affT[:, t_lo:T]), r=['affT'], w=['aff_d'])
                S.barrier()

        def phase56(l, with_ctx):
            with ExitStack() as es:
                idxT = sb(es, [128, 5, 16], U32); gateT = sb(es, [128, 5, 16])
                es2 = ExitStack()
                aw = sb(es2, [16, TL]); vals = sb(es2, [16, 512]); idx = sb(es2, [16, 512], U32); idxf = sb(es2, [16, 512])
                awc = sb(es2, [16, 256]); valsc = sb(es2, [16, 32]); idxc = sb(es2, [16, 32], U32); idxcf = sb(es2, [16, 32])
                S.dma('sp', lambda e: e.dma_start(out=aw[:], in_=aff_d[:, 256:T]), w=['aw'])
                for r_ in range(64):
                    S.op('dve', lambda e, r_=r_: e.max(out=vals[:, r_ * 8:(r_ + 1) * 8], in_=aw[:]), r=['aw'], w=['vals'])
                    S.op('dve', lambda e, r_=r_: e.max_index(out=idx[:, r_ * 8:(r_ + 1) * 8], in_max=vals[:, r_ * 8:(r_ + 1) * 8], in_values=aw[:]), r=['aw', 'vals'], w=['idx'])
                    S.op('dve', lambda e, r_=r_: e.match_replace(out=aw[:], in_to_replace=vals[:, r_ * 8:(r_ + 1) * 8], in_values=aw[:], imm_value=-1.0), r=['aw', 'vals'], w=['aw'])
                S.op('dve', lambda e: e.tensor_copy(idxf[:], idx[:]), r=['idx'], w=['idxf'])
                S.op('dve', lambda e: e.tensor_scalar(out=idxf[:], in0=idxf[:], scalar1=256.0, scalar2=None, op0=ALU.add), r=['idxf'], w=['idxf'])
                for st in range(4):
                    TR(ps[0][:, 0:16], idxf[:, st * 128:(st + 1) * 128], ['idxf'], [PK[0]], n=16)
                    S.op('dve', lambda e, st=st: e.tensor_copy(idxT[:, st, :], ps[0][:, 0:16]), r=[PK[0]], w=['idxT'])
                    TR(ps[1][:, 0:16], vals[:, st * 128:(st + 1) * 128], ['vals'], [PK[1]], n=16)
                    S.op('dve', lambda e, st=st: e.tensor_copy(gateT[:, st, :], ps[1][:, 0:16]), r=[PK[1]], w=['gateT'])
                if with_ctx:
                    S.dma('sp', lambda e: e.dma_start(out=awc[:], in_=aff_d[:, 0:256]), w=['awc'])
                    for r_ in range(4):
                        S.op('dve', lambda e, r_=r_: e.max(out=valsc[:, r_ * 8:(r_ + 1) * 8], in_=awc[:]), r=['awc'], w=['valsc'])
                        S.op('dve', lambda e, r_=r_: e.max_index(out=idxc[:, r_ * 8:(r_ + 1) * 8], in_max=valsc[:, r_ * 8:(r_ + 1) * 8], in_values=awc[:]), r=['awc', 'valsc'], w=['idxc'])
                        S.op('dve', lambda e, r_=r_: e.match_replace(out=awc[:], in_to_replace=valsc[:, r_ * 8:(r_ + 1) * 8], in_values=awc[:], imm_value=-1.0), r=['awc', 'valsc'], w=['awc'])
                    S.op('dve', lambda e: e.tensor_copy(idxcf[:], idxc[:]), r=['idxc'], w=['idxcf'])
                    TR(ps[0][0:32, 0:16], idxcf[:, :], ['idxcf'], [PK[0]], n=16)
                    S.op('dve', lambda e: e.tensor_copy(idxT[0:32, 4, :], ps[0][0:32, 0:16]), r=[PK[0]], w=['idxT'])
                    TR(ps[1][0:32, 0:16], valsc[:, :], ['valsc'], [PK[1]], n=16)
                    S.op('dve', lambda e: e.tensor_copy(gateT[0:32, 4, :], ps[1][0:32, 0:16]), r=[PK[1]], w=['gateT'])
                S.barrier(); es2.close()
                NS = 544 if with_ctx else 512
                nst = 5 if with_ctx else 4
                wg = [sb(es, [128, 8, D], BF16) for _ in range(2)]; wu = [sb(es, [128, 8, D], BF16) for _ in range(2)]; wd = [sb(es, [128, 8, D], BF16) for _ in range(2)]
                xe = [sb(es, [128, 5, D]) for _ in range(2)]
                xeT = sb(es, [128, 8, 544], BF16); hidT = sb(es, [128, 8, 544], BF16)
                sg = [sb(es, [128, 544]) for _ in range(2)]
                ye = [sb(es, [128, D]) for _ in range(2)]

                def load_expert(e_):
                    ep = e_ % 2
                    for c in range(8):
                        S.dma('pool', lambda e, c=c: e.dma_start(out=wg[ep][:, c, :], in_=w_gate[l, e_, c * 128:(c + 1) * 128, :]), w=[f'wg{ep}'])
                        S.dma('pool', lambda e, c=c: e.dma_start(out=wu[ep][:, c, :], in_=w_up[l, e_, c * 128:(c + 1) * 128, :]), w=[f'wu{ep}'])
                        S.dma('pool', lambda e, c=c: e.dma_start(out=wd[ep][:, c, :], in_=w_down[l, e_, c * 128:(c + 1) * 128, :]), w=[f'wd{ep}'])
                    for st in range(nst):
                        n = 128 if st < 4 else 32
                        S.dma('pool', lambda e, st=st, n=n: e.indirect_dma_start(out=xe[ep][0:n, st, :], out_offset=None, in_=hm_d[:, :],
                                                                              in_offset=bass.IndirectOffsetOnAxis(ap=idxT[0:n, st, e_:e_ + 1], axis=0)),
                              r=['idxT'], w=[f'xe{ep}'])

                load_expert(0)
                yc_ = 0
                for e_ in range(16):
                    ep = e_ % 2
                    if e_ + 1 < 16: load_expert(e_ + 1)
                    for st in range(nst):
                        n = 128 if st < 4 else 32
                        for half in range(2):
                            for c4 in range(4):
                                cc = half * 4 + c4
                                TR(ps[half][:, c4 * 128:c4 * 128 + n], xe[ep][0:n, st, cc * 128:(cc + 1) * 128], [f'xe{ep}'], [PK[half]], n=n)
                            src = ps[half][:, :].rearrange("p (c t) -> p c t", c=4)[:, :, 0:n]
                            if half == 0:
                                S.op('act', lambda e, st=st, n=n, src=src: e.copy(out=xeT[:, 0:4, st * 128:st * 128 + n], in_=src), r=[PK[0]], w=['xeT'])
                            else:
                                S.op('dve', lambda e, st=st, n=n, src=src: e.tensor_copy(xeT[:, 4:8, st * 128:st * 128 + n], src), r=[PK[1]], w=['xeT'])
                    for fc in range(8):
                        for (n0, n1) in ([(0, 512), (512, 544)] if with_ctx else [(0, 512)]):
                            pg = ps[2] if n0 == 0 else ps[4]; pu = ps[3] if n0 == 0 else ps[5]
                            pgk = PK[2] if n0 == 0 else PK[4]; puk = PK[3] if n0 == 0 else PK[5]
                            nn = n1 - n0
                            for c in range(8):
                                MM(pg[:, 0:nn], wg[ep][:, c, fc * 128:(fc + 1) * 128], xeT[:, c, n0:n1], c == 0, c == 7, [f'wg{ep}', 'xeT'], [pgk])
                            for c in range(8):
                                MM(pu[:, 0:nn], wu[ep][:, c, fc * 128:(fc + 1) * 128], xeT[:, c, n0:n1], c == 0, c == 7, [f'wu{ep}', 'xeT'], [puk])
                            sp2 = fc % 2
                            S.op('act', lambda e, pg=pg, nn=nn, sp2=sp2: e.activation(out=sg[sp2][:, 0:nn], in_=pg[:, 0:nn], func=AF.Silu), r=[pgk], w=[f'sg{sp2}'])
                            S.op('dve', lambda e, pu=pu, nn=nn, n0=n0, n1=n1, fc=fc, sp2=sp2: e.tensor_tensor(out=hidT[:, fc, n0:n1], in0=pu[:, 0:nn], in1=sg[sp2][:, 0:nn], op=ALU.mult), r=[puk, f'sg{sp2}'], w=['hidT'])
                    for st in range(nst):
                        n = 128 if st < 4 else 32
                        yp = yc_ % 2; yc_ += 1
                        for half in range(2):
                            for fc in range(8):
                                MM(ps[6 + half][0:n, :], hidT[:, fc, st * 128:st * 128 + n], wd[ep][:, fc, half * 512:(half + 1) * 512], fc == 0, fc == 7, ['hidT', f'wd{ep}'], [PK[6 + half]])
                            eng = 'act' if half == 0 else 'dve'
                            if half == 0:
                                S.op('act', lambda e, n=n, st=st, yp=yp: e.activation(out=ye[yp][0:n, 0:512], in_=ps[6][0:n, :], func=AF.Identity, scale=gateT[0:n, st, e_:e_ + 1]), r=[PK[6], 'gateT'], w=[f'ye{yp}'])
                            else:
                                S.op('dve', lambda e, n=n, st=st, yp=yp: e.tensor_scalar(out=ye[yp][0:n, 512:1024], in0=ps[7][0:n, :], scalar1=gateT[0:n, st, e_:e_ + 1], scalar2=None, op0=ALU.mult), r=[PK[7], 'gateT'], w=[f'ye{yp}'])
                        S.dma('pool', lambda e, st=st, n=n, yp=yp: e.indirect_dma_start(out=moe_d[:, :], out_offset=bass.IndirectOffsetOnAxis(ap=idxT[0:n, st, e_:e_ + 1], axis=0),
                                                                                  in_=ye[yp][0:n, :], in_offset=None, compute_op=ALU.add),
                              r=[f'ye{yp}', 'idxT'], w=['moe_d'])
                S.barrier()

        def phase7(l, dst, tiles, final):
            with ExitStack() as es:
                g2b = [sb(es, [128, D]) for _ in range(2)]
                lg = sb(es, [128, D]); lb = sb(es, [128, D])
                xt = [sb(es, [128, D]) for _ in range(2)]; ft = [sb(es, [128, D]) for _ in range(2)]
                rr = sb(es, [128, D]); xo = [sb(es, [128, D]) for _ in range(2)]
                st6 = sb(es, [128, 4, 6]); mv = sb(es, [128, 2]); rstd = sb(es, [128, 1]); nmr = sb(es, [128, 1])
                for m in range(2):
                    bcast_load(es, 'sp', g2b[m], 5, m, f'g2b{m}')
                vec_bcast('sp', lg, ln2_g[l], 'lg'); vec_bcast('sp', lb, ln2_b[l], 'lb')
                for ti in tiles:
                    p2 = ti % 2; m = 1 if ti < 2 else 0
                    S.dma('sp', lambda e, ti=ti, p2=p2: e.dma_start(out=xt[p2][:], in_=x1_d[ti * 128:(ti + 1) * 128, :]), w=[f'xt{p2}'])
                    S.dma('sp', lambda e, ti=ti, p2=p2: e.dma_start(out=ft[p2][:], in_=moe_d[ti * 128:(ti + 1) * 128, :]), w=[f'ft{p2}'])
                    S.op('dve', lambda e, p2=p2, m=m: e.tensor_tensor(out=rr[:], in0=ft[p2][:], in1=g2b[m][:], op=ALU.mult), r=[f'ft{p2}', f'g2b{m}'], w=['rr'])
                    S.op('dve', lambda e, p2=p2: e.scalar_tensor_tensor(out=rr[:], in0=xt[p2][:], scalar=ALPHA, in1=rr[:], op0=ALU.mult, op1=ALU.add), r=[f'xt{p2}', 'rr'], w=['rr'])
                    ln_stats('l7', rr, D, mv, rstd, nmr, st6, ['rr'])
                    S.op('act', lambda e: e.activation(out=rr[:], in_=rr[:], func=AF.Identity, bias=nmr[:, 0:1], scale=rstd[:, 0:1]), r=['rr', 'l7rs', 'l7nm'], w=['rr'])
                    S.op('dve', lambda e: e.tensor_tensor(out=rr[:], in0=rr[:], in1=lg[:], op=ALU.mult), r=['rr', 'lg'], w=['rr'])
                    S.op('dve', lambda e, p2=p2: e.tensor_tensor(out=xo[p2][:], in0=rr[:], in1=lb[:], op=ALU.add), r=['rr', 'lb'], w=[f'xo{p2}'])
                    if final:
                        S.dma('sp', lambda e, ti=ti, p2=p2: e.dma_start(out=dst[(ti - 2) * 128:(ti - 1) * 128, :], in_=xo[p2][:]), r=[f'xo{p2}'], w=[('out', ti)])
                    else:
                        S.dma('sp', lambda e, ti=ti, p2=p2: e.dma_start(out=dst[ti * 128:(ti + 1) * 128, :], in_=xo[p2][:]), r=[f'xo{p2}'], w=[('out', ti)])
                S.barrier()

        for l in range(nlayers):
            last = (l == 1)
            xsrc = xin if l == 0 else xs1
            tiles = list(range(2, NT)) if last else list(range(NT))
            phase0(l)
            if stop == 'p0': break
            phase1(l, xsrc)
            if stop == 'p1': break
            phase2(l, with_ctx_q=not last)
            if stop == 'p2': break
            phase3(l)
            if stop == 'p3': break
            phase4(l, xsrc, tiles)
            if stop == 'p4': break
            phase56(l, with_ctx=not last)
            if stop == 'p6': break
            phase7(l, y if last else xs1, tiles, final=last)
        S.barrier()
    return nc, dbg_outs


def _consts():
    c = {}
    c['c_ident'] = np.eye(128, dtype=np.float32)
    t = np.arange(TL)
    row = (t // 64).astype(np.float32); col = (t % 64).astype(np.float32)
    inv = (10000.0 ** (-np.arange(8, dtype=np.float32) * 2.0 / 16)).astype(np.float32)
    ang = np.concatenate([row[:, None] * inv, col[:, None] * inv], axis=-1).astype(np.float32)
    cos = np.cos(ang).astype(np.float32); sin = np.sin(ang).astype(np.float32)
    cosT = np.ones((32, T), np.float32); sinT = np.zeros((32, T), np.float32)
    for a in range(2):
        for i in range(8):
            cosT[a * 16 + i, TC:] = cos[:, a * 8 + i]; cosT[a * 16 + 8 + i, TC:] = cos[:, a * 8 + i]
            sinT[a * 16 + i, TC:] = -sin[:, a * 8 + i]; sinT[a * 16 + 8 + i, TC:] = sin[:, a * 8 + i]
    c['c_rope'] = np.stack([cosT, sinT]).astype(np.float32)
    s = np.arange(64)[:, None]; tt = np.arange(64)[None, :]
    m0 = (s <= tt).astype(np.float32); m1 = (s >= tt).astype(np.float32)
    c['c_mask'] = np.stack([np.concatenate([m0, m0], 0), np.concatenate([m1, m1], 0)]).astype(np.float32)
    misc = np.zeros((8, 16), np.float32)
    misc[4:, 0] = 1.0; misc[:4, 1] = 1.0; misc[4:, 1] = -1.0; misc[:4, 2] = 1.0
    c['c_misc'] = misc
    e8 = np.zeros((8, 8, NSTEP), np.float32)
    for k in range(8): e8[k, k, :] = 1.0
    c['c_eye8'] = e8
    pr = np.zeros((NSTEP, NSTEP), np.float32)
    for p in range(NSTEP):
        cidx = 3 - p if p < 4 else 71 - p
        pr[cidx, p] = 1.0
    c['c_prev'] = pr
    rst = np.ones((8, T), np.float32); rst[:, ::64] = 0.0
    c['c_rst'] = rst
    return c


def _prep_weights(inp):
    w = {}
    sw = np.arange(32)
    for a in range(2):
        for i in range(8):
            sw[a * 16 + i] = a * 16 + 8 + i; sw[a * 16 + 8 + i] = a * 16 + i
    cols = np.concatenate([np.arange(0, 512), np.arange(1696, 2208),
                           np.arange(512, 1152), np.arange(1152, 1184), 1152 + sw,
                           np.arange(1184, 1696), np.arange(2208, 2224)])
    assert cols.size == NWIN
    w['w_in'] = np.ascontiguousarray(inp['w_in'][:, :, cols]); w['b_in'] = np.ascontiguousarray(inp['b_in'][:, cols])
    uq = inp['w_uq']
    swc = np.concatenate([h * 96 + 64 + sw for h in range(8)])
    w['w_uq'] = np.ascontiguousarray(np.concatenate([uq, uq[:, :, swc]], axis=-1))
    ukv = inp['w_ukv']
    kc = np.concatenate([np.arange(h * 128, h * 128 + 64) for h in range(8)])
    vc = np.concatenate([np.arange(h * 128 + 64, h * 128 + 128) for h in range(8)])
    w['w_ukv'] = np.ascontiguousarray(np.concatenate([ukv[:, :, kc], ukv[:, :, vc]], axis=-1))
    w['ml_f_bias'] = np.ascontiguousarray(inp['ml_f_bias'].reshape(2, 8))
    for k in ['w_ada', 'b_ada', 'sg_ln_g', 'sg_ln_b', 'sg_w', 'sg_b', 'q_norm_g', 'kv_norm_g', 'ml_conv_w', 'ml_conv_b', 'ml_norm_g',
              'w_out', 'ln1_g', 'ln1_b', 'w_router', 'w_gate', 'w_up', 'w_down', 'ln2_g', 'ln2_b']:
        w[k] = np.ascontiguousarray(inp[k])
    return w


_CACHE = {}


def kernel(**inputs):
    inp = {k: np.asarray(v, dtype=np.float32) for k, v in inputs.items()}
    if 'nc' not in _CACHE:
        _CACHE['nc'] = build()[0]
    nc = _CACHE['nc']
    consts = _consts(); w = _prep_weights(inp)
    in_maps = []
    for b in range(8):
        m = dict(consts); m.update(w)
        m['xin'] = np.ascontiguousarray(np.concatenate([inp['ctx'][b], inp['x'][b]], axis=0))
        m['cvec'] = np.ascontiguousarray(np.stack([inp['c'][b], inp['c_ctx']]))
        in_maps.append(m)
    res = run_bass_kernel_spmd(nc, in_maps, core_ids=list(range(8)))
    return np.stack([np.asarray(r['y'], dtype=np.float32) for r in res.results], axis=0)
```

```python
import numpy as np
from contextlib import ExitStack
import concourse.bass as bass
import concourse.mybir as mybir
from concourse.bass_utils import run_bass_kernel_spmd

F32 = mybir.dt.float32; BF16 = mybir.dt.bfloat16; U32 = mybir.dt.uint32
AF = mybir.ActivationFunctionType; ALU = mybir.AluOpType; AX = mybir.AxisListType

D = 1024; T = 4352; NT = 34; TC = 256; TL = 4096
NWIN = 2256
ALPHA = 4 ** 0.25
EPS = 1e-6
MLA_SCALE = 96 ** -0.5
NSTEP = 68


class Sched:
    NSLOT = 6

    def __init__(self, nc, es):
        self.nc = nc
        self.E = {'pe': nc.tensor, 'dve': nc.vector, 'act': nc.scalar, 'pool': nc.gpsimd, 'sp': nc.sync}
        self.sem = {}; self.cnt = {}
        for k in self.E:
            self.sem[k] = es.enter_context(nc.semaphore('s_' + k)); self.cnt[k] = 0
        self.slots = {}
        for q in ('sp', 'pool'):
            self.slots[q] = []
            for i in range(self.NSLOT):
                k = f'd{q}{i}'
                self.sem[k] = es.enter_context(nc.semaphore('s_' + k)); self.cnt[k] = 0
                self.slots[q].append(k)
        self.rr = {'sp': 0, 'pool': 0}
        self.waited = {k: {} for k in self.E}
        self.lastw = {}; self.readers = {}

    def _wait(self, eng, ev):
        k, v = ev
        if v <= 0: return
        if k == eng and eng == 'pe': return
        if self.waited[eng].get(k, 0) >= v: return
        self.E[eng].wait_ge(self.sem[k], v); self.waited[eng][k] = v

    def _deps(self, eng, r, w):
        deps = {}
        def add(k, v):
            if deps.get(k, 0) < v: deps[k] = v
        for b in r:
            ev = self.lastw.get(b)
            if ev: add(*ev)
        for b in w:
            ev = self.lastw.get(b)
            if ev: add(*ev)
            for k, v in self.readers.get(b, {}).items(): add(k, v)
        for k, v in deps.items(): self._wait(eng, (k, v))

    def _commit(self, ev, r, w):
        for b in r:
            d = self.readers.setdefault(b, {})
            if d.get(ev[0], 0) < ev[1]: d[ev[0]] = ev[1]
        for b in w:
            self.lastw[b] = ev; self.readers[b] = {}

    def op(self, eng, fn, r=(), w=()):
        self._deps(eng, r, w)
        inst = fn(self.E[eng])
        self.cnt[eng] += 1
        inst.then_inc(self.sem[eng], 1)
        self._commit((eng, self.cnt[eng]), r, w)

    def dma(self, q, fn, r=(), w=()):
        self._deps(q, r, w)
        i = self.rr[q]; self.rr[q] = (i + 1) % self.NSLOT
        k = self.slots[q][i]
        self._wait(q, (k, self.cnt[k]))
        inst = fn(self.E[q])
        self.cnt[k] += 16
        inst.then_inc(self.sem[k], 16)
        self._commit((k, self.cnt[k]), r, w)

    def barrier(self):
        for e in self.E:
            for k in self.sem:
                if k == e: continue
                self._wait(e, (k, self.cnt[k]))
        self.lastw = {}; self.readers = {}


def build(debug=False, nlayers=2, stop=None):
    nc = bass.Bass("TRN2", target_bir_lowering=False)
    dbg_outs = []

    def din(name, shape, dt=F32):
        return nc.dram_tensor(name, list(shape), dt, kind="ExternalInput").ap()

    def dscr(name, shape, dt=F32):
        if debug:
            dbg_outs.append(name)
            return nc.dram_tensor(name, list(shape), dt, kind="ExternalOutput").ap()
        return nc.dram_tensor(name, list(shape), dt, kind="Internal").ap()

    L = 2
    xin = din("xin", [T, D]); cvec = din("cvec", [2, D])
    w_ada = din("w_ada", [L, D, 6 * D]); b_ada = din("b_ada", [L, 6 * D])
    w_in = din("w_in", [L, D, NWIN]); b_in = din("b_in", [L, NWIN])
    sg_ln_g = din("sg_ln_g", [L, 256]); sg_ln_b = din("sg_ln_b", [L, 256])
    sg_w = din("sg_w", [L, 4, 128, 128]); sg_b = din("sg_b", [L, 4, 128])
    q_norm_g = din("q_norm_g", [L, 384]); kv_norm_g = din("kv_norm_g", [L, 256])
    w_uq = din("w_uq", [L, 384, 1024]); w_ukv = din("w_ukv", [L, 256, 1024])
    ml_conv_w = din("ml_conv_w", [L, 5, 512]); ml_conv_b = din("ml_conv_b", [L, 512])
    ml_f_bias = din("ml_f_bias", [L, 8]); ml_norm_g = din("ml_norm_g", [L, 256])
    w_out = din("w_out", [L, D, D]); ln1_g = din("ln1_g", [L, D]); ln1_b = din("ln1_b", [L, D])
    w_router = din("w_router", [L, D, 16])
    w_gate = din("w_gate", [L, 16, D, D]); w_up = din("w_up", [L, 16, D, D]); w_down = din("w_down", [L, 16, D, D])
    ln2_g = din("ln2_g", [L, D]); ln2_b = din("ln2_b", [L, D])
    c_ident = din("c_ident", [128, 128])
    c_rope = din("c_rope", [2, 32, T])
    c_mask = din("c_mask", [2, 128, 64])
    c_misc = din("c_misc", [8, 16])
    c_eye8 = din("c_eye8", [8, 8, NSTEP])
    c_prev = din("c_prev", [NSTEP, NSTEP])
    c_rst = din("c_rst", [8, T])

    y = nc.dram_tensor("y", [TL, D], F32, kind="ExternalOutput").ap()

    xs1 = dscr("xs1", [T, D]); x1_d = dscr("x1_d", [T, D]); hm_d = dscr("hm_d", [T, D]); moe_d = dscr("moe_d", [T, D])
    catT_d = dscr("catT_d", [D, T], BF16)
    qT_d = dscr("qT_d", [8, 96, T], BF16); kT_d = dscr("kT_d", [8, 96, T], BF16)
    vaug_d = dscr("vaug_d", [T, 520], BF16)
    mlqk_d = dscr("mlqk_d", [8, 64, T]); gates_d = dscr("gates_d", [16, T])
    mlv_d = dscr("mlv_d", [T, 256], BF16); sigo_d = dscr("sigo_d", [T, 256])
    ada_d = dscr("ada_d", [96, 128])
    aff_d = dscr("aff_d", [16, T])

    with ExitStack() as top:
        S = Sched(nc, top)
        sbn = [0]

        def sb(es, shape, dt=F32, name=None):
            sbn[0] += 1
            return es.enter_context(nc.sbuf_tensor(f"{name or 't'}_{sbn[0]}", list(shape), dt))

        ps = [top.enter_context(nc.psum_tensor(f"ps{i}", [128, 512], F32)) for i in range(8)]
        PK = [f"ps{i}" for i in range(8)]
        ident = sb(top, [128, 128], F32, "ident")
        ones_bf = sb(top, [128, 512], BF16, "ones_bf")
        ones_f = sb(top, [128, 512], F32, "ones_f")
        zeros_f = sb(top, [128, 1024], F32, "zeros_f")
        epsc = sb(top, [128, 1], F32, "epsc")
        ada = sb(top, [128, 96], F32, "ada")
        adp = sb(top, [128, 96], F32, "adp")

        S.dma('sp', lambda e: e.dma_start(out=ident[:], in_=c_ident), w=['ident'])
        S.op('dve', lambda e: e.memset(ones_bf[:], 1.0), w=['ones_bf'])
        S.op('dve', lambda e: e.memset(ones_f[:], 1.0), w=['ones_f'])
        S.op('dve', lambda e: e.memset(zeros_f[:], 0.0), w=['zeros_f'])
        S.op('dve', lambda e: e.memset(epsc[:], EPS), w=['epsc'])

        def interleave(gens, width):
            active = []; it = iter(gens)
            while True:
                while len(active) < width:
                    g = next(it, None)
                    if g is None: break
                    active.append(g)
                if not active: break
                for g in list(active):
                    try: next(g)
                    except StopIteration: active.remove(g)

        def MM(out, lhsT, rhs, st, sp_, r, w):
            S.op('pe', lambda e: e.matmul(out, lhsT, rhs, start=st, stop=sp_), r, w)

        def TR(out, in_, r, w, n=128):
            S.op('pe', lambda e: e.transpose(out, in_, ident[:n, :n]), list(r) + ['ident'], w)

        def ln_stats(es_tag, xap, width, mv, rstd, nmr, stats, rk):
            nch = width // 256 if width >= 256 else 1
            cw = width // nch
            for c in range(nch):
                S.op('dve', lambda e, c=c: e.bn_stats(stats[:, c, :], xap[:, c * cw:(c + 1) * cw]), r=rk, w=[es_tag + 'st'])
            S.op('dve', lambda e: e.bn_aggr(mv[:], stats[:, 0:nch, :]), r=[es_tag + 'st'], w=[es_tag + 'mv'])
            S.op('act', lambda e: e.activation(out=rstd[:], in_=mv[:, 1:2], func=AF.Sqrt, bias=epsc[:, 0:1], scale=1.0),
                 r=[es_tag + 'mv', 'epsc'], w=[es_tag + 'rs'])
            S.op('dve', lambda e: e.reciprocal(rstd[:], rstd[:]), r=[es_tag + 'rs'], w=[es_tag + 'rs'])
            S.op('dve', lambda e: e.scalar_tensor_tensor(out=nmr[:], in0=mv[:, 0:1], scalar=-1.0, in1=rstd[:], op0=ALU.mult, op1=ALU.mult),
                 r=[es_tag + 'mv', es_tag + 'rs'], w=[es_tag + 'nm'])

        def phase0(l):
            with ExitStack() as es:
                cfm = sb(es, [128, 2, 8]); sfm = sb(es, [128, 2, 8])
                brow = sb(es, [1, 6 * D])
                wa = [sb(es, [128, 8, 768]) for _ in range(2)]
                adaT = sb(es, [96, 128])
                with nc.allow_non_contiguous_dma(reason="tiny"):
                    for m_ in range(2):
                        S.dma('sp', lambda e, m_=m_: e.dma_start(out=cfm[:, m_, :], in_=cvec[m_].rearrange("(c p) -> p c", p=128)), w=['cfm'])
                S.dma('sp', lambda e: e.dma_start(out=brow[:], in_=b_ada[l:l + 1, :]), w=['brow'])
                S.op('act', lambda e: e.activation(out=sfm[:], in_=cfm[:], func=AF.Silu), r=['cfm'], w=['sfm'])
                for blk in range(8):
                    wt = wa[blk % 2]; wk = f'wa{blk % 2}'
                    S.dma('sp', lambda e: e.dma_start(out=wt[:], in_=w_ada[l, :, blk * 768:(blk + 1) * 768].rearrange("(c p) n -> p c n", p=128)), w=[wk])
                    for nn in range(6):
                        n = blk * 6 + nn
                        for c in range(8):
                            MM(ps[0][:, n * 2:(n + 1) * 2], wt[:, c, nn * 128:(nn + 1) * 128], sfm[:, :, c], c == 0, False, [wk, 'sfm'], [PK[0]])
                        MM(ps[0][:, n * 2:(n + 1) * 2], brow[0:1, n * 128:(n + 1) * 128], ones_f[0:1, 0:2], False, True, ['brow', 'ones_f'], [PK[0]])
                S.op('dve', lambda e: e.tensor_copy(ada[:], ps[0][:, 0:96]), r=[PK[0]], w=['ada'])
                S.op('dve', lambda e: e.tensor_scalar(out=adp[:], in0=ada[:], scalar1=1.0, scalar2=None, op0=ALU.add), r=['ada'], w=['adp'])
                TR(ps[1][:96, 0:128], ada[:, :], ['ada'], [PK[1]])
                S.op('dve', lambda e: e.tensor_copy(adaT[:], ps[1][:96, 0:128]), r=[PK[1]], w=['adaT'])
                S.dma('sp', lambda e: e.dma_start(out=ada_d, in_=adaT[:]), r=['adaT'], w=['ada_d'])
                for ti in range(NT):
                    S.dma('sp', lambda e, ti=ti: e.dma_start(out=moe_d[ti * 128:(ti + 1) * 128, :], in_=zeros_f[:]), r=['zeros_f'], w=[('moe', ti)])
                S.barrier()

        def ada_col(j, cc, m):
            n = (j * 8 + cc) * 2 + m
            return n

        def bcast_load(es, q, dst, j, m, key):
            src = ada_d.rearrange("(n m) p -> m n p", m=2)[m, j * 8:(j + 1) * 8, :].partition_broadcast(128)
            S.dma(q, lambda e: e.dma_start(out=dst[:].rearrange("p (c f) -> p c f", c=8), in_=src), r=['ada_d'], w=[key])

        def vec_bcast(q, dst, vec_ap, key):
            S.dma(q, lambda e: e.dma_start(out=dst[:], in_=vec_ap.partition_broadcast(128)), w=[key])

        def blocks():
            out = [(0, 256, [0, 1])]
            for b in range(8):
                out.append((256 + 512 * b, 512, [2 + 4 * b + i for i in range(4)]))
            return out

        def phase1(l, xsrc):
            with ExitStack() as es:
                win = sb(es, [128, 8, NWIN], BF16); brow = sb(es, [1, NWIN], BF16)
                wuq = sb(es, [128, 3, 1024], BF16); wukv = sb(es, [128, 2, 1024], BF16)
                wsT = sb(es, [128, 4, 128], BF16)
                w32 = sb(es, [128, 3, 1024]); wsf = sb(es, [128, 512]); gq = sb(es, [128, 3]); gkv = sb(es, [128, 2])
                sgbT = sb(es, [128, 4]); lng = sb(es, [128, 256]); lnb = sb(es, [128, 256])
                rope = [sb(es, [96, 2, 512]) for _ in range(2)]
                xt = [sb(es, [128, D]) for _ in range(2)]
                xn = [sb(es, [128, D]) for _ in range(2)]
                xmT = [sb(es, [128, 8, 512], BF16) for _ in range(2)]
                st6 = sb(es, [128, 4, 6]); mv = sb(es, [128, 2]); rstd = sb(es, [128, 1]); nmr = sb(es, [128, 1])
                st6b = sb(es, [128, 4, 6]); mvb = sb(es, [128, 2]); rstdb = sb(es, [128, 1]); nmrb = sb(es, [128, 1])
                gl = [sb(es, [128, 512]) for _ in range(2)]
                vn = sb(es, [128, 256]); vnb = sb(es, [128, 256], BF16)
                ya = [sb(es, [128, 256]) for _ in range(2)]
                yaT = [sb(es, [128, 2, 128], BF16) for _ in range(2)]
                mlv = [sb(es, [128, 256], BF16) for _ in range(2)]
                sgo = [sb(es, [128, 256]) for _ in range(2)]
                cqT = sb(es, [128, 3, 512], BF16); ckvT = sb(es, [128, 2, 512], BF16)
                sq = [sb(es, [128, 512]) for _ in range(2)]
                rq = sb(es, [96, 512]); rkv = sb(es, [64, 512]); rkvc = sb(es, [128, 4])
                qs = [sb(es, [96, 512]) for _ in range(2)]; qb = [sb(es, [96, 512]) for _ in range(2)]
                r1 = sb(es, [96, 512]); r2 = sb(es, [96, 512])
                qTt = [sb(es, [96, 512], BF16) for _ in range(2)]
                kTt = sb(es, [96, 8, 512], BF16)
                krt = sb(es, [96, 512], BF16)
                va = [sb(es, [128, 8, 65], BF16) for _ in range(2)]
                fm32 = [sb(es, [64, 512]) for _ in range(2)]

                for c in range(8):
                    S.dma('pool', lambda e, c=c: e.dma_start(out=win[:, c, :], in_=w_in[l, c * 128:(c + 1) * 128, :]), w=['win'])
                S.dma('pool', lambda e: e.dma_start(out=brow[:], in_=b_in[l:l + 1, :]), w=['brow'])
                with nc.allow_non_contiguous_dma(reason="tiny"):
                    S.dma('sp', lambda e: e.dma_start(out=gq[:], in_=q_norm_g[l].rearrange("(c p) -> p c", p=128)), w=['gq'])
                    S.dma('sp', lambda e: e.dma_start(out=gkv[:], in_=kv_norm_g[l].rearrange("(c p) -> p c", p=128)), w=['gkv'])
                    S.dma('sp', lambda e: e.dma_start(out=sgbT[:], in_=sg_b[l].rearrange("g t -> t g")), w=['sgbT'])
                S.dma('sp', lambda e: e.dma_start(out=w32[:], in_=w_uq[l].rearrange("(c p) n -> p c n", p=128)), w=['w32'])
                for c in range(3):
                    S.op('dve', lambda e, c=c: e.tensor_scalar(out=wuq[:, c, :], in0=w32[:, c, :], scalar1=gq[:, c:c + 1], scalar2=None, op0=ALU.mult),
                         r=['w32', 'gq'], w=['wuq'])
                S.dma('sp', lambda e: e.dma_start(out=w32[:, 0:2, :], in_=w_ukv[l].rearrange("(c p) n -> p c n", p=128)), r=[], w=['w32'])
                for c in range(2):
                    S.op('dve', lambda e, c=c: e.tensor_scalar(out=wukv[:, c, :], in0=w32[:, c, :], scalar1=gkv[:, c:c + 1], scalar2=None, op0=ALU.mult),
                         r=['w32', 'gkv'], w=['wukv'])
                S.dma('sp', lambda e: e.dma_start(out=wsf[:].rearrange("p (g s) -> p g s", g=4), in_=sg_w[l].rearrange("g t s -> t g s")), w=['wsf'])
                for g in range(4):
                    TR(ps[0][:, g * 128:(g + 1) * 128], wsf[:, g * 128:(g + 1) * 128], ['wsf'], [PK[0]])
                S.op('dve', lambda e: e.tensor_copy(wsT[:].rearrange("p g t -> p (g t)"), ps[0][:, 0:512]), r=[PK[0]], w=['wsT'])
                vec_bcast('sp', lng, sg_ln_g[l], 'lng'); vec_bcast('sp', lnb, sg_ln_b[l], 'lnb')

                bi = 0
                for (t0, N, tiles) in blocks():
                    bp = bi % 2; bi += 1
                    xm = xmT[bp]; xmk = f'xmT{bp}'
                    m_ada = 1 if t0 == 0 else 0
                    S.dma('sp', lambda e, bp=bp: e.dma_start(out=rope[bp][64:96, :, 0:N], in_=c_rope[:, :, t0:t0 + N].rearrange("a p t -> p a t")), w=[f'rope{bp}'])
                    for li, ti in enumerate(tiles):
                        p2 = ti % 2
                        xk = f'xt{p2}'; xnk = f'xn{p2}'
                        S.dma('sp', lambda e, ti=ti, p2=p2: e.dma_start(out=xt[p2][:], in_=xsrc[ti * 128:(ti + 1) * 128, :]), w=[xk])
                        ln_stats('l1', xt[p2], D, mv, rstd, nmr, st6, [xk])
                        S.op('act', lambda e, p2=p2: e.activation(out=xn[p2][:], in_=xt[p2][:], func=AF.Identity, bias=nmr[:, 0:1], scale=rstd[:, 0:1]),
                             r=[xk, 'l1rs', 'l1nm'], w=[xnk])
                        for half in range(2):
                            pb = ps[half]
                            for c4 in range(4):
                                cc = half * 4 + c4
                                TR(pb[:, c4 * 128:(c4 + 1) * 128], xn[p2][:, cc * 128:(cc + 1) * 128], [xnk], [PK[half]])
                            for c4 in range(4):
                                cc = half * 4 + c4
                                eng = 'act' if c4 % 2 == 0 else 'dve'
                                if eng == 'act':
                                    S.op('act', lambda e, cc=cc, c4=c4, pb=pb, li=li: e.activation(out=xm[:, cc, li * 128:(li + 1) * 128], in_=pb[:, c4 * 128:(c4 + 1) * 128], func=AF.Identity,
                                                                                         bias=ada[:, ada_col(0, cc, m_ada):ada_col(0, cc, m_ada) + 1], scale=adp[:, ada_col(1, cc, m_ada):ada_col(1, cc, m_ada) + 1]),
                                         r=[PK[half], 'ada', 'adp'], w=[xmk])
                                else:
                                    S.op('dve', lambda e, cc=cc, c4=c4, pb=pb, li=li: e.tensor_scalar(out=xm[:, cc, li * 128:(li + 1) * 128], in0=pb[:, c4 * 128:(c4 + 1) * 128],
                                                                                            scalar1=adp[:, ada_col(1, cc, m_ada):ada_col(1, cc, m_ada) + 1], scalar2=ada[:, ada_col(0, cc, m_ada):ada_col(0, cc, m_ada) + 1],
                                                                                            op0=ALU.mult, op1=ALU.add),
                                         r=[PK[half], 'ada', 'adp'], w=[xmk])
                        for c in range(8):
                            MM(ps[2][:, :], xm[:, c, li * 128:(li + 1) * 128], win[:, c, 0:512], c == 0, False, [xmk, 'win'], [PK[2]])
                        MM(ps[2][:, :], ones_bf[0:1, 0:128], brow[0:1, 0:512], False, True, ['ones_bf', 'brow'], [PK[2]])
                        g_ = gl[p2]; gk = f'gl{p2}'
                        S.op('act', lambda e, g_=g_: e.activation(out=g_[:], in_=ps[2][:, :], func=AF.Gelu_apprx_tanh), r=[PK[2]], w=[gk])
                        ln_stats('l2', g_[:, 256:512], 256, mvb, rstdb, nmrb, st6b, [gk])
                        S.op('act', lambda e, g_=g_: e.activation(out=vn[:], in_=g_[:, 256:512], func=AF.Identity, bias=nmrb[:, 0:1], scale=rstdb[:, 0:1]),
                             r=[gk, 'l2rs', 'l2nm'], w=['vn'])
                        S.op('dve', lambda e: e.tensor_tensor(out=vn[:], in0=vn[:], in1=lng[:], op=ALU.mult), r=['vn', 'lng'], w=['vn'])
                        S.op('dve', lambda e: e.tensor_tensor(out=vnb[:], in0=vn[:], in1=lnb[:], op=ALU.add), r=['vn', 'lnb'], w=['vnb'])
                        for g in range(4):
                            MM(ps[3][:, g * 64:(g + 1) * 64], wsT[:, g, :], vnb[:, g * 64:(g + 1) * 64], True, True, ['wsT', 'vnb'], [PK[3]])
                        yat = ya[p2]; yak = f'ya{p2}'
                        for g in range(4):
                            S.op('dve', lambda e, g=g, yat=yat, g_=g_: e.scalar_tensor_tensor(out=yat[:, g * 64:(g + 1) * 64], in0=ps[3][:, g * 64:(g + 1) * 64], scalar=sgbT[:, g:g + 1],
                                                                                       in1=g_[:, g * 64:(g + 1) * 64], op0=ALU.add, op1=ALU.mult),
                                 r=[PK[3], 'sgbT', gk], w=[yak])
                        for c in range(2):
                            TR(ps[3][:, 256 + c * 128:256 + (c + 1) * 128], yat[:, c * 128:(c + 1) * 128], [yak], [PK[3]])
                        yT = yaT[p2]; yTk = f'yaT{p2}'
                        S.op('act', lambda e, yT=yT: e.copy(out=yT[:].rearrange("p c t -> p (c t)"), in_=ps[3][:, 256:512]), r=[PK[3]], w=[yTk])
                        S.dma('sp', lambda e, yT=yT, ti=ti: e.dma_start(out=catT_d[0:256, ti * 128:(ti + 1) * 128].rearrange("(c p) t -> p c t", p=128), in_=yT[:]), r=[yTk], w=[('catA', ti)])
                        for c in range(8):
                            MM(ps[4][:, :], xm[:, c, li * 128:(li + 1) * 128], win[:, c, 512:1024], c == 0, False, [xmk, 'win'], [PK[4]])
                        MM(ps[4][:, :], ones_bf[0:1, 0:128], brow[0:1, 512:1024], False, True, ['ones_bf', 'brow'], [PK[4]])
                        S.op('dve', lambda e, p2=p2: e.tensor_copy(mlv[p2][:], ps[4][:, 0:256]), r=[PK[4]], w=[f'mlv{p2}'])
                        S.op('act', lambda e, p2=p2: e.activation(out=sgo[p2][:], in_=ps[4][:, 256:512], func=AF.Sigmoid), r=[PK[4]], w=[f'sgo{p2}'])
                        S.dma('sp', lambda e, p2=p2, ti=ti: e.dma_start(out=mlv_d[ti * 128:(ti + 1) * 128, :], in_=mlv[p2][:]), r=[f'mlv{p2}'], w=[('mlv_d', ti)])
                        S.dma('sp', lambda e, p2=p2, ti=ti: e.dma_start(out=sigo_d[ti * 128:(ti + 1) * 128, :], in_=sgo[p2][:]), r=[f'sgo{p2}'], w=[('sigo_d', ti)])

                    FM0 = 1024
                    def fm_mm(pst, pk, col0, M, pbase=0):
                        for c in range(8):
                            MM(pst[pbase:pbase + M, 0:N], win[:, c, col0:col0 + M], xm[:, c, 0:N], c == 0, False, ['win', xmk], [pk])
                        MM(pst[pbase:pbase + M, 0:N], brow[0:1, col0:col0 + M], ones_bf[0:1, 0:N], False, True, ['brow', 'ones_bf'], [pk])
                    for c in range(3):
                        fm_mm(ps[5], PK[5], FM0 + c * 128, 128)
                        S.op('act', lambda e, c=c: e.copy(out=cqT[:, c, 0:N], in_=ps[5][:, 0:N]), r=[PK[5]], w=['cqT'])
                        S.op('act', lambda e, c=c: e.activation(out=sq[c % 2][:, 0:N], in_=ps[5][:, 0:N], func=AF.Square), r=[PK[5]], w=[f'sq{c % 2}'])
                        MM(ps[6][0:96, 0:N], ones_f[:, 0:96], sq[c % 2][:, 0:N], c == 0, c == 2, ['ones_f', f'sq{c % 2}'], [PK[6]])
                    S.op('act', lambda e: e.activation(out=rq[:, 0:N], in_=ps[6][0:96, 0:N], func=AF.Sqrt, bias=epsc[0:96, 0:1], scale=1.0 / 384), r=[PK[6], 'epsc'], w=['rq'])
                    S.op('dve', lambda e: e.reciprocal(rq[:, 0:N], rq[:, 0:N]), r=['rq'], w=['rq'])
                    rp = rope[bp]; rpk = f'rope{bp}'
                    for h in range(8):
                        hp = h % 2
                        for c in range(3):
                            MM(ps[5][0:96, 0:N], wuq[:, c, h * 96:(h + 1) * 96], cqT[:, c, 0:N], c == 0, c == 2, ['wuq', 'cqT'], [PK[5]])
                        for c in range(3):
                            MM(ps[7][64:96, 0:N], wuq[:, c, 768 + h * 32:768 + (h + 1) * 32], cqT[:, c, 0:N], c == 0, c == 2, ['wuq', 'cqT'], [PK[7]])
                        S.op('dve', lambda e, hp=hp: e.tensor_tensor(out=qs[hp][:, 0:N], in0=ps[5][0:96, 0:N], in1=rq[:, 0:N], op=ALU.mult), r=[PK[5], 'rq'], w=[f'qs{hp}'])
                        S.op('dve', lambda e, hp=hp: e.tensor_tensor(out=qb[hp][64:96, 0:N], in0=ps[7][64:96, 0:N], in1=rq[64:96, 0:N], op=ALU.mult), r=[PK[7], 'rq'], w=[f'qb{hp}'])
                        S.op('act', lambda e, hp=hp: e.copy(out=qTt[hp][0:64, 0:N], in_=qs[hp][0:64, 0:N]), r=[f'qs{hp}'], w=[f'qTt{hp}'])
                        S.op('pool', lambda e, hp=hp: e.tensor_tensor(out=r1[64:96, 0:N], in0=qs[hp][64:96, 0:N], in1=rp[64:96, 0, 0:N], op=ALU.mult), r=[f'qs{hp}', rpk], w=['r1'])
                        S.op('pool', lambda e, hp=hp: e.tensor_tensor(out=r2[64:96, 0:N], in0=qb[hp][64:96, 0:N], in1=rp[64:96, 1, 0:N], op=ALU.mult), r=[f'qb{hp}', rpk], w=['r2'])
                        S.op('pool', lambda e, hp=hp: e.tensor_tensor(out=qTt[hp][64:96, 0:N], in0=r1[64:96, 0:N], in1=r2[64:96, 0:N], op=ALU.add), r=['r1', 'r2'], w=[f'qTt{hp}'])
                        S.dma('sp', lambda e, hp=hp, h=h: e.dma_start(out=qT_d[h, :, t0:t0 + N], in_=qTt[hp][:, 0:N]), r=[f'qTt{hp}'], w=[('qT_d', h, t0)])
                    for c in range(2):
                        fm_mm(ps[5], PK[5], FM0 + 384 + c * 128, 128)
                        S.op('act', lambda e, c=c: e.copy(out=ckvT[:, c, 0:N], in_=ps[5][:, 0:N]), r=[PK[5]], w=['ckvT'])
                        S.op('act', lambda e, c=c: e.activation(out=sq[c][:, 0:N], in_=ps[5][:, 0:N], func=AF.Square), r=[PK[5]], w=[f'sq{c}'])
                    for c in range(2):
                        MM(ps[6][0:64, 0:N], ones_f[:, 0:64], sq[c][:, 0:N], c == 0, c == 1, ['ones_f', f'sq{c}'], [PK[6]])
                    S.op('act', lambda e: e.activation(out=rkv[:, 0:N], in_=ps[6][0:64, 0:N], func=AF.Sqrt, bias=epsc[0:64, 0:1], scale=1.0 / 256), r=[PK[6], 'epsc'], w=['rkv'])
                    S.op('dve', lambda e: e.reciprocal(rkv[:, 0:N], rkv[:, 0:N]), r=['rkv'], w=['rkv'])
                    for li in range(len(tiles)):
                        for c in range(2):
                            MM(ps[7][:, li:li + 1], sq[c][:, li * 128:(li + 1) * 128], ones_f[:, 0:1], c == 0, c == 1, [f'sq{c}', 'ones_f'], [PK[7]])
                    nl = len(tiles)
                    S.op('act', lambda e: e.activation(out=rkvc[:, 0:nl], in_=ps[7][:, 0:nl], func=AF.Sqrt, bias=epsc[:, 0:1], scale=1.0 / 256), r=[PK[7], 'epsc'], w=['rkvc'])
                    S.op('dve', lambda e: e.reciprocal(rkvc[:, 0:nl], rkvc[:, 0:nl]), r=['rkvc'], w=['rkvc'])
                    fm_mm(ps[5], PK[5], FM0 + 640, 32, pbase=64)
                    fm_mm(ps[7], PK[7], FM0 + 672, 32, pbase=64)
                    S.op('dve', lambda e: e.tensor_tensor(out=r1[64:96, 0:N], in0=ps[5][64:96, 0:N], in1=rp[64:96, 0, 0:N], op=ALU.mult), r=[PK[5], rpk], w=['r1'])
                    S.op('dve', lambda e: e.tensor_tensor(out=r2[64:96, 0:N], in0=ps[7][64:96, 0:N], in1=rp[64:96, 1, 0:N], op=ALU.mult), r=[PK[7], rpk], w=['r2'])
                    S.op('dve', lambda e: e.tensor_tensor(out=krt[64:96, 0:N], in0=r1[64:96, 0:N], in1=r2[64:96, 0:N], op=ALU.add), r=['r1', 'r2'], w=['krt'])
                    for h in range(8):
                        for c in range(2):
                            MM(ps[5][0:64, 0:N], wukv[:, c, h * 64:(h + 1) * 64], ckvT[:, c, 0:N], c == 0, c == 1, ['wukv', 'ckvT'], [PK[5]])
                        S.op('dve', lambda e, h=h: e.tensor_tensor(out=kTt[0:64, h, 0:N], in0=ps[5][0:64, 0:N], in1=rkv[:, 0:N], op=ALU.mult), r=[PK[5], 'rkv'], w=['kTt'])
                        S.op('pool', lambda e, h=h: e.tensor_copy(kTt[64:96, h, 0:N], krt[64:96, 0:N]), r=['krt'], w=['kTt'])
                    S.dma('sp', lambda e: e.dma_start(out=kT_d[:, :, t0:t0 + N].rearrange("h p t -> p h t"), in_=kTt[:, :, 0:N]), r=['kTt'], w=[('kT_d', t0)])
                    for li, ti in enumerate(tiles):
                        vp = li % 2
                        for c in range(2):
                            MM(ps[6][:, :], ckvT[:, c, li * 128:(li + 1) * 128], wukv[:, c, 512:1024], c == 0, c == 1, ['ckvT', 'wukv'], [PK[6]])
                        S.op('dve', lambda e, vp=vp: e.memset(va[vp][:, :, 64:65], 1.0), w=[f'va{vp}'])
                        S.op('dve', lambda e, vp=vp, li=li: e.tensor_scalar(out=va[vp][:, :, 0:64], in0=ps[6][:, :].rearrange("p (h d) -> p h d", h=8), scalar1=rkvc[:, li:li + 1], scalar2=None, op0=ALU.mult),
                             r=[PK[6], 'rkvc'], w=[f'va{vp}'])
                        S.dma('sp', lambda e, vp=vp, ti=ti: e.dma_start(out=vaug_d[ti * 128:(ti + 1) * 128, :], in_=va[vp][:].rearrange("p h d -> p (h d)")), r=[f'va{vp}'], w=[('vaug_d', ti)])
                    for g in range(8):
                        fp_ = g % 2
                        fm_mm(ps[5], PK[5], FM0 + 704 + g * 64, 64)
                        S.op('act', lambda e, fp_=fp_: e.copy(out=fm32[fp_][:, 0:N], in_=ps[5][0:64, 0:N]), r=[PK[5]], w=[f'fm32{fp_}'])
                        S.dma('sp', lambda e, fp_=fp_, g=g: e.dma_start(out=mlqk_d[g, :, t0:t0 + N], in_=fm32[fp_][:, 0:N]), r=[f'fm32{fp_}'], w=[('mlqk_d', g, t0)])
                    fm_mm(ps[5], PK[5], FM0 + 1216, 16)
                    S.op('act', lambda e: e.copy(out=fm32[0][0:16, 0:N], in_=ps[5][0:16, 0:N]), r=[PK[5]], w=['fm320'])
                    S.dma('sp', lambda e: e.dma_start(out=gates_d[:, t0:t0 + N], in_=fm32[0][0:16, 0:N]), r=['fm320'], w=[('gates_d', t0)])
                S.barrier()

        def phase2(l, with_ctx_q):
            with ExitStack() as es:
                vaug = sb(es, [128, NT, 520], BF16)
                kT = [sb(es, [96, T], BF16) for _ in range(2)]
                qT = [sb(es, [96, T], BF16) for _ in range(2)]
                pT = [sb(es, [128, 512], BF16) for _ in range(3)]
                oT = sb(es, [65, 512]); sel = sb(es, [65, 64]); rec = sb(es, [64, 512])
                on = [sb(es, [64, 512], BF16) for _ in range(2)]
                S.dma('sp', lambda e: e.dma_start(out=vaug[:], in_=vaug_d.rearrange("(t p) c -> p t c", p=128)), w=['vaug'])
                S.op('dve', lambda e: e.memset(sel[:], 0.0), w=['sel'])
                S.op('dve', lambda e: e.memset(sel[64:65, :], 1.0), w=['sel'])
                def load_head(h):
                    hp = h % 2
                    S.dma('sp', lambda e: e.dma_start(out=kT[hp][:], in_=kT_d[h]), w=[f'kT{hp}'])
                    S.dma('sp', lambda e: e.dma_start(out=qT[hp][:], in_=qT_d[h]), w=[f'qT{hp}'])
                items = []
                for h in range(8):
                    for bi_, (t0, N, tiles) in enumerate(blocks()):
                        if t0 == 0 and not with_ctx_q: continue
                        ktiles = [0, 1] if t0 == 0 else list(range(NT))
                        for i, kt in enumerate(ktiles):
                            items.append((h, t0, N, i, kt, len(ktiles)))
                LAG = 2
                blk_ctr = [0]

                def do_pv(n):
                    h, t0, N, i, kt, nk = items[n]
                    hp = h % 2; pk = n % 3
                    ob = blk_ctr[0] % 2
                    MM(ps[ob][0:65, 0:N], vaug[:, kt, h * 65:(h + 1) * 65], pT[pk][:, 0:N], i == 0, i == nk - 1, ['vaug', f'pT{pk}'], [PK[ob]])
                    if i == nk - 1:
                        blk_ctr[0] += 1
                        S.op('dve', lambda e: e.tensor_copy(oT[:, 0:N], ps[ob][0:65, 0:N]), r=[PK[ob]], w=['oT'])
                        MM(ps[5][0:64, 0:N], sel[:, :], oT[:, 0:N], True, True, ['sel', 'oT'], [PK[5]])
                        S.op('dve', lambda e: e.reciprocal(rec[:, 0:N], ps[5][0:64, 0:N]), r=[PK[5]], w=['rec'])
                        op_ = blk_ctr[0] % 2
                        S.op('dve', lambda e: e.tensor_tensor(out=on[op_][:, 0:N], in0=oT[0:64, 0:N], in1=rec[:, 0:N], op=ALU.mult), r=['oT', 'rec'], w=[f'on{op_}'])
                        S.dma('sp', lambda e: e.dma_start(out=catT_d[256 + h * 64:256 + (h + 1) * 64, t0:t0 + N], in_=on[op_][:, 0:N]), r=[f'on{op_}'], w=[('catB', h, t0)])

                load_head(0)
                cur_h = -1
                for n, (h, t0, N, i, kt, nk) in enumerate(items):
                    if h != cur_h:
                        cur_h = h
                        if h + 1 < 8: load_head(h + 1)
                    hp = h % 2; sbk = 2 + n % 3; pk = n % 3
                    MM(ps[sbk][:, 0:N], kT[hp][:, kt * 128:(kt + 1) * 128], qT[hp][:, t0:t0 + N], True, True, [f'kT{hp}', f'qT{hp}'], [PK[sbk]])
                    S.op('act', lambda e, sbk=sbk, pk=pk, N=N: e.activation(out=pT[pk][:, 0:N], in_=ps[sbk][:, 0:N], func=AF.Exp, scale=MLA_SCALE), r=[PK[sbk]], w=[f'pT{pk}'])
                    if n >= LAG: do_pv(n - LAG)
                for n in range(max(0, len(items) - LAG), len(items)):
                    do_pv(n)
                S.barrier()

        def chunk_of(d, p):
            if d == 0: return p
            return 3 - p if p < 4 else 71 - p

        def phase3(l):
            with ExitStack() as es:
                qT = sb(es, [64, 4, T], BF16); kT = sb(es, [64, 4, T], BF16)
                ktok = sb(es, [128, NT, 256], BF16)
                vaug = sb(es, [128, NT, 4, 65], BF16)
                cw = sb(es, [64, 5, 8]); cb = sb(es, [64, 8])
                cols = sb(es, [128, NT, 24]); bc = sb(es, [64, 3, 8, NSTEP])
                msk = sb(es, [128, 2, 64]); misc = sb(es, [8, 16]); eye8 = sb(es, [8, 8, NSTEP]); prev = sb(es, [NSTEP, NSTEP])
                fbn = sb(es, [8, 1])
                S.dma('sp', lambda e: e.dma_start(out=msk[:], in_=c_mask.rearrange("d p t -> p d t")), w=['msk'])
                S.dma('sp', lambda e: e.dma_start(out=misc[:], in_=c_misc), w=['misc'])
                S.dma('sp', lambda e: e.dma_start(out=eye8[:], in_=c_eye8), w=['eye8'])
                S.dma('sp', lambda e: e.dma_start(out=prev[:], in_=c_prev), w=['prev'])
                with nc.allow_non_contiguous_dma(reason="tiny"):
                    for j_ in range(5):
                        S.dma('sp', lambda e, j_=j_: e.dma_start(out=cw[:, j_, :], in_=ml_conv_w[l, j_].rearrange("(g p) -> p g", p=64)), w=['cw'])
                    S.dma('sp', lambda e: e.dma_start(out=cb[:], in_=ml_conv_b[l].rearrange("(g p) -> p g", p=64)), w=['cb'])
                    S.dma('sp', lambda e: e.dma_start(out=fbn[:], in_=ml_f_bias[l].rearrange("(a b) -> a b", b=1)), w=['fbn'])
                S.op('dve', lambda e: e.tensor_scalar(out=fbn[:], in0=fbn[:], scalar1=-1.0, scalar2=None, op0=ALU.mult), r=['fbn'], w=['fbn'])
                for ti in range(NT):
                    S.dma('sp', lambda e, ti=ti: e.dma_start(out=vaug[:, ti, :, 0:64], in_=mlv_d[ti * 128:(ti + 1) * 128, :].rearrange("p (h d) -> p h d", h=4)), w=['vaug'])
                S.op('dve', lambda e: e.memset(vaug[:, :, :, 64:65], 1.0), w=['vaug'])
                with ExitStack() as es2:
                    pre = [sb(es2, [64, 4360])] * 2
                    acc = [sb(es2, [64, 4356])] * 2
                    for g in range(8):
                        gp = 0; pr = pre[gp]; ac = acc[gp]; prk = f'pre{gp}'; ack = f'acc{gp}'
                        S.op('pool', lambda e, pr=pr: e.memset(pr[:, 0:2], 0.0), w=[prk])
                        S.op('pool', lambda e, pr=pr: e.memset(pr[:, 258:262], 0.0), w=[prk])
                        S.op('pool', lambda e, pr=pr: e.memset(pr[:, 4358:4360], 0.0), w=[prk])
                        S.dma('sp', lambda e, pr=pr, g=g: e.dma_start(out=pr[:, 2:258], in_=mlqk_d[g, :, 0:256]), w=[prk])
                        S.dma('sp', lambda e, pr=pr, g=g: e.dma_start(out=pr[:, 262:4358], in_=mlqk_d[g, :, 256:T]), w=[prk])
                        S.op('dve', lambda e, pr=pr, ac=ac, g=g: e.tensor_scalar(out=ac[:], in0=pr[:, 0:4356], scalar1=cw[:, 0, g:g + 1], scalar2=cb[:, g:g + 1], op0=ALU.mult, op1=ALU.add),
                             r=[prk, 'cw', 'cb'], w=[ack])
                        for j in range(1, 5):
                            S.op('dve', lambda e, pr=pr, ac=ac, g=g, j=j: e.scalar_tensor_tensor(out=ac[:], in0=pr[:, j:j + 4356], scalar=cw[:, j, g:g + 1], in1=ac[:], op0=ALU.mult, op1=ALU.add),
                                 r=[prk, 'cw', ack], w=[ack])
                        S.op('act', lambda e, ac=ac: e.activation(out=ac[:], in_=ac[:], func=AF.Silu), r=[ack], w=[ack])
                        dst = qT if g < 4 else kT; dk = 'qT3' if g < 4 else 'kT3'; h = g % 4
                        sc = 1.0 if g < 4 else 0.125
                        S.op('act', lambda e, ac=ac, dst=dst, h=h, sc=sc: e.mul(out=dst[:, h, 0:256], in_=ac[:, 0:256], mul=sc), r=[ack], w=[dk])
                        S.op('act', lambda e, ac=ac, dst=dst, h=h, sc=sc: e.mul(out=dst[:, h, 256:T], in_=ac[:, 260:4356], mul=sc), r=[ack], w=[dk])
                        if g >= 4:
                            S.op('act', lambda e, ac=ac: e.mul(out=ac[:], in_=ac[:], mul=0.125), r=[ack], w=[ack])
                            for ti in range(NT):
                                c0 = ti * 128 if ti < 2 else ti * 128 + 4
                                pb = 1 + ti % 2
                                TR(ps[pb][:, 0:64], ac[:, c0:c0 + 128], [ack], [PK[pb]], n=64)
                                if ti % 2:
                                    S.op('dve', lambda e, ti=ti, pb=pb, h=h: e.tensor_copy(ktok[:, ti, h * 64:(h + 1) * 64], ps[pb][:, 0:64]), r=[PK[pb]], w=['ktok'])
                                else:
                                    S.op('act', lambda e, ti=ti, pb=pb, h=h: e.copy(out=ktok[:, ti, h * 64:(h + 1) * 64], in_=ps[pb][:, 0:64]), r=[PK[pb]], w=['ktok'])
                    S.barrier()
                with ExitStack() as es2:
                    SEG = 2176; CH = 34
                    Bc = sb(es2, [8, T]); U = sb(es2, [8, T])
                    G = [sb(es2, [8, SEG]) for _ in range(4)]
                    tot = sb(es2, [8, NSTEP]); umax = sb(es2, [8, NSTEP])
                    sm = {k: sb(es2, [8, NSTEP], name='sm_' + k) for k in ['be_s', 'ml_s', 'um_s', 'm', 'mprev', 'a', 's', 'mm_s', 'inter', 'mm_n', 't1', 't2']}
                    xT = sb(es2, [NSTEP, 8]); exp3 = sb(es2, [8, 8, NSTEP])
                    for sg_ in range(2):
                        o = sg_ * SEG; co = sg_ * CH
                        LI = G[0]; LF = G[1]; rst = G[2]; TMP = G[3]
                        S.dma('sp', lambda e: e.dma_start(out=rst[:], in_=c_rst[:, o:o + SEG]), w=['G2'])
                        for d in range(2):
                            S.dma('sp', lambda e, d=d: e.dma_start(out=LI[d * 4:(d + 1) * 4, :], in_=gates_d[d * 8:d * 8 + 4, o:o + SEG]), w=['G0'])
                            S.dma('sp', lambda e, d=d: e.dma_start(out=LF[d * 4:(d + 1) * 4, :], in_=gates_d[d * 8 + 4:d * 8 + 8, o:o + SEG]), w=['G1'])
                        S.op('act', lambda e: e.activation(out=LF[:], in_=LF[:], func=AF.Exp, bias=fbn[:, 0:1], scale=-1.0), r=['G1', 'fbn'], w=['G1'])
                        S.op('act', lambda e: e.activation(out=LF[:], in_=LF[:], func=AF.Ln, bias=ones_f[0:8, 0:1], scale=1.0), r=['G1', 'ones_f'], w=['G1'])
                        S.op('dve', lambda e: e.tensor_scalar(out=LF[:], in0=LF[:], scalar1=-1.0, scalar2=None, op0=ALU.mult), r=['G1'], w=['G1'])
                        Bs = Bc[:, o:o + SEG]; Us = U[:, o:o + SEG]
                        S.op('dve', lambda e: e.tensor_tensor_scan(out=Bs, data0=rst[:], data1=LF[:], initial=0.0, op0=ALU.mult, op1=ALU.add), r=['G2', 'G1'], w=['Bc'])
                        S.op('dve', lambda e: e.tensor_reduce(out=tot[:, co:co + CH], in_=LF[:].rearrange("p (c t) -> p c t", t=64), axis=AX.X, op=ALU.add), r=['G1'], w=['tot'])
                        S.op('dve', lambda e: e.tensor_tensor(out=TMP[:].rearrange("p (c t) -> p c t", t=64), in0=LF[:].rearrange("p (c t) -> p c t", t=64),
                                                              in1=tot[:, co:co + CH].unsqueeze(2).to_broadcast([8, CH, 64]), op=ALU.add), r=['G1', 'tot'], w=['G3'])
                        S.op('dve', lambda e: e.tensor_scalar(out=TMP[:], in0=TMP[:], scalar1=misc[:, 0:1], scalar2=None, op0=ALU.mult), r=['G3', 'misc'], w=['G3'])
                        S.op('dve', lambda e: e.scalar_tensor_tensor(out=Bs, in0=Bs, scalar=misc[:, 1:2], in1=TMP[:], op0=ALU.mult, op1=ALU.add), r=['Bc', 'misc', 'G3'], w=['Bc'])
                        S.op('dve', lambda e: e.tensor_tensor(out=Us, in0=LI[:], in1=Bs, op=ALU.subtract), r=['G0', 'Bc'], w=['U'])
                        S.op('dve', lambda e: e.tensor_reduce(out=umax[:, co:co + CH], in_=Us.rearrange("p (c t) -> p c t", t=64), axis=AX.X, op=ALU.max), r=['U'], w=['umax'])

                    def to_scan_order(dst, src, sk, dk):
                        TR(ps[1][0:NSTEP, 0:8], src[:, :], [sk], [PK[1]], n=8)
                        S.op('dve', lambda e: e.tensor_copy(xT[:], ps[1][0:NSTEP, 0:8]), r=[PK[1]], w=['xT'])
                        MM(ps[2][0:8, 0:NSTEP], xT[:, :], prev[:, :], True, True, ['xT', 'prev'], [PK[2]])
                        S.op('dve', lambda e: e.tensor_scalar(out=sm['t1'][:], in0=ps[2][0:8, 0:NSTEP], scalar1=misc[:, 0:1], scalar2=None, op0=ALU.mult), r=[PK[2], 'misc'], w=['t1'])
                        S.op('dve', lambda e: e.scalar_tensor_tensor(out=dst[:], in0=src[:], scalar=misc[:, 2:3], in1=sm['t1'][:], op0=ALU.mult, op1=ALU.add), r=[sk, 'misc', 't1'], w=[dk])

                    to_scan_order(sm['be_s'], tot, 'tot', 'be_s')
                    to_scan_order(sm['um_s'], umax, 'umax', 'um_s')
                    S.op('dve', lambda e: e.tensor_tensor(out=sm['ml_s'][:], in0=sm['be_s'][:], in1=sm['um_s'][:], op=ALU.add), r=['be_s', 'um_s'], w=['ml_s'])
                    S.op('dve', lambda e: e.tensor_tensor_scan(out=sm['m'][:], data0=sm['be_s'][:], data1=sm['ml_s'][:], initial=0.0, op0=ALU.add, op1=ALU.max), r=['be_s', 'ml_s'], w=['m'])
                    S.op('dve', lambda e: e.memset(sm['mprev'][:, 0:1], 0.0), w=['mprev'])
                    S.op('dve', lambda e: e.tensor_copy(sm['mprev'][:, 1:NSTEP], sm['m'][:, 0:NSTEP - 1]), r=['m'], w=['mprev'])
                    S.op('dve', lambda e: e.tensor_tensor(out=sm['t2'][:], in0=sm['be_s'][:], in1=sm['mprev'][:], op=ALU.add), r=['be_s', 'mprev'], w=['t2'])
                    S.op('dve', lambda e: e.tensor_tensor(out=sm['t2'][:], in0=sm['t2'][:], in1=sm['m'][:], op=ALU.subtract), r=['t2', 'm'], w=['t2'])
                    S.op('act', lambda e: e.activation(out=sm['a'][:], in_=sm['t2'][:], func=AF.Exp), r=['t2'], w=['a'])
                    S.op('dve', lambda e: e.tensor_tensor(out=sm['t2'][:], in0=sm['ml_s'][:], in1=sm['m'][:], op=ALU.subtract), r=['ml_s', 'm', 'a'], w=['t2'])
                    S.op('act', lambda e: e.activation(out=sm['s'][:], in_=sm['t2'][:], func=AF.Exp), r=['t2'], w=['s'])
                    S.op('dve', lambda e: e.tensor_tensor(out=sm['mm_s'][:], in0=sm['mprev'][:], in1=sm['um_s'][:], op=ALU.max), r=['mprev', 'um_s'], w=['mm_s'])
                    S.op('dve', lambda e: e.tensor_tensor(out=sm['t2'][:], in0=sm['mprev'][:], in1=sm['mm_s'][:], op=ALU.subtract), r=['mprev', 'mm_s', 's'], w=['t2'])
                    S.op('act', lambda e: e.activation(out=sm['inter'][:], in_=sm['t2'][:], func=AF.Exp), r=['t2'], w=['inter'])
                    to_scan_order(sm['mm_n'], sm['mm_s'], 'mm_s', 'mm_n')
                    for wi, nm in enumerate(['a', 's', 'inter']):
                        S.op('dve', lambda e, nm=nm: e.tensor_tensor(out=exp3[:], in0=eye8[:], in1=sm[nm][:].unsqueeze(1).to_broadcast([8, 8, NSTEP]), op=ALU.mult), r=['eye8', nm], w=['exp3'])
                        for hf in range(2):
                            MM(ps[3][0:64, 0:4 * NSTEP], ones_f[0:8, 0:64], exp3[:, hf * 4:(hf + 1) * 4, :].rearrange("p a b -> p (a b)"), True, True, ['ones_f', 'exp3'], [PK[3]])
                            S.op('dve', lambda e, wi=wi, hf=hf: e.tensor_copy(bc[:, wi, hf * 4:(hf + 1) * 4, :].rearrange("p a b -> p (a b)"), ps[3][0:64, 0:4 * NSTEP]), r=[PK[3]], w=['bc'])
                    for sg_ in range(2):
                        o = sg_ * SEG; co = sg_ * CH
                        TMP = G[3]; RW = G[0:3]
                        u3 = U[:, o:o + SEG].rearrange("p (c t) -> p c t", t=64); b3 = Bc[:, o:o + SEG].rearrange("p (c t) -> p c t", t=64)
                        t3 = TMP[:].rearrange("p (c t) -> p c t", t=64)
                        S.op('dve', lambda e: e.tensor_tensor(out=t3, in0=u3, in1=umax[:, co:co + CH].unsqueeze(2).to_broadcast([8, CH, 64]), op=ALU.subtract), r=['U', 'umax'], w=['G3'])
                        S.op('act', lambda e: e.activation(out=RW[0][:], in_=TMP[:], func=AF.Exp), r=['G3'], w=['G0'])
                        S.op('dve', lambda e: e.tensor_tensor(out=t3, in0=u3, in1=sm['mm_n'][:, co:co + CH].unsqueeze(2).to_broadcast([8, CH, 64]), op=ALU.subtract), r=['U', 'mm_n'], w=['G3'])
                        S.op('act', lambda e: e.activation(out=RW[1][:], in_=TMP[:], func=AF.Exp), r=['G3'], w=['G1'])
                        S.op('dve', lambda e: e.tensor_tensor(out=t3, in0=b3, in1=sm['mm_n'][:, co:co + CH].unsqueeze(2).to_broadcast([8, CH, 64]), op=ALU.add), r=['Bc', 'mm_n'], w=['G3'])
                        S.op('act', lambda e: e.activation(out=RW[2][:], in_=TMP[:], func=AF.Exp, scale=-1.0), r=['G3'], w=['G2'])
                        for tl in range(17):
                            ti = sg_ * 17 + tl
                            pb = 1 + ti % 2
                            for k3 in range(3):
                                TR(ps[pb][:, k3 * 8:(k3 + 1) * 8], RW[k3][:, tl * 128:(tl + 1) * 128], [f'G{k3}'], [PK[pb]], n=8)
                            S.op('dve', lambda e, ti=ti, pb=pb: e.tensor_copy(cols[:, ti, :], ps[pb][:, 0:24]), r=[PK[pb]], w=['cols'])
                    S.barrier()
                hsum = sb(es, [128, NT, 256])
                S.op('pool', lambda e: e.memset(hsum[:], 0.0), w=[('hsum', ti_) for ti_ in range(NT)])
                Cst = sb(es, [64, 8, 65]); C0b = [sb(es, [64, 4, 65], BF16) for _ in range(2)]
                tmpC_ = [sb(es, [64, 4, 65]) for _ in range(2)]
                wv = [sb(es, [128, 4, 65], BF16) for _ in range(2)]
                tS = [sb(es, [128, 4, 64]) for _ in range(2)]
                pTm = [sb(es, [128, 4, 64], BF16) for _ in range(2)]
                tI_ = [sb(es, [128, 260]) for _ in range(2)]; tH_ = [sb(es, [128, 260]) for _ in range(2)]
                dn_ = [sb(es, [128, 4]) for _ in range(2)]; hd = [sb(es, [128, 4, 64]) for _ in range(2)]
                S.op('dve', lambda e: e.memset(Cst[:], 0.0), w=['Cst0', 'Cst1'])
                def chain(d):
                    for p in range(NSTEP):
                        c = chunk_of(d, p); ti = c // 2; hb = c % 2; P0 = hb * 64; P1 = P0 + 64
                        t0 = c * 64
                        ip = d
                        tmpC = tmpC_[d]; tI = tI_[d]; tH = tH_[d]; dn = dn_[d]
                        pC = ps[d * 4]; pS = ps[d * 4 + 1]; pA = ps[d * 4 + 2]; pB = ps[d * 4 + 3]
                        kC = PK[d * 4]; kS = PK[d * 4 + 1]; kA = PK[d * 4 + 2]; kB = PK[d * 4 + 3]
                        Ck = f'Cst{d}'; tCk = f'tmpC{d}'; tIk = f'tI{d}'; tHk = f'tH{d}'; dnk = f'dn{d}'
                        S.op('dve', lambda e, ip=ip, ti=ti, d=d, P0=P0, P1=P1: e.tensor_tensor(out=wv[ip][P0:P1], in0=vaug[P0:P1, ti], in1=cols[P0:P1, ti, d * 4:(d + 1) * 4].unsqueeze(2).to_broadcast([64, 4, 65]), op=ALU.mult),
                             r=['vaug', 'cols'], w=[f'wv{ip}'])
                        yield
                        for h in range(4):
                            MM(pC[0:64, h * 65:(h + 1) * 65], ktok[P0:P1, ti, h * 64:(h + 1) * 64], wv[ip][P0:P1, h, :], True, True, ['ktok', f'wv{ip}'], [kC])
                        yield
                        S.op('dve', lambda e, ip=ip, d=d, p=p: e.tensor_tensor(out=C0b[ip][:], in0=Cst[:, d * 4:(d + 1) * 4, :], in1=bc[:, 2, d * 4:(d + 1) * 4, p].unsqueeze(2).to_broadcast([64, 4, 65]), op=ALU.mult),
                             r=[Ck, 'bc'], w=[f'C0b{ip}'])
                        yield
                        S.op('dve', lambda e, d=d, p=p: e.tensor_tensor(out=tmpC[:], in0=pC[0:64, 0:260].rearrange("p (h c) -> p h c", h=4), in1=bc[:, 1, d * 4:(d + 1) * 4, p].unsqueeze(2).to_broadcast([64, 4, 65]), op=ALU.mult),
                             r=[kC, 'bc'], w=[tCk])
                        yield
                        S.op('dve', lambda e, d=d, p=p: e.tensor_tensor(out=Cst[:, d * 4:(d + 1) * 4, :], in0=Cst[:, d * 4:(d + 1) * 4, :], in1=bc[:, 0, d * 4:(d + 1) * 4, p].unsqueeze(2).to_broadcast([64, 4, 65]), op=ALU.mult),
                             r=[Ck, 'bc'], w=[Ck])
                        yield
                        S.op('dve', lambda e, d=d: e.tensor_tensor(out=Cst[:, d * 4:(d + 1) * 4, :], in0=Cst[:, d * 4:(d + 1) * 4, :], in1=tmpC[:], op=ALU.add), r=[Ck, tCk], w=[Ck])
                        yield
                        for h in range(4):
                            MM(pS[P0:P1, h * 64:(h + 1) * 64], kT[:, h, t0:t0 + 64], qT[:, h, t0:t0 + 64], True, True, ['kT3', 'qT3'], [kS])
                        yield
                        S.op('dve', lambda e, ip=ip, ti=ti, d=d, P0=P0, P1=P1: e.tensor_tensor(out=tS[ip][P0:P1], in0=pS[P0:P1, 0:256].rearrange("p (h t) -> p h t", h=4),
                                                                                       in1=cols[P0:P1, ti, 8 + d * 4:8 + (d + 1) * 4].unsqueeze(2).to_broadcast([64, 4, 64]), op=ALU.mult),
                             r=[kS, 'cols'], w=[f'tS{ip}'])
                        yield
                        S.op('pool', lambda e, ip=ip, d=d, P0=P0, P1=P1: e.tensor_tensor(out=pTm[ip][P0:P1], in0=tS[ip][P0:P1], in1=msk[P0:P1, d, :].unsqueeze(1).to_broadcast([64, 4, 64]), op=ALU.mult),
                             r=[f'tS{ip}', 'msk'], w=[f'pTm{ip}'])
                        yield
                        for h in range(4):
                            MM(pA[P0:P1, h * 65:(h + 1) * 65], pTm[ip][P0:P1, h, :], vaug[P0:P1, ti, h, :], True, True, [f'pTm{ip}', 'vaug'], [kA])
                        yield
                        for h in range(4):
                            MM(pB[P0:P1, h * 65:(h + 1) * 65], qT[:, h, t0:t0 + 64], C0b[ip][:, h, :], True, True, ['qT3', f'C0b{ip}'], [kB])
                        yield
                        S.op('act', lambda e, P0=P0, P1=P1: e.copy(out=tI[P0:P1, :], in_=pB[P0:P1, 0:260]), r=[kB], w=[tIk])
                        yield
                        S.op('dve', lambda e, P0=P0, P1=P1: e.tensor_tensor(out=tH[P0:P1, :], in0=pA[P0:P1, 0:260], in1=tI[P0:P1, :], op=ALU.add), r=[kA, tIk], w=[tHk])
                        yield
                        ph = tH[P0:P1, :].rearrange("p (h c) -> p h c", h=4)
                        S.op('act', lambda e, ph=ph, P0=P0, P1=P1: e.activation(out=dn[P0:P1, :], in_=ph[:, :, 64], func=AF.Abs), r=[tHk], w=[dnk])
                        yield
                        S.op('dve', lambda e, ti=ti, d=d, P0=P0, P1=P1: e.tensor_tensor(out=dn[P0:P1, :], in0=dn[P0:P1, :], in1=cols[P0:P1, ti, 16 + d * 4:16 + (d + 1) * 4], op=ALU.max), r=[dnk, 'cols'], w=[dnk])
                        yield
                        S.op('dve', lambda e, P0=P0, P1=P1: e.reciprocal(dn[P0:P1, :], dn[P0:P1, :]), r=[dnk], w=[dnk])
                        yield
                        S.op('dve', lambda e, ip=ip, ph=ph, P0=P0, P1=P1: e.tensor_tensor(out=hd[ip][P0:P1], in0=ph[:, :, 0:64], in1=dn[P0:P1, :].unsqueeze(2).to_broadcast([64, 4, 64]), op=ALU.mult),
                             r=[tHk, dnk], w=[f'hd{ip}'])
                        yield
                        S.op('pool', lambda e, ip=ip, ti=ti, P0=P0, P1=P1: e.tensor_tensor(out=hsum[P0:P1, ti, :], in0=hsum[P0:P1, ti, :], in1=hd[ip][P0:P1].rearrange("p h d -> p (h d)"), op=ALU.add),
                             r=[('hsum', ti), f'hd{ip}'], w=[('hsum', ti)])
                        yield

                interleave([chain(0), chain(1)], 2)
                with ExitStack() as es2:
                    ngb = sb(es2, [128, 256]); so = [sb(es2, [128, 256]) for _ in range(2)]
                    mu = sb(es2, [128, 4]); var = sb(es2, [128, 4]); cen = sb(es2, [128, 4, 64]); sqq = sb(es2, [128, 4, 64])
                    yc = [sb(es2, [128, 256]) for _ in range(2)]; ycT = [sb(es2, [128, 2, 128], BF16) for _ in range(2)]
                    vec_bcast('sp', ngb, ml_norm_g[l], 'ngb')
                    for ti in range(NT):
                        p2 = ti % 2
                        S.dma('sp', lambda e, ti=ti, p2=p2: e.dma_start(out=so[p2][:], in_=sigo_d[ti * 128:(ti + 1) * 128, :]), w=[f'so{p2}'])
                        h3 = hsum[:, ti, :].rearrange("p (h d) -> p h d", h=4)
                        S.op('dve', lambda e, h3=h3: e.tensor_reduce(out=mu[:], in_=h3, axis=AX.X, op=ALU.add), r=[('hsum', ti)], w=['mu'])
                        S.op('dve', lambda e: e.tensor_scalar(out=mu[:], in0=mu[:], scalar1=1.0 / 64, scalar2=None, op0=ALU.mult), r=['mu'], w=['mu'])
                        S.op('dve', lambda e, h3=h3: e.tensor_tensor(out=cen[:], in0=h3, in1=mu[:].unsqueeze(2).to_broadcast([128, 4, 64]), op=ALU.subtract), r=[('hsum', ti), 'mu'], w=['cen'])
                        S.op('dve', lambda e: e.tensor_tensor(out=sqq[:], in0=cen[:], in1=cen[:], op=ALU.mult), r=['cen'], w=['sqq'])
                        S.op('dve', lambda e: e.tensor_reduce(out=var[:], in_=sqq[:], axis=AX.X, op=ALU.add), r=['sqq'], w=['var'])
                        S.op('act', lambda e: e.activation(out=var[:], in_=var[:], func=AF.Sqrt, bias=epsc[:, 0:1], scale=1.0 / 64), r=['var', 'epsc'], w=['var'])
                        S.op('dve', lambda e: e.reciprocal(var[:], var[:]), r=['var'], w=['var'])
                        S.op('dve', lambda e: e.tensor_tensor(out=cen[:], in0=cen[:], in1=var[:].unsqueeze(2).to_broadcast([128, 4, 64]), op=ALU.mult), r=['cen', 'var'], w=['cen'])
                        S.op('dve', lambda e: e.tensor_tensor(out=cen[:].rearrange("p h d -> p (h d)"), in0=cen[:].rearrange("p h d -> p (h d)"), in1=ngb[:], op=ALU.mult), r=['cen', 'ngb'], w=['cen'])
                        S.op('dve', lambda e, p2=p2: e.tensor_tensor(out=yc[p2][:], in0=cen[:].rearrange("p h d -> p (h d)"), in1=so[p2][:], op=ALU.mult), r=['cen', f'so{p2}'], w=[f'yc{p2}'])
                        for c in range(2):
                            TR(ps[1 + p2][:, c * 128:(c + 1) * 128], yc[p2][:, c * 128:(c + 1) * 128], [f'yc{p2}'], [PK[1 + p2]])
                        S.op('act', lambda e, p2=p2: e.copy(out=ycT[p2][:].rearrange("p c t -> p (c t)"), in_=ps[1 + p2][:, 0:256]), r=[PK[1 + p2]], w=[f'ycT{p2}'])
                        S.dma('sp', lambda e, p2=p2, ti=ti: e.dma_start(out=catT_d[768:1024, ti * 128:(ti + 1) * 128].rearrange("(c p) t -> p c t", p=128), in_=ycT[p2][:]), r=[f'ycT{p2}'], w=[('catC', ti)])
                S.barrier()

        def phase4(l, xsrc, tiles):
            with ExitStack() as es:
                wout = sb(es, [128, 8, D], BF16); wr = sb(es, [128, 8, 16])
                g1b = [sb(es, [128, D]) for _ in range(2)]; scb = [sb(es, [128, D]) for _ in range(2)]; shb = [sb(es, [128, D]) for _ in range(2)]
                lg = sb(es, [128, D]); lb = sb(es, [128, D])
                cat = [sb(es, [128, 8, 128], BF16) for _ in range(2)]
                xt = [sb(es, [128, D]) for _ in range(2)]
                rr_ = [sb(es, [128, D]) for _ in range(2)]; x1 = [sb(es, [128, D]) for _ in range(2)]; hm = [sb(es, [128, D]) for _ in range(2)]
                hmT_ = [sb(es, [128, 8, 128]) for _ in range(2)]
                st6_ = [sb(es, [128, 4, 6]) for _ in range(2)]; mv_ = [sb(es, [128, 2]) for _ in range(2)]; rstd_ = [sb(es, [128, 1]) for _ in range(2)]; nmr_ = [sb(es, [128, 1]) for _ in range(2)]
                lgt_ = [sb(es, [128, 16]) for _ in range(2)]; mx_ = [sb(es, [128, 1]) for _ in range(2)]; ssum_ = [sb(es, [128, 1]) for _ in range(2)]; affT = sb(es, [16, T])
                for c in range(8):
                    S.dma('pool', lambda e, c=c: e.dma_start(out=wout[:, c, :], in_=w_out[l, c * 128:(c + 1) * 128, :]), w=['wout'])
                S.dma('sp', lambda e: e.dma_start(out=wr[:], in_=w_router[l].rearrange("(c p) n -> p c n", p=128)), w=['wr'])
                for m in range(2):
                    bcast_load(es, 'sp', g1b[m], 2, m, f'g1b{m}'); bcast_load(es, 'sp', scb[m], 4, m, f'scb{m}'); bcast_load(es, 'sp', shb[m], 3, m, f'shb{m}')
                    S.op('dve', lambda e, m=m: e.tensor_scalar(out=scb[m][:], in0=scb[m][:], scalar1=1.0, scalar2=None, op0=ALU.add), r=[f'scb{m}'], w=[f'scb{m}'])
                vec_bcast('sp', lg, ln1_g[l], 'lg'); vec_bcast('sp', lb, ln1_b[l], 'lb')
                def tile4(ti):
                    p2 = ti % 2; m = 1 if ti < 2 else 0
                    rr = rr_[p2]; hmT = hmT_[p2]; st6 = st6_[p2]; mv = mv_[p2]; rstd = rstd_[p2]; nmr = nmr_[p2]; lgt = lgt_[p2]; mx = mx_[p2]; ssum = ssum_[p2]
                    rk = f'rr{p2}'; hk = f'hmT{p2}'; lk = f'lgt{p2}'; mk = f'mx{p2}'; sk = f'ssum{p2}'
                    bA = ps[3 * p2]; bB = ps[3 * p2 + 1]; bC = ps[3 * p2 + 2]; kA = PK[3 * p2]; kB = PK[3 * p2 + 1]; kC = PK[3 * p2 + 2]
                    pbs = [bA, bB]; pks = [kA, kB]
                    with nc.allow_non_contiguous_dma(reason="catT tile"):
                        S.dma('sp', lambda e: e.dma_start(out=cat[p2][:], in_=catT_d[:, ti * 128:(ti + 1) * 128].rearrange("(c p) t -> p c t", p=128)), w=[f'cat{p2}'])
                    S.dma('sp', lambda e: e.dma_start(out=xt[p2][:], in_=xsrc[ti * 128:(ti + 1) * 128, :]), w=[f'xt{p2}'])
                    yield
                    for half in range(2):
                        for c in range(8):
                            MM(pbs[half][:, :], cat[p2][:, c, :], wout[:, c, half * 512:(half + 1) * 512], c == 0, c == 7, [f'cat{p2}', 'wout'], [pks[half]])
                        yield
                        S.op('dve', lambda e: e.tensor_tensor(out=rr[:, half * 512:(half + 1) * 512], in0=pbs[half][:, :], in1=g1b[m][:, half * 512:(half + 1) * 512], op=ALU.mult),
                             r=[pks[half], f'g1b{m}'], w=[rk])
                        yield
                    S.op('dve', lambda e: e.scalar_tensor_tensor(out=rr[:], in0=xt[p2][:], scalar=ALPHA, in1=rr[:], op0=ALU.mult, op1=ALU.add), r=[f'xt{p2}', rk], w=[rk])
                    yield
                    ln_stats(f'l4{p2}', rr, D, mv, rstd, nmr, st6, [rk])
                    yield
                    S.op('act', lambda e: e.activation(out=rr[:], in_=rr[:], func=AF.Identity, bias=nmr[:, 0:1], scale=rstd[:, 0:1]), r=[rk, f'l4{p2}rs', f'l4{p2}nm'], w=[rk])
                    yield
                    S.op('dve', lambda e: e.tensor_tensor(out=rr[:], in0=rr[:], in1=lg[:], op=ALU.mult), r=[rk, 'lg'], w=[rk])
                    yield
                    S.op('dve', lambda e: e.tensor_tensor(out=x1[p2][:], in0=rr[:], in1=lb[:], op=ALU.add), r=[rk, 'lb'], w=[f'x1{p2}'])
                    S.dma('sp', lambda e: e.dma_start(out=x1_d[ti * 128:(ti + 1) * 128, :], in_=x1[p2][:]), r=[f'x1{p2}'], w=[('x1_d', ti)])
                    yield
                    ln_stats(f'l5{p2}', x1[p2], D, mv, rstd, nmr, st6, [f'x1{p2}'])
                    yield
                    S.op('act', lambda e: e.activation(out=rr[:], in_=x1[p2][:], func=AF.Identity, bias=nmr[:, 0:1], scale=rstd[:, 0:1]), r=[f'x1{p2}', f'l5{p2}rs', f'l5{p2}nm'], w=[rk])
                    yield
                    S.op('dve', lambda e: e.tensor_tensor(out=rr[:], in0=rr[:], in1=scb[m][:], op=ALU.mult), r=[rk, f'scb{m}'], w=[rk])
                    yield
                    S.op('dve', lambda e: e.tensor_tensor(out=hm[p2][:], in0=rr[:], in1=shb[m][:], op=ALU.add), r=[rk, f'shb{m}'], w=[f'hm{p2}'])
                    S.dma('sp', lambda e: e.dma_start(out=hm_d[ti * 128:(ti + 1) * 128, :], in_=hm[p2][:]), r=[f'hm{p2}'], w=[('hm_d', ti)])
                    yield
                    for half in range(2):
                        for c4 in range(4):
                            TR(pbs[half][:, c4 * 128:(c4 + 1) * 128], hm[p2][:, (half * 4 + c4) * 128:(half * 4 + c4 + 1) * 128], [f'hm{p2}'], [pks[half]])
                        yield
                        if half == 0:
                            S.op('act', lambda e: e.copy(out=hmT[:, 0:4, :].rearrange("p c t -> p (c t)"), in_=pbs[0][:, :]), r=[pks[0]], w=[hk])
                        else:
                            S.op('dve', lambda e: e.tensor_copy(hmT[:, 4:8, :].rearrange("p c t -> p (c t)"), pbs[1][:, :]), r=[pks[1]], w=[hk])
                        yield
                    for c in range(8):
                        MM(bC[:, 0:16], hmT[:, c, :], wr[:, c, :], c == 0, c == 7, [hk, 'wr'], [kC])
                    yield
                    S.op('dve', lambda e: e.tensor_reduce(out=mx[:], in_=bC[:, 0:16], axis=AX.X, op=ALU.max), r=[kC], w=[mk])
                    yield
                    S.op('dve', lambda e: e.tensor_scalar(out=mx[:], in0=mx[:], scalar1=-1.0, scalar2=None, op0=ALU.mult), r=[mk], w=[mk])
                    yield
                    S.op('act', lambda e: e.activation(out=lgt[:], in_=bC[:, 0:16], func=AF.Exp, bias=mx[:, 0:1], scale=1.0, accum_out=ssum[:]), r=[kC, mk], w=[lk, sk])
                    yield
                    S.op('dve', lambda e: e.reciprocal(ssum[:], ssum[:]), r=[sk], w=[sk])
                    yield
                    S.op('dve', lambda e: e.tensor_scalar(out=lgt[:], in0=lgt[:], scalar1=ssum[:, 0:1], scalar2=None, op0=ALU.mult), r=[lk, sk], w=[lk])
                    yield
                    TR(bC[0:16, 128:256], lgt[:, :], [lk], [kC])
                    yield
                    S.op('dve', lambda e: e.tensor_copy(affT[:, ti * 128:(ti + 1) * 128], bC[0:16, 128:256]), r=[kC], w=['affT'])
                    yield
                interleave([tile4(ti) for ti in tiles], 2)
                t_lo = tiles[0] * 128
                S.dma('sp', lambda e: e.dma_start(out=aff_d[:, t_lo:T], in_=affT[:, t_lo:T]), r=['affT'], w=['aff_d'])
                S.barrier()

        def phase56(l, with_ctx):
            with ExitStack() as es:
                idxT = sb(es, [128, 5, 16], U32); gateT = sb(es, [128, 5, 16])
                wg = [sb(es, [128, 8, D], BF16) for _ in range(2)]; wu = [sb(es, [128, 8, D], BF16) for _ in range(2)]; wd = [sb(es, [128, 8, D], BF16) for _ in range(2)]

                def load_w(e_):
                    ep = e_ % 2
                    for c in range(8):
                        S.dma('pool', lambda e, c=c: e.dma_start(out=wg[ep][:, c, :], in_=w_gate[l, e_, c * 128:(c + 1) * 128, :]), w=[f'wg{ep}'])
                        S.dma('pool', lambda e, c=c: e.dma_start(out=wu[ep][:, c, :], in_=w_up[l, e_, c * 128:(c + 1) * 128, :]), w=[f'wu{ep}'])
                        S.dma('pool', lambda e, c=c: e.dma_start(out=wd[ep][:, c, :], in_=w_down[l, e_, c * 128:(c + 1) * 128, :]), w=[f'wd{ep}'])
                load_w(0)
                es2 = ExitStack()
                aw = sb(es2, [16, TL]); vals = sb(es2, [16, 512]); idx = sb(es2, [16, 512], U32); idxf = sb(es2, [16, 512])
                awc = sb(es2, [16, 256]); valsc = sb(es2, [16, 32]); idxc = sb(es2, [16, 32], U32); idxcf = sb(es2, [16, 32])
                S.dma('sp', lambda e: e.dma_start(out=aw[:], in_=aff_d[:, 256:T]), w=['aw'])
                for r_ in range(64):
                    S.op('dve', lambda e, r_=r_: e.max(out=vals[:, r_ * 8:(r_ + 1) * 8], in_=aw[:]), r=['aw'], w=['vals'])
                    S.op('dve', lambda e, r_=r_: e.max_index(out=idx[:, r_ * 8:(r_ + 1) * 8], in_max=vals[:, r_ * 8:(r_ + 1) * 8], in_values=aw[:]), r=['aw', 'vals'], w=['idx'])
                    S.op('dve', lambda e, r_=r_: e.match_replace(out=aw[:], in_to_replace=vals[:, r_ * 8:(r_ + 1) * 8], in_values=aw[:], imm_value=-1.0), r=['aw', 'vals'], w=['aw'])
                S.op('dve', lambda e: e.tensor_copy(idxf[:], idx[:]), r=['idx'], w=['idxf'])
                S.op('dve', lambda e: e.tensor_scalar(out=idxf[:], in0=idxf[:], scalar1=256.0, scalar2=None, op0=ALU.add), r=['idxf'], w=['idxf'])
                for st in range(4):
                    TR(ps[0][:, 0:16], idxf[:, st * 128:(st + 1) * 128], ['idxf'], [PK[0]], n=16)
                    S.op('dve', lambda e, st=st: e.tensor_copy(idxT[:, st, :], ps[0][:, 0:16]), r=[PK[0]], w=['idxT'])
                    TR(ps[1][:, 0:16], vals[:, st * 128:(st + 1) * 128], ['vals'], [PK[1]], n=16)
                    S.op('dve', lambda e, st=st: e.tensor_copy(gateT[:, st, :], ps[1][:, 0:16]), r=[PK[1]], w=['gateT'])
                if with_ctx:
                    S.dma('sp', lambda e: e.dma_start(out=awc[:], in_=aff_d[:, 0:256]), w=['awc'])
                    for r_ in range(4):
                        S.op('dve', lambda e, r_=r_: e.max(out=valsc[:, r_ * 8:(r_ + 1) * 8], in_=awc[:]), r=['awc'], w=['valsc'])
                        S.op('dve', lambda e, r_=r_: e.max_index(out=idxc[:, r_ * 8:(r_ + 1) * 8], in_max=valsc[:, r_ * 8:(r_ + 1) * 8], in_values=awc[:]), r=['awc', 'valsc'], w=['idxc'])
                        S.op('dve', lambda e, r_=r_: e.match_replace(out=awc[:], in_to_replace=valsc[:, r_ * 8:(r_ + 1) * 8], in_values=awc[:], imm_value=-1.0), r=['awc', 'valsc'], w=['awc'])
                    S.op('dve', lambda e: e.tensor_copy(idxcf[:], idxc[:]), r=['idxc'], w=['idxcf'])
                    TR(ps[0][0:32, 0:16], idxcf[:, :], ['idxcf'], [PK[0]], n=16)
                    S.op('dve', lambda e: e.tensor_copy(idxT[0:32, 4, :], ps[0][0:32, 0:16]), r=[PK[0]], w=['idxT'])
                    TR(ps[1][0:32, 0:16], valsc[:, :], ['valsc'], [PK[1]], n=16)
                    S.op('dve', lambda e: e.tensor_copy(gateT[0:32, 4, :], ps[1][0:32, 0:16]), r=[PK[1]], w=['gateT'])
                S.barrier(); es2.close()
                NS = 544 if with_ctx else 512
                nst = 5 if with_ctx else 4
                xe = [sb(es, [128, 5, D]) for _ in range(2)]
                xeT = sb(es, [128, 8, 544], BF16); hidT = sb(es, [128, 8, 544], BF16)
                sg = [sb(es, [128, 544]) for _ in range(2)]
                ye = [sb(es, [128, D]) for _ in range(2)]

                def load_expert(e_):
                    ep = e_ % 2
                    if e_ > 0: load_w(e_)
                    for st in range(nst):
                        n = 128 if st < 4 else 32
                        S.dma('pool', lambda e, st=st, n=n: e.indirect_dma_start(out=xe[ep][0:n, st, :], out_offset=None, in_=hm_d[:, :],
                                                                              in_offset=bass.IndirectOffsetOnAxis(ap=idxT[0:n, st, e_:e_ + 1], axis=0)),
                              r=['idxT'], w=[f'xe{ep}'])

                load_expert(0)
                yc_ = 0
                for e_ in range(16):
                    ep = e_ % 2
                    if e_ + 1 < 16: load_expert(e_ + 1)
                    for st in range(nst):
                        n = 128 if st < 4 else 32
                        for half in range(2):
                            for c4 in range(4):
                                cc = half * 4 + c4
                                TR(ps[half][:, c4 * 128:c4 * 128 + n], xe[ep][0:n, st, cc * 128:(cc + 1) * 128], [f'xe{ep}'], [PK[half]], n=n)
                            src = ps[half][:, :].rearrange("p (c t) -> p c t", c=4)[:, :, 0:n]
                            if half == 0:
                                S.op('act', lambda e, st=st, n=n, src=src: e.copy(out=xeT[:, 0:4, st * 128:st * 128 + n], in_=src), r=[PK[0]], w=['xeT'])
                            else:
                                S.op('dve', lambda e, st=st, n=n, src=src: e.tensor_copy(xeT[:, 4:8, st * 128:st * 128 + n], src), r=[PK[1]], w=['xeT'])
                    for fc in range(8):
                        for (n0, n1) in ([(0, 512), (512, 544)] if with_ctx else [(0, 512)]):
                            pg = ps[2] if n0 == 0 else ps[4]; pu = ps[3] if n0 == 0 else ps[5]
                            pgk = PK[2] if n0 == 0 else PK[4]; puk = PK[3] if n0 == 0 else PK[5]
                            nn = n1 - n0
                            for c in range(8):
                                MM(pg[:, 0:nn], wg[ep][:, c, fc * 128:(fc + 1) * 128], xeT[:, c, n0:n1], c == 0, c == 7, [f'wg{ep}', 'xeT'], [pgk])
                            for c in range(8):
                                MM(pu[:, 0:nn], wu[ep][:, c, fc * 128:(fc + 1) * 128], xeT[:, c, n0:n1], c == 0, c == 7, [f'wu{ep}', 'xeT'], [puk])
                            sp2 = fc % 2
                            S.op('act', lambda e, pg=pg, nn=nn, sp2=sp2: e.activation(out=sg[sp2][:, 0:nn], in_=pg[:, 0:nn], func=AF.Silu), r=[pgk], w=[f'sg{sp2}'])
                            S.op('dve', lambda e, pu=pu, nn=nn, n0=n0, n1=n1, fc=fc, sp2=sp2: e.tensor_tensor(out=hidT[:, fc, n0:n1], in0=pu[:, 0:nn], in1=sg[sp2][:, 0:nn], op=ALU.mult), r=[puk, f'sg{sp2}'], w=['hidT'])
                    for st in range(nst):
                        n = 128 if st < 4 else 32
                        yp = yc_ % 2; yc_ += 1
                        for half in range(2):
                            for fc in range(8):
                                MM(ps[6 + half][0:n, :], hidT[:, fc, st * 128:st * 128 + n], wd[ep][:, fc, half * 512:(half + 1) * 512], fc == 0, fc == 7, ['hidT', f'wd{ep}'], [PK[6 + half]])
                            eng = 'act' if half == 0 else 'dve'
                            if half == 0:
                                S.op('act', lambda e, n=n, st=st, yp=yp: e.activation(out=ye[yp][0:n, 0:512], in_=ps[6][0:n, :], func=AF.Identity, scale=gateT[0:n, st, e_:e_ + 1]), r=[PK[6], 'gateT'], w=[f'ye{yp}'])
                            else:
                                S.op('dve', lambda e, n=n, st=st, yp=yp: e.tensor_scalar(out=ye[yp][0:n, 512:1024], in0=ps[7][0:n, :], scalar1=gateT[0:n, st, e_:e_ + 1], scalar2=None, op0=ALU.mult), r=[PK[7], 'gateT'], w=[f'ye{yp}'])
                        S.dma('pool', lambda e, st=st, n=n, yp=yp: e.indirect_dma_start(out=moe_d[:, :], out_offset=bass.IndirectOffsetOnAxis(ap=idxT[0:n, st, e_:e_ + 1], axis=0),
                                                                                  in_=ye[yp][0:n, :], in_offset=None, compute_op=ALU.add),
                              r=[f'ye{yp}', 'idxT'], w=['moe_d'])
                S.barrier()

        def phase7(l, dst, tiles, final):
            with ExitStack() as es:
                g2b = [sb(es, [128, D]) for _ in range(2)]
                lg = sb(es, [128, D]); lb = sb(es, [128, D])
                xt = [sb(es, [128, D]) for _ in range(2)]; ft = [sb(es, [128, D]) for _ in range(2)]
                rr_ = [sb(es, [128, D]) for _ in range(2)]; xo = [sb(es, [128, D]) for _ in range(2)]
                st6_ = [sb(es, [128, 4, 6]) for _ in range(2)]; mv_ = [sb(es, [128, 2]) for _ in range(2)]; rstd_ = [sb(es, [128, 1]) for _ in range(2)]; nmr_ = [sb(es, [128, 1]) for _ in range(2)]
                for m in range(2):
                    bcast_load(es, 'sp', g2b[m], 5, m, f'g2b{m}')
                vec_bcast('sp', lg, ln2_g[l], 'lg'); vec_bcast('sp', lb, ln2_b[l], 'lb')

                def tile7(ti):
                    p2 = ti % 2; m = 1 if ti < 2 else 0
                    rr = rr_[p2]; st6 = st6_[p2]; mv = mv_[p2]; rstd = rstd_[p2]; nmr = nmr_[p2]; rk = f'rr{p2}'
                    S.dma('sp', lambda e: e.dma_start(out=xt[p2][:], in_=x1_d[ti * 128:(ti + 1) * 128, :]), w=[f'xt{p2}'])
                    S.dma('sp', lambda e: e.dma_start(out=ft[p2][:], in_=moe_d[ti * 128:(ti + 1) * 128, :]), w=[f'ft{p2}'])
                    yield
                    S.op('dve', lambda e: e.tensor_tensor(out=rr[:], in0=ft[p2][:], in1=g2b[m][:], op=ALU.mult), r=[f'ft{p2}', f'g2b{m}'], w=[rk])
                    yield
                    S.op('dve', lambda e: e.scalar_tensor_tensor(out=rr[:], in0=xt[p2][:], scalar=ALPHA, in1=rr[:], op0=ALU.mult, op1=ALU.add), r=[f'xt{p2}', rk], w=[rk])
                    yield
                    ln_stats(f'l7{p2}', rr, D, mv, rstd, nmr, st6, [rk])
                    yield
                    S.op('act', lambda e: e.activation(out=rr[:], in_=rr[:], func=AF.Identity, bias=nmr[:, 0:1], scale=rstd[:, 0:1]), r=[rk, f'l7{p2}rs', f'l7{p2}nm'], w=[rk])
                    yield
                    S.op('dve', lambda e: e.tensor_tensor(out=rr[:], in0=rr[:], in1=lg[:], op=ALU.mult), r=[rk, 'lg'], w=[rk])
                    yield
                    S.op('dve', lambda e: e.tensor_tensor(out=xo[p2][:], in0=rr[:], in1=lb[:], op=ALU.add), r=[rk, 'lb'], w=[f'xo{p2}'])
                    if final:
                        S.dma('sp', lambda e: e.dma_start(out=dst[(ti - 2) * 128:(ti - 1) * 128, :], in_=xo[p2][:]), r=[f'xo{p2}'], w=[('out', ti)])
                    else:
                        S.dma('sp', lambda e: e.dma_start(out=dst[ti * 128:(ti + 1) * 128, :], in_=xo[p2][:]), r=[f'xo{p2}'], w=[('out', ti)])
                    yield
                interleave([tile7(ti) for ti in tiles], 2)
                S.barrier()

        for l in range(nlayers):
            last = (l == 1)
            xsrc = xin if l == 0 else xs1
            tiles = list(range(2, NT)) if last else list(range(NT))
            phase0(l)
            if stop == 'p0': break
            phase1(l, xsrc)
            if stop == 'p1': break
            phase2(l, with_ctx_q=not last)
            if stop == 'p2': break
            phase3(l)
            if stop == 'p3': break
            phase4(l, xsrc, tiles)
            if stop == 'p4': break
            phase56(l, with_ctx=not last)
            if stop == 'p6': break
            phase7(l, y if last else xs1, tiles, final=last)
        S.barrier()
    return nc, dbg_outs


def _consts():
    c = {}
    c['c_ident'] = np.eye(128, dtype=np.float32)
    t = np.arange(TL)
    row = (t // 64).astype(np.float32); col = (t % 64).astype(np.float32)
    inv = (10000.0 ** (-np.arange(8, dtype=np.float32) * 2.0 / 16)).astype(np.float32)
    ang = np.concatenate([row[:, None] * inv, col[:, None] * inv], axis=-1).astype(np.float32)
    cos = np.cos(ang).astype(np.float32); sin = np.sin(ang).astype(np.float32)
    cosT = np.ones((32, T), np.float32); sinT = np.zeros((32, T), np.float32)
    for a in range(2):
        for i in range(8):
            cosT[a * 16 + i, TC:] = cos[:, a * 8 + i]; cosT[a * 16 + 8 + i, TC:] = cos[:, a * 8 + i]
            sinT[a * 16 + i, TC:] = -sin[:, a * 8 + i]; sinT[a * 16 + 8 + i, TC:] = sin[:, a * 8 + i]
    c['c_rope'] = np.stack([cosT, sinT]).astype(np.float32)
    s = np.arange(64)[:, None]; tt = np.arange(64)[None, :]
    m0 = (s <= tt).astype(np.float32); m1 = (s >= tt).astype(np.float32)
    c['c_mask'] = np.stack([np.concatenate([m0, m0], 0), np.concatenate([m1, m1], 0)]).astype(np.float32)
    misc = np.zeros((8, 16), np.float32)
    misc[4:, 0] = 1.0; misc[:4, 1] = 1.0; misc[4:, 1] = -1.0; misc[:4, 2] = 1.0
    c['c_misc'] = misc
    e8 = np.zeros((8, 8, NSTEP), np.float32)
    for k in range(8): e8[k, k, :] = 1.0
    c['c_eye8'] = e8
    pr = np.zeros((NSTEP, NSTEP), np.float32)
    for p in range(NSTEP):
        cidx = 3 - p if p < 4 else 71 - p
        pr[cidx, p] = 1.0
    c['c_prev'] = pr
    rst = np.ones((8, T), np.float32); rst[:, ::64] = 0.0
    c['c_rst'] = rst
    return c


def _prep_weights(inp):
    w = {}
    sw = np.arange(32)
    for a in range(2):
        for i in range(8):
            sw[a * 16 + i] = a * 16 + 8 + i; sw[a * 16 + 8 + i] = a * 16 + i
    cols = np.concatenate([np.arange(0, 512), np.arange(1696, 2208),
                           np.arange(512, 1152), np.arange(1152, 1184), 1152 + sw,
                           np.arange(1184, 1696), np.arange(2208, 2224)])
    assert cols.size == NWIN
    w['w_in'] = np.ascontiguousarray(inp['w_in'][:, :, cols]); w['b_in'] = np.ascontiguousarray(inp['b_in'][:, cols])
    uq = inp['w_uq']
    swc = np.concatenate([h * 96 + 64 + sw for h in range(8)])
    w['w_uq'] = np.ascontiguousarray(np.concatenate([uq, uq[:, :, swc]], axis=-1))
    ukv = inp['w_ukv']
    kc = np.concatenate([np.arange(h * 128, h * 128 + 64) for h in range(8)])
    vc = np.concatenate([np.arange(h * 128 + 64, h * 128 + 128) for h in range(8)])
    w['w_ukv'] = np.ascontiguousarray(np.concatenate([ukv[:, :, kc], ukv[:, :, vc]], axis=-1))
    w['ml_f_bias'] = np.ascontiguousarray(inp['ml_f_bias'].reshape(2, 8))
    for k in ['w_ada', 'b_ada', 'sg_ln_g', 'sg_ln_b', 'sg_w', 'sg_b', 'q_norm_g', 'kv_norm_g', 'ml_conv_w', 'ml_conv_b', 'ml_norm_g',
              'w_out', 'ln1_g', 'ln1_b', 'w_router', 'w_gate', 'w_up', 'w_down', 'ln2_g', 'ln2_b']:
        w[k] = np.ascontiguousarray(inp[k])
    return w


_CACHE = {}


def kernel(**inputs):
    inp = {k: np.asarray(v, dtype=np.float32) for k, v in inputs.items()}
    if 'nc' not in _CACHE:
        _CACHE['nc'] = build()[0]
    nc = _CACHE['nc']
    consts = _consts(); w = _prep_weights(inp)
    in_maps = []
    for b in range(8):
        m = dict(consts); m.update(w)
        m['xin'] = np.ascontiguousarray(np.concatenate([inp['ctx'][b], inp['x'][b]], axis=0))
        m['cvec'] = np.ascontiguousarray(np.stack([inp['c'][b], inp['c_ctx']]))
        in_maps.append(m)
    res = run_bass_kernel_spmd(nc, in_maps, core_ids=list(range(8)))
    return np.stack([np.asarray(r['y'], dtype=np.float32) for r in res.results], axis=0)
```

```python
import numpy as np
from contextlib import ExitStack
import concourse.bass as bass
import concourse.mybir as mybir
from concourse.bass_utils import run_bass_kernel_spmd

F32 = mybir.dt.float32; BF16 = mybir.dt.bfloat16; U32 = mybir.dt.uint32
AF = mybir.ActivationFunctionType; ALU = mybir.AluOpType; AX = mybir.AxisListType

D = 1024; T = 4352; NT = 34; TC = 256; TL = 4096
NWIN = 2256
ALPHA = 4 ** 0.25
EPS = 1e-6
MLA_SCALE = 96 ** -0.5
NSTEP = 68


class Sched:
    NSLOT = 6

    def __init__(self, nc, es):
        self.nc = nc
        self.E = {'pe': nc.tensor, 'dve': nc.vector, 'act': nc.scalar, 'pool': nc.gpsimd, 'sp': nc.sync}
        self.sem = {}; self.cnt = {}
        for k in self.E:
            self.sem[k] = es.enter_context(nc.semaphore('s_' + k)); self.cnt[k] = 0
        self.slots = {}
        for q in ('sp', 'pool'):
            self.slots[q] = []
            for i in range(self.NSLOT):
                k = f'd{q}{i}'
                self.sem[k] = es.enter_context(nc.semaphore('s_' + k)); self.cnt[k] = 0
                self.slots[q].append(k)
        self.rr = {'sp': 0, 'pool': 0}
        self.waited = {k: {} for k in self.E}
        self.lastw = {}; self.readers = {}

    def _wait(self, eng, ev):
        k, v = ev
        if v <= 0: return
        if k == eng and eng == 'pe': return
        if self.waited[eng].get(k, 0) >= v: return
        self.E[eng].wait_ge(self.sem[k], v); self.waited[eng][k] = v

    def _deps(self, eng, r, w):
        deps = {}
        def add(k, v):
            if deps.get(k, 0) < v: deps[k] = v
        for b in r:
            ev = self.lastw.get(b)
            if ev: add(*ev)
        for b in w:
            ev = self.lastw.get(b)
            if ev: add(*ev)
            for k, v in self.readers.get(b, {}).items(): add(k, v)
        for k, v in deps.items(): self._wait(eng, (k, v))

    def _commit(self, ev, r, w):
        for b in r:
            d = self.readers.setdefault(b, {})
            if d.get(ev[0], 0) < ev[1]: d[ev[0]] = ev[1]
        for b in w:
            self.lastw[b] = ev; self.readers[b] = {}

    def op(self, eng, fn, r=(), w=()):
        self._deps(eng, r, w)
        inst = fn(self.E[eng])
        self.cnt[eng] += 1
        inst.then_inc(self.sem[eng], 1)
        self._commit((eng, self.cnt[eng]), r, w)

    def dma(self, q, fn, r=(), w=()):
        self._deps(q, r, w)
        i = self.rr[q]; self.rr[q] = (i + 1) % self.NSLOT
        k = self.slots[q][i]
        self._wait(q, (k, self.cnt[k]))
        inst = fn(self.E[q])
        self.cnt[k] += 16
        inst.then_inc(self.sem[k], 16)
        self._commit((k, self.cnt[k]), r, w)

    def barrier(self):
        for e in self.E:
            for k in self.sem:
                if k == e: continue
                self._wait(e, (k, self.cnt[k]))
        self.lastw = {}; self.readers = {}


def build(debug=False, nlayers=2, stop=None):
    nc = bass.Bass("TRN2", target_bir_lowering=False)
    dbg_outs = []

    def din(name, shape, dt=F32):
        return nc.dram_tensor(name, list(shape), dt, kind="ExternalInput").ap()

    def dscr(name, shape, dt=F32):
        if debug:
            dbg_outs.append(name)
            return nc.dram_tensor(name, list(shape), dt, kind="ExternalOutput").ap()
        return nc.dram_tensor(name, list(shape), dt, kind="Internal").ap()

    L = 2
    xin = din("xin", [T, D]); cvec = din("cvec", [2, D])
    w_ada = din("w_ada", [L, D, 6 * D]); b_ada = din("b_ada", [L, 6 * D])
    w_in = din("w_in", [L, D, NWIN]); b_in = din("b_in", [L, NWIN])
    sg_ln_g = din("sg_ln_g", [L, 256]); sg_ln_b = din("sg_ln_b", [L, 256])
    sg_w = din("sg_w", [L, 4, 128, 128]); sg_b = din("sg_b", [L, 4, 128])
    q_norm_g = din("q_norm_g", [L, 384]); kv_norm_g = din("kv_norm_g", [L, 256])
    w_uq = din("w_uq", [L, 384, 1024]); w_ukv = din("w_ukv", [L, 256, 1024])
    ml_conv_w = din("ml_conv_w", [L, 5, 512]); ml_conv_b = din("ml_conv_b", [L, 512])
    ml_f_bias = din("ml_f_bias", [L, 8]); ml_norm_g = din("ml_norm_g", [L, 256])
    w_out = din("w_out", [L, D, D]); ln1_g = din("ln1_g", [L, D]); ln1_b = din("ln1_b", [L, D])
    w_router = din("w_router", [L, D, 16])
    w_gate = din("w_gate", [L, 16, D, D]); w_up = din("w_up", [L, 16, D, D]); w_down = din("w_down", [L, 16, D, D])
    ln2_g = din("ln2_g", [L, D]); ln2_b = din("ln2_b", [L, D])
    c_ident = din("c_ident", [128, 128])
    c_rope = din("c_rope", [2, 32, T])
    c_mask = din("c_mask", [2, 128, 64])
    c_misc = din("c_misc", [8, 16])
    c_eye8 = din("c_eye8", [8, 8, NSTEP])
    c_prev = din("c_prev", [NSTEP, NSTEP])
    c_rst = din("c_rst", [8, T])

    y = nc.dram_tensor("y", [TL, D], F32, kind="ExternalOutput").ap()

    xs1 = dscr("xs1", [T, D]); x1_d = dscr("x1_d", [T, D]); hm_d = dscr("hm_d", [T, D]); moe_d = dscr("moe_d", [T, D])
    catT_d = dscr("catT_d", [D, T], BF16)
    qT_d = dscr("qT_d", [8, 96, T], BF16); kT_d = dscr("kT_d", [8, 96, T], BF16)
    vaug_d = dscr("vaug_d", [T, 520], BF16)
    mlqk_d = dscr("mlqk_d", [8, 64, T]); gates_d = dscr("gates_d", [16, T])
    mlv_d = dscr("mlv_d", [T, 256], BF16); sigo_d = dscr("sigo_d", [T, 256])
    ada_d = dscr("ada_d", [96, 128])
    aff_d = dscr("aff_d", [16, T])

    with ExitStack() as top:
        S = Sched(nc, top)
        sbn = [0]

        def sb(es, shape, dt=F32, name=None):
            sbn[0] += 1
            return es.enter_context(nc.sbuf_tensor(f"{name or 't'}_{sbn[0]}", list(shape), dt))

        ps = [top.enter_context(nc.psum_tensor(f"ps{i}", [128, 512], F32)) for i in range(8)]
        PK = [f"ps{i}" for i in range(8)]
        ident = sb(top, [128, 128], F32, "ident")
        ones_bf = sb(top, [128, 512], BF16, "ones_bf")
        ones_f = sb(top, [128, 512], F32, "ones_f")
        zeros_f = sb(top, [128, 1024], F32, "zeros_f")
        epsc = sb(top, [128, 1], F32, "epsc")
        ada = sb(top, [128, 96], F32, "ada")
        adp = sb(top, [128, 96], F32, "adp")

        S.dma('sp', lambda e: e.dma_start(out=ident[:], in_=c_ident), w=['ident'])
        S.op('dve', lambda e: e.memset(ones_bf[:], 1.0), w=['ones_bf'])
        S.op('dve', lambda e: e.memset(ones_f[:], 1.0), w=['ones_f'])
        S.op('dve', lambda e: e.memset(zeros_f[:], 0.0), w=['zeros_f'])
        S.op('dve', lambda e: e.memset(epsc[:], EPS), w=['epsc'])

        def interleave(gens, width):
            active = []; it = iter(gens)
            while True:
                while len(active) < width:
                    g = next(it, None)
                    if g is None: break
                    active.append(g)
                if not active: break
                for g in list(active):
                    try: next(g)
                    except StopIteration: active.remove(g)

        def MM(out, lhsT, rhs, st, sp_, r, w):
            S.op('pe', lambda e: e.matmul(out, lhsT, rhs, start=st, stop=sp_), r, w)

        def TR(out, in_, r, w, n=128):
            S.op('pe', lambda e: e.transpose(out, in_, ident[:n, :n]), list(r) + ['ident'], w)

        def ln_stats(es_tag, xap, width, mv, rstd, nmr, stats, rk):
            nch = width // 256 if width >= 256 else 1
            cw = width // nch
            for c in range(nch):
                S.op('dve', lambda e, c=c: e.bn_stats(stats[:, c, :], xap[:, c * cw:(c + 1) * cw]), r=rk, w=[es_tag + 'st'])
            S.op('dve', lambda e: e.bn_aggr(mv[:], stats[:, 0:nch, :]), r=[es_tag + 'st'], w=[es_tag + 'mv'])
            S.op('act', lambda e: e.activation(out=rstd[:], in_=mv[:, 1:2], func=AF.Sqrt, bias=epsc[:, 0:1], scale=1.0),
                 r=[es_tag + 'mv', 'epsc'], w=[es_tag + 'rs'])
            S.op('dve', lambda e: e.reciprocal(rstd[:], rstd[:]), r=[es_tag + 'rs'], w=[es_tag + 'rs'])
            S.op('dve', lambda e: e.scalar_tensor_tensor(out=nmr[:], in0=mv[:, 0:1], scalar=-1.0, in1=rstd[:], op0=ALU.mult, op1=ALU.mult),
                 r=[es_tag + 'mv', es_tag + 'rs'], w=[es_tag + 'nm'])

        def phase0(l):
            with ExitStack() as es:
                cfm = sb(es, [128, 2, 8]); sfm = sb(es, [128, 2, 8])
                brow = sb(es, [1, 6 * D])
                wa = [sb(es, [128, 8, 768]) for _ in range(2)]
                adaT = sb(es, [96, 128])
                with nc.allow_non_contiguous_dma(reason="tiny"):
                    for m_ in range(2):
                        S.dma('sp', lambda e, m_=m_: e.dma_start(out=cfm[:, m_, :], in_=cvec[m_].rearrange("(c p) -> p c", p=128)), w=['cfm'])
                S.dma('sp', lambda e: e.dma_start(out=brow[:], in_=b_ada[l:l + 1, :]), w=['brow'])
                S.op('act', lambda e: e.activation(out=sfm[:], in_=cfm[:], func=AF.Silu), r=['cfm'], w=['sfm'])
                for blk in range(8):
                    wt = wa[blk % 2]; wk = f'wa{blk % 2}'
                    S.dma('sp', lambda e: e.dma_start(out=wt[:], in_=w_ada[l, :, blk * 768:(blk + 1) * 768].rearrange("(c p) n -> p c n", p=128)), w=[wk])
                    for nn in range(6):
                        n = blk * 6 + nn
                        for c in range(8):
                            MM(ps[0][:, n * 2:(n + 1) * 2], wt[:, c, nn * 128:(nn + 1) * 128], sfm[:, :, c], c == 0, False, [wk, 'sfm'], [PK[0]])
                        MM(ps[0][:, n * 2:(n + 1) * 2], brow[0:1, n * 128:(n + 1) * 128], ones_f[0:1, 0:2], False, True, ['brow', 'ones_f'], [PK[0]])
                S.op('dve', lambda e: e.tensor_copy(ada[:], ps[0][:, 0:96]), r=[PK[0]], w=['ada'])
                S.op('dve', lambda e: e.tensor_scalar(out=adp[:], in0=ada[:], scalar1=1.0, scalar2=None, op0=ALU.add), r=['ada'], w=['adp'])
                TR(ps[1][:96, 0:128], ada[:, :], ['ada'], [PK[1]])
                S.op('dve', lambda e: e.tensor_copy(adaT[:], ps[1][:96, 0:128]), r=[PK[1]], w=['adaT'])
                S.dma('sp', lambda e: e.dma_start(out=ada_d, in_=adaT[:]), r=['adaT'], w=['ada_d'])
                for ti in range(NT):
                    S.dma('sp', lambda e, ti=ti: e.dma_start(out=moe_d[ti * 128:(ti + 1) * 128, :], in_=zeros_f[:]), r=['zeros_f'], w=[('moe', ti)])
                S.barrier()

        def ada_col(j, cc, m):
            n = (j * 8 + cc) * 2 + m
            return n

        def bcast_load(es, q, dst, j, m, key):
            src = ada_d.rearrange("(n m) p -> m n p", m=2)[m, j * 8:(j + 1) * 8, :].partition_broadcast(128)
            S.dma(q, lambda e: e.dma_start(out=dst[:].rearrange("p (c f) -> p c f", c=8), in_=src), r=['ada_d'], w=[key])

        def vec_bcast(q, dst, vec_ap, key):
            S.dma(q, lambda e: e.dma_start(out=dst[:], in_=vec_ap.partition_broadcast(128)), w=[key])

        def blocks():
            out = [(0, 256, [0, 1])]
            for b in range(8):
                out.append((256 + 512 * b, 512, [2 + 4 * b + i for i in range(4)]))
            return out

        def phase1(l, xsrc):
            with ExitStack() as es:
                win = sb(es, [128, 8, NWIN], BF16); brow = sb(es, [1, NWIN], BF16)
                wuq = sb(es, [128, 3, 1024], BF16); wukv = sb(es, [128, 2, 1024], BF16)
                wsT = sb(es, [128, 4, 128], BF16)
                w32 = sb(es, [128, 3, 1024]); wsf = sb(es, [128, 512]); gq = sb(es, [128, 3]); gkv = sb(es, [128, 2])
                sgbT = sb(es, [128, 4]); lng = sb(es, [128, 256]); lnb = sb(es, [128, 256])
                rope = [sb(es, [96, 2, 512]) for _ in range(2)]
                xt = [sb(es, [128, D]) for _ in range(2)]
                xn = [sb(es, [128, D]) for _ in range(2)]
                xmT = [sb(es, [128, 8, 512], BF16) for _ in range(2)]
                st6 = sb(es, [128, 4, 6]); mv = sb(es, [128, 2]); rstd = sb(es, [128, 1]); nmr = sb(es, [128, 1])
                st6b = sb(es, [128, 4, 6]); mvb = sb(es, [128, 2]); rstdb = sb(es, [128, 1]); nmrb = sb(es, [128, 1])
                gl = [sb(es, [128, 512]) for _ in range(2)]
                vn = sb(es, [128, 256]); vnb = sb(es, [128, 256], BF16)
                ya = [sb(es, [128, 256]) for _ in range(2)]
                yaT = [sb(es, [128, 2, 128], BF16) for _ in range(2)]
                mlv = [sb(es, [128, 256], BF16) for _ in range(2)]
                sgo = [sb(es, [128, 256]) for _ in range(2)]
                cqT = sb(es, [128, 3, 512], BF16); ckvT = sb(es, [128, 2, 512], BF16)
                sq = [sb(es, [128, 512]) for _ in range(2)]
                rq = sb(es, [96, 512]); rkv = sb(es, [64, 512]); rkvc = sb(es, [128, 4])
                qs = [sb(es, [96, 512]) for _ in range(2)]; qb = [sb(es, [96, 512]) for _ in range(2)]
                r1 = sb(es, [96, 512]); r2 = sb(es, [96, 512])
                qTt = [sb(es, [96, 512], BF16) for _ in range(2)]
                kTt = sb(es, [96, 8, 512], BF16)
                krt = sb(es, [96, 512], BF16)
                va = [sb(es, [128, 8, 65], BF16) for _ in range(2)]
                fm32 = [sb(es, [64, 512]) for _ in range(2)]

                for c in range(8):
                    S.dma('pool', lambda e, c=c: e.dma_start(out=win[:, c, :], in_=w_in[l, c * 128:(c + 1) * 128, :]), w=['win'])
                S.dma('pool', lambda e: e.dma_start(out=brow[:], in_=b_in[l:l + 1, :]), w=['brow'])
                with nc.allow_non_contiguous_dma(reason="tiny"):
                    S.dma('sp', lambda e: e.dma_start(out=gq[:], in_=q_norm_g[l].rearrange("(c p) -> p c", p=128)), w=['gq'])
                    S.dma('sp', lambda e: e.dma_start(out=gkv[:], in_=kv_norm_g[l].rearrange("(c p) -> p c", p=128)), w=['gkv'])
                    S.dma('sp', lambda e: e.dma_start(out=sgbT[:], in_=sg_b[l].rearrange("g t -> t g")), w=['sgbT'])
                S.dma('sp', lambda e: e.dma_start(out=w32[:], in_=w_uq[l].rearrange("(c p) n -> p c n", p=128)), w=['w32'])
                for c in range(3):
                    S.op('dve', lambda e, c=c: e.tensor_scalar(out=wuq[:, c, :], in0=w32[:, c, :], scalar1=gq[:, c:c + 1], scalar2=None, op0=ALU.mult),
                         r=['w32', 'gq'], w=['wuq'])
                S.dma('sp', lambda e: e.dma_start(out=w32[:, 0:2, :], in_=w_ukv[l].rearrange("(c p) n -> p c n", p=128)), r=[], w=['w32'])
                for c in range(2):
                    S.op('dve', lambda e, c=c: e.tensor_scalar(out=wukv[:, c, :], in0=w32[:, c, :], scalar1=gkv[:, c:c + 1], scalar2=None, op0=ALU.mult),
                         r=['w32', 'gkv'], w=['wukv'])
                S.dma('sp', lambda e: e.dma_start(out=wsf[:].rearrange("p (g s) -> p g s", g=4), in_=sg_w[l].rearrange("g t s -> t g s")), w=['wsf'])
                for g in range(4):
                    TR(ps[0][:, g * 128:(g + 1) * 128], wsf[:, g * 128:(g + 1) * 128], ['wsf'], [PK[0]])
                S.op('dve', lambda e: e.tensor_copy(wsT[:].rearrange("p g t -> p (g t)"), ps[0][:, 0:512]), r=[PK[0]], w=['wsT'])
                vec_bcast('sp', lng, sg_ln_g[l], 'lng'); vec_bcast('sp', lnb, sg_ln_b[l], 'lnb')

                import itertools
                ring = itertools.cycle([5, 0, 1, 2, 3, 4])
                bi = 0
                for (t0, N, tiles) in blocks():
                    bp = bi % 2; bi += 1
                    xm = xmT[bp]; xmk = f'xmT{bp}'
                    m_ada = 1 if t0 == 0 else 0
                    S.dma('sp', lambda e, bp=bp: e.dma_start(out=rope[bp][64:96, :, 0:N], in_=c_rope[:, :, t0:t0 + N].rearrange("a p t -> p a t")), w=[f'rope{bp}'])
                    for li, ti in enumerate(tiles):
                        p2 = ti % 2
                        xk = f'xt{p2}'; xnk = f'xn{p2}'
                        S.dma('sp', lambda e, ti=ti, p2=p2: e.dma_start(out=xt[p2][:], in_=xsrc[ti * 128:(ti + 1) * 128, :]), w=[xk])
                        ln_stats('l1', xt[p2], D, mv, rstd, nmr, st6, [xk])
                        S.op('act', lambda e, p2=p2: e.activation(out=xn[p2][:], in_=xt[p2][:], func=AF.Identity, bias=nmr[:, 0:1], scale=rstd[:, 0:1]),
                             r=[xk, 'l1rs', 'l1nm'], w=[xnk])
                        for half in range(2):
                            pb = ps[half]
                            for c4 in range(4):
                                cc = half * 4 + c4
                                TR(pb[:, c4 * 128:(c4 + 1) * 128], xn[p2][:, cc * 128:(cc + 1) * 128], [xnk], [PK[half]])
                            for c4 in range(4):
                                cc = half * 4 + c4
                                eng = 'act' if c4 % 2 == 0 else 'dve'
                                if eng == 'act':
                                    S.op('act', lambda e, cc=cc, c4=c4, pb=pb, li=li: e.activation(out=xm[:, cc, li * 128:(li + 1) * 128], in_=pb[:, c4 * 128:(c4 + 1) * 128], func=AF.Identity,
                                                                                         bias=ada[:, ada_col(0, cc, m_ada):ada_col(0, cc, m_ada) + 1], scale=adp[:, ada_col(1, cc, m_ada):ada_col(1, cc, m_ada) + 1]),
                                         r=[PK[half], 'ada', 'adp'], w=[xmk])
                                else:
                                    S.op('dve', lambda e, cc=cc, c4=c4, pb=pb, li=li: e.tensor_scalar(out=xm[:, cc, li * 128:(li + 1) * 128], in0=pb[:, c4 * 128:(c4 + 1) * 128],
                                                                                            scalar1=adp[:, ada_col(1, cc, m_ada):ada_col(1, cc, m_ada) + 1], scalar2=ada[:, ada_col(0, cc, m_ada):ada_col(0, cc, m_ada) + 1],
                                                                                            op0=ALU.mult, op1=ALU.add),
                                         r=[PK[half], 'ada', 'adp'], w=[xmk])
                        for c in range(8):
                            MM(ps[2][:, :], xm[:, c, li * 128:(li + 1) * 128], win[:, c, 0:512], c == 0, False, [xmk, 'win'], [PK[2]])
                        MM(ps[2][:, :], ones_bf[0:1, 0:128], brow[0:1, 0:512], False, True, ['ones_bf', 'brow'], [PK[2]])
                        g_ = gl[p2]; gk = f'gl{p2}'
                        S.op('act', lambda e, g_=g_: e.activation(out=g_[:], in_=ps[2][:, :], func=AF.Gelu_apprx_tanh), r=[PK[2]], w=[gk])
                        ln_stats('l2', g_[:, 256:512], 256, mvb, rstdb, nmrb, st6b, [gk])
                        S.op('act', lambda e, g_=g_: e.activation(out=vn[:], in_=g_[:, 256:512], func=AF.Identity, bias=nmrb[:, 0:1], scale=rstdb[:, 0:1]),
                             r=[gk, 'l2rs', 'l2nm'], w=['vn'])
                        S.op('dve', lambda e: e.tensor_tensor(out=vn[:], in0=vn[:], in1=lng[:], op=ALU.mult), r=['vn', 'lng'], w=['vn'])
                        S.op('dve', lambda e: e.tensor_tensor(out=vnb[:], in0=vn[:], in1=lnb[:], op=ALU.add), r=['vn', 'lnb'], w=['vnb'])
                        for g in range(4):
                            MM(ps[3][:, g * 64:(g + 1) * 64], wsT[:, g, :], vnb[:, g * 64:(g + 1) * 64], True, True, ['wsT', 'vnb'], [PK[3]])
                        yat = ya[p2]; yak = f'ya{p2}'
                        for g in range(4):
                            S.op('dve', lambda e, g=g, yat=yat, g_=g_: e.scalar_tensor_tensor(out=yat[:, g * 64:(g + 1) * 64], in0=ps[3][:, g * 64:(g + 1) * 64], scalar=sgbT[:, g:g + 1],
                                                                                       in1=g_[:, g * 64:(g + 1) * 64], op0=ALU.add, op1=ALU.mult),
                                 r=[PK[3], 'sgbT', gk], w=[yak])
                        for c in range(2):
                            TR(ps[3][:, 256 + c * 128:256 + (c + 1) * 128], yat[:, c * 128:(c + 1) * 128], [yak], [PK[3]])
                        yT = yaT[p2]; yTk = f'yaT{p2}'
                        S.op('act', lambda e, yT=yT: e.copy(out=yT[:].rearrange("p c t -> p (c t)"), in_=ps[3][:, 256:512]), r=[PK[3]], w=[yTk])
                        S.dma('sp', lambda e, yT=yT, ti=ti: e.dma_start(out=catT_d[0:256, ti * 128:(ti + 1) * 128].rearrange("(c p) t -> p c t", p=128), in_=yT[:]), r=[yTk], w=[('catA', ti)])
                        for c in range(8):
                            MM(ps[4][:, :], xm[:, c, li * 128:(li + 1) * 128], win[:, c, 512:1024], c == 0, False, [xmk, 'win'], [PK[4]])
                        MM(ps[4][:, :], ones_bf[0:1, 0:128], brow[0:1, 512:1024], False, True, ['ones_bf', 'brow'], [PK[4]])
                        S.op('dve', lambda e, p2=p2: e.tensor_copy(mlv[p2][:], ps[4][:, 0:256]), r=[PK[4]], w=[f'mlv{p2}'])
                        S.op('act', lambda e, p2=p2: e.activation(out=sgo[p2][:], in_=ps[4][:, 256:512], func=AF.Sigmoid), r=[PK[4]], w=[f'sgo{p2}'])
                        S.dma('sp', lambda e, p2=p2, ti=ti: e.dma_start(out=mlv_d[ti * 128:(ti + 1) * 128, :], in_=mlv[p2][:]), r=[f'mlv{p2}'], w=[('mlv_d', ti)])
                        S.dma('sp', lambda e, p2=p2, ti=ti: e.dma_start(out=sigo_d[ti * 128:(ti + 1) * 128, :], in_=sgo[p2][:]), r=[f'sgo{p2}'], w=[('sigo_d', ti)])

                    FM0 = 1024
                    def fm_mm(pst, pk, col0, M, pbase=0):
                        for c in range(8):
                            MM(pst[pbase:pbase + M, 0:N], win[:, c, col0:col0 + M], xm[:, c, 0:N], c == 0, False, ['win', xmk], [pk])
                        MM(pst[pbase:pbase + M, 0:N], brow[0:1, col0:col0 + M], ones_bf[0:1, 0:N], False, True, ['brow', 'ones_bf'], [pk])
                    for c in range(3):
                        b5 = next(ring)
                        fm_mm(ps[b5], PK[b5], FM0 + c * 128, 128)
                        S.op('act', lambda e, c=c: e.copy(out=cqT[:, c, 0:N], in_=ps[b5][:, 0:N]), r=[PK[b5]], w=['cqT'])
                        S.op('act', lambda e, c=c: e.activation(out=sq[c % 2][:, 0:N], in_=ps[b5][:, 0:N], func=AF.Square), r=[PK[b5]], w=[f'sq{c % 2}'])
                        MM(ps[6][0:96, 0:N], ones_f[:, 0:96], sq[c % 2][:, 0:N], c == 0, c == 2, ['ones_f', f'sq{c % 2}'], [PK[6]])
                    S.op('act', lambda e: e.activation(out=rq[:, 0:N], in_=ps[6][0:96, 0:N], func=AF.Sqrt, bias=epsc[0:96, 0:1], scale=1.0 / 384), r=[PK[6], 'epsc'], w=['rq'])
                    S.op('dve', lambda e: e.reciprocal(rq[:, 0:N], rq[:, 0:N]), r=['rq'], w=['rq'])
                    rp = rope[bp]; rpk = f'rope{bp}'
                    for h in range(8):
                        hp = h % 2
                        b5 = next(ring)
                        for c in range(3):
                            MM(ps[b5][0:96, 0:N], wuq[:, c, h * 96:(h + 1) * 96], cqT[:, c, 0:N], c == 0, c == 2, ['wuq', 'cqT'], [PK[b5]])
                        for c in range(3):
                            MM(ps[7][64:96, 0:N], wuq[:, c, 768 + h * 32:768 + (h + 1) * 32], cqT[:, c, 0:N], c == 0, c == 2, ['wuq', 'cqT'], [PK[7]])
                        S.op('dve', lambda e, hp=hp: e.tensor_tensor(out=qs[hp][:, 0:N], in0=ps[b5][0:96, 0:N], in1=rq[:, 0:N], op=ALU.mult), r=[PK[b5], 'rq'], w=[f'qs{hp}'])
                        S.op('dve', lambda e, hp=hp: e.tensor_tensor(out=qb[hp][64:96, 0:N], in0=ps[7][64:96, 0:N], in1=rq[64:96, 0:N], op=ALU.mult), r=[PK[7], 'rq'], w=[f'qb{hp}'])
                        S.op('act', lambda e, hp=hp: e.copy(out=qTt[hp][0:64, 0:N], in_=qs[hp][0:64, 0:N]), r=[f'qs{hp}'], w=[f'qTt{hp}'])
                        S.op('pool', lambda e, hp=hp: e.tensor_tensor(out=r1[64:96, 0:N], in0=qs[hp][64:96, 0:N], in1=rp[64:96, 0, 0:N], op=ALU.mult), r=[f'qs{hp}', rpk], w=['r1'])
                        S.op('pool', lambda e, hp=hp: e.tensor_tensor(out=r2[64:96, 0:N], in0=qb[hp][64:96, 0:N], in1=rp[64:96, 1, 0:N], op=ALU.mult), r=[f'qb{hp}', rpk], w=['r2'])
                        S.op('pool', lambda e, hp=hp: e.tensor_tensor(out=qTt[hp][64:96, 0:N], in0=r1[64:96, 0:N], in1=r2[64:96, 0:N], op=ALU.add), r=['r1', 'r2'], w=[f'qTt{hp}'])
                        S.dma('sp', lambda e, hp=hp, h=h: e.dma_start(out=qT_d[h, :, t0:t0 + N], in_=qTt[hp][:, 0:N]), r=[f'qTt{hp}'], w=[('qT_d', h, t0)])
                    for c in range(2):
                        b5 = next(ring)
                        fm_mm(ps[b5], PK[b5], FM0 + 384 + c * 128, 128)
                        S.op('act', lambda e, c=c: e.copy(out=ckvT[:, c, 0:N], in_=ps[b5][:, 0:N]), r=[PK[b5]], w=['ckvT'])
                        S.op('act', lambda e, c=c: e.activation(out=sq[c][:, 0:N], in_=ps[b5][:, 0:N], func=AF.Square), r=[PK[b5]], w=[f'sq{c}'])
                    for c in range(2):
                        MM(ps[6][0:64, 0:N], ones_f[:, 0:64], sq[c][:, 0:N], c == 0, c == 1, ['ones_f', f'sq{c}'], [PK[6]])
                    S.op('act', lambda e: e.activation(out=rkv[:, 0:N], in_=ps[6][0:64, 0:N], func=AF.Sqrt, bias=epsc[0:64, 0:1], scale=1.0 / 256), r=[PK[6], 'epsc'], w=['rkv'])
                    S.op('dve', lambda e: e.reciprocal(rkv[:, 0:N], rkv[:, 0:N]), r=['rkv'], w=['rkv'])
                    for li in range(len(tiles)):
                        for c in range(2):
                            MM(ps[7][:, li:li + 1], sq[c][:, li * 128:(li + 1) * 128], ones_f[:, 0:1], c == 0, c == 1, [f'sq{c}', 'ones_f'], [PK[7]])
                    nl = len(tiles)
                    S.op('act', lambda e: e.activation(out=rkvc[:, 0:nl], in_=ps[7][:, 0:nl], func=AF.Sqrt, bias=epsc[:, 0:1], scale=1.0 / 256), r=[PK[7], 'epsc'], w=['rkvc'])
                    S.op('dve', lambda e: e.reciprocal(rkvc[:, 0:nl], rkvc[:, 0:nl]), r=['rkvc'], w=['rkvc'])
                    b5 = next(ring)
                    fm_mm(ps[b5], PK[b5], FM0 + 640, 32, pbase=64)
                    fm_mm(ps[7], PK[7], FM0 + 672, 32, pbase=64)
                    S.op('dve', lambda e: e.tensor_tensor(out=r1[64:96, 0:N], in0=ps[b5][64:96, 0:N], in1=rp[64:96, 0, 0:N], op=ALU.mult), r=[PK[b5], rpk], w=['r1'])
                    S.op('dve', lambda e: e.tensor_tensor(out=r2[64:96, 0:N], in0=ps[7][64:96, 0:N], in1=rp[64:96, 1, 0:N], op=ALU.mult), r=[PK[7], rpk], w=['r2'])
                    S.op('dve', lambda e: e.tensor_tensor(out=krt[64:96, 0:N], in0=r1[64:96, 0:N], in1=r2[64:96, 0:N], op=ALU.add), r=['r1', 'r2'], w=['krt'])
                    for h in range(8):
                        b5 = next(ring)
                        for c in range(2):
                            MM(ps[b5][0:64, 0:N], wukv[:, c, h * 64:(h + 1) * 64], ckvT[:, c, 0:N], c == 0, c == 1, ['wukv', 'ckvT'], [PK[b5]])
                        S.op('dve', lambda e, h=h: e.tensor_tensor(out=kTt[0:64, h, 0:N], in0=ps[b5][0:64, 0:N], in1=rkv[:, 0:N], op=ALU.mult), r=[PK[b5], 'rkv'], w=['kTt'])
                        S.op('pool', lambda e, h=h: e.tensor_copy(kTt[64:96, h, 0:N], krt[64:96, 0:N]), r=['krt'], w=['kTt'])
                    S.dma('sp', lambda e: e.dma_start(out=kT_d[:, :, t0:t0 + N].rearrange("h p t -> p h t"), in_=kTt[:, :, 0:N]), r=['kTt'], w=[('kT_d', t0)])
                    for li, ti in enumerate(tiles):
                        vp = li % 2
                        for c in range(2):
                            MM(ps[6][:, :], ckvT[:, c, li * 128:(li + 1) * 128], wukv[:, c, 512:1024], c == 0, c == 1, ['ckvT', 'wukv'], [PK[6]])
                        S.op('dve', lambda e, vp=vp: e.memset(va[vp][:, :, 64:65], 1.0), w=[f'va{vp}'])
                        S.op('dve', lambda e, vp=vp, li=li: e.tensor_scalar(out=va[vp][:, :, 0:64], in0=ps[6][:, :].rearrange("p (h d) -> p h d", h=8), scalar1=rkvc[:, li:li + 1], scalar2=None, op0=ALU.mult),
                             r=[PK[6], 'rkvc'], w=[f'va{vp}'])
                        S.dma('sp', lambda e, vp=vp, ti=ti: e.dma_start(out=vaug_d[ti * 128:(ti + 1) * 128, :], in_=va[vp][:].rearrange("p h d -> p (h d)")), r=[f'va{vp}'], w=[('vaug_d', ti)])
                    for g in range(8):
                        fp_ = g % 2
                        b5 = next(ring)
                        fm_mm(ps[b5], PK[b5], FM0 + 704 + g * 64, 64)
                        S.op('act', lambda e, fp_=fp_: e.copy(out=fm32[fp_][:, 0:N], in_=ps[b5][0:64, 0:N]), r=[PK[b5]], w=[f'fm32{fp_}'])
                        S.dma('sp', lambda e, fp_=fp_, g=g: e.dma_start(out=mlqk_d[g, :, t0:t0 + N], in_=fm32[fp_][:, 0:N]), r=[f'fm32{fp_}'], w=[('mlqk_d', g, t0)])
                    b5 = next(ring)
                    fm_mm(ps[b5], PK[b5], FM0 + 1216, 16)
                    S.op('act', lambda e: e.copy(out=fm32[0][0:16, 0:N], in_=ps[b5][0:16, 0:N]), r=[PK[b5]], w=['fm320'])
                    S.dma('sp', lambda e: e.dma_start(out=gates_d[:, t0:t0 + N], in_=fm32[0][0:16, 0:N]), r=['fm320'], w=[('gates_d', t0)])
                S.barrier()

        def phase2(l, with_ctx_q):
            with ExitStack() as es:
                vaug = sb(es, [128, NT, 520], BF16)
                kT = [sb(es, [96, T], BF16) for _ in range(2)]
                qT = [sb(es, [96, T], BF16) for _ in range(2)]
                pT = [sb(es, [128, 512], BF16) for _ in range(3)]
                oT = sb(es, [65, 512]); sel = sb(es, [65, 64]); rec = sb(es, [64, 512])
                on = [sb(es, [64, 512], BF16) for _ in range(2)]
                S.dma('sp', lambda e: e.dma_start(out=vaug[:], in_=vaug_d.rearrange("(t p) c -> p t c", p=128)), w=['vaug'])
                S.op('dve', lambda e: e.memset(sel[:], 0.0), w=['sel'])
                S.op('dve', lambda e: e.memset(sel[64:65, :], 1.0), w=['sel'])
                def load_head(h):
                    hp = h % 2
                    S.dma('sp', lambda e: e.dma_start(out=kT[hp][:], in_=kT_d[h]), w=[f'kT{hp}'])
                    S.dma('sp', lambda e: e.dma_start(out=qT[hp][:], in_=qT_d[h]), w=[f'qT{hp}'])
                items = []
                for h in range(8):
                    for bi_, (t0, N, tiles) in enumerate(blocks()):
                        if t0 == 0 and not with_ctx_q: continue
                        ktiles = [0, 1] if t0 == 0 else list(range(NT))
                        for i, kt in enumerate(ktiles):
                            items.append((h, t0, N, i, kt, len(ktiles)))
                LAG = 2
                blk_ctr = [0]

                def do_pv(n):
                    h, t0, N, i, kt, nk = items[n]
                    hp = h % 2; pk = n % 3
                    ob = blk_ctr[0] % 2
                    MM(ps[ob][0:65, 0:N], vaug[:, kt, h * 65:(h + 1) * 65], pT[pk][:, 0:N], i == 0, i == nk - 1, ['vaug', f'pT{pk}'], [PK[ob]])
                    if i == nk - 1:
                        blk_ctr[0] += 1
                        S.op('dve', lambda e: e.tensor_copy(oT[:, 0:N], ps[ob][0:65, 0:N]), r=[PK[ob]], w=['oT'])
                        MM(ps[5][0:64, 0:N], sel[:, :], oT[:, 0:N], True, True, ['sel', 'oT'], [PK[5]])
                        S.op('dve', lambda e: e.reciprocal(rec[:, 0:N], ps[5][0:64, 0:N]), r=[PK[5]], w=['rec'])
                        op_ = blk_ctr[0] % 2
                        S.op('dve', lambda e: e.tensor_tensor(out=on[op_][:, 0:N], in0=oT[0:64, 0:N], in1=rec[:, 0:N], op=ALU.mult), r=['oT', 'rec'], w=[f'on{op_}'])
                        S.dma('sp', lambda e: e.dma_start(out=catT_d[256 + h * 64:256 + (h + 1) * 64, t0:t0 + N], in_=on[op_][:, 0:N]), r=[f'on{op_}'], w=[('catB', h, t0)])

                load_head(0)
                cur_h = -1
                for n, (h, t0, N, i, kt, nk) in enumerate(items):
                    if h != cur_h:
                        cur_h = h
                        if h + 1 < 8: load_head(h + 1)
                    hp = h % 2; sbk = 2 + n % 3; pk = n % 3
                    MM(ps[sbk][:, 0:N], kT[hp][:, kt * 128:(kt + 1) * 128], qT[hp][:, t0:t0 + N], True, True, [f'kT{hp}', f'qT{hp}'], [PK[sbk]])
                    S.op('act', lambda e, sbk=sbk, pk=pk, N=N: e.activation(out=pT[pk][:, 0:N], in_=ps[sbk][:, 0:N], func=AF.Exp, scale=MLA_SCALE), r=[PK[sbk]], w=[f'pT{pk}'])
                    if n >= LAG: do_pv(n - LAG)
                for n in range(max(0, len(items) - LAG), len(items)):
                    do_pv(n)
                S.barrier()

        def chunk_of(d, p):
            if d == 0: return p
            return 3 - p if p < 4 else 71 - p

        def phase3(l):
            with ExitStack() as es:
                qT = sb(es, [64, 4, T], BF16); kT = sb(es, [64, 4, T], BF16)
                ktok = sb(es, [128, NT, 256], BF16)
                vaug = sb(es, [128, NT, 4, 65], BF16)
                cw = sb(es, [64, 5, 8]); cb = sb(es, [64, 8])
                cols = sb(es, [128, NT, 24]); bc = sb(es, [64, 3, 8, NSTEP])
                msk = sb(es, [128, 2, 64]); misc = sb(es, [8, 16]); eye8 = sb(es, [8, 8, NSTEP]); prev = sb(es, [NSTEP, NSTEP])
                fbn = sb(es, [8, 1])
                S.dma('sp', lambda e: e.dma_start(out=msk[:], in_=c_mask.rearrange("d p t -> p d t")), w=['msk'])
                S.dma('sp', lambda e: e.dma_start(out=misc[:], in_=c_misc), w=['misc'])
                S.dma('sp', lambda e: e.dma_start(out=eye8[:], in_=c_eye8), w=['eye8'])
                S.dma('sp', lambda e: e.dma_start(out=prev[:], in_=c_prev), w=['prev'])
                with nc.allow_non_contiguous_dma(reason="tiny"):
                    for j_ in range(5):
                        S.dma('sp', lambda e, j_=j_: e.dma_start(out=cw[:, j_, :], in_=ml_conv_w[l, j_].rearrange("(g p) -> p g", p=64)), w=['cw'])
                    S.dma('sp', lambda e: e.dma_start(out=cb[:], in_=ml_conv_b[l].rearrange("(g p) -> p g", p=64)), w=['cb'])
                    S.dma('sp', lambda e: e.dma_start(out=fbn[:], in_=ml_f_bias[l].rearrange("(a b) -> a b", b=1)), w=['fbn'])
                S.op('dve', lambda e: e.tensor_scalar(out=fbn[:], in0=fbn[:], scalar1=-1.0, scalar2=None, op0=ALU.mult), r=['fbn'], w=['fbn'])
                for ti in range(NT):
                    S.dma('sp', lambda e, ti=ti: e.dma_start(out=vaug[:, ti, :, 0:64], in_=mlv_d[ti * 128:(ti + 1) * 128, :].rearrange("p (h d) -> p h d", h=4)), w=['vaug'])
                S.op('dve', lambda e: e.memset(vaug[:, :, :, 64:65], 1.0), w=['vaug'])
                with ExitStack() as es2:
                    pre = [sb(es2, [64, 4360])] * 2
                    acc = [sb(es2, [64, 4356])] * 2
                    for g in range(8):
                        gp = 0; pr = pre[gp]; ac = acc[gp]; prk = f'pre{gp}'; ack = f'acc{gp}'
                        S.op('pool', lambda e, pr=pr: e.memset(pr[:, 0:2], 0.0), w=[prk])
                        S.op('pool', lambda e, pr=pr: e.memset(pr[:, 258:262], 0.0), w=[prk])
                        S.op('pool', lambda e, pr=pr: e.memset(pr[:, 4358:4360], 0.0), w=[prk])
                        S.dma('sp', lambda e, pr=pr, g=g: e.dma_start(out=pr[:, 2:258], in_=mlqk_d[g, :, 0:256]), w=[prk])
                        S.dma('sp', lambda e, pr=pr, g=g: e.dma_start(out=pr[:, 262:4358], in_=mlqk_d[g, :, 256:T]), w=[prk])
                        S.op('dve', lambda e, pr=pr, ac=ac, g=g: e.tensor_scalar(out=ac[:], in0=pr[:, 0:4356], scalar1=cw[:, 0, g:g + 1], scalar2=cb[:, g:g + 1], op0=ALU.mult, op1=ALU.add),
                             r=[prk, 'cw', 'cb'], w=[ack])
                        for j in range(1, 5):
                            S.op('dve', lambda e, pr=pr, ac=ac, g=g, j=j: e.scalar_tensor_tensor(out=ac[:], in0=pr[:, j:j + 4356], scalar=cw[:, j, g:g + 1], in1=ac[:], op0=ALU.mult, op1=ALU.add),
                                 r=[prk, 'cw', ack], w=[ack])
                        S.op('act', lambda e, ac=ac: e.activation(out=ac[:], in_=ac[:], func=AF.Silu), r=[ack], w=[ack])
                        dst = qT if g < 4 else kT; dk = 'qT3' if g < 4 else 'kT3'; h = g % 4
                        sc = 1.0 if g < 4 else 0.125
                        S.op('act', lambda e, ac=ac, dst=dst, h=h, sc=sc: e.mul(out=dst[:, h, 0:256], in_=ac[:, 0:256], mul=sc), r=[ack], w=[dk])
                        S.op('act', lambda e, ac=ac, dst=dst, h=h, sc=sc: e.mul(out=dst[:, h, 256:T], in_=ac[:, 260:4356], mul=sc), r=[ack], w=[dk])
                        if g >= 4:
                            S.op('act', lambda e, ac=ac: e.mul(out=ac[:], in_=ac[:], mul=0.125), r=[ack], w=[ack])
                            for ti in range(NT):
                                c0 = ti * 128 if ti < 2 else ti * 128 + 4
                                pb = 1 + ti % 2
                                TR(ps[pb][:, 0:64], ac[:, c0:c0 + 128], [ack], [PK[pb]], n=64)
                                if ti % 2:
                                    S.op('dve', lambda e, ti=ti, pb=pb, h=h: e.tensor_copy(ktok[:, ti, h * 64:(h + 1) * 64], ps[pb][:, 0:64]), r=[PK[pb]], w=['ktok'])
                                else:
                                    S.op('act', lambda e, ti=ti, pb=pb, h=h: e.copy(out=ktok[:, ti, h * 64:(h + 1) * 64], in_=ps[pb][:, 0:64]), r=[PK[pb]], w=['ktok'])
                    S.barrier()
                with ExitStack() as es2:
                    SEG = 2176; CH = 34
                    Bc = sb(es2, [8, T]); U = sb(es2, [8, T])
                    G = [sb(es2, [8, SEG]) for _ in range(4)]
                    tot = sb(es2, [8, NSTEP]); umax = sb(es2, [8, NSTEP])
                    sm = {k: sb(es2, [8, NSTEP], name='sm_' + k) for k in ['be_s', 'ml_s', 'um_s', 'm', 'mprev', 'a', 's', 'mm_s', 'inter', 'mm_n', 't1', 't2']}
                    xT = sb(es2, [NSTEP, 8]); exp3 = sb(es2, [8, 8, NSTEP])
                    for sg_ in range(2):
                        o = sg_ * SEG; co = sg_ * CH
                        LI = G[0]; LF = G[1]; rst = G[2]; TMP = G[3]
                        S.dma('sp', lambda e: e.dma_start(out=rst[:], in_=c_rst[:, o:o + SEG]), w=['G2'])
                        for d in range(2):
                            S.dma('sp', lambda e, d=d: e.dma_start(out=LI[d * 4:(d + 1) * 4, :], in_=gates_d[d * 8:d * 8 + 4, o:o + SEG]), w=['G0'])
                            S.dma('sp', lambda e, d=d: e.dma_start(out=LF[d * 4:(d + 1) * 4, :], in_=gates_d[d * 8 + 4:d * 8 + 8, o:o + SEG]), w=['G1'])
                        S.op('act', lambda e: e.activation(out=LF[:], in_=LF[:], func=AF.Exp, bias=fbn[:, 0:1], scale=-1.0), r=['G1', 'fbn'], w=['G1'])
                        S.op('act', lambda e: e.activation(out=LF[:], in_=LF[:], func=AF.Ln, bias=ones_f[0:8, 0:1], scale=1.0), r=['G1', 'ones_f'], w=['G1'])
                        S.op('dve', lambda e: e.tensor_scalar(out=LF[:], in0=LF[:], scalar1=-1.0, scalar2=None, op0=ALU.mult), r=['G1'], w=['G1'])
                        Bs = Bc[:, o:o + SEG]; Us = U[:, o:o + SEG]
                        S.op('dve', lambda e: e.tensor_tensor_scan(out=Bs, data0=rst[:], data1=LF[:], initial=0.0, op0=ALU.mult, op1=ALU.add), r=['G2', 'G1'], w=['Bc'])
                        S.op('dve', lambda e: e.tensor_reduce(out=tot[:, co:co + CH], in_=LF[:].rearrange("p (c t) -> p c t", t=64), axis=AX.X, op=ALU.add), r=['G1'], w=['tot'])
                        S.op('dve', lambda e: e.tensor_tensor(out=TMP[:].rearrange("p (c t) -> p c t", t=64), in0=LF[:].rearrange("p (c t) -> p c t", t=64),
                                                              in1=tot[:, co:co + CH].unsqueeze(2).to_broadcast([8, CH, 64]), op=ALU.add), r=['G1', 'tot'], w=['G3'])
                        S.op('dve', lambda e: e.tensor_scalar(out=TMP[:], in0=TMP[:], scalar1=misc[:, 0:1], scalar2=None, op0=ALU.mult), r=['G3', 'misc'], w=['G3'])
                        S.op('dve', lambda e: e.scalar_tensor_tensor(out=Bs, in0=Bs, scalar=misc[:, 1:2], in1=TMP[:], op0=ALU.mult, op1=ALU.add), r=['Bc', 'misc', 'G3'], w=['Bc'])
                        S.op('dve', lambda e: e.tensor_tensor(out=Us, in0=LI[:], in1=Bs, op=ALU.subtract), r=['G0', 'Bc'], w=['U'])
                        S.op('dve', lambda e: e.tensor_reduce(out=umax[:, co:co + CH], in_=Us.rearrange("p (c t) -> p c t", t=64), axis=AX.X, op=ALU.max), r=['U'], w=['umax'])

                    def to_scan_order(dst, src, sk, dk):
                        TR(ps[1][0:NSTEP, 0:8], src[:, :], [sk], [PK[1]], n=8)
                        S.op('dve', lambda e: e.tensor_copy(xT[:], ps[1][0:NSTEP, 0:8]), r=[PK[1]], w=['xT'])
                        MM(ps[2][0:8, 0:NSTEP], xT[:, :], prev[:, :], True, True, ['xT', 'prev'], [PK[2]])
                        S.op('dve', lambda e: e.tensor_scalar(out=sm['t1'][:], in0=ps[2][0:8, 0:NSTEP], scalar1=misc[:, 0:1], scalar2=None, op0=ALU.mult), r=[PK[2], 'misc'], w=['t1'])
                        S.op('dve', lambda e: e.scalar_tensor_tensor(out=dst[:], in0=src[:], scalar=misc[:, 2:3], in1=sm['t1'][:], op0=ALU.mult, op1=ALU.add), r=[sk, 'misc', 't1'], w=[dk])

                    to_scan_order(sm['be_s'], tot, 'tot', 'be_s')
                    to_scan_order(sm['um_s'], umax, 'umax', 'um_s')
                    S.op('dve', lambda e: e.tensor_tensor(out=sm['ml_s'][:], in0=sm['be_s'][:], in1=sm['um_s'][:], op=ALU.add), r=['be_s', 'um_s'], w=['ml_s'])
                    S.op('dve', lambda e: e.tensor_tensor_scan(out=sm['m'][:], data0=sm['be_s'][:], data1=sm['ml_s'][:], initial=0.0, op0=ALU.add, op1=ALU.max), r=['be_s', 'ml_s'], w=['m'])
                    S.op('dve', lambda e: e.memset(sm['mprev'][:, 0:1], 0.0), w=['mprev'])
                    S.op('dve', lambda e: e.tensor_copy(sm['mprev'][:, 1:NSTEP], sm['m'][:, 0:NSTEP - 1]), r=['m'], w=['mprev'])
                    S.op('dve', lambda e: e.tensor_tensor(out=sm['t2'][:], in0=sm['be_s'][:], in1=sm['mprev'][:], op=ALU.add), r=['be_s', 'mprev'], w=['t2'])
                    S.op('dve', lambda e: e.tensor_tensor(out=sm['t2'][:], in0=sm['t2'][:], in1=sm['m'][:], op=ALU.subtract), r=['t2', 'm'], w=['t2'])
                    S.op('act', lambda e: e.activation(out=sm['a'][:], in_=sm['t2'][:], func=AF.Exp), r=['t2'], w=['a'])
                    S.op('dve', lambda e: e.tensor_tensor(out=sm['t2'][:], in0=sm['ml_s'][:], in1=sm['m'][:], op=ALU.subtract), r=['ml_s', 'm', 'a'], w=['t2'])
                    S.op('act', lambda e: e.activation(out=sm['s'][:], in_=sm['t2'][:], func=AF.Exp), r=['t2'], w=['s'])
                    S.op('dve', lambda e: e.tensor_tensor(out=sm['mm_s'][:], in0=sm['mprev'][:], in1=sm['um_s'][:], op=ALU.max), r=['mprev', 'um_s'], w=['mm_s'])
                    S.op('dve', lambda e: e.tensor_tensor(out=sm['t2'][:], in0=sm['mprev'][:], in1=sm['mm_s'][:], op=ALU.subtract), r=['mprev', 'mm_s', 's'], w=['t2'])
                    S.op('act', lambda e: e.activation(out=sm['inter'][:], in_=sm['t2'][:], func=AF.Exp), r=['t2'], w=['inter'])
                    to_scan_order(sm['mm_n'], sm['mm_s'], 'mm_s', 'mm_n')
                    for wi, nm in enumerate(['a', 's', 'inter']):
                        S.op('dve', lambda e, nm=nm: e.tensor_tensor(out=exp3[:], in0=eye8[:], in1=sm[nm][:].unsqueeze(1).to_broadcast([8, 8, NSTEP]), op=ALU.mult), r=['eye8', nm], w=['exp3'])
                        for hf in range(2):
                            MM(ps[3][0:64, 0:4 * NSTEP], ones_f[0:8, 0:64], exp3[:, hf * 4:(hf + 1) * 4, :].rearrange("p a b -> p (a b)"), True, True, ['ones_f', 'exp3'], [PK[3]])
                            S.op('dve', lambda e, wi=wi, hf=hf: e.tensor_copy(bc[:, wi, hf * 4:(hf + 1) * 4, :].rearrange("p a b -> p (a b)"), ps[3][0:64, 0:4 * NSTEP]), r=[PK[3]], w=['bc'])
                    for sg_ in range(2):
                        o = sg_ * SEG; co = sg_ * CH
                        TMP = G[3]; RW = G[0:3]
                        u3 = U[:, o:o + SEG].rearrange("p (c t) -> p c t", t=64); b3 = Bc[:, o:o + SEG].rearrange("p (c t) -> p c t", t=64)
                        t3 = TMP[:].rearrange("p (c t) -> p c t", t=64)
                        S.op('dve', lambda e: e.tensor_tensor(out=t3, in0=u3, in1=umax[:, co:co + CH].unsqueeze(2).to_broadcast([8, CH, 64]), op=ALU.subtract), r=['U', 'umax'], w=['G3'])
                        S.op('act', lambda e: e.activation(out=RW[0][:], in_=TMP[:], func=AF.Exp), r=['G3'], w=['G0'])
                        S.op('dve', lambda e: e.tensor_tensor(out=t3, in0=u3, in1=sm['mm_n'][:, co:co + CH].unsqueeze(2).to_broadcast([8, CH, 64]), op=ALU.subtract), r=['U', 'mm_n'], w=['G3'])
                        S.op('act', lambda e: e.activation(out=RW[1][:], in_=TMP[:], func=AF.Exp), r=['G3'], w=['G1'])
                        S.op('dve', lambda e: e.tensor_tensor(out=t3, in0=b3, in1=sm['mm_n'][:, co:co + CH].unsqueeze(2).to_broadcast([8, CH, 64]), op=ALU.add), r=['Bc', 'mm_n'], w=['G3'])
                        S.op('act', lambda e: e.activation(out=RW[2][:], in_=TMP[:], func=AF.Exp, scale=-1.0), r=['G3'], w=['G2'])
                        for tl in range(17):
                            ti = sg_ * 17 + tl
                            pb = 1 + ti % 2
                            for k3 in range(3):
                                TR(ps[pb][:, k3 * 8:(k3 + 1) * 8], RW[k3][:, tl * 128:(tl + 1) * 128], [f'G{k3}'], [PK[pb]], n=8)
                            S.op('dve', lambda e, ti=ti, pb=pb: e.tensor_copy(cols[:, ti, :], ps[pb][:, 0:24]), r=[PK[pb]], w=['cols'])
                    S.barrier()
                hsum = sb(es, [128, NT, 256])
                S.op('pool', lambda e: e.memset(hsum[:], 0.0), w=[('hsum', ti_) for ti_ in range(NT)])
                Cst = sb(es, [64, 8, 65]); C0b = [sb(es, [64, 4, 65], BF16) for _ in range(2)]
                tmpC_ = [sb(es, [64, 4, 65]) for _ in range(2)]
                wv = [sb(es, [128, 4, 65], BF16) for _ in range(2)]
                tS = [sb(es, [128, 4, 64]) for _ in range(2)]
                pTm = [sb(es, [128, 4, 64], BF16) for _ in range(2)]
                tI_ = [sb(es, [128, 260]) for _ in range(2)]; tH_ = [sb(es, [128, 260]) for _ in range(2)]
                dn_ = [sb(es, [128, 4]) for _ in range(2)]; hd = [sb(es, [128, 4, 64]) for _ in range(2)]
                S.op('dve', lambda e: e.memset(Cst[:], 0.0), w=['Cst0', 'Cst1'])
                def chain(d):
                    for p in range(NSTEP):
                        c = chunk_of(d, p); ti = c // 2; hb = c % 2; P0 = hb * 64; P1 = P0 + 64
                        t0 = c * 64
                        ip = d
                        tmpC = tmpC_[d]; tI = tI_[d]; tH = tH_[d]; dn = dn_[d]
                        pC = ps[d * 4]; pS = ps[d * 4 + 1]; pA = ps[d * 4 + 2]; pB = ps[d * 4 + 3]
                        kC = PK[d * 4]; kS = PK[d * 4 + 1]; kA = PK[d * 4 + 2]; kB = PK[d * 4 + 3]
                        Ck = f'Cst{d}'; tCk = f'tmpC{d}'; tIk = f'tI{d}'; tHk = f'tH{d}'; dnk = f'dn{d}'
                        S.op('dve', lambda e, ip=ip, ti=ti, d=d, P0=P0, P1=P1: e.tensor_tensor(out=wv[ip][P0:P1], in0=vaug[P0:P1, ti], in1=cols[P0:P1, ti, d * 4:(d + 1) * 4].unsqueeze(2).to_broadcast([64, 4, 65]), op=ALU.mult),
                             r=['vaug', 'cols'], w=[f'wv{ip}'])
                        yield
                        for h in range(4):
                            MM(pC[0:64, h * 65:(h + 1) * 65], ktok[P0:P1, ti, h * 64:(h + 1) * 64], wv[ip][P0:P1, h, :], True, True, ['ktok', f'wv{ip}'], [kC])
                        yield
                        S.op('dve', lambda e, ip=ip, d=d, p=p: e.tensor_tensor(out=C0b[ip][:], in0=Cst[:, d * 4:(d + 1) * 4, :], in1=bc[:, 2, d * 4:(d + 1) * 4, p].unsqueeze(2).to_broadcast([64, 4, 65]), op=ALU.mult),
                             r=[Ck, 'bc'], w=[f'C0b{ip}'])
                        yield
                        S.op('dve', lambda e, d=d, p=p: e.tensor_tensor(out=tmpC[:], in0=pC[0:64, 0:260].rearrange("p (h c) -> p h c", h=4), in1=bc[:, 1, d * 4:(d + 1) * 4, p].unsqueeze(2).to_broadcast([64, 4, 65]), op=ALU.mult),
                             r=[kC, 'bc'], w=[tCk])
                        yield
                        S.op('dve', lambda e, d=d, p=p: e.tensor_tensor(out=Cst[:, d * 4:(d + 1) * 4, :], in0=Cst[:, d * 4:(d + 1) * 4, :], in1=bc[:, 0, d * 4:(d + 1) * 4, p].unsqueeze(2).to_broadcast([64, 4, 65]), op=ALU.mult),
                             r=[Ck, 'bc'], w=[Ck])
                        yield
                        S.op('dve', lambda e, d=d: e.tensor_tensor(out=Cst[:, d * 4:(d + 1) * 4, :], in0=Cst[:, d * 4:(d + 1) * 4, :], in1=tmpC[:], op=ALU.add), r=[Ck, tCk], w=[Ck])
                        yield
                        for h in range(4):
                            MM(pS[P0:P1, h * 64:(h + 1) * 64], kT[:, h, t0:t0 + 64], qT[:, h, t0:t0 + 64], True, True, ['kT3', 'qT3'], [kS])
                        yield
                        S.op('dve', lambda e, ip=ip, ti=ti, d=d, P0=P0, P1=P1: e.tensor_tensor(out=tS[ip][P0:P1], in0=pS[P0:P1, 0:256].rearrange("p (h t) -> p h t", h=4),
                                                                                       in1=cols[P0:P1, ti, 8 + d * 4:8 + (d + 1) * 4].unsqueeze(2).to_broadcast([64, 4, 64]), op=ALU.mult),
                             r=[kS, 'cols'], w=[f'tS{ip}'])
                        yield
                        S.op('pool', lambda e, ip=ip, d=d, P0=P0, P1=P1: e.tensor_tensor(out=pTm[ip][P0:P1], in0=tS[ip][P0:P1], in1=msk[P0:P1, d, :].unsqueeze(1).to_broadcast([64, 4, 64]), op=ALU.mult),
                             r=[f'tS{ip}', 'msk'], w=[f'pTm{ip}'])
                        yield
                        for h in range(4):
                            MM(pA[P0:P1, h * 65:(h + 1) * 65], pTm[ip][P0:P1, h, :], vaug[P0:P1, ti, h, :], True, True, [f'pTm{ip}', 'vaug'], [kA])
                        yield
                        for h in range(4):
                            MM(pB[P0:P1, h * 65:(h + 1) * 65], qT[:, h, t0:t0 + 64], C0b[ip][:, h, :], True, True, ['qT3', f'C0b{ip}'], [kB])
                        yield
                        S.op('act', lambda e, P0=P0, P1=P1: e.copy(out=tI[P0:P1, :], in_=pB[P0:P1, 0:260]), r=[kB], w=[tIk])
                        yield
                        S.op('dve', lambda e, P0=P0, P1=P1: e.tensor_tensor(out=tH[P0:P1, :], in0=pA[P0:P1, 0:260], in1=tI[P0:P1, :], op=ALU.add), r=[kA, tIk], w=[tHk])
                        yield
                        ph = tH[P0:P1, :].rearrange("p (h c) -> p h c", h=4)
                        S.op('act', lambda e, ph=ph, P0=P0, P1=P1: e.activation(out=dn[P0:P1, :], in_=ph[:, :, 64], func=AF.Abs), r=[tHk], w=[dnk])
                        yield
                        S.op('dve', lambda e, ti=ti, d=d, P0=P0, P1=P1: e.tensor_tensor(out=dn[P0:P1, :], in0=dn[P0:P1, :], in1=cols[P0:P1, ti, 16 + d * 4:16 + (d + 1) * 4], op=ALU.max), r=[dnk, 'cols'], w=[dnk])
                        yield
                        S.op('dve', lambda e, P0=P0, P1=P1: e.reciprocal(dn[P0:P1, :], dn[P0:P1, :]), r=[dnk], w=[dnk])
                        yield
                        S.op('dve', lambda e, ip=ip, ph=ph, P0=P0, P1=P1: e.tensor_tensor(out=hd[ip][P0:P1], in0=ph[:, :, 0:64], in1=dn[P0:P1, :].unsqueeze(2).to_broadcast([64, 4, 64]), op=ALU.mult),
                             r=[tHk, dnk], w=[f'hd{ip}'])
                        yield
                        S.op('pool', lambda e, ip=ip, ti=ti, P0=P0, P1=P1: e.tensor_tensor(out=hsum[P0:P1, ti, :], in0=hsum[P0:P1, ti, :], in1=hd[ip][P0:P1].rearrange("p h d -> p (h d)"), op=ALU.add),
                             r=[('hsum', ti), f'hd{ip}'], w=[('hsum', ti)])
                        yield

                interleave([chain(0), chain(1)], 2)
                with ExitStack() as es2:
                    ngb = sb(es2, [128, 256]); so = [sb(es2, [128, 256]) for _ in range(2)]
                    mu = sb(es2, [128, 4]); var = sb(es2, [128, 4]); cen = sb(es2, [128, 4, 64]); sqq = sb(es2, [128, 4, 64])
                    yc = [sb(es2, [128, 256]) for _ in range(2)]; ycT = [sb(es2, [128, 2, 128], BF16) for _ in range(2)]
                    vec_bcast('sp', ngb, ml_norm_g[l], 'ngb')
                    for ti in range(NT):
                        p2 = ti % 2
                        S.dma('sp', lambda e, ti=ti, p2=p2: e.dma_start(out=so[p2][:], in_=sigo_d[ti * 128:(ti + 1) * 128, :]), w=[f'so{p2}'])
                        h3 = hsum[:, ti, :].rearrange("p (h d) -> p h d", h=4)
                        S.op('dve', lambda e, h3=h3: e.tensor_reduce(out=mu[:], in_=h3, axis=AX.X, op=ALU.add), r=[('hsum', ti)], w=['mu'])
                        S.op('dve', lambda e: e.tensor_scalar(out=mu[:], in0=mu[:], scalar1=1.0 / 64, scalar2=None, op0=ALU.mult), r=['mu'], w=['mu'])
                        S.op('dve', lambda e, h3=h3: e.tensor_tensor(out=cen[:], in0=h3, in1=mu[:].unsqueeze(2).to_broadcast([128, 4, 64]), op=ALU.subtract), r=[('hsum', ti), 'mu'], w=['cen'])
                        S.op('dve', lambda e: e.tensor_tensor(out=sqq[:], in0=cen[:], in1=cen[:], op=ALU.mult), r=['cen'], w=['sqq'])
                        S.op('dve', lambda e: e.tensor_reduce(out=var[:], in_=sqq[:], axis=AX.X, op=ALU.add), r=['sqq'], w=['var'])
                        S.op('act', lambda e: e.activation(out=var[:], in_=var[:], func=AF.Sqrt, bias=epsc[:, 0:1], scale=1.0 / 64), r=['var', 'epsc'], w=['var'])
                        S.op('dve', lambda e: e.reciprocal(var[:], var[:]), r=['var'], w=['var'])
                        S.op('dve', lambda e: e.tensor_tensor(out=cen[:], in0=cen[:], in1=var[:].unsqueeze(2).to_broadcast([128, 4, 64]), op=ALU.mult), r=['cen', 'var'], w=['cen'])
                        S.op('dve', lambda e: e.tensor_tensor(out=cen[:].rearrange("p h d -> p (h d)"), in0=cen[:].rearrange("p h d -> p (h d)"), in1=ngb[:], op=ALU.mult), r=['cen', 'ngb'], w=['cen'])
                        S.op('dve', lambda e, p2=p2: e.tensor_tensor(out=yc[p2][:], in0=cen[:].rearrange("p h d -> p (h d)"), in1=so[p2][:], op=ALU.mult), r=['cen', f'so{p2}'], w=[f'yc{p2}'])
                        for c in range(2):
                            TR(ps[1 + p2][:, c * 128:(c + 1) * 128], yc[p2][:, c * 128:(c + 1) * 128], [f'yc{p2}'], [PK[1 + p2]])
                        S.op('act', lambda e, p2=p2: e.copy(out=ycT[p2][:].rearrange("p c t -> p (c t)"), in_=ps[1 + p2][:, 0:256]), r=[PK[1 + p2]], w=[f'ycT{p2}'])
                        S.dma('sp', lambda e, p2=p2, ti=ti: e.dma_start(out=catT_d[768:1024, ti * 128:(ti + 1) * 128].rearrange("(c p) t -> p c t", p=128), in_=ycT[p2][:]), r=[f'ycT{p2}'], w=[('catC', ti)])
                S.barrier()

        def phase4(l, xsrc, tiles):
            with ExitStack() as es:
                wout = sb(es, [128, 8, D], BF16); wr = sb(es, [128, 8, 16])
                g1b = [sb(es, [128, D]) for _ in range(2)]; scb = [sb(es, [128, D]) for _ in range(2)]; shb = [sb(es, [128, D]) for _ in range(2)]
                lg = sb(es, [128, D]); lb = sb(es, [128, D])
                cat = [sb(es, [128, 8, 128], BF16) for _ in range(2)]
                xt = [sb(es, [128, D]) for _ in range(2)]
                rr_ = [sb(es, [128, D]) for _ in range(2)]; x1 = [sb(es, [128, D]) for _ in range(2)]; hm = [sb(es, [128, D]) for _ in range(2)]
                hmT_ = [sb(es, [128, 8, 128]) for _ in range(2)]
                st6_ = [sb(es, [128, 4, 6]) for _ in range(2)]; mv_ = [sb(es, [128, 2]) for _ in range(2)]; rstd_ = [sb(es, [128, 1]) for _ in range(2)]; nmr_ = [sb(es, [128, 1]) for _ in range(2)]
                lgt_ = [sb(es, [128, 16]) for _ in range(2)]; mx_ = [sb(es, [128, 1]) for _ in range(2)]; ssum_ = [sb(es, [128, 1]) for _ in range(2)]; affT = sb(es, [16, T])
                for c in range(8):
                    S.dma('pool', lambda e, c=c: e.dma_start(out=wout[:, c, :], in_=w_out[l, c * 128:(c + 1) * 128, :]), w=['wout'])
                S.dma('sp', lambda e: e.dma_start(out=wr[:], in_=w_router[l].rearrange("(c p) n -> p c n", p=128)), w=['wr'])
                for m in range(2):
                    bcast_load(es, 'sp', g1b[m], 2, m, f'g1b{m}'); bcast_load(es, 'sp', scb[m], 4, m, f'scb{m}'); bcast_load(es, 'sp', shb[m], 3, m, f'shb{m}')
                    S.op('dve', lambda e, m=m: e.tensor_scalar(out=scb[m][:], in0=scb[m][:], scalar1=1.0, scalar2=None, op0=ALU.add), r=[f'scb{m}'], w=[f'scb{m}'])
                vec_bcast('sp', lg, ln1_g[l], 'lg'); vec_bcast('sp', lb, ln1_b[l], 'lb')
                def tile4(ti):
                    p2 = ti % 2; m = 1 if ti < 2 else 0
                    rr = rr_[p2]; hmT = hmT_[p2]; st6 = st6_[p2]; mv = mv_[p2]; rstd = rstd_[p2]; nmr = nmr_[p2]; lgt = lgt_[p2]; mx = mx_[p2]; ssum = ssum_[p2]
                    rk = f'rr{p2}'; hk = f'hmT{p2}'; lk = f'lgt{p2}'; mk = f'mx{p2}'; sk = f'ssum{p2}'
                    bA = ps[3 * p2]; bB = ps[3 * p2 + 1]; bC = ps[3 * p2 + 2]; kA = PK[3 * p2]; kB = PK[3 * p2 + 1]; kC = PK[3 * p2 + 2]
                    pbs = [bA, bB]; pks = [kA, kB]
                    with nc.allow_non_contiguous_dma(reason="catT tile"):
                        S.dma('sp', lambda e: e.dma_start(out=cat[p2][:], in_=catT_d[:, ti * 128:(ti + 1) * 128].rearrange("(c p) t -> p c t", p=128)), w=[f'cat{p2}'])
                    S.dma('sp', lambda e: e.dma_start(out=xt[p2][:], in_=xsrc[ti * 128:(ti + 1) * 128, :]), w=[f'xt{p2}'])
                    yield
                    for half in range(2):
                        for c in range(8):
                            MM(pbs[half][:, :], cat[p2][:, c, :], wout[:, c, half * 512:(half + 1) * 512], c == 0, c == 7, [f'cat{p2}', 'wout'], [pks[half]])
                        yield
                        S.op('dve', lambda e: e.tensor_tensor(out=rr[:, half * 512:(half + 1) * 512], in0=pbs[half][:, :], in1=g1b[m][:, half * 512:(half + 1) * 512], op=ALU.mult),
                             r=[pks[half], f'g1b{m}'], w=[rk])
                        yield
                    S.op('dve', lambda e: e.scalar_tensor_tensor(out=rr[:], in0=xt[p2][:], scalar=ALPHA, in1=rr[:], op0=ALU.mult, op1=ALU.add), r=[f'xt{p2}', rk], w=[rk])
                    yield
                    ln_stats(f'l4{p2}', rr, D, mv, rstd, nmr, st6, [rk])
                    yield
                    S.op('act', lambda e: e.activation(out=rr[:], in_=rr[:], func=AF.Identity, bias=nmr[:, 0:1], scale=rstd[:, 0:1]), r=[rk, f'l4{p2}rs', f'l4{p2}nm'], w=[rk])
                    yield
                    S.op('dve', lambda e: e.tensor_tensor(out=rr[:], in0=rr[:], in1=lg[:], op=ALU.mult), r=[rk, 'lg'], w=[rk])
                    yield
                    S.op('dve', lambda e: e.tensor_tensor(out=x1[p2][:], in0=rr[:], in1=lb[:], op=ALU.add), r=[rk, 'lb'], w=[f'x1{p2}'])
                    S.dma('sp', lambda e: e.dma_start(out=x1_d[ti * 128:(ti + 1) * 128, :], in_=x1[p2][:]), r=[f'x1{p2}'], w=[('x1_d', ti)])
                    yield
                    ln_stats(f'l5{p2}', x1[p2], D, mv, rstd, nmr, st6, [f'x1{p2}'])
                    yield
                    S.op('act', lambda e: e.activation(out=rr[:], in_=x1[p2][:], func=AF.Identity, bias=nmr[:, 0:1], scale=rstd[:, 0:1]), r=[f'x1{p2}', f'l5{p2}rs', f'l5{p2}nm'], w=[rk])
                    yield
                    S.op('dve', lambda e: e.tensor_tensor(out=rr[:], in0=rr[:], in1=scb[m][:], op=ALU.mult), r=[rk, f'scb{m}'], w=[rk])
                    yield
                    S.op('dve', lambda e: e.tensor_tensor(out=hm[p2][:], in0=rr[:], in1=shb[m][:], op=ALU.add), r=[rk, f'shb{m}'], w=[f'hm{p2}'])
                    S.dma('sp', lambda e: e.dma_start(out=hm_d[ti * 128:(ti + 1) * 128, :], in_=hm[p2][:]), r=[f'hm{p2}'], w=[('hm_d', ti)])
                    yield
                    for half in range(2):
                        for c4 in range(4):
                            TR(pbs[half][:, c4 * 128:(c4 + 1) * 128], hm[p2][:, (half * 4 + c4) * 128:(half * 4 + c4 + 1) * 128], [f'hm{p2}'], [pks[half]])
                        yield
                        if half == 0:
                            S.op('act', lambda e: e.copy(out=hmT[:, 0:4, :].rearrange("p c t -> p (c t)"), in_=pbs[0][:, :]), r=[pks[0]], w=[hk])
                        else:
                            S.op('dve', lambda e: e.tensor_copy(hmT[:, 4:8, :].rearrange("p c t -> p (c t)"), pbs[1][:, :]), r=[pks[1]], w=[hk])
                        yield
                    for c in range(8):
                        MM(bC[:, 0:16], hmT[:, c, :], wr[:, c, :], c == 0, c == 7, [hk, 'wr'], [kC])
                    yield
                    S.op('dve', lambda e: e.tensor_reduce(out=mx[:], in_=bC[:, 0:16], axis=AX.X, op=ALU.max), r=[kC], w=[mk])
                    yield
                    S.op('dve', lambda e: e.tensor_scalar(out=mx[:], in0=mx[:], scalar1=-1.0, scalar2=None, op0=ALU.mult), r=[mk], w=[mk])
                    yield
                    S.op('act', lambda e: e.activation(out=lgt[:], in_=bC[:, 0:16], func=AF.Exp, bias=mx[:, 0:1], scale=1.0, accum_out=ssum[:]), r=[kC, mk], w=[lk, sk])
                    yield
                    S.op('dve', lambda e: e.reciprocal(ssum[:], ssum[:]), r=[sk], w=[sk])
                    yield
                    S.op('dve', lambda e: e.tensor_scalar(out=lgt[:], in0=lgt[:], scalar1=ssum[:, 0:1], scalar2=None, op0=ALU.mult), r=[lk, sk], w=[lk])
                    yield
                    TR(bC[0:16, 128:256], lgt[:, :], [lk], [kC])
                    yield
                    S.op('dve', lambda e: e.tensor_copy(affT[:, ti * 128:(ti + 1) * 128], bC[0:16, 128:256]), r=[kC], w=['affT'])
                    yield
                interleave([tile4(ti) for ti in tiles], 2)
                t_lo = tiles[0] * 128
                S.dma('sp', lambda e: e.dma_start(out=aff_d[:, t_lo:T], in_=affT[:, t_lo:T]), r=['affT'], w=['aff_d'])
                S.barrier()

        def phase56(l, with_ctx):
            with ExitStack() as es:
                idxT = sb(es, [128, 5, 16], U32); gateT = sb(es, [128, 5, 16])
                wg = [sb(es, [128, 8, D], BF16) for _ in range(2)]; wu = [sb(es, [128, 8, D], BF16) for _ in range(2)]; wd = [sb(es, [128, 8, D], BF16) for _ in range(2)]

                def load_w(e_):
                    ep = e_ % 2
                    for c in range(8):
                        S.dma('pool', lambda e, c=c: e.dma_start(out=wg[ep][:, c, :], in_=w_gate[l, e_, c * 128:(c + 1) * 128, :]), w=[f'wg{ep}'])
                        S.dma('pool', lambda e, c=c: e.dma_start(out=wu[ep][:, c, :], in_=w_up[l, e_, c * 128:(c + 1) * 128, :]), w=[f'wu{ep}'])
                        S.dma('pool', lambda e, c=c: e.dma_start(out=wd[ep][:, c, :], in_=w_down[l, e_, c * 128:(c + 1) * 128, :]), w=[f'wd{ep}'])
                load_w(0)
                es2 = ExitStack()
                aw = sb(es2, [16, TL]); vals = sb(es2, [16, 512]); idx = sb(es2, [16, 512], U32); idxf = sb(es2, [16, 512])
                awc = sb(es2, [16, 256]); valsc = sb(es2, [16, 32]); idxc = sb(es2, [16, 32], U32); idxcf = sb(es2, [16, 32])
                S.dma('sp', lambda e: e.dma_start(out=aw[:], in_=aff_d[:, 256:T]), w=['aw'])
                for r_ in range(64):
                    S.op('dve', lambda e, r_=r_: e.max(out=vals[:, r_ * 8:(r_ + 1) * 8], in_=aw[:]), r=['aw'], w=['vals'])
                    S.op('dve', lambda e, r_=r_: e.max_index(out=idx[:, r_ * 8:(r_ + 1) * 8], in_max=vals[:, r_ * 8:(r_ + 1) * 8], in_values=aw[:]), r=['aw', 'vals'], w=['idx'])
                    S.op('dve', lambda e, r_=r_: e.match_replace(out=aw[:], in_to_replace=vals[:, r_ * 8:(r_ + 1) * 8], in_values=aw[:], imm_value=-1.0), r=['aw', 'vals'], w=['aw'])
                S.op('dve', lambda e: e.tensor_copy(idxf[:], idx[:]), r=['idx'], w=['idxf'])
                S.op('dve', lambda e: e.tensor_scalar(out=idxf[:], in0=idxf[:], scalar1=256.0, scalar2=None, op0=ALU.add), r=['idxf'], w=['idxf'])
                for st in range(4):
                    TR(ps[0][:, 0:16], idxf[:, st * 128:(st + 1) * 128], ['idxf'], [PK[0]], n=16)
                    S.op('dve', lambda e, st=st: e.tensor_copy(idxT[:, st, :], ps[0][:, 0:16]), r=[PK[0]], w=['idxT'])
                    TR(ps[1][:, 0:16], vals[:, st * 128:(st + 1) * 128], ['vals'], [PK[1]], n=16)
                    S.op('dve', lambda e, st=st: e.tensor_copy(gateT[:, st, :], ps[1][:, 0:16]), r=[PK[1]], w=['gateT'])
                if with_ctx:
                    S.dma('sp', lambda e: e.dma_start(out=awc[:], in_=aff_d[:, 0:256]), w=['awc'])
                    for r_ in range(4):
                        S.op('dve', lambda e, r_=r_: e.max(out=valsc[:, r_ * 8:(r_ + 1) * 8], in_=awc[:]), r=['awc'], w=['valsc'])
                        S.op('dve', lambda e, r_=r_: e.max_index(out=idxc[:, r_ * 8:(r_ + 1) * 8], in_max=valsc[:, r_ * 8:(r_ + 1) * 8], in_values=awc[:]), r=['awc', 'valsc'], w=['idxc'])
                        S.op('dve', lambda e, r_=r_: e.match_replace(out=awc[:], in_to_replace=valsc[:, r_ * 8:(r_ + 1) * 8], in_values=awc[:], imm_value=-1.0), r=['awc', 'valsc'], w=['awc'])
                    S.op('dve', lambda e: e.tensor_copy(idxcf[:], idxc[:]), r=['idxc'], w=['idxcf'])
                    TR(ps[0][0:32, 0:16], idxcf[:, :], ['idxcf'], [PK[0]], n=16)
                    S.op('dve', lambda e: e.tensor_copy(idxT[0:32, 4, :], ps[0][0:32, 0:16]), r=[PK[0]], w=['idxT'])
                    TR(ps[1][0:32, 0:16], valsc[:, :], ['valsc'], [PK[1]], n=16)
                    S.op('dve', lambda e: e.tensor_copy(gateT[0:32, 4, :], ps[1][0:32, 0:16]), r=[PK[1]], w=['gateT'])
                S.barrier(); es2.close()
                NS = 544 if with_ctx else 512
                nst = 5 if with_ctx else 4
                xe = [sb(es, [128, 5, D]) for _ in range(2)]
                xeT = sb(es, [128, 8, 544], BF16); hidT = sb(es, [128, 8, 544], BF16)
                sg = [sb(es, [128, 544]) for _ in range(2)]
                ye = [sb(es, [128, D]) for _ in range(2)]

                def load_expert(e_):
                    ep = e_ % 2
                    if e_ > 0: load_w(e_)
                    for st in range(nst):
                        n = 128 if st < 4 else 32
                        S.dma('pool', lambda e, st=st, n=n: e.indirect_dma_start(out=xe[ep][0:n, st, :], out_offset=None, in_=hm_d[:, :],
                                                                              in_offset=bass.IndirectOffsetOnAxis(ap=idxT[0:n, st, e_:e_ + 1], axis=0)),
                              r=['idxT'], w=[f'xe{ep}'])

                load_expert(0)
                yc_ = 0
                for e_ in range(16):
                    ep = e_ % 2
                    if e_ + 1 < 16: load_expert(e_ + 1)
                    for st in range(nst):
                        n = 128 if st < 4 else 32
                        for half in range(2):
                            for c4 in range(4):
                                cc = half * 4 + c4
                                TR(ps[half][:, c4 * 128:c4 * 128 + n], xe[ep][0:n, st, cc * 128:(cc + 1) * 128], [f'xe{ep}'], [PK[half]], n=n)
                            src = ps[half][:, :].rearrange("p (c t) -> p c t", c=4)[:, :, 0:n]
                            if half == 0:
                                S.op('act', lambda e, st=st, n=n, src=src: e.copy(out=xeT[:, 0:4, st * 128:st * 128 + n], in_=src), r=[PK[0]], w=['xeT'])
                            else:
                                S.op('dve', lambda e, st=st, n=n, src=src: e.tensor_copy(xeT[:, 4:8, st * 128:st * 128 + n], src), r=[PK[1]], w=['xeT'])
                    for fc in range(8):
                        for (n0, n1) in ([(0, 512), (512, 544)] if with_ctx else [(0, 512)]):
                            pg = ps[2] if n0 == 0 else ps[4]; pu = ps[3] if n0 == 0 else ps[5]
                            pgk = PK[2] if n0 == 0 else PK[4]; puk = PK[3] if n0 == 0 else PK[5]
                            nn = n1 - n0
                            for c in range(8):
                                MM(pg[:, 0:nn], wg[ep][:, c, fc * 128:(fc + 1) * 128], xeT[:, c, n0:n1], c == 0, c == 7, [f'wg{ep}', 'xeT'], [pgk])
                            for c in range(8):
                                MM(pu[:, 0:nn], wu[ep][:, c, fc * 128:(fc + 1) * 128], xeT[:, c, n0:n1], c == 0, c == 7, [f'wu{ep}', 'xeT'], [puk])
                            sp2 = fc % 2
                            S.op('act', lambda e, pg=pg, nn=nn, sp2=sp2: e.activation(out=sg[sp2][:, 0:nn], in_=pg[:, 0:nn], func=AF.Silu), r=[pgk], w=[f'sg{sp2}'])
                            S.op('dve', lambda e, pu=pu, nn=nn, n0=n0, n1=n1, fc=fc, sp2=sp2: e.tensor_tensor(out=hidT[:, fc, n0:n1], in0=pu[:, 0:nn], in1=sg[sp2][:, 0:nn], op=ALU.mult), r=[puk, f'sg{sp2}'], w=['hidT'])
                    for st in range(nst):
                        n = 128 if st < 4 else 32
                        yp = yc_ % 2; yc_ += 1
                        for half in range(2):
                            for fc in range(8):
                                MM(ps[6 + half][0:n, :], hidT[:, fc, st * 128:st * 128 + n], wd[ep][:, fc, half * 512:(half + 1) * 512], fc == 0, fc == 7, ['hidT', f'wd{ep}'], [PK[6 + half]])
                            eng = 'act' if half == 0 else 'dve'
                            if half == 0:
                                S.op('act', lambda e, n=n, st=st, yp=yp: e.activation(out=ye[yp][0:n, 0:512], in_=ps[6][0:n, :], func=AF.Identity, scale=gateT[0:n, st, e_:e_ + 1]), r=[PK[6], 'gateT'], w=[f'ye{yp}'])
                            else:
                                S.op('dve', lambda e, n=n, st=st, yp=yp: e.tensor_scalar(out=ye[yp][0:n, 512:1024], in0=ps[7][0:n, :], scalar1=gateT[0:n, st, e_:e_ + 1], scalar2=None, op0=ALU.mult), r=[PK[7], 'gateT'], w=[f'ye{yp}'])
                        S.dma('pool', lambda e, st=st, n=n, yp=yp: e.indirect_dma_start(out=moe_d[:, :], out_offset=bass.IndirectOffsetOnAxis(ap=idxT[0:n, st, e_:e_ + 1], axis=0),
                                                                                  in_=ye[yp][0:n, :], in_offset=None, compute_op=ALU.add),
                              r=[f'ye{yp}', 'idxT'], w=['moe_d'])
                S.barrier()

        def phase7(l, dst, tiles, final):
            with ExitStack() as es:
                g2b = [sb(es, [128, D]) for _ in range(2)]
                lg = sb(es, [128, D]); lb = sb(es, [128, D])
                xt = [sb(es, [128, D]) for _ in range(2)]; ft = [sb(es, [128, D]) for _ in range(2)]
                rr_ = [sb(es, [128, D]) for _ in range(2)]; xo = [sb(es, [128, D]) for _ in range(2)]
                st6_ = [sb(es, [128, 4, 6]) for _ in range(2)]; mv_ = [sb(es, [128, 2]) for _ in range(2)]; rstd_ = [sb(es, [128, 1]) for _ in range(2)]; nmr_ = [sb(es, [128, 1]) for _ in range(2)]
                for m in range(2):
                    bcast_load(es, 'sp', g2b[m], 5, m, f'g2b{m}')
                vec_bcast('sp', lg, ln2_g[l], 'lg'); vec_bcast('sp', lb, ln2_b[l], 'lb')

                def tile7(ti):
                    p2 = ti % 2; m = 1 if ti < 2 else 0
                    rr = rr_[p2]; st6 = st6_[p2]; mv = mv_[p2]; rstd = rstd_[p2]; nmr = nmr_[p2]; rk = f'rr{p2}'
                    S.dma('sp', lambda e: e.dma_start(out=xt[p2][:], in_=x1_d[ti * 128:(ti + 1) * 128, :]), w=[f'xt{p2}'])
                    S.dma('sp', lambda e: e.dma_start(out=ft[p2][:], in_=moe_d[ti * 128:(ti + 1) * 128, :]), w=[f'ft{p2}'])
                    yield
                    S.op('dve', lambda e: e.tensor_tensor(out=rr[:], in0=ft[p2][:], in1=g2b[m][:], op=ALU.mult), r=[f'ft{p2}', f'g2b{m}'], w=[rk])
                    yield
                    S.op('dve', lambda e: e.scalar_tensor_tensor(out=rr[:], in0=xt[p2][:], scalar=ALPHA, in1=rr[:], op0=ALU.mult, op1=ALU.add), r=[f'xt{p2}', rk], w=[rk])
                    yield
                    ln_stats(f'l7{p2}', rr, D, mv, rstd, nmr, st6, [rk])
                    yield
                    S.op('act', lambda e: e.activation(out=rr[:], in_=rr[:], func=AF.Identity, bias=nmr[:, 0:1], scale=rstd[:, 0:1]), r=[rk, f'l7{p2}rs', f'l7{p2}nm'], w=[rk])
                    yield
                    S.op('dve', lambda e: e.tensor_tensor(out=rr[:], in0=rr[:], in1=lg[:], op=ALU.mult), r=[rk, 'lg'], w=[rk])
                    yield
                    S.op('dve', lambda e: e.tensor_tensor(out=xo[p2][:], in0=rr[:], in1=lb[:], op=ALU.add), r=[rk, 'lb'], w=[f'xo{p2}'])
                    if final:
                        S.dma('sp', lambda e: e.dma_start(out=dst[(ti - 2) * 128:(ti - 1) * 128, :], in_=xo[p2][:]), r=[f'xo{p2}'], w=[('out', ti)])
                    else:
                        S.dma('sp', lambda e: e.dma_start(out=dst[ti * 128:(ti + 1) * 128, :], in_=xo[p2][:]), r=[f'xo{p2}'], w=[('out', ti)])
                    yield
                interleave([tile7(ti) for ti in tiles], 2)
                S.barrier()

        for l in range(nlayers):
            last = (l == 1)
            xsrc = xin if l == 0 else xs1
            tiles = list(range(2, NT)) if last else list(range(NT))
            phase0(l)
            if stop == 'p0': break
            phase1(l, xsrc)
            if stop == 'p1': break
            phase2(l, with_ctx_q=not last)
            if stop == 'p2': break
            phase3(l)
            if stop == 'p3': break
            phase4(l, xsrc, tiles)
            if stop == 'p4': break
            phase56(l, with_ctx=not last)
            if stop == 'p6': break
            phase7(l, y if last else xs1, tiles, final=last)
        S.barrier()
    return nc, dbg_outs


def _consts():
    c = {}
    c['c_ident'] = np.eye(128, dtype=np.float32)
    t = np.arange(TL)
    row = (t // 64).astype(np.float32); col = (t % 64).astype(np.float32)
    inv = (10000.0 ** (-np.arange(8, dtype=np.float32) * 2.0 / 16)).astype(np.float32)
    ang = np.concatenate([row[:, None] * inv, col[:, None] * inv], axis=-1).astype(np.float32)
    cos = np.cos(ang).astype(np.float32); sin = np.sin(ang).astype(np.float32)
    cosT = np.ones((32, T), np.float32); sinT = np.zeros((32, T), np.float32)
    for a in range(2):
        for i in range(8):
            cosT[a * 16 + i, TC:] = cos[:, a * 8 + i]; cosT[a * 16 + 8 + i, TC:] = cos[:, a * 8 + i]
            sinT[a * 16 + i, TC:] = -sin[:, a * 8 + i]; sinT[a * 16 + 8 + i, TC:] = sin[:, a * 8 + i]
    c['c_rope'] = np.stack([cosT, sinT]).astype(np.float32)
    s = np.arange(64)[:, None]; tt = np.arange(64)[None, :]
    m0 = (s <= tt).astype(np.float32); m1 = (s >= tt).astype(np.float32)
    c['c_mask'] = np.stack([np.concatenate([m0, m0], 0), np.concatenate([m1, m1], 0)]).astype(np.float32)
    misc = np.zeros((8, 16), np.float32)
    misc[4:, 0] = 1.0; misc[:4, 1] = 1.0; misc[4:, 1] = -1.0; misc[:4, 2] = 1.0
    c['c_misc'] = misc
    e8 = np.zeros((8, 8, NSTEP), np.float32)
    for k in range(8): e8[k, k, :] = 1.0
    c['c_eye8'] = e8
    pr = np.zeros((NSTEP, NSTEP), np.float32)
    for p in range(NSTEP):
        cidx = 3 - p if p < 4 else 71 - p
        pr[cidx, p] = 1.0
    c['c_prev'] = pr
    rst = np.ones((8, T), np.float32); rst[:, ::64] = 0.0
    c['c_rst'] = rst
    return c


def _prep_weights(inp):
    w = {}
    sw = np.arange(32)
    for a in range(2):
        for i in range(8):
            sw[a * 16 + i] = a * 16 + 8 + i; sw[a * 16 + 8 + i] = a * 16 + i
    cols = np.concatenate([np.arange(0, 512), np.arange(1696, 2208),
                           np.arange(512, 1152), np.arange(1152, 1184), 1152 + sw,
                           np.arange(1184, 1696), np.arange(2208, 2224)])
    assert cols.size == NWIN
    w['w_in'] = np.ascontiguousarray(inp['w_in'][:, :, cols]); w['b_in'] = np.ascontiguousarray(inp['b_in'][:, cols])
    uq = inp['w_uq']
    swc = np.concatenate([h * 96 + 64 + sw for h in range(8)])
    w['w_uq'] = np.ascontiguousarray(np.concatenate([uq, uq[:, :, swc]], axis=-1))
    ukv = inp['w_ukv']
    kc = np.concatenate([np.arange(h * 128, h * 128 + 64) for h in range(8)])
    vc = np.concatenate([np.arange(h * 128 + 64, h * 128 + 128) for h in range(8)])
    w['w_ukv'] = np.ascontiguousarray(np.concatenate([ukv[:, :, kc], ukv[:, :, vc]], axis=-1))
    w['ml_f_bias'] = np.ascontiguousarray(inp['ml_f_bias'].reshape(2, 8))
    for k in ['w_ada', 'b_ada', 'sg_ln_g', 'sg_ln_b', 'sg_w', 'sg_b', 'q_norm_g', 'kv_norm_g', 'ml_conv_w', 'ml_conv_b', 'ml_norm_g',
              'w_out', 'ln1_g', 'ln1_b', 'w_router', 'w_gate', 'w_up', 'w_down', 'ln2_g', 'ln2_b']:
        w[k] = np.ascontiguousarray(inp[k])
    return w


_CACHE = {}


def kernel(**inputs):
    inp = {k: np.asarray(v, dtype=np.float32) for k, v in inputs.items()}
    if 'nc' not in _CACHE:
        _CACHE['nc'] = build()[0]
    nc = _CACHE['nc']
    consts = _consts(); w = _prep_weights(inp)
    in_maps = []
    for b in range(8):
        m = dict(consts); m.update(w)
        m['xin'] = np.ascontiguousarray(np.concatenate([inp['ctx'][b], inp['x'][b]], axis=0))
        m['cvec'] = np.ascontiguousarray(np.stack([inp['c'][b], inp['c_ctx']]))
        in_maps.append(m)
    res = run_bass_kernel_spmd(nc, in_maps, core_ids=list(range(8)))
    return np.stack([np.asarray(r['y'], dtype=np.float32) for r in res.results], axis=0)
```

```python
import numpy as np
from contextlib import ExitStack
import concourse.bass as bass
import concourse.mybir as mybir
from concourse.bass_utils import run_bass_kernel_spmd

F32 = mybir.dt.float32; BF16 = mybir.dt.bfloat16; U32 = mybir.dt.uint32
AF = mybir.ActivationFunctionType; ALU = mybir.AluOpType; AX = mybir.AxisListType

D = 1024; T = 4352; NT = 34; TC = 256; TL = 4096
NWIN = 2256
ALPHA = 4 ** 0.25
EPS = 1e-6
MLA_SCALE = 96 ** -0.5
NSTEP = 68


class Sched:
    NSLOT = 6

    def __init__(self, nc, es):
        self.nc = nc
        self.E = {'pe': nc.tensor, 'dve': nc.vector, 'act': nc.scalar, 'pool': nc.gpsimd, 'sp': nc.sync}
        self.sem = {}; self.cnt = {}
        for k in self.E:
            self.sem[k] = es.enter_context(nc.semaphore('s_' + k)); self.cnt[k] = 0
        self.slots = {}
        for q in ('sp', 'pool'):
            self.slots[q] = []
            for i in range(self.NSLOT):
                k = f'd{q}{i}'
                self.sem[k] = es.enter_context(nc.semaphore('s_' + k)); self.cnt[k] = 0
                self.slots[q].append(k)
        self.rr = {'sp': 0, 'pool': 0}
        self.waited = {k: {} for k in self.E}
        self.lastw = {}; self.readers = {}

    def _wait(self, eng, ev):
        k, v = ev
        if v <= 0: return
        if k == eng and eng == 'pe': return
        if self.waited[eng].get(k, 0) >= v: return
        self.E[eng].wait_ge(self.sem[k], v); self.waited[eng][k] = v

    def _deps(self, eng, r, w):
        deps = {}
        def add(k, v):
            if deps.get(k, 0) < v: deps[k] = v
        for b in r:
            ev = self.lastw.get(b)
            if ev: add(*ev)
        for b in w:
            ev = self.lastw.get(b)
            if ev: add(*ev)
            for k, v in self.readers.get(b, {}).items(): add(k, v)
        for k, v in deps.items(): self._wait(eng, (k, v))

    def _commit(self, ev, r, w):
        for b in r:
            d = self.readers.setdefault(b, {})
            if d.get(ev[0], 0) < ev[1]: d[ev[0]] = ev[1]
        for b in w:
            self.lastw[b] = ev; self.readers[b] = {}

    def op(self, eng, fn, r=(), w=()):
        self._deps(eng, r, w)
        inst = fn(self.E[eng])
        self.cnt[eng] += 1
        inst.then_inc(self.sem[eng], 1)
        self._commit((eng, self.cnt[eng]), r, w)

    def dma(self, q, fn, r=(), w=()):
        self._deps(q, r, w)
        i = self.rr[q]; self.rr[q] = (i + 1) % self.NSLOT
        k = self.slots[q][i]
        self._wait(q, (k, self.cnt[k]))
        inst = fn(self.E[q])
        self.cnt[k] += 16
        inst.then_inc(self.sem[k], 16)
        self._commit((k, self.cnt[k]), r, w)

    def barrier(self):
        for e in self.E:
            for k in self.sem:
                if k == e: continue
                self._wait(e, (k, self.cnt[k]))
        self.lastw = {}; self.readers = {}


def build(debug=False, nlayers=2, stop=None):
    nc = bass.Bass("TRN2", target_bir_lowering=False)
    dbg_outs = []

    def din(name, shape, dt=F32):
        return nc.dram_tensor(name, list(shape), dt, kind="ExternalInput").ap()

    def dscr(name, shape, dt=F32):
        if debug:
            dbg_outs.append(name)
            return nc.dram_tensor(name, list(shape), dt, kind="ExternalOutput").ap()
        return nc.dram_tensor(name, list(shape), dt, kind="Internal").ap()

    L = 2
    xin = din("xin", [T, D]); cvec = din("cvec", [2, D])
    w_ada = din("w_ada", [L, D, 6 * D]); b_ada = din("b_ada", [L, 6 * D])
    w_in = din("w_in", [L, D, NWIN]); b_in = din("b_in", [L, NWIN])
    sg_ln_g = din("sg_ln_g", [L, 256]); sg_ln_b = din("sg_ln_b", [L, 256])
    sg_w = din("sg_w", [L, 4, 128, 128]); sg_b = din("sg_b", [L, 4, 128])
    q_norm_g = din("q_norm_g", [L, 384]); kv_norm_g = din("kv_norm_g", [L, 256])
    w_uq = din("w_uq", [L, 384, 1024]); w_ukv = din("w_ukv", [L, 256, 1024])
    ml_conv_w = din("ml_conv_w", [L, 5, 512]); ml_conv_b = din("ml_conv_b", [L, 512])
    ml_f_bias = din("ml_f_bias", [L, 8]); ml_norm_g = din("ml_norm_g", [L, 256])
    w_out = din("w_out", [L, D, D]); ln1_g = din("ln1_g", [L, D]); ln1_b = din("ln1_b", [L, D])
    w_router = din("w_router", [L, D, 16])
    w_gate = din("w_gate", [L, 16, D, D]); w_up = din("w_up", [L, 16, D, D]); w_down = din("w_down", [L, 16, D, D])
    ln2_g = din("ln2_g", [L, D]); ln2_b = din("ln2_b", [L, D])
    c_ident = din("c_ident", [128, 128])
    c_rope = din("c_rope", [2, 32, T])
    c_mask = din("c_mask", [2, 128, 64])
    c_misc = din("c_misc", [8, 16])
    c_eye8 = din("c_eye8", [8, 8, NSTEP])
    c_prev = din("c_prev", [NSTEP, NSTEP])
    c_rst = din("c_rst", [8, T])

    y = nc.dram_tensor("y", [TL, D], F32, kind="ExternalOutput").ap()

    xs1 = dscr("xs1", [T, D]); x1_d = dscr("x1_d", [T, D]); hm_d = dscr("hm_d", [T, D]); moe_d = dscr("moe_d", [T, D])
    catT_d = dscr("catT_d", [D, T], BF16)
    qT_d = dscr("qT_d", [8, 96, T], BF16); kT_d = dscr("kT_d", [8, 96, T], BF16)
    vaug_d = dscr("vaug_d", [T, 520], BF16)
    mlqk_d = dscr("mlqk_d", [8, 64, T]); gates_d = dscr("gates_d", [16, T])
    mlv_d = dscr("mlv_d", [T, 256], BF16); sigo_d = dscr("sigo_d", [T, 256])
    ada_d = dscr("ada_d", [96, 128])
    aff_d = dscr("aff_d", [16, T])

    with ExitStack() as top:
        S = Sched(nc, top)
        sbn = [0]

        def sb(es, shape, dt=F32, name=None):
            sbn[0] += 1
            return es.enter_context(nc.sbuf_tensor(f"{name or 't'}_{sbn[0]}", list(shape), dt))

        ps = [top.enter_context(nc.psum_tensor(f"ps{i}", [128, 512], F32)) for i in range(8)]
        PK = [f"ps{i}" for i in range(8)]
        ident = sb(top, [128, 128], F32, "ident")
        ones_bf = sb(top, [128, 512], BF16, "ones_bf")
        ones_f = sb(top, [128, 512], F32, "ones_f")
        zeros_f = sb(top, [128, 1024], F32, "zeros_f")
        epsc = sb(top, [128, 1], F32, "epsc")
        ada = sb(top, [128, 96], F32, "ada")
        adp = sb(top, [128, 96], F32, "adp")

        S.dma('sp', lambda e: e.dma_start(out=ident[:], in_=c_ident), w=['ident'])
        S.op('dve', lambda e: e.memset(ones_bf[:], 1.0), w=['ones_bf'])
        S.op('dve', lambda e: e.memset(ones_f[:], 1.0), w=['ones_f'])
        S.op('dve', lambda e: e.memset(zeros_f[:], 0.0), w=['zeros_f'])
        S.op('dve', lambda e: e.memset(epsc[:], EPS), w=['epsc'])

        def interleave(gens, width):
            active = []; it = iter(gens)
            while True:
                while len(active) < width:
                    g = next(it, None)
                    if g is None: break
                    active.append(g)
                if not active: break
                for g in list(active):
                    try: next(g)
                    except StopIteration: active.remove(g)

        def MM(out, lhsT, rhs, st, sp_, r, w):
            S.op('pe', lambda e: e.matmul(out, lhsT, rhs, start=st, stop=sp_), r, w)

        def TR(out, in_, r, w, n=128):
            S.op('pe', lambda e: e.transpose(out, in_, ident[:n, :n]), list(r) + ['ident'], w)

        def ln_stats(es_tag, xap, width, mv, rstd, nmr, stats, rk):
            nch = width // 256 if width >= 256 else 1
            cw = width // nch
            for c in range(nch):
                S.op('dve', lambda e, c=c: e.bn_stats(stats[:, c, :], xap[:, c * cw:(c + 1) * cw]), r=rk, w=[es_tag + 'st'])
            S.op('dve', lambda e: e.bn_aggr(mv[:], stats[:, 0:nch, :]), r=[es_tag + 'st'], w=[es_tag + 'mv'])
            S.op('act', lambda e: e.activation(out=rstd[:], in_=mv[:, 1:2], func=AF.Sqrt, bias=epsc[:, 0:1], scale=1.0),
                 r=[es_tag + 'mv', 'epsc'], w=[es_tag + 'rs'])
            S.op('dve', lambda e: e.reciprocal(rstd[:], rstd[:]), r=[es_tag + 'rs'], w=[es_tag + 'rs'])
            S.op('dve', lambda e: e.scalar_tensor_tensor(out=nmr[:], in0=mv[:, 0:1], scalar=-1.0, in1=rstd[:], op0=ALU.mult, op1=ALU.mult),
                 r=[es_tag + 'mv', es_tag + 'rs'], w=[es_tag + 'nm'])

        def phase0(l):
            with ExitStack() as es:
                cfm = sb(es, [128, 2, 8]); sfm = sb(es, [128, 2, 8])
                brow = sb(es, [1, 6 * D])
                wa = [sb(es, [128, 8, 768]) for _ in range(2)]
                adaT = sb(es, [96, 128])
                with nc.allow_non_contiguous_dma(reason="tiny"):
                    for m_ in range(2):
                        S.dma('sp', lambda e, m_=m_: e.dma_start(out=cfm[:, m_, :], in_=cvec[m_].rearrange("(c p) -> p c", p=128)), w=['cfm'])
                S.dma('sp', lambda e: e.dma_start(out=brow[:], in_=b_ada[l:l + 1, :]), w=['brow'])
                S.op('act', lambda e: e.activation(out=sfm[:], in_=cfm[:], func=AF.Silu), r=['cfm'], w=['sfm'])
                for blk in range(8):
                    wt = wa[blk % 2]; wk = f'wa{blk % 2}'
                    S.dma('sp', lambda e: e.dma_start(out=wt[:], in_=w_ada[l, :, blk * 768:(blk + 1) * 768].rearrange("(c p) n -> p c n", p=128)), w=[wk])
                    for nn in range(6):
                        n = blk * 6 + nn
                        for c in range(8):
                            MM(ps[0][:, n * 2:(n + 1) * 2], wt[:, c, nn * 128:(nn + 1) * 128], sfm[:, :, c], c == 0, False, [wk, 'sfm'], [PK[0]])
                        MM(ps[0][:, n * 2:(n + 1) * 2], brow[0:1, n * 128:(n + 1) * 128], ones_f[0:1, 0:2], False, True, ['brow', 'ones_f'], [PK[0]])
                S.op('dve', lambda e: e.tensor_copy(ada[:], ps[0][:, 0:96]), r=[PK[0]], w=['ada'])
                S.op('dve', lambda e: e.tensor_scalar(out=adp[:], in0=ada[:], scalar1=1.0, scalar2=None, op0=ALU.add), r=['ada'], w=['adp'])
                TR(ps[1][:96, 0:128], ada[:, :], ['ada'], [PK[1]])
                S.op('dve', lambda e: e.tensor_copy(adaT[:], ps[1][:96, 0:128]), r=[PK[1]], w=['adaT'])
                S.dma('sp', lambda e: e.dma_start(out=ada_d, in_=adaT[:]), r=['adaT'], w=['ada_d'])
                for ti in range(NT):
                    S.dma('sp', lambda e, ti=ti: e.dma_start(out=moe_d[ti * 128:(ti + 1) * 128, :], in_=zeros_f[:]), r=['zeros_f'], w=[('moe', ti)])
                S.barrier()

        def ada_col(j, cc, m):
            n = (j * 8 + cc) * 2 + m
            return n

        def bcast_load(es, q, dst, j, m, key):
            src = ada_d.rearrange("(n m) p -> m n p", m=2)[m, j * 8:(j + 1) * 8, :].partition_broadcast(128)
            S.dma(q, lambda e: e.dma_start(out=dst[:].rearrange("p (c f) -> p c f", c=8), in_=src), r=['ada_d'], w=[key])

        def vec_bcast(q, dst, vec_ap, key):
            S.dma(q, lambda e: e.dma_start(out=dst[:], in_=vec_ap.partition_broadcast(128)), w=[key])

        def blocks():
            out = [(0, 256, [0, 1])]
            for b in range(8):
                out.append((256 + 512 * b, 512, [2 + 4 * b + i for i in range(4)]))
            return out

        def phase1(l, xsrc):
            with ExitStack() as es:
                win = sb(es, [128, 8, NWIN], BF16); brow = sb(es, [1, NWIN], BF16)
                wuq = sb(es, [128, 3, 1024], BF16); wukv = sb(es, [128, 2, 1024], BF16)
                wsT = sb(es, [128, 4, 128], BF16)
                w32 = sb(es, [128, 3, 1024]); wsf = sb(es, [128, 512]); gq = sb(es, [128, 3]); gkv = sb(es, [128, 2])
                sgbT = sb(es, [128, 4]); lng = sb(es, [128, 256]); lnb = sb(es, [128, 256])
                rope = [sb(es, [96, 2, 512]) for _ in range(2)]
                xt = [sb(es, [128, D]) for _ in range(2)]
                xn = [sb(es, [128, D]) for _ in range(2)]
                xmT = [sb(es, [128, 8, 512], BF16) for _ in range(2)]
                st6 = sb(es, [128, 4, 6]); mv = sb(es, [128, 2]); rstd = sb(es, [128, 1]); nmr = sb(es, [128, 1])
                st6b = sb(es, [128, 4, 6]); mvb = sb(es, [128, 2]); rstdb = sb(es, [128, 1]); nmrb = sb(es, [128, 1])
                gl = [sb(es, [128, 512]) for _ in range(2)]
                vn = sb(es, [128, 256]); vnb = sb(es, [128, 256], BF16)
                ya = [sb(es, [128, 256]) for _ in range(2)]
                yaT = [sb(es, [128, 2, 128], BF16) for _ in range(2)]
                mlv = [sb(es, [128, 256], BF16) for _ in range(2)]
                sgo = [sb(es, [128, 256]) for _ in range(2)]
                cqT = sb(es, [128, 3, 512], BF16); ckvT = sb(es, [128, 2, 512], BF16)
                sq = [sb(es, [128, 512]) for _ in range(2)]
                rq = sb(es, [96, 512]); rkv = sb(es, [64, 512]); rkvc = sb(es, [128, 4])
                qs = [sb(es, [96, 512]) for _ in range(2)]; qb = [sb(es, [96, 512]) for _ in range(2)]
                r1 = sb(es, [96, 512]); r2 = sb(es, [96, 512])
                qTt = [sb(es, [96, 512], BF16) for _ in range(2)]
                kTt = sb(es, [96, 8, 512], BF16)
                krt = sb(es, [96, 512], BF16)
                va = [sb(es, [128, 8, 65], BF16) for _ in range(2)]
                fm32 = [sb(es, [64, 512]) for _ in range(2)]

                for c in range(8):
                    S.dma('pool', lambda e, c=c: e.dma_start(out=win[:, c, :], in_=w_in[l, c * 128:(c + 1) * 128, :]), w=['win'])
                S.dma('pool', lambda e: e.dma_start(out=brow[:], in_=b_in[l:l + 1, :]), w=['brow'])
                with nc.allow_non_contiguous_dma(reason="tiny"):
                    S.dma('sp', lambda e: e.dma_start(out=gq[:], in_=q_norm_g[l].rearrange("(c p) -> p c", p=128)), w=['gq'])
                    S.dma('sp', lambda e: e.dma_start(out=gkv[:], in_=kv_norm_g[l].rearrange("(c p) -> p c", p=128)), w=['gkv'])
                    S.dma('sp', lambda e: e.dma_start(out=sgbT[:], in_=sg_b[l].rearrange("g t -> t g")), w=['sgbT'])
                S.dma('sp', lambda e: e.dma_start(out=w32[:], in_=w_uq[l].rearrange("(c p) n -> p c n", p=128)), w=['w32'])
                for c in range(3):
                    S.op('dve', lambda e, c=c: e.tensor_scalar(out=wuq[:, c, :], in0=w32[:, c, :], scalar1=gq[:, c:c + 1], scalar2=None, op0=ALU.mult),
                         r=['w32', 'gq'], w=['wuq'])
                S.dma('sp', lambda e: e.dma_start(out=w32[:, 0:2, :], in_=w_ukv[l].rearrange("(c p) n -> p c n", p=128)), r=[], w=['w32'])
                for c in range(2):
                    S.op('dve', lambda e, c=c: e.tensor_scalar(out=wukv[:, c, :], in0=w32[:, c, :], scalar1=gkv[:, c:c + 1], scalar2=None, op0=ALU.mult),
                         r=['w32', 'gkv'], w=['wukv'])
                S.dma('sp', lambda e: e.dma_start(out=wsf[:].rearrange("p (g s) -> p g s", g=4), in_=sg_w[l].rearrange("g t s -> t g s")), w=['wsf'])
                for g in range(4):
                    TR(ps[0][:, g * 128:(g + 1) * 128], wsf[:, g * 128:(g + 1) * 128], ['wsf'], [PK[0]])
                S.op('dve', lambda e: e.tensor_copy(wsT[:].rearrange("p g t -> p (g t)"), ps[0][:, 0:512]), r=[PK[0]], w=['wsT'])
                vec_bcast('sp', lng, sg_ln_g[l], 'lng'); vec_bcast('sp', lnb, sg_ln_b[l], 'lnb')

                def gen_tm(bi, t0, N, tiles):
                    bp = bi % 2
                    xm = xmT[bp]; xmk = f'xmT{bp}'
                    m_ada = 1 if t0 == 0 else 0
                    S.dma('sp', lambda e, bp=bp: e.dma_start(out=rope[bp][64:96, :, 0:N], in_=c_rope[:, :, t0:t0 + N].rearrange("a p t -> p a t")), w=[f'rope{bp}'])
                    for li, ti in enumerate(tiles):
                        p2 = ti % 2
                        xk = f'xt{p2}'; xnk = f'xn{p2}'
                        S.dma('sp', lambda e, ti=ti, p2=p2: e.dma_start(out=xt[p2][:], in_=xsrc[ti * 128:(ti + 1) * 128, :]), w=[xk])
                        ln_stats('l1', xt[p2], D, mv, rstd, nmr, st6, [xk])
                        yield
                        S.op('act', lambda e, p2=p2: e.activation(out=xn[p2][:], in_=xt[p2][:], func=AF.Identity, bias=nmr[:, 0:1], scale=rstd[:, 0:1]),
                             r=[xk, 'l1rs', 'l1nm'], w=[xnk])
                        yield
                        for half in range(2):
                            pb = ps[half]
                            for c4 in range(4):
                                cc = half * 4 + c4
                                TR(pb[:, c4 * 128:(c4 + 1) * 128], xn[p2][:, cc * 128:(cc + 1) * 128], [xnk], [PK[half]])
                            for c4 in range(4):
                                cc = half * 4 + c4
                                eng = 'act' if c4 % 2 == 0 else 'dve'
                                if eng == 'act':
                                    S.op('act', lambda e, cc=cc, c4=c4, pb=pb, li=li: e.activation(out=xm[:, cc, li * 128:(li + 1) * 128], in_=pb[:, c4 * 128:(c4 + 1) * 128], func=AF.Identity,
                                                                                         bias=ada[:, ada_col(0, cc, m_ada):ada_col(0, cc, m_ada) + 1], scale=adp[:, ada_col(1, cc, m_ada):ada_col(1, cc, m_ada) + 1]),
                                         r=[PK[half], 'ada', 'adp'], w=[xmk])
                                    yield
                                else:
                                    S.op('dve', lambda e, cc=cc, c4=c4, pb=pb, li=li: e.tensor_scalar(out=xm[:, cc, li * 128:(li + 1) * 128], in0=pb[:, c4 * 128:(c4 + 1) * 128],
                                                                                            scalar1=adp[:, ada_col(1, cc, m_ada):ada_col(1, cc, m_ada) + 1], scalar2=ada[:, ada_col(0, cc, m_ada):ada_col(0, cc, m_ada) + 1],
                                                                                            op0=ALU.mult, op1=ALU.add),
                                         r=[PK[half], 'ada', 'adp'], w=[xmk])
                                    yield
                        for c in range(8):
                            MM(ps[2][:, :], xm[:, c, li * 128:(li + 1) * 128], win[:, c, 0:512], c == 0, False, [xmk, 'win'], [PK[2]])
                        MM(ps[2][:, :], ones_bf[0:1, 0:128], brow[0:1, 0:512], False, True, ['ones_bf', 'brow'], [PK[2]])
                        yield
                        g_ = gl[p2]; gk = f'gl{p2}'
                        S.op('act', lambda e, g_=g_: e.activation(out=g_[:], in_=ps[2][:, :], func=AF.Gelu_apprx_tanh), r=[PK[2]], w=[gk])
                        yield
                        ln_stats('l2', g_[:, 256:512], 256, mvb, rstdb, nmrb, st6b, [gk])
                        yield
                        S.op('act', lambda e, g_=g_: e.activation(out=vn[:], in_=g_[:, 256:512], func=AF.Identity, bias=nmrb[:, 0:1], scale=rstdb[:, 0:1]),
                             r=[gk, 'l2rs', 'l2nm'], w=['vn'])
                        yield
                        S.op('dve', lambda e: e.tensor_tensor(out=vn[:], in0=vn[:], in1=lng[:], op=ALU.mult), r=['vn', 'lng'], w=['vn'])
                        yield
                        S.op('dve', lambda e: e.tensor_tensor(out=vnb[:], in0=vn[:], in1=lnb[:], op=ALU.add), r=['vn', 'lnb'], w=['vnb'])
                        yield
                        for g in range(4):
                            MM(ps[3][:, g * 64:(g + 1) * 64], wsT[:, g, :], vnb[:, g * 64:(g + 1) * 64], True, True, ['wsT', 'vnb'], [PK[3]])
                        yield
                        yat = ya[p2]; yak = f'ya{p2}'
                        for g in range(4):
                            S.op('dve', lambda e, g=g, yat=yat, g_=g_: e.scalar_tensor_tensor(out=yat[:, g * 64:(g + 1) * 64], in0=ps[3][:, g * 64:(g + 1) * 64], scalar=sgbT[:, g:g + 1],
                                                                                       in1=g_[:, g * 64:(g + 1) * 64], op0=ALU.add, op1=ALU.mult),
                                 r=[PK[3], 'sgbT', gk], w=[yak])
                            yield
                        for c in range(2):
                            TR(ps[3][:, 256 + c * 128:256 + (c + 1) * 128], yat[:, c * 128:(c + 1) * 128], [yak], [PK[3]])
                        yield
                        yT = yaT[p2]; yTk = f'yaT{p2}'
                        S.op('act', lambda e, yT=yT: e.copy(out=yT[:].rearrange("p c t -> p (c t)"), in_=ps[3][:, 256:512]), r=[PK[3]], w=[yTk])
                        yield
                        S.dma('sp', lambda e, yT=yT, ti=ti: e.dma_start(out=catT_d[0:256, ti * 128:(ti + 1) * 128].rearrange("(c p) t -> p c t", p=128), in_=yT[:]), r=[yTk], w=[('catA', ti)])
                        for c in range(8):
                            MM(ps[4][:, :], xm[:, c, li * 128:(li + 1) * 128], win[:, c, 512:1024], c == 0, False, [xmk, 'win'], [PK[4]])
                        MM(ps[4][:, :], ones_bf[0:1, 0:128], brow[0:1, 512:1024], False, True, ['ones_bf', 'brow'], [PK[4]])
                        yield
                        S.op('dve', lambda e, p2=p2: e.tensor_copy(mlv[p2][:], ps[4][:, 0:256]), r=[PK[4]], w=[f'mlv{p2}'])
                        yield
                        S.op('act', lambda e, p2=p2: e.activation(out=sgo[p2][:], in_=ps[4][:, 256:512], func=AF.Sigmoid), r=[PK[4]], w=[f'sgo{p2}'])
                        yield
                        S.dma('sp', lambda e, p2=p2, ti=ti: e.dma_start(out=mlv_d[ti * 128:(ti + 1) * 128, :], in_=mlv[p2][:]), r=[f'mlv{p2}'], w=[('mlv_d', ti)])
                        S.dma('sp', lambda e, p2=p2, ti=ti: e.dma_start(out=sigo_d[ti * 128:(ti + 1) * 128, :], in_=sgo[p2][:]), r=[f'sgo{p2}'], w=[('sigo_d', ti)])


                def gen_fm(bi, t0, N, tiles):
                    bp = bi % 2; xm = xmT[bp]; xmk = f'xmT{bp}'
                    FM0 = 1024
                    def fm_mm(pst, pk, col0, M, pbase=0):
                        for c in range(8):
                            MM(pst[pbase:pbase + M, 0:N], win[:, c, col0:col0 + M], xm[:, c, 0:N], c == 0, False, ['win', xmk], [pk])
                        MM(pst[pbase:pbase + M, 0:N], brow[0:1, col0:col0 + M], ones_bf[0:1, 0:N], False, True, ['brow', 'ones_bf'], [pk])
                    for c in range(3):
                        fm_mm(ps[5], PK[5], FM0 + c * 128, 128)
                        yield
                        S.op('act', lambda e, c=c: e.copy(out=cqT[:, c, 0:N], in_=ps[5][:, 0:N]), r=[PK[5]], w=['cqT'])
                        yield
                        S.op('act', lambda e, c=c: e.activation(out=sq[c % 2][:, 0:N], in_=ps[5][:, 0:N], func=AF.Square), r=[PK[5]], w=[f'sq{c % 2}'])
                        yield
                        MM(ps[6][0:96, 0:N], ones_f[:, 0:96], sq[c % 2][:, 0:N], c == 0, c == 2, ['ones_f', f'sq{c % 2}'], [PK[6]])
                        yield
                    S.op('act', lambda e: e.activation(out=rq[:, 0:N], in_=ps[6][0:96, 0:N], func=AF.Sqrt, bias=epsc[0:96, 0:1], scale=1.0 / 384), r=[PK[6], 'epsc'], w=['rq'])
                    yield
                    S.op('dve', lambda e: e.reciprocal(rq[:, 0:N], rq[:, 0:N]), r=['rq'], w=['rq'])
                    yield
                    rp = rope[bp]; rpk = f'rope{bp}'
                    for h in range(8):
                        hp = h % 2
                        for c in range(3):
                            MM(ps[5][0:96, 0:N], wuq[:, c, h * 96:(h + 1) * 96], cqT[:, c, 0:N], c == 0, c == 2, ['wuq', 'cqT'], [PK[5]])
                        yield
                        for c in range(3):
                            MM(ps[7][64:96, 0:N], wuq[:, c, 768 + h * 32:768 + (h + 1) * 32], cqT[:, c, 0:N], c == 0, c == 2, ['wuq', 'cqT'], [PK[7]])
                        yield
                        S.op('dve', lambda e, hp=hp: e.tensor_tensor(out=qs[hp][:, 0:N], in0=ps[5][0:96, 0:N], in1=rq[:, 0:N], op=ALU.mult), r=[PK[5], 'rq'], w=[f'qs{hp}'])
                        yield
                        S.op('dve', lambda e, hp=hp: e.tensor_tensor(out=qb[hp][64:96, 0:N], in0=ps[7][64:96, 0:N], in1=rq[64:96, 0:N], op=ALU.mult), r=[PK[7], 'rq'], w=[f'qb{hp}'])
                        yield
                        S.op('act', lambda e, hp=hp: e.copy(out=qTt[hp][0:64, 0:N], in_=qs[hp][0:64, 0:N]), r=[f'qs{hp}'], w=[f'qTt{hp}'])
                        yield
                        S.op('pool', lambda e, hp=hp: e.tensor_tensor(out=r1[64:96, 0:N], in0=qs[hp][64:96, 0:N], in1=rp[64:96, 0, 0:N], op=ALU.mult), r=[f'qs{hp}', rpk], w=['r1'])
                        yield
                        S.op('pool', lambda e, hp=hp: e.tensor_tensor(out=r2[64:96, 0:N], in0=qb[hp][64:96, 0:N], in1=rp[64:96, 1, 0:N], op=ALU.mult), r=[f'qb{hp}', rpk], w=['r2'])
                        yield
                        S.op('pool', lambda e, hp=hp: e.tensor_tensor(out=qTt[hp][64:96, 0:N], in0=r1[64:96, 0:N], in1=r2[64:96, 0:N], op=ALU.add), r=['r1', 'r2'], w=[f'qTt{hp}'])
                        yield
                        S.dma('sp', lambda e, hp=hp, h=h: e.dma_start(out=qT_d[h, :, t0:t0 + N], in_=qTt[hp][:, 0:N]), r=[f'qTt{hp}'], w=[('qT_d', h, t0)])
                    for c in range(2):
                        fm_mm(ps[5], PK[5], FM0 + 384 + c * 128, 128)
                        yield
                        S.op('act', lambda e, c=c: e.copy(out=ckvT[:, c, 0:N], in_=ps[5][:, 0:N]), r=[PK[5]], w=['ckvT'])
                        yield
                        S.op('act', lambda e, c=c: e.activation(out=sq[c][:, 0:N], in_=ps[5][:, 0:N], func=AF.Square), r=[PK[5]], w=[f'sq{c}'])
                        yield
                    for c in range(2):
                        MM(ps[6][0:64, 0:N], ones_f[:, 0:64], sq[c][:, 0:N], c == 0, c == 1, ['ones_f', f'sq{c}'], [PK[6]])
                    yield
                    S.op('act', lambda e: e.activation(out=rkv[:, 0:N], in_=ps[6][0:64, 0:N], func=AF.Sqrt, bias=epsc[0:64, 0:1], scale=1.0 / 256), r=[PK[6], 'epsc'], w=['rkv'])
                    yield
                    S.op('dve', lambda e: e.reciprocal(rkv[:, 0:N], rkv[:, 0:N]), r=['rkv'], w=['rkv'])
                    yield
                    for li in range(len(tiles)):
                        for c in range(2):
                            MM(ps[7][:, li:li + 1], sq[c][:, li * 128:(li + 1) * 128], ones_f[:, 0:1], c == 0, c == 1, [f'sq{c}', 'ones_f'], [PK[7]])
                        yield
                    nl = len(tiles)
                    S.op('act', lambda e: e.activation(out=rkvc[:, 0:nl], in_=ps[7][:, 0:nl], func=AF.Sqrt, bias=epsc[:, 0:1], scale=1.0 / 256), r=[PK[7], 'epsc'], w=['rkvc'])
                    yield
                    S.op('dve', lambda e: e.reciprocal(rkvc[:, 0:nl], rkvc[:, 0:nl]), r=['rkvc'], w=['rkvc'])
                    yield
                    fm_mm(ps[5], PK[5], FM0 + 640, 32, pbase=64)
                    yield
                    fm_mm(ps[7], PK[7], FM0 + 672, 32, pbase=64)
                    yield
                    S.op('dve', lambda e: e.tensor_tensor(out=r1[64:96, 0:N], in0=ps[5][64:96, 0:N], in1=rp[64:96, 0, 0:N], op=ALU.mult), r=[PK[5], rpk], w=['r1'])
                    yield
                    S.op('dve', lambda e: e.tensor_tensor(out=r2[64:96, 0:N], in0=ps[7][64:96, 0:N], in1=rp[64:96, 1, 0:N], op=ALU.mult), r=[PK[7], rpk], w=['r2'])
                    yield
                    S.op('dve', lambda e: e.tensor_tensor(out=krt[64:96, 0:N], in0=r1[64:96, 0:N], in1=r2[64:96, 0:N], op=ALU.add), r=['r1', 'r2'], w=['krt'])
                    yield
                    for h in range(8):
                        for c in range(2):
                            MM(ps[5][0:64, 0:N], wukv[:, c, h * 64:(h + 1) * 64], ckvT[:, c, 0:N], c == 0, c == 1, ['wukv', 'ckvT'], [PK[5]])
                        yield
                        S.op('dve', lambda e, h=h: e.tensor_tensor(out=kTt[0:64, h, 0:N], in0=ps[5][0:64, 0:N], in1=rkv[:, 0:N], op=ALU.mult), r=[PK[5], 'rkv'], w=['kTt'])
                        yield
                        S.op('pool', lambda e, h=h: e.tensor_copy(kTt[64:96, h, 0:N], krt[64:96, 0:N]), r=['krt'], w=['kTt'])
                        yield
                    S.dma('sp', lambda e: e.dma_start(out=kT_d[:, :, t0:t0 + N].rearrange("h p t -> p h t"), in_=kTt[:, :, 0:N]), r=['kTt'], w=[('kT_d', t0)])
                    for li, ti in enumerate(tiles):
                        vp = li % 2
                        for c in range(2):
                            MM(ps[6][:, :], ckvT[:, c, li * 128:(li + 1) * 128], wukv[:, c, 512:1024], c == 0, c == 1, ['ckvT', 'wukv'], [PK[6]])
                        yield
                        S.op('dve', lambda e, vp=vp: e.memset(va[vp][:, :, 64:65], 1.0), w=[f'va{vp}'])
                        yield
                        S.op('dve', lambda e, vp=vp, li=li: e.tensor_scalar(out=va[vp][:, :, 0:64], in0=ps[6][:, :].rearrange("p (h d) -> p h d", h=8), scalar1=rkvc[:, li:li + 1], scalar2=None, op0=ALU.mult),
                             r=[PK[6], 'rkvc'], w=[f'va{vp}'])
                        yield
                        S.dma('sp', lambda e, vp=vp, ti=ti: e.dma_start(out=vaug_d[ti * 128:(ti + 1) * 128, :], in_=va[vp][:].rearrange("p h d -> p (h d)")), r=[f'va{vp}'], w=[('vaug_d', ti)])
                    for g in range(8):
                        fp_ = g % 2
                        fm_mm(ps[5], PK[5], FM0 + 704 + g * 64, 64)
                        yield
                        S.op('act', lambda e, fp_=fp_: e.copy(out=fm32[fp_][:, 0:N], in_=ps[5][0:64, 0:N]), r=[PK[5]], w=[f'fm32{fp_}'])
                        yield
                        S.dma('sp', lambda e, fp_=fp_, g=g: e.dma_start(out=mlqk_d[g, :, t0:t0 + N], in_=fm32[fp_][:, 0:N]), r=[f'fm32{fp_}'], w=[('mlqk_d', g, t0)])
                    fm_mm(ps[5], PK[5], FM0 + 1216, 16)
                    yield
                    S.op('act', lambda e: e.copy(out=fm32[0][0:16, 0:N], in_=ps[5][0:16, 0:N]), r=[PK[5]], w=['fm320'])
                    yield
                    S.dma('sp', lambda e: e.dma_start(out=gates_d[:, t0:t0 + N], in_=fm32[0][0:16, 0:N]), r=['fm320'], w=[('gates_d', t0)])


                def run_all(g):
                    for _ in g: pass

                def pair(g1, g2):
                    a1 = True; a2 = g2 is not None
                    while a1 or a2:
                        if a1:
                            try: next(g1)
                            except StopIteration: a1 = False
                        if a2:
                            try: next(g2)
                            except StopIteration: a2 = False
                blks = blocks()
                run_all(gen_tm(0, *blks[0]))
                for bi_ in range(len(blks)):
                    nxt = gen_tm(bi_ + 1, *blks[bi_ + 1]) if bi_ + 1 < len(blks) else None
                    pair(gen_fm(bi_, *blks[bi_]), nxt)
                S.barrier()

        def phase2(l, with_ctx_q):
            with ExitStack() as es:
                vaug = sb(es, [128, NT, 520], BF16)
                kT = [sb(es, [96, T], BF16) for _ in range(2)]
                qT = [sb(es, [96, T], BF16) for _ in range(2)]
                pT = [sb(es, [128, 512], BF16) for _ in range(3)]
                oT = sb(es, [65, 512]); sel = sb(es, [65, 64]); rec = sb(es, [64, 512])
                on = [sb(es, [64, 512], BF16) for _ in range(2)]
                S.dma('sp', lambda e: e.dma_start(out=vaug[:], in_=vaug_d.rearrange("(t p) c -> p t c", p=128)), w=['vaug'])
                S.op('dve', lambda e: e.memset(sel[:], 0.0), w=['sel'])
                S.op('dve', lambda e: e.memset(sel[64:65, :], 1.0), w=['sel'])
                def load_head(h):
                    hp = h % 2
                    S.dma('sp', lambda e: e.dma_start(out=kT[hp][:], in_=kT_d[h]), w=[f'kT{hp}'])
                    S.dma('sp', lambda e: e.dma_start(out=qT[hp][:], in_=qT_d[h]), w=[f'qT{hp}'])
                items = []
                for h in range(8):
                    for bi_, (t0, N, tiles) in enumerate(blocks()):
                        if t0 == 0 and not with_ctx_q: continue
                        ktiles = [0, 1] if t0 == 0 else list(range(NT))
                        for i, kt in enumerate(ktiles):
                            items.append((h, t0, N, i, kt, len(ktiles)))
                LAG = 2
                blk_ctr = [0]

                def do_pv(n):
                    h, t0, N, i, kt, nk = items[n]
                    hp = h % 2; pk = n % 3
                    ob = blk_ctr[0] % 2
                    MM(ps[ob][0:65, 0:N], vaug[:, kt, h * 65:(h + 1) * 65], pT[pk][:, 0:N], i == 0, i == nk - 1, ['vaug', f'pT{pk}'], [PK[ob]])
                    if i == nk - 1:
                        blk_ctr[0] += 1
                        S.op('dve', lambda e: e.tensor_copy(oT[:, 0:N], ps[ob][0:65, 0:N]), r=[PK[ob]], w=['oT'])
                        MM(ps[5][0:64, 0:N], sel[:, :], oT[:, 0:N], True, True, ['sel', 'oT'], [PK[5]])
                        S.op('dve', lambda e: e.reciprocal(rec[:, 0:N], ps[5][0:64, 0:N]), r=[PK[5]], w=['rec'])
                        op_ = blk_ctr[0] % 2
                        S.op('dve', lambda e: e.tensor_tensor(out=on[op_][:, 0:N], in0=oT[0:64, 0:N], in1=rec[:, 0:N], op=ALU.mult), r=['oT', 'rec'], w=[f'on{op_}'])
                        S.dma('sp', lambda e: e.dma_start(out=catT_d[256 + h * 64:256 + (h + 1) * 64, t0:t0 + N], in_=on[op_][:, 0:N]), r=[f'on{op_}'], w=[('catB', h, t0)])

                load_head(0)
                cur_h = -1
                for n, (h, t0, N, i, kt, nk) in enumerate(items):
                    if h != cur_h:
                        cur_h = h
                        if h + 1 < 8: load_head(h + 1)
                    hp = h % 2; sbk = 2 + n % 3; pk = n % 3
                    MM(ps[sbk][:, 0:N], kT[hp][:, kt * 128:(kt + 1) * 128], qT[hp][:, t0:t0 + N], True, True, [f'kT{hp}', f'qT{hp}'], [PK[sbk]])
                    S.op('act', lambda e, sbk=sbk, pk=pk, N=N: e.activation(out=pT[pk][:, 0:N], in_=ps[sbk][:, 0:N], func=AF.Exp, scale=MLA_SCALE), r=[PK[sbk]], w=[f'pT{pk}'])
                    if n >= LAG: do_pv(n - LAG)
                for n in range(max(0, len(items) - LAG), len(items)):
                    do_pv(n)
                S.barrier()

        def chunk_of(d, p):
            if d == 0: return p
            return 3 - p if p < 4 else 71 - p

        def phase3(l):
            with ExitStack() as es:
                qT = sb(es, [64, 4, T], BF16); kT = sb(es, [64, 4, T], BF16)
                ktok = sb(es, [128, NT, 256], BF16)
                vaug = sb(es, [128, NT, 4, 65], BF16)
                cw = sb(es, [64, 5, 8]); cb = sb(es, [64, 8])
                cols = sb(es, [128, NT, 24]); bc = sb(es, [64, 3, 8, NSTEP])
                msk = sb(es, [128, 2, 64]); misc = sb(es, [8, 16]); eye8 = sb(es, [8, 8, NSTEP]); prev = sb(es, [NSTEP, NSTEP])
                fbn = sb(es, [8, 1])
                S.dma('sp', lambda e: e.dma_start(out=msk[:], in_=c_mask.rearrange("d p t -> p d t")), w=['msk'])
                S.dma('sp', lambda e: e.dma_start(out=misc[:], in_=c_misc), w=['misc'])
                S.dma('sp', lambda e: e.dma_start(out=eye8[:], in_=c_eye8), w=['eye8'])
                S.dma('sp', lambda e: e.dma_start(out=prev[:], in_=c_prev), w=['prev'])
                with nc.allow_non_contiguous_dma(reason="tiny"):
                    for j_ in range(5):
                        S.dma('sp', lambda e, j_=j_: e.dma_start(out=cw[:, j_, :], in_=ml_conv_w[l, j_].rearrange("(g p) -> p g", p=64)), w=['cw'])
                    S.dma('sp', lambda e: e.dma_start(out=cb[:], in_=ml_conv_b[l].rearrange("(g p) -> p g", p=64)), w=['cb'])
                    S.dma('sp', lambda e: e.dma_start(out=fbn[:], in_=ml_f_bias[l].rearrange("(a b) -> a b", b=1)), w=['fbn'])
                S.op('dve', lambda e: e.tensor_scalar(out=fbn[:], in0=fbn[:], scalar1=-1.0, scalar2=None, op0=ALU.mult), r=['fbn'], w=['fbn'])
                for ti in range(NT):
                    S.dma('sp', lambda e, ti=ti: e.dma_start(out=vaug[:, ti, :, 0:64], in_=mlv_d[ti * 128:(ti + 1) * 128, :].rearrange("p (h d) -> p h d", h=4)), w=['vaug'])
                S.op('dve', lambda e: e.memset(vaug[:, :, :, 64:65], 1.0), w=['vaug'])
                with ExitStack() as es2:
                    pre = [sb(es2, [64, 4360])] * 2
                    acc = [sb(es2, [64, 4356])] * 2
                    for g in range(8):
                        gp = 0; pr = pre[gp]; ac = acc[gp]; prk = f'pre{gp}'; ack = f'acc{gp}'
                        S.op('pool', lambda e, pr=pr: e.memset(pr[:, 0:2], 0.0), w=[prk])
                        S.op('pool', lambda e, pr=pr: e.memset(pr[:, 258:262], 0.0), w=[prk])
                        S.op('pool', lambda e, pr=pr: e.memset(pr[:, 4358:4360], 0.0), w=[prk])
                        S.dma('sp', lambda e, pr=pr, g=g: e.dma_start(out=pr[:, 2:258], in_=mlqk_d[g, :, 0:256]), w=[prk])
                        S.dma('sp', lambda e, pr=pr, g=g: e.dma_start(out=pr[:, 262:4358], in_=mlqk_d[g, :, 256:T]), w=[prk])
                        S.op('dve', lambda e, pr=pr, ac=ac, g=g: e.tensor_scalar(out=ac[:], in0=pr[:, 0:4356], scalar1=cw[:, 0, g:g + 1], scalar2=cb[:, g:g + 1], op0=ALU.mult, op1=ALU.add),
                             r=[prk, 'cw', 'cb'], w=[ack])
                        for j in range(1, 5):
                            S.op('dve', lambda e, pr=pr, ac=ac, g=g, j=j: e.scalar_tensor_tensor(out=ac[:], in0=pr[:, j:j + 4356], scalar=cw[:, j, g:g + 1], in1=ac[:], op0=ALU.mult, op1=ALU.add),
                                 r=[prk, 'cw', ack], w=[ack])
                        S.op('act', lambda e, ac=ac: e.activation(out=ac[:], in_=ac[:], func=AF.Silu), r=[ack], w=[ack])
                        dst = qT if g < 4 else kT; dk = 'qT3' if g < 4 else 'kT3'; h = g % 4
                        sc = 1.0 if g < 4 else 0.125
                        S.op('act', lambda e, ac=ac, dst=dst, h=h, sc=sc: e.mul(out=dst[:, h, 0:256], in_=ac[:, 0:256], mul=sc), r=[ack], w=[dk])
                        S.op('act', lambda e, ac=ac, dst=dst, h=h, sc=sc: e.mul(out=dst[:, h, 256:T], in_=ac[:, 260:4356], mul=sc), r=[ack], w=[dk])
                        if g >= 4:
                            S.op('act', lambda e, ac=ac: e.mul(out=ac[:], in_=ac[:], mul=0.125), r=[ack], w=[ack])
                            for ti in range(NT):
                                c0 = ti * 128 if ti < 2 else ti * 128 + 4
                                pb = 1 + ti % 2
                                TR(ps[pb][:, 0:64], ac[:, c0:c0 + 128], [ack], [PK[pb]], n=64)
                                if ti % 2:
                                    S.op('dve', lambda e, ti=ti, pb=pb, h=h: e.tensor_copy(ktok[:, ti, h * 64:(h + 1) * 64], ps[pb][:, 0:64]), r=[PK[pb]], w=['ktok'])
                                else:
                                    S.op('act', lambda e, ti=ti, pb=pb, h=h: e.copy(out=ktok[:, ti, h * 64:(h + 1) * 64], in_=ps[pb][:, 0:64]), r=[PK[pb]], w=['ktok'])
                    S.barrier()
                with ExitStack() as es2:
                    SEG = 2176; CH = 34
                    Bc = sb(es2, [8, T]); U = sb(es2, [8, T])
                    G = [sb(es2, [8, SEG]) for _ in range(4)]
                    tot = sb(es2, [8, NSTEP]); umax = sb(es2, [8, NSTEP])
                    sm = {k: sb(es2, [8, NSTEP], name='sm_' + k) for k in ['be_s', 'ml_s', 'um_s', 'm', 'mprev', 'a', 's', 'mm_s', 'inter', 'mm_n', 't1', 't2']}
                    xT = sb(es2, [NSTEP, 8]); exp3 = sb(es2, [8, 8, NSTEP])
                    for sg_ in range(2):
                        o = sg_ * SEG; co = sg_ * CH
                        LI = G[0]; LF = G[1]; rst = G[2]; TMP = G[3]
                        S.dma('sp', lambda e: e.dma_start(out=rst[:], in_=c_rst[:, o:o + SEG]), w=['G2'])
                        for d in range(2):
                            S.dma('sp', lambda e, d=d: e.dma_start(out=LI[d * 4:(d + 1) * 4, :], in_=gates_d[d * 8:d * 8 + 4, o:o + SEG]), w=['G0'])
                            S.dma('sp', lambda e, d=d: e.dma_start(out=LF[d * 4:(d + 1) * 4, :], in_=gates_d[d * 8 + 4:d * 8 + 8, o:o + SEG]), w=['G1'])
                        S.op('act', lambda e: e.activation(out=LF[:], in_=LF[:], func=AF.Exp, bias=fbn[:, 0:1], scale=-1.0), r=['G1', 'fbn'], w=['G1'])
                        S.op('act', lambda e: e.activation(out=LF[:], in_=LF[:], func=AF.Ln, bias=ones_f[0:8, 0:1], scale=1.0), r=['G1', 'ones_f'], w=['G1'])
                        S.op('dve', lambda e: e.tensor_scalar(out=LF[:], in0=LF[:], scalar1=-1.0, scalar2=None, op0=ALU.mult), r=['G1'], w=['G1'])
                        Bs = Bc[:, o:o + SEG]; Us = U[:, o:o + SEG]
                        S.op('dve', lambda e: e.tensor_tensor_scan(out=Bs, data0=rst[:], data1=LF[:], initial=0.0, op0=ALU.mult, op1=ALU.add), r=['G2', 'G1'], w=['Bc'])
                        S.op('dve', lambda e: e.tensor_reduce(out=tot[:, co:co + CH], in_=LF[:].rearrange("p (c t) -> p c t", t=64), axis=AX.X, op=ALU.add), r=['G1'], w=['tot'])
                        S.op('dve', lambda e: e.tensor_tensor(out=TMP[:].rearrange("p (c t) -> p c t", t=64), in0=LF[:].rearrange("p (c t) -> p c t", t=64),
                                                              in1=tot[:, co:co + CH].unsqueeze(2).to_broadcast([8, CH, 64]), op=ALU.add), r=['G1', 'tot'], w=['G3'])
                        S.op('dve', lambda e: e.tensor_scalar(out=TMP[:], in0=TMP[:], scalar1=misc[:, 0:1], scalar2=None, op0=ALU.mult), r=['G3', 'misc'], w=['G3'])
                        S.op('dve', lambda e: e.scalar_tensor_tensor(out=Bs, in0=Bs, scalar=misc[:, 1:2], in1=TMP[:], op0=ALU.mult, op1=ALU.add), r=['Bc', 'misc', 'G3'], w=['Bc'])
                        S.op('dve', lambda e: e.tensor_tensor(out=Us, in0=LI[:], in1=Bs, op=ALU.subtract), r=['G0', 'Bc'], w=['U'])
                        S.op('dve', lambda e: e.tensor_reduce(out=umax[:, co:co + CH], in_=Us.rearrange("p (c t) -> p c t", t=64), axis=AX.X, op=ALU.max), r=['U'], w=['umax'])

                    def to_scan_order(dst, src, sk, dk):
                        TR(ps[1][0:NSTEP, 0:8], src[:, :], [sk], [PK[1]], n=8)
                        S.op('dve', lambda e: e.tensor_copy(xT[:], ps[1][0:NSTEP, 0:8]), r=[PK[1]], w=['xT'])
                        MM(ps[2][0:8, 0:NSTEP], xT[:, :], prev[:, :], True, True, ['xT', 'prev'], [PK[2]])
                        S.op('dve', lambda e: e.tensor_scalar(out=sm['t1'][:], in0=ps[2][0:8, 0:NSTEP], scalar1=misc[:, 0:1], scalar2=None, op0=ALU.mult), r=[PK[2], 'misc'], w=['t1'])
                        S.op('dve', lambda e: e.scalar_tensor_tensor(out=dst[:], in0=src[:], scalar=misc[:, 2:3], in1=sm['t1'][:], op0=ALU.mult, op1=ALU.add), r=[sk, 'misc', 't1'], w=[dk])

                    to_scan_order(sm['be_s'], tot, 'tot', 'be_s')
                    to_scan_order(sm['um_s'], umax, 'umax', 'um_s')
                    S.op('dve', lambda e: e.tensor_tensor(out=sm['ml_s'][:], in0=sm['be_s'][:], in1=sm['um_s'][:], op=ALU.add), r=['be_s', 'um_s'], w=['ml_s'])
                    S.op('dve', lambda e: e.tensor_tensor_scan(out=sm['m'][:], data0=sm['be_s'][:], data1=sm['ml_s'][:], initial=0.0, op0=ALU.add, op1=ALU.max), r=['be_s', 'ml_s'], w=['m'])
                    S.op('dve', lambda e: e.memset(sm['mprev'][:, 0:1], 0.0), w=['mprev'])
                    S.op('dve', lambda e: e.tensor_copy(sm['mprev'][:, 1:NSTEP], sm['m'][:, 0:NSTEP - 1]), r=['m'], w=['mprev'])
                    S.op('dve', lambda e: e.tensor_tensor(out=sm['t2'][:], in0=sm['be_s'][:], in1=sm['mprev'][:], op=ALU.add), r=['be_s', 'mprev'], w=['t2'])
                    S.op('dve', lambda e: e.tensor_tensor(out=sm['t2'][:], in0=sm['t2'][:], in1=sm['m'][:], op=ALU.subtract), r=['t2', 'm'], w=['t2'])
                    S.op('act', lambda e: e.activation(out=sm['a'][:], in_=sm['t2'][:], func=AF.Exp), r=['t2'], w=['a'])
                    S.op('dve', lambda e: e.tensor_tensor(out=sm['t2'][:], in0=sm['ml_s'][:], in1=sm['m'][:], op=ALU.subtract), r=['ml_s', 'm', 'a'], w=['t2'])
                    S.op('act', lambda e: e.activation(out=sm['s'][:], in_=sm['t2'][:], func=AF.Exp), r=['t2'], w=['s'])
                    S.op('dve', lambda e: e.tensor_tensor(out=sm['mm_s'][:], in0=sm['mprev'][:], in1=sm['um_s'][:], op=ALU.max), r=['mprev', 'um_s'], w=['mm_s'])
                    S.op('dve', lambda e: e.tensor_tensor(out=sm['t2'][:], in0=sm['mprev'][:], in1=sm['mm_s'][:], op=ALU.subtract), r=['mprev', 'mm_s', 's'], w=['t2'])
                    S.op('act', lambda e: e.activation(out=sm['inter'][:], in_=sm['t2'][:], func=AF.Exp), r=['t2'], w=['inter'])
                    to_scan_order(sm['mm_n'], sm['mm_s'], 'mm_s', 'mm_n')
                    for wi, nm in enumerate(['a', 's', 'inter']):
                        S.op('dve', lambda e, nm=nm: e.tensor_tensor(out=exp3[:], in0=eye8[:], in1=sm[nm][:].unsqueeze(1).to_broadcast([8, 8, NSTEP]), op=ALU.mult), r=['eye8', nm], w=['exp3'])
                        for hf in range(2):
                            MM(ps[3][0:64, 0:4 * NSTEP], ones_f[0:8, 0:64], exp3[:, hf * 4:(hf + 1) * 4, :].rearrange("p a b -> p (a b)"), True, True, ['ones_f', 'exp3'], [PK[3]])
                            S.op('dve', lambda e, wi=wi, hf=hf: e.tensor_copy(bc[:, wi, hf * 4:(hf + 1) * 4, :].rearrange("p a b -> p (a b)"), ps[3][0:64, 0:4 * NSTEP]), r=[PK[3]], w=['bc'])
                    for sg_ in range(2):
                        o = sg_ * SEG; co = sg_ * CH
                        TMP = G[3]; RW = G[0:3]
                        u3 = U[:, o:o + SEG].rearrange("p (c t) -> p c t", t=64); b3 = Bc[:, o:o + SEG].rearrange("p (c t) -> p c t", t=64)
                        t3 = TMP[:].rearrange("p (c t) -> p c t", t=64)
                        S.op('dve', lambda e: e.tensor_tensor(out=t3, in0=u3, in1=umax[:, co:co + CH].unsqueeze(2).to_broadcast([8, CH, 64]), op=ALU.subtract), r=['U', 'umax'], w=['G3'])
                        S.op('act', lambda e: e.activation(out=RW[0][:], in_=TMP[:], func=AF.Exp), r=['G3'], w=['G0'])
                        S.op('dve', lambda e: e.tensor_tensor(out=t3, in0=u3, in1=sm['mm_n'][:, co:co + CH].unsqueeze(2).to_broadcast([8, CH, 64]), op=ALU.subtract), r=['U', 'mm_n'], w=['G3'])
                        S.op('act', lambda e: e.activation(out=RW[1][:], in_=TMP[:], func=AF.Exp), r=['G3'], w=['G1'])
                        S.op('dve', lambda e: e.tensor_tensor(out=t3, in0=b3, in1=sm['mm_n'][:, co:co + CH].unsqueeze(2).to_broadcast([8, CH, 64]), op=ALU.add), r=['Bc', 'mm_n'], w=['G3'])
                        S.op('act', lambda e: e.activation(out=RW[2][:], in_=TMP[:], func=AF.Exp, scale=-1.0), r=['G3'], w=['G2'])
                        for tl in range(17):
                            ti = sg_ * 17 + tl
                            pb = 1 + ti % 2
                            for k3 in range(3):
                                TR(ps[pb][:, k3 * 8:(k3 + 1) * 8], RW[k3][:, tl * 128:(tl + 1) * 128], [f'G{k3}'], [PK[pb]], n=8)
                            S.op('dve', lambda e, ti=ti, pb=pb: e.tensor_copy(cols[:, ti, :], ps[pb][:, 0:24]), r=[PK[pb]], w=['cols'])
                    S.barrier()
                hsum = sb(es, [128, NT, 256])
                S.op('pool', lambda e: e.memset(hsum[:], 0.0), w=[('hsum', ti_) for ti_ in range(NT)])
                Cst = sb(es, [64, 8, 65]); C0b = [sb(es, [64, 4, 65], BF16) for _ in range(2)]
                tmpC_ = [sb(es, [64, 4, 65]) for _ in range(2)]
                wv = [sb(es, [128, 4, 65], BF16) for _ in range(2)]
                tS = [sb(es, [128, 4, 64]) for _ in range(2)]
                pTm = [sb(es, [128, 4, 64], BF16) for _ in range(2)]
                tI_ = [sb(es, [128, 260]) for _ in range(2)]; tH_ = [sb(es, [128, 260]) for _ in range(2)]
                dn_ = [sb(es, [128, 4]) for _ in range(2)]; hd = [sb(es, [128, 4, 64]) for _ in range(2)]
                S.op('dve', lambda e: e.memset(Cst[:], 0.0), w=['Cst0', 'Cst1'])
                def chain(d):
                    for p in range(NSTEP):
                        c = chunk_of(d, p); ti = c // 2; hb = c % 2; P0 = hb * 64; P1 = P0 + 64
                        t0 = c * 64
                        ip = d
                        tmpC = tmpC_[d]; tI = tI_[d]; tH = tH_[d]; dn = dn_[d]
                        pC = ps[d * 4]; pS = ps[d * 4 + 1]; pA = ps[d * 4 + 2]; pB = ps[d * 4 + 3]
                        kC = PK[d * 4]; kS = PK[d * 4 + 1]; kA = PK[d * 4 + 2]; kB = PK[d * 4 + 3]
                        Ck = f'Cst{d}'; tCk = f'tmpC{d}'; tIk = f'tI{d}'; tHk = f'tH{d}'; dnk = f'dn{d}'
                        S.op('dve', lambda e, ip=ip, ti=ti, d=d, P0=P0, P1=P1: e.tensor_tensor(out=wv[ip][P0:P1], in0=vaug[P0:P1, ti], in1=cols[P0:P1, ti, d * 4:(d + 1) * 4].unsqueeze(2).to_broadcast([64, 4, 65]), op=ALU.mult),
                             r=['vaug', 'cols'], w=[f'wv{ip}'])
                        yield
                        for h in range(4):
                            MM(pC[0:64, h * 65:(h + 1) * 65], ktok[P0:P1, ti, h * 64:(h + 1) * 64], wv[ip][P0:P1, h, :], True, True, ['ktok', f'wv{ip}'], [kC])
                        yield
                        S.op('dve', lambda e, ip=ip, d=d, p=p: e.tensor_tensor(out=C0b[ip][:], in0=Cst[:, d * 4:(d + 1) * 4, :], in1=bc[:, 2, d * 4:(d + 1) * 4, p].unsqueeze(2).to_broadcast([64, 4, 65]), op=ALU.mult),
                             r=[Ck, 'bc'], w=[f'C0b{ip}'])
                        yield
                        S.op('dve', lambda e, d=d, p=p: e.tensor_tensor(out=tmpC[:], in0=pC[0:64, 0:260].rearrange("p (h c) -> p h c", h=4), in1=bc[:, 1, d * 4:(d + 1) * 4, p].unsqueeze(2).to_broadcast([64, 4, 65]), op=ALU.mult),
                             r=[kC, 'bc'], w=[tCk])
                        yield
                        S.op('dve', lambda e, d=d, p=p: e.tensor_tensor(out=Cst[:, d * 4:(d + 1) * 4, :], in0=Cst[:, d * 4:(d + 1) * 4, :], in1=bc[:, 0, d * 4:(d + 1) * 4, p].unsqueeze(2).to_broadcast([64, 4, 65]), op=ALU.mult),
                             r=[Ck, 'bc'], w=[Ck])
                        yield
                        S.op('dve', lambda e, d=d: e.tensor_tensor(out=Cst[:, d * 4:(d + 1) * 4, :], in0=Cst[:, d * 4:(d + 1) * 4, :], in1=tmpC[:], op=ALU.add), r=[Ck, tCk], w=[Ck])
                        yield
                        for h in range(4):
                            MM(pS[P0:P1, h * 64:(h + 1) * 64], kT[:, h, t0:t0 + 64], qT[:, h, t0:t0 + 64], True, True, ['kT3', 'qT3'], [kS])
                        yield
                        S.op('dve', lambda e, ip=ip, ti=ti, d=d, P0=P0, P1=P1: e.tensor_tensor(out=tS[ip][P0:P1], in0=pS[P0:P1, 0:256].rearrange("p (h t) -> p h t", h=4),
                                                                                       in1=cols[P0:P1, ti, 8 + d * 4:8 + (d + 1) * 4].unsqueeze(2).to_broadcast([64, 4, 64]), op=ALU.mult),
                             r=[kS, 'cols'], w=[f'tS{ip}'])
                        yield
                        S.op('pool', lambda e, ip=ip, d=d, P0=P0, P1=P1: e.tensor_tensor(out=pTm[ip][P0:P1], in0=tS[ip][P0:P1], in1=msk[P0:P1, d, :].unsqueeze(1).to_broadcast([64, 4, 64]), op=ALU.mult),
                             r=[f'tS{ip}', 'msk'], w=[f'pTm{ip}'])
                        yield
                        for h in range(4):
                            MM(pA[P0:P1, h * 65:(h + 1) * 65], pTm[ip][P0:P1, h, :], vaug[P0:P1, ti, h, :], True, True, [f'pTm{ip}', 'vaug'], [kA])
                        yield
                        for h in range(4):
                            MM(pB[P0:P1, h * 65:(h + 1) * 65], qT[:, h, t0:t0 + 64], C0b[ip][:, h, :], True, True, ['qT3', f'C0b{ip}'], [kB])
                        yield
                        S.op('act', lambda e, P0=P0, P1=P1: e.copy(out=tI[P0:P1, :], in_=pB[P0:P1, 0:260]), r=[kB], w=[tIk])
                        yield
                        S.op('dve', lambda e, P0=P0, P1=P1: e.tensor_tensor(out=tH[P0:P1, :], in0=pA[P0:P1, 0:260], in1=tI[P0:P1, :], op=ALU.add), r=[kA, tIk], w=[tHk])
                        yield
                        ph = tH[P0:P1, :].rearrange("p (h c) -> p h c", h=4)
                        S.op('act', lambda e, ph=ph, P0=P0, P1=P1: e.activation(out=dn[P0:P1, :], in_=ph[:, :, 64], func=AF.Abs), r=[tHk], w=[dnk])
                        yield
                        S.op('dve', lambda e, ti=ti, d=d, P0=P0, P1=P1: e.tensor_tensor(out=dn[P0:P1, :], in0=dn[P0:P1, :], in1=cols[P0:P1, ti, 16 + d * 4:16 + (d + 1) * 4], op=ALU.max), r=[dnk, 'cols'], w=[dnk])
                        yield
                        S.op('dve', lambda e, P0=P0, P1=P1: e.reciprocal(dn[P0:P1, :], dn[P0:P1, :]), r=[dnk], w=[dnk])
                        yield
                        S.op('dve', lambda e, ip=ip, ph=ph, P0=P0, P1=P1: e.tensor_tensor(out=hd[ip][P0:P1], in0=ph[:, :, 0:64], in1=dn[P0:P1, :].unsqueeze(2).to_broadcast([64, 4, 64]), op=ALU.mult),
                             r=[tHk, dnk], w=[f'hd{ip}'])
                        yield
                        S.op('pool', lambda e, ip=ip, ti=ti, P0=P0, P1=P1: e.tensor_tensor(out=hsum[P0:P1, ti, :], in0=hsum[P0:P1, ti, :], in1=hd[ip][P0:P1].rearrange("p h d -> p (h d)"), op=ALU.add),
                             r=[('hsum', ti), f'hd{ip}'], w=[('hsum', ti)])
                        yield

                interleave([chain(0), chain(1)], 2)
                with ExitStack() as es2:
                    ngb = sb(es2, [128, 256]); so = [sb(es2, [128, 256]) for _ in range(2)]
                    mu = sb(es2, [128, 4]); var = sb(es2, [128, 4]); cen = sb(es2, [128, 4, 64]); sqq = sb(es2, [128, 4, 64])
                    yc = [sb(es2, [128, 256]) for _ in range(2)]; ycT = [sb(es2, [128, 2, 128], BF16) for _ in range(2)]
                    vec_bcast('sp', ngb, ml_norm_g[l], 'ngb')
                    for ti in range(NT):
                        p2 = ti % 2
                        S.dma('sp', lambda e, ti=ti, p2=p2: e.dma_start(out=so[p2][:], in_=sigo_d[ti * 128:(ti + 1) * 128, :]), w=[f'so{p2}'])
                        h3 = hsum[:, ti, :].rearrange("p (h d) -> p h d", h=4)
                        S.op('dve', lambda e, h3=h3: e.tensor_reduce(out=mu[:], in_=h3, axis=AX.X, op=ALU.add), r=[('hsum', ti)], w=['mu'])
                        S.op('dve', lambda e: e.tensor_scalar(out=mu[:], in0=mu[:], scalar1=1.0 / 64, scalar2=None, op0=ALU.mult), r=['mu'], w=['mu'])
                        S.op('dve', lambda e, h3=h3: e.tensor_tensor(out=cen[:], in0=h3, in1=mu[:].unsqueeze(2).to_broadcast([128, 4, 64]), op=ALU.subtract), r=[('hsum', ti), 'mu'], w=['cen'])
                        S.op('dve', lambda e: e.tensor_tensor(out=sqq[:], in0=cen[:], in1=cen[:], op=ALU.mult), r=['cen'], w=['sqq'])
                        S.op('dve', lambda e: e.tensor_reduce(out=var[:], in_=sqq[:], axis=AX.X, op=ALU.add), r=['sqq'], w=['var'])
                        S.op('act', lambda e: e.activation(out=var[:], in_=var[:], func=AF.Sqrt, bias=epsc[:, 0:1], scale=1.0 / 64), r=['var', 'epsc'], w=['var'])
                        S.op('dve', lambda e: e.reciprocal(var[:], var[:]), r=['var'], w=['var'])
                        S.op('dve', lambda e: e.tensor_tensor(out=cen[:], in0=cen[:], in1=var[:].unsqueeze(2).to_broadcast([128, 4, 64]), op=ALU.mult), r=['cen', 'var'], w=['cen'])
                        S.op('dve', lambda e: e.tensor_tensor(out=cen[:].rearrange("p h d -> p (h d)"), in0=cen[:].rearrange("p h d -> p (h d)"), in1=ngb[:], op=ALU.mult), r=['cen', 'ngb'], w=['cen'])
                        S.op('dve', lambda e, p2=p2: e.tensor_tensor(out=yc[p2][:], in0=cen[:].rearrange("p h d -> p (h d)"), in1=so[p2][:], op=ALU.mult), r=['cen', f'so{p2}'], w=[f'yc{p2}'])
                        for c in range(2):
                            TR(ps[1 + p2][:, c * 128:(c + 1) * 128], yc[p2][:, c * 128:(c + 1) * 128], [f'yc{p2}'], [PK[1 + p2]])
                        S.op('act', lambda e, p2=p2: e.copy(out=ycT[p2][:].rearrange("p c t -> p (c t)"), in_=ps[1 + p2][:, 0:256]), r=[PK[1 + p2]], w=[f'ycT{p2}'])
                        S.dma('sp', lambda e, p2=p2, ti=ti: e.dma_start(out=catT_d[768:1024, ti * 128:(ti + 1) * 128].rearrange("(c p) t -> p c t", p=128), in_=ycT[p2][:]), r=[f'ycT{p2}'], w=[('catC', ti)])
                S.barrier()

        def phase4(l, xsrc, tiles):
            with ExitStack() as es:
                wout = sb(es, [128, 8, D], BF16); wr = sb(es, [128, 8, 16])
                g1b = [sb(es, [128, D]) for _ in range(2)]; scb = [sb(es, [128, D]) for _ in range(2)]; shb = [sb(es, [128, D]) for _ in range(2)]
                lg = sb(es, [128, D]); lb = sb(es, [128, D])
                cat = [sb(es, [128, 8, 128], BF16) for _ in range(2)]
                xt = [sb(es, [128, D]) for _ in range(2)]
                rr_ = [sb(es, [128, D]) for _ in range(2)]; x1 = [sb(es, [128, D]) for _ in range(2)]; hm = [sb(es, [128, D]) for _ in range(2)]
                hmT_ = [sb(es, [128, 8, 128]) for _ in range(2)]
                st6_ = [sb(es, [128, 4, 6]) for _ in range(2)]; mv_ = [sb(es, [128, 2]) for _ in range(2)]; rstd_ = [sb(es, [128, 1]) for _ in range(2)]; nmr_ = [sb(es, [128, 1]) for _ in range(2)]
                lgt_ = [sb(es, [128, 16]) for _ in range(2)]; mx_ = [sb(es, [128, 1]) for _ in range(2)]; ssum_ = [sb(es, [128, 1]) for _ in range(2)]; affT = sb(es, [16, T])
                for c in range(8):
                    S.dma('pool', lambda e, c=c: e.dma_start(out=wout[:, c, :], in_=w_out[l, c * 128:(c + 1) * 128, :]), w=['wout'])
                S.dma('sp', lambda e: e.dma_start(out=wr[:], in_=w_router[l].rearrange("(c p) n -> p c n", p=128)), w=['wr'])
                for m in range(2):
                    bcast_load(es, 'sp', g1b[m], 2, m, f'g1b{m}'); bcast_load(es, 'sp', scb[m], 4, m, f'scb{m}'); bcast_load(es, 'sp', shb[m], 3, m, f'shb{m}')
                    S.op('dve', lambda e, m=m: e.tensor_scalar(out=scb[m][:], in0=scb[m][:], scalar1=1.0, scalar2=None, op0=ALU.add), r=[f'scb{m}'], w=[f'scb{m}'])
                vec_bcast('sp', lg, ln1_g[l], 'lg'); vec_bcast('sp', lb, ln1_b[l], 'lb')
                def tile4(ti):
                    p2 = ti % 2; m = 1 if ti < 2 else 0
                    rr = rr_[p2]; hmT = hmT_[p2]; st6 = st6_[p2]; mv = mv_[p2]; rstd = rstd_[p2]; nmr = nmr_[p2]; lgt = lgt_[p2]; mx = mx_[p2]; ssum = ssum_[p2]
                    rk = f'rr{p2}'; hk = f'hmT{p2}'; lk = f'lgt{p2}'; mk = f'mx{p2}'; sk = f'ssum{p2}'
                    bA = ps[3 * p2]; bB = ps[3 * p2 + 1]; bC = ps[3 * p2 + 2]; kA = PK[3 * p2]; kB = PK[3 * p2 + 1]; kC = PK[3 * p2 + 2]
                    pbs = [bA, bB]; pks = [kA, kB]
                    with nc.allow_non_contiguous_dma(reason="catT tile"):
                        S.dma('sp', lambda e: e.dma_start(out=cat[p2][:], in_=catT_d[:, ti * 128:(ti + 1) * 128].rearrange("(c p) t -> p c t", p=128)), w=[f'cat{p2}'])
                    S.dma('sp', lambda e: e.dma_start(out=xt[p2][:], in_=xsrc[ti * 128:(ti + 1) * 128, :]), w=[f'xt{p2}'])
                    yield
                    for half in range(2):
                        for c in range(8):
                            MM(pbs[half][:, :], cat[p2][:, c, :], wout[:, c, half * 512:(half + 1) * 512], c == 0, c == 7, [f'cat{p2}', 'wout'], [pks[half]])
                        yield
                        S.op('dve', lambda e: e.tensor_tensor(out=rr[:, half * 512:(half + 1) * 512], in0=pbs[half][:, :], in1=g1b[m][:, half * 512:(half + 1) * 512], op=ALU.mult),
                             r=[pks[half], f'g1b{m}'], w=[rk])
                        yield
                    S.op('dve', lambda e: e.scalar_tensor_tensor(out=rr[:], in0=xt[p2][:], scalar=ALPHA, in1=rr[:], op0=ALU.mult, op1=ALU.add), r=[f'xt{p2}', rk], w=[rk])
                    yield
                    ln_stats(f'l4{p2}', rr, D, mv, rstd, nmr, st6, [rk])
                    yield
                    S.op('act', lambda e: e.activation(out=rr[:], in_=rr[:], func=AF.Identity, bias=nmr[:, 0:1], scale=rstd[:, 0:1]), r=[rk, f'l4{p2}rs', f'l4{p2}nm'], w=[rk])
                    yield
                    S.op('dve', lambda e: e.tensor_tensor(out=rr[:], in0=rr[:], in1=lg[:], op=ALU.mult), r=[rk, 'lg'], w=[rk])
                    yield
                    S.op('dve', lambda e: e.tensor_tensor(out=x1[p2][:], in0=rr[:], in1=lb[:], op=ALU.add), r=[rk, 'lb'], w=[f'x1{p2}'])
                    S.dma('sp', lambda e: e.dma_start(out=x1_d[ti * 128:(ti + 1) * 128, :], in_=x1[p2][:]), r=[f'x1{p2}'], w=[('x1_d', ti)])
                    yield
                    ln_stats(f'l5{p2}', x1[p2], D, mv, rstd, nmr, st6, [f'x1{p2}'])
                    yield
                    S.op('act', lambda e: e.activation(out=rr[:], in_=x1[p2][:], func=AF.Identity, bias=nmr[:, 0:1], scale=rstd[:, 0:1]), r=[f'x1{p2}', f'l5{p2}rs', f'l5{p2}nm'], w=[rk])
                    yield
                    S.op('dve', lambda e: e.tensor_tensor(out=rr[:], in0=rr[:], in1=scb[m][:], op=ALU.mult), r=[rk, f'scb{m}'], w=[rk])
                    yield
                    S.op('dve', lambda e: e.tensor_tensor(out=hm[p2][:], in0=rr[:], in1=shb[m][:], op=ALU.add), r=[rk, f'shb{m}'], w=[f'hm{p2}'])
                    S.dma('sp', lambda e: e.dma_start(out=hm_d[ti * 128:(ti + 1) * 128, :], in_=hm[p2][:]), r=[f'hm{p2}'], w=[('hm_d', ti)])
                    yield
                    for half in range(2):
                        for c4 in range(4):
                            TR(pbs[half][:, c4 * 128:(c4 + 1) * 128], hm[p2][:, (half * 4 + c4) * 128:(half * 4 + c4 + 1) * 128], [f'hm{p2}'], [pks[half]])
                        yield
                        if half == 0:
                            S.op('act', lambda e: e.copy(out=hmT[:, 0:4, :].rearrange("p c t -> p (c t)"), in_=pbs[0][:, :]), r=[pks[0]], w=[hk])
                        else:
                            S.op('dve', lambda e: e.tensor_copy(hmT[:, 4:8, :].rearrange("p c t -> p (c t)"), pbs[1][:, :]), r=[pks[1]], w=[hk])
                        yield
                    for c in range(8):
                        MM(bC[:, 0:16], hmT[:, c, :], wr[:, c, :], c == 0, c == 7, [hk, 'wr'], [kC])
                    yield
                    S.op('dve', lambda e: e.tensor_reduce(out=mx[:], in_=bC[:, 0:16], axis=AX.X, op=ALU.max), r=[kC], w=[mk])
                    yield
                    S.op('dve', lambda e: e.tensor_scalar(out=mx[:], in0=mx[:], scalar1=-1.0, scalar2=None, op0=ALU.mult), r=[mk], w=[mk])
                    yield
                    S.op('act', lambda e: e.activation(out=lgt[:], in_=bC[:, 0:16], func=AF.Exp, bias=mx[:, 0:1], scale=1.0, accum_out=ssum[:]), r=[kC, mk], w=[lk, sk])
                    yield
                    S.op('dve', lambda e: e.reciprocal(ssum[:], ssum[:]), r=[sk], w=[sk])
                    yield
                    S.op('dve', lambda e: e.tensor_scalar(out=lgt[:], in0=lgt[:], scalar1=ssum[:, 0:1], scalar2=None, op0=ALU.mult), r=[lk, sk], w=[lk])
                    yield
                    TR(bC[0:16, 128:256], lgt[:, :], [lk], [kC])
                    yield
                    S.op('dve', lambda e: e.tensor_copy(affT[:, ti * 128:(ti + 1) * 128], bC[0:16, 128:256]), r=[kC], w=['affT'])
                    yield
                interleave([tile4(ti) for ti in tiles], 2)
                t_lo = tiles[0] * 128
                S.dma('sp', lambda e: e.dma_start(out=aff_d[:, t_lo:T], in_=affT[:, t_lo:T]), r=['affT'], w=['aff_d'])
                S.barrier()

        def phase56(l, with_ctx):
            with ExitStack() as es:
                idxT = sb(es, [128, 5, 16], U32); gateT = sb(es, [128, 5, 16])
                wg = [sb(es, [128, 8, D], BF16) for _ in range(2)]; wu = [sb(es, [128, 8, D], BF16) for _ in range(2)]; wd = [sb(es, [128, 8, D], BF16) for _ in range(2)]

                def load_w(e_):
                    ep = e_ % 2
                    for c in range(8):
                        S.dma('pool', lambda e, c=c: e.dma_start(out=wg[ep][:, c, :], in_=w_gate[l, e_, c * 128:(c + 1) * 128, :]), w=[f'wg{ep}'])
                        S.dma('pool', lambda e, c=c: e.dma_start(out=wu[ep][:, c, :], in_=w_up[l, e_, c * 128:(c + 1) * 128, :]), w=[f'wu{ep}'])
                        S.dma('pool', lambda e, c=c: e.dma_start(out=wd[ep][:, c, :], in_=w_down[l, e_, c * 128:(c + 1) * 128, :]), w=[f'wd{ep}'])
                load_w(0)
                es2 = ExitStack()
                aw = sb(es2, [16, TL]); vals = sb(es2, [16, 512]); idx = sb(es2, [16, 512], U32); idxf = sb(es2, [16, 512])
                awc = sb(es2, [16, 256]); valsc = sb(es2, [16, 32]); idxc = sb(es2, [16, 32], U32); idxcf = sb(es2, [16, 32])
                S.dma('sp', lambda e: e.dma_start(out=aw[:], in_=aff_d[:, 256:T]), w=['aw'])
                for r_ in range(64):
                    S.op('dve', lambda e, r_=r_: e.max(out=vals[:, r_ * 8:(r_ + 1) * 8], in_=aw[:]), r=['aw'], w=['vals'])
                    S.op('dve', lambda e, r_=r_: e.max_index(out=idx[:, r_ * 8:(r_ + 1) * 8], in_max=vals[:, r_ * 8:(r_ + 1) * 8], in_values=aw[:]), r=['aw', 'vals'], w=['idx'])
                    S.op('dve', lambda e, r_=r_: e.match_replace(out=aw[:], in_to_replace=vals[:, r_ * 8:(r_ + 1) * 8], in_values=aw[:], imm_value=-1.0), r=['aw', 'vals'], w=['aw'])
                S.op('dve', lambda e: e.tensor_copy(idxf[:], idx[:]), r=['idx'], w=['idxf'])
                S.op('dve', lambda e: e.tensor_scalar(out=idxf[:], in0=idxf[:], scalar1=256.0, scalar2=None, op0=ALU.add), r=['idxf'], w=['idxf'])
                for st in range(4):
                    TR(ps[0][:, 0:16], idxf[:, st * 128:(st + 1) * 128], ['idxf'], [PK[0]], n=16)
                    S.op('dve', lambda e, st=st: e.tensor_copy(idxT[:, st, :], ps[0][:, 0:16]), r=[PK[0]], w=['idxT'])
                    TR(ps[1][:, 0:16], vals[:, st * 128:(st + 1) * 128], ['vals'], [PK[1]], n=16)
                    S.op('dve', lambda e, st=st: e.tensor_copy(gateT[:, st, :], ps[1][:, 0:16]), r=[PK[1]], w=['gateT'])
                if with_ctx:
                    S.dma('sp', lambda e: e.dma_start(out=awc[:], in_=aff_d[:, 0:256]), w=['awc'])
                    for r_ in range(4):
                        S.op('dve', lambda e, r_=r_: e.max(out=valsc[:, r_ * 8:(r_ + 1) * 8], in_=awc[:]), r=['awc'], w=['valsc'])
                        S.op('dve', lambda e, r_=r_: e.max_index(out=idxc[:, r_ * 8:(r_ + 1) * 8], in_max=valsc[:, r_ * 8:(r_ + 1) * 8], in_values=awc[:]), r=['awc', 'valsc'], w=['idxc'])
                        S.op('dve', lambda e, r_=r_: e.match_replace(out=awc[:], in_to_replace=valsc[:, r_ * 8:(r_ + 1) * 8], in_values=awc[:], imm_value=-1.0), r=['awc', 'valsc'], w=['awc'])
                    S.op('dve', lambda e: e.tensor_copy(idxcf[:], idxc[:]), r=['idxc'], w=['idxcf'])
                    TR(ps[0][0:32, 0:16], idxcf[:, :], ['idxcf'], [PK[0]], n=16)
                    S.op('dve', lambda e: e.tensor_copy(idxT[0:32, 4, :], ps[0][0:32, 0:16]), r=[PK[0]], w=['idxT'])
                    TR(ps[1][0:32, 0:16], valsc[:, :], ['valsc'], [PK[1]], n=16)
                    S.op('dve', lambda e: e.tensor_copy(gateT[0:32, 4, :], ps[1][0:32, 0:16]), r=[PK[1]], w=['gateT'])
                S.barrier(); es2.close()
                NS = 544 if with_ctx else 512
                nst = 5 if with_ctx else 4
                xe = [sb(es, [128, 5, D]) for _ in range(2)]
                xeT = sb(es, [128, 8, 544], BF16); hidT = sb(es, [128, 8, 544], BF16)
                sg = [sb(es, [128, 544]) for _ in range(2)]
                ye = [sb(es, [128, D]) for _ in range(2)]

                def load_expert(e_):
                    ep = e_ % 2
                    if e_ > 0: load_w(e_)
                    for st in range(nst):
                        n = 128 if st < 4 else 32
                        S.dma('pool', lambda e, st=st, n=n: e.indirect_dma_start(out=xe[ep][0:n, st, :], out_offset=None, in_=hm_d[:, :],
                                                                              in_offset=bass.IndirectOffsetOnAxis(ap=idxT[0:n, st, e_:e_ + 1], axis=0)),
                              r=['idxT'], w=[f'xe{ep}'])

                load_expert(0)
                yc_ = 0
                for e_ in range(16):
                    ep = e_ % 2
                    if e_ + 1 < 16: load_expert(e_ + 1)
                    for st in range(nst):
                        n = 128 if st < 4 else 32
                        for half in range(2):
                            for c4 in range(4):
                                cc = half * 4 + c4
                                TR(ps[half][:, c4 * 128:c4 * 128 + n], xe[ep][0:n, st, cc * 128:(cc + 1) * 128], [f'xe{ep}'], [PK[half]], n=n)
                            src = ps[half][:, :].rearrange("p (c t) -> p c t", c=4)[:, :, 0:n]
                            if half == 0:
                                S.op('act', lambda e, st=st, n=n, src=src: e.copy(out=xeT[:, 0:4, st * 128:st * 128 + n], in_=src), r=[PK[0]], w=['xeT'])
                            else:
                                S.op('dve', lambda e, st=st, n=n, src=src: e.tensor_copy(xeT[:, 4:8, st * 128:st * 128 + n], src), r=[PK[1]], w=['xeT'])
                    for fc in range(8):
                        for (n0, n1) in ([(0, 512), (512, 544)] if with_ctx else [(0, 512)]):
                            pg = ps[2] if n0 == 0 else ps[4]; pu = ps[3] if n0 == 0 else ps[5]
                            pgk = PK[2] if n0 == 0 else PK[4]; puk = PK[3] if n0 == 0 else PK[5]
                            nn = n1 - n0
                            for c in range(8):
                                MM(pg[:, 0:nn], wg[ep][:, c, fc * 128:(fc + 1) * 128], xeT[:, c, n0:n1], c == 0, c == 7, [f'wg{ep}', 'xeT'], [pgk])
                            for c in range(8):
                                MM(pu[:, 0:nn], wu[ep][:, c, fc * 128:(fc + 1) * 128], xeT[:, c, n0:n1], c == 0, c == 7, [f'wu{ep}', 'xeT'], [puk])
                            sp2 = fc % 2
                            S.op('act', lambda e, pg=pg, nn=nn, sp2=sp2: e.activation(out=sg[sp2][:, 0:nn], in_=pg[:, 0:nn], func=AF.Silu), r=[pgk], w=[f'sg{sp2}'])
                            S.op('dve', lambda e, pu=pu, nn=nn, n0=n0, n1=n1, fc=fc, sp2=sp2: e.tensor_tensor(out=hidT[:, fc, n0:n1], in0=pu[:, 0:nn], in1=sg[sp2][:, 0:nn], op=ALU.mult), r=[puk, f'sg{sp2}'], w=['hidT'])
                    for st in range(nst):
                        n = 128 if st < 4 else 32
                        yp = yc_ % 2; yc_ += 1
                        for half in range(2):
                            for fc in range(8):
                                MM(ps[6 + half][0:n, :], hidT[:, fc, st * 128:st * 128 + n], wd[ep][:, fc, half * 512:(half + 1) * 512], fc == 0, fc == 7, ['hidT', f'wd{ep}'], [PK[6 + half]])
                            eng = 'act' if half == 0 else 'dve'
                            if half == 0:
                                S.op('act', lambda e, n=n, st=st, yp=yp: e.activation(out=ye[yp][0:n, 0:512], in_=ps[6][0:n, :], func=AF.Identity, scale=gateT[0:n, st, e_:e_ + 1]), r=[PK[6], 'gateT'], w=[f'ye{yp}'])
                            else:
                                S.op('dve', lambda e, n=n, st=st, yp=yp: e.tensor_scalar(out=ye[yp][0:n, 512:1024], in0=ps[7][0:n, :], scalar1=gateT[0:n, st, e_:e_ + 1], scalar2=None, op0=ALU.mult), r=[PK[7], 'gateT'], w=[f'ye{yp}'])
                        S.dma('pool', lambda e, st=st, n=n, yp=yp: e.indirect_dma_start(out=moe_d[:, :], out_offset=bass.IndirectOffsetOnAxis(ap=idxT[0:n, st, e_:e_ + 1], axis=0),
                                                                                  in_=ye[yp][0:n, :], in_offset=None, compute_op=ALU.add),
                              r=[f'ye{yp}', 'idxT'], w=['moe_d'])
                S.barrier()

        def phase7(l, dst, tiles, final):
            with ExitStack() as es:
                g2b = [sb(es, [128, D]) for _ in range(2)]
                lg = sb(es, [128, D]); lb = sb(es, [128, D])
                xt = [sb(es, [128, D]) for _ in range(2)]; ft = [sb(es, [128, D]) for _ in range(2)]
                rr_ = [sb(es, [128, D]) for _ in range(2)]; xo = [sb(es, [128, D]) for _ in range(2)]
                st6_ = [sb(es, [128, 4, 6]) for _ in range(2)]; mv_ = [sb(es, [128, 2]) for _ in range(2)]; rstd_ = [sb(es, [128, 1]) for _ in range(2)]; nmr_ = [sb(es, [128, 1]) for _ in range(2)]
                for m in range(2):
                    bcast_load(es, 'sp', g2b[m], 5, m, f'g2b{m}')
                vec_bcast('sp', lg, ln2_g[l], 'lg'); vec_bcast('sp', lb, ln2_b[l], 'lb')

                def tile7(ti):
                    p2 = ti % 2; m = 1 if ti < 2 else 0
                    rr = rr_[p2]; st6 = st6_[p2]; mv = mv_[p2]; rstd = rstd_[p2]; nmr = nmr_[p2]; rk = f'rr{p2}'
                    S.dma('sp', lambda e: e.dma_start(out=xt[p2][:], in_=x1_d[ti * 128:(ti + 1) * 128, :]), w=[f'xt{p2}'])
                    S.dma('sp', lambda e: e.dma_start(out=ft[p2][:], in_=moe_d[ti * 128:(ti + 1) * 128, :]), w=[f'ft{p2}'])
                    yield
                    S.op('dve', lambda e: e.tensor_tensor(out=rr[:], in0=ft[p2][:], in1=g2b[m][:], op=ALU.mult), r=[f'ft{p2}', f'g2b{m}'], w=[rk])
                    yield
                    S.op('dve', lambda e: e.scalar_tensor_tensor(out=rr[:], in0=xt[p2][:], scalar=ALPHA, in1=rr[:], op0=ALU.mult, op1=ALU.add), r=[f'xt{p2}', rk], w=[rk])
                    yield
                    ln_stats(f'l7{p2}', rr, D, mv, rstd, nmr, st6, [rk])
                    yield
                    S.op('act', lambda e: e.activation(out=rr[:], in_=rr[:], func=AF.Identity, bias=nmr[:, 0:1], scale=rstd[:, 0:1]), r=[rk, f'l7{p2}rs', f'l7{p2}nm'], w=[rk])
                    yield
                    S.op('dve', lambda e: e.tensor_tensor(out=rr[:], in0=rr[:], in1=lg[:], op=ALU.mult), r=[rk, 'lg'], w=[rk])
                    yield
                    S.op('dve', lambda e: e.tensor_tensor(out=xo[p2][:], in0=rr[:], in1=lb[:], op=ALU.add), r=[rk, 'lb'], w=[f'xo{p2}'])
                    if final:
                        S.dma('sp', lambda e: e.dma_start(out=dst[(ti - 2) * 128:(ti - 1) * 128, :], in_=xo[p2][:]), r=[f'xo{p2}'], w=[('out', ti)])
                    else:
                        S.dma('sp', lambda e: e.dma_start(out=dst[ti * 128:(ti + 1) * 128, :], in_=xo[p2][:]), r=[f'xo{p2}'], w=[('out', ti)])
                    yield
                interleave([tile7(ti) for ti in tiles], 2)
                S.barrier()

        for l in range(nlayers):
            last = (l == 1)
            xsrc = xin if l == 0 else xs1
            tiles = list(range(2, NT)) if last else list(range(NT))
            phase0(l)
            if stop == 'p0': break
            phase1(l, xsrc)
            if stop == 'p1': break
            phase2(l, with_ctx_q=not last)
            if stop == 'p2': break
            phase3(l)
            if stop == 'p3': break
            phase4(l, xsrc, tiles)
            if stop == 'p4': break
            phase56(l, with_ctx=not last)
            if stop == 'p6': break
            phase7(l, y if last else xs1, tiles, final=last)
        S.barrier()
    return nc, dbg_outs


def _consts():
    c = {}
    c['c_ident'] = np.eye(128, dtype=np.float32)
    t = np.arange(TL)
    row = (t // 64).astype(np.float32); col = (t % 64).astype(np.float32)
    inv = (10000.0 ** (-np.arange(8, dtype=np.float32) * 2.0 / 16)).astype(np.float32)
    ang = np.concatenate([row[:, None] * inv, col[:, None] * inv], axis=-1).astype(np.float32)
    cos = np.cos(ang).astype(np.float32); sin = np.sin(ang).astype(np.float32)
    cosT = np.ones((32, T), np.float32); sinT = np.zeros((32, T), np.float32)
    for a in range(2):
        for i in range(8):
            cosT[a * 16 + i, TC:] = cos[:, a * 8 + i]; cosT[a * 16 + 8 + i, TC:] = cos[:, a * 8 + i]
            sinT[a * 16 + i, TC:] = -sin[:, a * 8 + i]; sinT[a * 16 + 8 + i, TC:] = sin[:, a * 8 + i]
    c['c_rope'] = np.stack([cosT, sinT]).astype(np.float32)
    s = np.arange(64)[:, None]; tt = np.arange(64)[None, :]
    m0 = (s <= tt).astype(np.float32); m1 = (s >= tt).astype(np.float32)
    c['c_mask'] = np.stack([np.concatenate([m0, m0], 0), np.concatenate([m1, m1], 0)]).astype(np.float32)
    misc = np.zeros((8, 16), np.float32)
    misc[4:, 0] = 1.0; misc[:4, 1] = 1.0; misc[4:, 1] = -1.0; misc[:4, 2] = 1.0
    c['c_misc'] = misc
    e8 = np.zeros((8, 8, NSTEP), np.float32)
    for k in range(8): e8[k, k, :] = 1.0
    c['c_eye8'] = e8
    pr = np.zeros((NSTEP, NSTEP), np.float32)
    for p in range(NSTEP):
        cidx = 3 - p if p < 4 else 71 - p
        pr[cidx, p] = 1.0
    c['c_prev'] = pr
    rst = np.ones((8, T), np.float32); rst[:, ::64] = 0.0
    c['c_rst'] = rst
    return c


def _prep_weights(inp):
    w = {}
    sw = np.arange(32)
    for a in range(2):
        for i in range(8):
            sw[a * 16 + i] = a * 16 + 8 + i; sw[a * 16 + 8 + i] = a * 16 + i
    cols = np.concatenate([np.arange(0, 512), np.arange(1696, 2208),
                           np.arange(512, 1152), np.arange(1152, 1184), 1152 + sw,
                           np.arange(1184, 1696), np.arange(2208, 2224)])
    assert cols.size == NWIN
    w['w_in'] = np.ascontiguousarray(inp['w_in'][:, :, cols]); w['b_in'] = np.ascontiguousarray(inp['b_in'][:, cols])
    uq = inp['w_uq']
    swc = np.concatenate([h * 96 + 64 + sw for h in range(8)])
    w['w_uq'] = np.ascontiguousarray(np.concatenate([uq, uq[:, :, swc]], axis=-1))
    ukv = inp['w_ukv']
    kc = np.concatenate([np.arange(h * 128, h * 128 + 64) for h in range(8)])
    vc = np.concatenate([np.arange(h * 128 + 64, h * 128 + 128) for h in range(8)])
    w['w_ukv'] = np.ascontiguousarray(np.concatenate([ukv[:, :, kc], ukv[:, :, vc]], axis=-1))
    w['ml_f_bias'] = np.ascontiguousarray(inp['ml_f_bias'].reshape(2, 8))
    for k in ['w_ada', 'b_ada', 'sg_ln_g', 'sg_ln_b', 'sg_w', 'sg_b', 'q_norm_g', 'kv_norm_g', 'ml_conv_w', 'ml_conv_b', 'ml_norm_g',
              'w_out', 'ln1_g', 'ln1_b', 'w_router', 'w_gate', 'w_up', 'w_down', 'ln2_g', 'ln2_b']:
        w[k] = np.ascontiguousarray(inp[k])
    return w


_CACHE = {}


def kernel(**inputs):
    inp = {k: np.asarray(v, dtype=np.float32) for k, v in inputs.items()}
    if 'nc' not in _CACHE:
        _CACHE['nc'] = build()[0]
    nc = _CACHE['nc']
    consts = _consts(); w = _prep_weights(inp)
    in_maps = []
    for b in range(8):
        m = dict(consts); m.update(w)
        m['xin'] = np.ascontiguousarray(np.concatenate([inp['ctx'][b], inp['x'][b]], axis=0))
        m['cvec'] = np.ascontiguousarray(np.stack([inp['c'][b], inp['c_ctx']]))
        in_maps.append(m)
    res = run_bass_kernel_spmd(nc, in_maps, core_ids=list(range(8)))
    return np.stack([np.asarray(r['y'], dtype=np.float32) for r in res.results], axis=0)
```
